# Optimizing a Trainium2 kernel written in Bass

```python
import jax
import jax.numpy as jnp
from jax import lax
import numpy as np

D_MODEL = 2048
BATCH = 2
SEQ = 8192
DEPTH = 4

GRID_W = 64
CTX_LEN = 256
N_MIXERS = 3
N_MOD = 9
D_FF = 5632
NORM_EPS = 1e-6
CONV_WIDTH = 31
GLA_HEADS = 4
GLA_DK = D_MODEL // (2 * GLA_HEADS)
GLA_DV = D_MODEL // GLA_HEADS
GLA_GATE_RANK = 16
GLA_GATE_NORMALIZER = 16.0
GLA_CHUNK = 64
ROPE_THETA = 10000.0
NA_HEADS = 16
NA_HEAD_DIM = D_MODEL // NA_HEADS
NA_WIN_R = 8
NA_WIN_C = 16

kernel_name = 'hybrid_conv_gla_natten_dit'


def rms_norm(x, g):
    xf = x.astype(jnp.float32)
    y = xf * lax.rsqrt(jnp.mean(xf * xf, axis=-1, keepdims=True) + NORM_EPS)
    return (y * g.astype(jnp.float32)).astype(x.dtype)


def layer_norm(x, g, b):
    xf = x.astype(jnp.float32)
    mu = jnp.mean(xf, axis=-1, keepdims=True)
    var = jnp.mean(jnp.square(xf - mu), axis=-1, keepdims=True)
    y = (xf - mu) * lax.rsqrt(var + NORM_EPS) * g.astype(jnp.float32) + b.astype(jnp.float32)
    return y.astype(x.dtype)


def modulate(x, g, shift, scale):
    return rms_norm(x, g) * (1 + scale) + shift


def sandwich(x, fn, g_pre, g_post, shift, scale, gate, weight):
    return x + weight * gate * rms_norm(fn(modulate(x, g_pre, shift, scale)), g_post)


def swiglu(h, w_gate, w_up, w_down):
    return (jax.nn.silu(h @ w_gate) * (h @ w_up)) @ w_down


def conv_module(h, w_in, b_in, w_dw, b_dw, ln_g, ln_b, w_out, b_out):
    a, g = jnp.split(h @ w_in + b_in, 2, axis=-1)
    u = a * jax.nn.sigmoid(g)
    u = lax.conv_general_dilated(u, w_dw[:, None, :], window_strides=(1,),
                                 padding=[(CONV_WIDTH // 2, CONV_WIDTH // 2)],
                                 dimension_numbers=('NWC', 'WIO', 'NWC'),
                                 feature_group_count=u.shape[-1]) + b_dw
    return jax.nn.silu(layer_norm(u, ln_g, ln_b)) @ w_out + b_out


def rope_1d(t, pos):
    n = t.shape[-1]
    inv = ROPE_THETA ** (-jnp.arange(0, n, 2, dtype=jnp.float32) / n)
    ang = pos.astype(jnp.float32)[:, None] * inv
    cos, sin = jnp.cos(ang)[:, None, :], jnp.sin(ang)[:, None, :]
    t1, t2 = jnp.split(t.astype(jnp.float32), 2, axis=-1)
    return jnp.concatenate([t1 * cos - t2 * sin, t1 * sin + t2 * cos], axis=-1)


def axial_rope(t, rows, cols):
    tr, tc = jnp.split(t, 2, axis=-1)
    return jnp.concatenate([rope_1d(tr, rows), rope_1d(tc, cols)], axis=-1)


def gla_chunk_scan(q, k, v, log_a, s0):
    b, h, l, _ = q.shape
    n = l // GLA_CHUNK

    def chunks(t):
        return jnp.moveaxis(t.reshape(b, h, n, GLA_CHUNK, t.shape[-1]), 2, 0)

    order_mask = jnp.tril(jnp.ones((GLA_CHUNK, GLA_CHUNK), dtype=bool))

    def step(s, inp):
        qc, kc, vc, ac = inp
        cum = jnp.cumsum(ac, axis=-2)
        last = cum[..., -1:, :]
        qe = qc * jnp.exp(cum)
        scores = jnp.einsum('bhtd,bhsd->bhts', qe, kc * jnp.exp(-cum))
        scores = jnp.where(order_mask, scores, 0.0)
        o = jnp.einsum('bhtd,bhdv->bhtv', qe, s) + jnp.einsum('bhts,bhsv->bhtv', scores, vc)
        s = jnp.exp(last)[..., 0, :, None] * s + jnp.einsum('bhsd,bhsv->bhdv', kc * jnp.exp(last - cum), vc)
        return s, o

    s_fin, o = lax.scan(step, s0, (chunks(q), chunks(k), chunks(v), chunks(log_a)))
    return jnp.moveaxis(o, 0, 2).reshape(b, h, l, v.shape[-1]), s_fin


def gla_mixer(h, hc, ctx_out, w_q, w_k, w_v, w_r, b_r, w_a1, w_a2, b_a, norm_g, w_o):
    def project(t, pos):
        bt, lt, _ = t.shape
        q = (t @ w_q).reshape(bt, lt, GLA_HEADS, GLA_DK)
        k = (t @ w_k).reshape(bt, lt, GLA_HEADS, GLA_DK)
        if pos is not None:
            q, k = axial_rope(q, *pos), axial_rope(k, *pos)
        v = t @ w_v

        def log_gate(d):
            z = (t @ w_a1[d]) @ w_a2[d] + b_a[d]
            return jax.nn.log_sigmoid(z.astype(jnp.float32)) / GLA_GATE_NORMALIZER

        def heads(a):
            return jnp.swapaxes(a.reshape(bt, lt, GLA_HEADS, -1), 1, 2).astype(jnp.float32)

        return heads(q) * GLA_DK ** -0.5, heads(k), heads(v), heads(log_gate(0)), heads(log_gate(1))

    def bidirectional(qkva, s_f, s_b):
        q, k, v, la_f, la_b = qkva
        o_f, s_f = gla_chunk_scan(q, k, v, la_f, s_f)
        fl = lambda a: jnp.flip(a, axis=2)
        o_b, s_b = gla_chunk_scan(fl(q), fl(k), fl(v), fl(la_b), s_b)
        return o_f + fl(o_b), s_f, s_b

    def readout(t, o):
        bt, lt, _ = t.shape
        o = rms_norm(jnp.swapaxes(o, 1, 2), norm_g).astype(t.dtype)
        r = jax.nn.silu(t @ w_r + b_r).reshape(bt, lt, GLA_HEADS, GLA_DV)
        return (o * r).reshape(bt, lt, GLA_HEADS * GLA_DV) @ w_o

    s0 = jnp.zeros((hc.shape[0], GLA_HEADS, GLA_DK, GLA_DV), jnp.float32)
    o_c, s_f, s_b = bidirectional(project(hc, None), s0, s0)
    t = jnp.arange(h.shape[1])
    o, _, _ = bidirectional(project(h, (t // GRID_W, t % GRID_W)), s_f, s_b)
    y = readout(h, o)
    yc = readout(hc, o_c) if ctx_out else None
    return y, yc


def na_mixer(h, hc, ctx_out, w_qkv, rpb, w_o):
    b, l, d = h.shape
    lc = hc.shape[1]
    rows = l // GRID_W
    kr = min(NA_WIN_R, rows)
    scale = NA_HEAD_DIM ** -0.5
    q_c, k_c, v_c = (hc @ w_qkv).reshape(b, lc, 3, NA_HEADS, NA_HEAD_DIM).transpose(2, 0, 3, 1, 4)
    qkv = (h @ w_qkv).reshape(b, rows, GRID_W, 3, NA_HEADS, NA_HEAD_DIM).transpose(3, 1, 0, 4, 2, 5)
    q, k, v = qkv[0], qkv[1], qkv[2]
    col = jnp.arange(GRID_W)
    col_idx = jnp.clip(col - NA_WIN_C // 2, 0, GRID_W - NA_WIN_C)[:, None] + jnp.arange(NA_WIN_C)
    col_bias_idx = col_idx - col[:, None] + (NA_WIN_C - 1)

    def row_block(args):
        q_r, r = args
        rs = jnp.clip(r - kr // 2, 0, rows - kr)
        k_b = lax.dynamic_slice_in_dim(k, rs, kr, axis=0)[:, :, :, col_idx]
        v_b = lax.dynamic_slice_in_dim(v, rs, kr, axis=0)[:, :, :, col_idx]
        row_bias_idx = rs + jnp.arange(kr) - r + (NA_WIN_R - 1)
        bias = rpb[:, row_bias_idx[None, :, None], col_bias_idx[:, None, :]]
        s_lat = jnp.einsum('bhqd,rbhqjd->bhqrj', q_r, k_b) * scale + bias
        s_ctx = jnp.einsum('bhqd,bhkd->bhqk', q_r, k_c) * scale
        s = jnp.concatenate([s_lat.reshape(b, NA_HEADS, GRID_W, kr * NA_WIN_C), s_ctx], axis=-1)
        p = jax.nn.softmax(s.astype(jnp.float32), axis=-1).astype(v.dtype)
        p_lat = p[..., :kr * NA_WIN_C].reshape(b, NA_HEADS, GRID_W, kr, NA_WIN_C)
        return (jnp.einsum('bhqrj,rbhqjd->bhqd', p_lat, v_b)
                + jnp.einsum('bhqk,bhkd->bhqd', p[..., kr * NA_WIN_C:], v_c))

    o = lax.map(row_block, (q, jnp.arange(rows)))
    y = o.transpose(1, 0, 3, 2, 4).reshape(b, l, d) @ w_o
    yc = None
    if ctx_out:
        p_c = jax.nn.softmax(jnp.einsum('bhqd,bhkd->bhqk', q_c, k_c).astype(jnp.float32) * scale, axis=-1)
        o_c = jnp.einsum('bhqk,bhkd->bhqd', p_c.astype(v_c.dtype), v_c)
        yc = o_c.transpose(0, 2, 1, 3).reshape(b, lc, d) @ w_o
    return y, yc


def setup_inputs(seed: int = 0) -> dict:
    key = jax.random.key(seed)
    ks = iter(jax.random.split(key, 40))
    nrm = lambda shape, s: jax.random.normal(next(ks), shape, jnp.float32) * s
    D = D_MODEL
    n_conv, n_gla, n_na = (DEPTH + 2) // 3, (DEPTH + 1) // 3, DEPTH // 3
    return {
        'x': nrm((BATCH, SEQ, D), 1.0),
        'c': nrm((BATCH, D), 1.0),
        'ctx': nrm((BATCH, CTX_LEN, D), 1.0),
        'c_ctx': nrm((D,), 1.0),
        'ada_w': nrm((DEPTH, D, N_MOD * D), 0.5 * D ** -0.5),
        'ada_b': nrm((DEPTH, N_MOD * D), 0.02),
        'norm_g': 1.0 + nrm((DEPTH, 6, D), 0.02),
        'ffn_w_gate': nrm((DEPTH, 2, D, D_FF), D ** -0.5),
        'ffn_w_up': nrm((DEPTH, 2, D, D_FF), D ** -0.5),
        'ffn_w_down': nrm((DEPTH, 2, D_FF, D), D_FF ** -0.5),
        'conv_w_in': nrm((n_conv, D, 2 * D), D ** -0.5),
        'conv_b_in': nrm((n_conv, 2 * D), 0.02),
        'conv_w_dw': nrm((n_conv, CONV_WIDTH, D), CONV_WIDTH ** -0.5),
        'conv_b_dw': nrm((n_conv, D), 0.02),
        'conv_ln_g': 1.0 + nrm((n_conv, D), 0.02),
        'conv_ln_b': nrm((n_conv, D), 0.02),
        'conv_w_out': nrm((n_conv, D, D), D ** -0.5),
        'conv_b_out': nrm((n_conv, D), 0.02),
        'gla_w_q': nrm((n_gla, D, GLA_HEADS * GLA_DK), D ** -0.5),
        'gla_w_k': nrm((n_gla, D, GLA_HEADS * GLA_DK), D ** -0.5),
        'gla_w_v': nrm((n_gla, D, GLA_HEADS * GLA_DV), D ** -0.5),
        'gla_w_r': nrm((n_gla, D, GLA_HEADS * GLA_DV), D ** -0.5),
        'gla_b_r': nrm((n_gla, GLA_HEADS * GLA_DV), 0.02),
        'gla_w_a1': nrm((n_gla, 2, D, GLA_GATE_RANK), D ** -0.5),
        'gla_w_a2': nrm((n_gla, 2, GLA_GATE_RANK, GLA_HEADS * GLA_DK), GLA_GATE_RANK ** -0.5),
        'gla_b_a': nrm((n_gla, 2, GLA_HEADS * GLA_DK), 0.02),
        'gla_norm_g': 1.0 + nrm((n_gla, GLA_DV), 0.02),
        'gla_w_o': nrm((n_gla, GLA_HEADS * GLA_DV, D), (GLA_HEADS * GLA_DV) ** -0.5),
        'na_w_qkv': nrm((n_na, D, 3 * D), D ** -0.5),
        'na_rpb': nrm((n_na, NA_HEADS, 2 * NA_WIN_R - 1, 2 * NA_WIN_C - 1), 0.02),
        'na_w_o': nrm((n_na, D, D), D ** -0.5),
    }


def reference(x, c, ctx, c_ctx, ada_w, ada_b, norm_g, ffn_w_gate, ffn_w_up, ffn_w_down,
              conv_w_in, conv_b_in, conv_w_dw, conv_b_dw, conv_ln_g, conv_ln_b, conv_w_out, conv_b_out,
              gla_w_q, gla_w_k, gla_w_v, gla_w_r, gla_b_r, gla_w_a1, gla_w_a2, gla_b_a, gla_norm_g, gla_w_o,
              na_w_qkv, na_rpb, na_w_o):
    reads_ctx = [(i % N_MIXERS) != 0 for i in range(DEPTH)]
    x_lat, x_ctx = x, ctx
    for i in range(DEPTH):
        kind, j = i % N_MIXERS, i // N_MIXERS
        ctx_out = any(reads_ctx[i + 1:])
        ctx_in = ctx_out or reads_ctx[i]
        m = (jax.nn.silu(c) @ ada_w[i] + ada_b[i]).reshape(-1, N_MOD, 1, D_MODEL)
        mc = (jax.nn.silu(c_ctx) @ ada_w[i] + ada_b[i]).reshape(N_MOD, D_MODEL)
        g = norm_g[i]
        ffn1 = lambda t: swiglu(t, ffn_w_gate[i, 0], ffn_w_up[i, 0], ffn_w_down[i, 0])
        ffn2 = lambda t: swiglu(t, ffn_w_gate[i, 1], ffn_w_up[i, 1], ffn_w_down[i, 1])

        x_lat = sandwich(x_lat, ffn1, g[0], g[1], m[:, 0], m[:, 1], m[:, 2], 0.5)
        if ctx_in:
            x_ctx = sandwich(x_ctx, ffn1, g[0], g[1], mc[0], mc[1], mc[2], 0.5)

        h = modulate(x_lat, g[2], m[:, 3], m[:, 4])
        hc = modulate(x_ctx, g[2], mc[3], mc[4]) if ctx_in else None
        if kind == 0:
            conv = lambda t: conv_module(t, conv_w_in[j], conv_b_in[j], conv_w_dw[j], conv_b_dw[j],
                                         conv_ln_g[j], conv_ln_b[j], conv_w_out[j], conv_b_out[j])
            y = conv(h)
            yc = conv(hc) if ctx_out else None
        elif kind == 1:
            y, yc = gla_mixer(h, hc, ctx_out, gla_w_q[j], gla_w_k[j], gla_w_v[j], gla_w_r[j], gla_b_r[j],
                              gla_w_a1[j], gla_w_a2[j], gla_b_a[j], gla_norm_g[j], gla_w_o[j])
        else:
            y, yc = na_mixer(h, hc, ctx_out, na_w_qkv[j], na_rpb[j], na_w_o[j])
        x_lat = x_lat + m[:, 5] * rms_norm(y, g[3])

        x_lat = sandwich(x_lat, ffn2, g[4], g[5], m[:, 6], m[:, 7], m[:, 8], 0.5)
        if ctx_out:
            x_ctx = x_ctx + mc[5] * rms_norm(yc, g[3])
            x_ctx = sandwich(x_ctx, ffn2, g[4], g[5], mc[6], mc[7], mc[8], 0.5)
    return x_lat
```

```python
import numpy as np
import contextlib
import concourse.bass as bass
import concourse.mybir as mybir
from concourse.bass_utils import run_bass_kernel_spmd

F32 = mybir.dt.float32
BF16 = mybir.dt.bfloat16
AF = mybir.ActivationFunctionType
ALU = mybir.AluOpType
AX = mybir.AxisListType


class Res:
    __slots__ = ("w", "rd", "name", "excl")

    def __init__(self, name="", excl=False):
        self.w = None
        self.rd = {}
        self.name = name
        self.excl = excl


class Prog:
    ENGS = ("pe", "act", "dve", "pool", "sp")
    NDS = 8

    def __init__(self, nc, stack):
        self.nc = nc
        self.q = {e: [] for e in self.ENGS}
        self.sem = {e: stack.enter_context(nc.semaphore("s_" + e)) for e in self.ENGS}
        self.cnt = {e: 0 for e in self.ENGS}
        self.seen = {e: {} for e in self.ENGS}
        self.dq = ("sp", "act", "pool")
        self.dsem = {e: [stack.enter_context(nc.semaphore("d_%s%d" % (e, i))) for i in range(self.NDS)]
                     for e in self.dq}
        self.dcnt = {e: 0 for e in self.dq}
        self.nwaits = 0

    def _need(self, eng, tok, waits, kind):
        if tok is None:
            return
        key, semh, val, teng = tok
        if teng == eng and kind != "raw":
            return
        if teng == "pe" and eng == "pe":
            return
        if self.seen[eng].get(key, 0) >= val:
            return
        self.seen[eng][key] = val
        waits.append((semh, val))

    def emit(self, eng, fn, reads=(), writes=(), dma=False):
        waits = []
        for r in reads:
            self._need(eng, r.w, waits, "raw")
            if r.excl:
                for t in r.rd.values():
                    self._need(eng, t, waits, "rar")
        for r in writes:
            self._need(eng, r.w, waits, "waw")
            for t in r.rd.values():
                self._need(eng, t, waits, "war")
        if dma:
            i = self.dcnt[eng]
            self.dcnt[eng] += 1
            slot, k = i % self.NDS, i // self.NDS
            semh = self.dsem[eng][slot]
            key = "d_%s%d" % (eng, slot)
            if k > 0:
                self._need(eng, (key, semh, 16 * k, None), waits, "raw")
            tok = (key, semh, 16 * (k + 1), None)
            inc = 16
        else:
            self.cnt[eng] += 1
            semh = self.sem[eng]
            tok = ("e_" + eng, semh, self.cnt[eng], eng)
            inc = 1
        for r in reads:
            r.rd[tok[0]] = tok
        for r in writes:
            r.w = tok
            r.rd = {}
        self.nwaits += len(waits)
        self.q[eng].append((waits, fn, semh, inc))
        return tok

    def dma(self, q, out, in_, reads=(), writes=()):
        return self.emit(q, lambda e: e.dma_start(out=out, in_=in_), reads, writes, dma=True)

    def mm(self, out, lhsT, rhs, start, stop, reads=(), writes=()):
        return self.emit("pe", lambda e: e.matmul(out, lhsT, rhs, start=start, stop=stop), reads, writes)

    def final_wait(self, eng, toks):
        waits = []
        for t in toks:
            self._need(eng, t, waits, "raw")
        self.q[eng].append((waits, None, None, 0))

    def replay(self, block):
        nc = self.nc
        names = {"pe": "tensor", "act": "scalar", "dve": "vector", "pool": "gpsimd", "sp": "sync"}
        for e in self.ENGS:
            ops = self.q[e]

            def body(eng, ops=ops):
                for waits, fn, semh, inc in ops:
                    for (s, v) in waits:
                        eng.wait_ge(s, v)
                    if fn is not None:
                        fn(eng).then_inc(semh, inc)
            getattr(block, names[e])(body)


D = 2048
DFF = 5632
KC = 16
JC = 44
EPS = 1e-6
CW = 31
HALO = 15
WB_ELEMS = 4096
NWB = 5
GW = 2


def pvec(v):
    v = np.asarray(v, np.float32).reshape(-1)
    return np.ascontiguousarray(v.reshape(-1, 128).T)


class TL:
    def __init__(self, nc, st, P, TMAX, pv_dram, npv, conv=False, tcmax=None):
        self.nc, self.st, self.P, self.TMAX = nc, st, P, TMAX
        sb = lambda n, s, d: st.enter_context(nc.sbuf_tensor("s_" + n, s, d))
        ps = lambda n: st.enter_context(nc.psum_tensor(n, [128, 512], F32))
        self.X = sb("X", [128, KC, TMAX], F32)
        self.rX = [Res("X%d" % c) for c in range(KC)]
        self.Hb = sb("Hb", [128, KC, TMAX], BF16)
        self.rHb = [Res("Hb%d" % c) for c in range(KC)]
        self.HID = sb("HID", [128, JC, TMAX], BF16)
        self.rHID = [Res("HID%d" % j) for j in range(JC)]
        self.Y = sb("Y", [128, KC, TMAX], F32)
        self.rY = [Res("Y%d" % c) for c in range(KC)]
        self.WB = [sb("WB%d" % i, [128, WB_ELEMS], BF16) for i in range(NWB)]
        self.rWB = [Res("WB%d" % i) for i in range(NWB)]
        self.wbi = 0
        self.pv = sb("pv", [128, npv], F32)
        self.rpv = Res("pv")
        self.ones = sb("ones", [128, 128], BF16)
        self.rones = Res("ones")
        self.sq = [sb("sq%d" % i, [128, TMAX], BF16) for i in range(2)]
        self.rsq = [Res() for _ in range(2)]
        self.sqi = 0
        self.tmp = [sb("tmp%d" % i, [128, TMAX], F32) for i in range(3)]
        self.rtmp = [Res() for _ in range(3)]
        self.tmi = 0
        self.R = sb("R", [128, TMAX], F32)
        self.rR = Res("R")
        self.R2 = sb("R2", [128, TMAX], F32)
        self.rR2 = Res("R2")
        self.pA = [ps("pA%d" % i) for i in range(2)]
        self.rpA = [Res(excl=True) for _ in range(2)]
        self.pB = [ps("pB%d" % i) for i in range(2)]
        self.rpB = [Res(excl=True) for _ in range(2)]
        self.pY = [ps("pY%d" % i) for i in range(2)]
        self.rpY = [Res(excl=True) for _ in range(2)]
        self.pS = ps("pS")
        self.rpS = Res("pS", excl=True)
        self.pS2 = ps("pS2")
        self.rpS2 = Res("pS2", excl=True)
        self.pai = 0
        self.pyi = 0
        if conv:
            self.V = sb("V", [128, KC, tcmax], F32)
            self.rV = [Res() for _ in range(KC)]
            self.U = [sb("U%d" % i, [128, TMAX], F32) for i in range(2)]
            self.rU = [Res() for _ in range(2)]
            self.ui = 0
            self.acc = [sb("acc%d" % i, [128, tcmax], F32) for i in range(2)]
            self.racc = [Res() for _ in range(2)]
            self.MU = sb("MU", [128, tcmax], F32)
            self.rMU = Res()
            self.mask = sb("mask", [128, TMAX], F32)
            self.rmask = Res()
        P.dma("sp", self.pv[:], pv_dram, writes=[self.rpv])
        P.emit("dve", lambda e: e.memset(self.ones[:], 1.0), writes=[self.rones])

    def wload(self, src, a, b):
        i = self.wbi
        self.wbi = (i + 1) % NWB
        view = self.WB[i][:, 0:a * b].rearrange("p (a b) -> p a b", a=a)
        self.P.dma("pool", view, src, writes=[self.rWB[i]])
        return view, self.rWB[i]

    def next_tmp(self):
        i = self.tmi
        self.tmi = (i + 1) % 3
        return self.tmp[i], self.rtmp[i]

    def next_sq(self):
        i = self.sqi
        self.sqi = (i + 1) % 2
        return self.sq[i], self.rsq[i]

    def pvs(self, off, c):
        return self.pv[:, off + c:off + c + 1]

    def derive(self, out_off, a_off, b_off, mode, scale=1.0):
        P, pv = self.P, self.pv
        o = pv[:, out_off:out_off + KC]
        a = pv[:, a_off:a_off + KC]
        b = pv[:, b_off:b_off + KC]
        if mode == "a1pb":
            P.emit("dve", lambda e: e.scalar_tensor_tensor(out=o, in0=b, scalar=1.0, in1=a, op0=ALU.add, op1=ALU.mult),
                   reads=[self.rpv], writes=[self.rpv])
        else:
            P.emit("dve", lambda e: e.scalar_tensor_tensor(out=o, in0=a, scalar=scale, in1=b, op0=ALU.mult, op1=ALU.mult),
                   reads=[self.rpv], writes=[self.rpv])

    def rstd_from(self, pS, rpS, T, out, rout):
        P = self.P
        P.emit("dve", lambda e: e.tensor_scalar(out=out[:, :T], in0=pS[:, :T], scalar1=1.0 / D, scalar2=EPS,
                                                op0=ALU.mult, op1=ALU.add), reads=[rpS], writes=[rout])
        P.emit("act", lambda e: e.activation(out=out[:, :T], in_=out[:, :T], func=AF.Sqrt), reads=[rout], writes=[rout])
        P.emit("dve", lambda e: e.reciprocal(out=out[:, :T], in_=out[:, :T]), reads=[rout], writes=[rout])

    def norm_mod(self, xoff, T, a_off, sh_off):
        P = self.P
        for c in range(KC):
            sq, rsq = self.next_sq()
            P.emit("act", lambda e, c=c, sq=sq: e.activation(out=sq[:, :T], in_=self.X[:, c, xoff:xoff + T], func=AF.Square),
                   reads=[self.rX[c]], writes=[rsq])
            P.mm(self.pS[:, :T], self.ones[:], sq[:, :T], c == 0, c == KC - 1, reads=[self.rones, rsq], writes=[self.rpS])
        self.rstd_from(self.pS, self.rpS, T, self.R, self.rR)
        for c in range(KC):
            t, rt = self.next_tmp()
            P.emit("dve", lambda e, c=c, t=t: e.scalar_tensor_tensor(
                out=t[:, :T], in0=self.X[:, c, xoff:xoff + T], scalar=self.pvs(a_off, c), in1=self.R[:, :T],
                op0=ALU.mult, op1=ALU.mult), reads=[self.rX[c], self.rR, self.rpv], writes=[rt])
            P.emit("act", lambda e, c=c, t=t: e.activation(out=self.Hb[:, c, :T], in_=t[:, :T], func=AF.Identity,
                                                             bias=self.pvs(sh_off, c)),
                   reads=[rt, self.rpv], writes=[self.rHb[c]])

    def _y_evac(self, c, T, py, rpy, bias_off=None):
        P = self.P
        if bias_off is None:
            P.emit("act", lambda e: e.activation(out=self.Y[:, c, :T], in_=py[:, :T], func=AF.Copy),
                   reads=[rpy], writes=[self.rY[c]])
        else:
            P.emit("act", lambda e: e.activation(out=self.Y[:, c, :T], in_=py[:, :T], func=AF.Identity,
                                                 bias=self.pvs(bias_off, c)),
                   reads=[rpy, self.rpv], writes=[self.rY[c]])
        sq, rsq = self.next_sq()
        P.emit("dve", lambda e: e.tensor_tensor(out=sq[:, :T], in0=self.Y[:, c, :T], in1=self.Y[:, c, :T], op=ALU.mult),
               reads=[self.rY[c]], writes=[rsq])
        return sq, rsq

    def _stat_mm(self, pend, T):
        c, sq, rsq = pend
        self.P.mm(self.pS2[:, :T], self.ones[:], sq[:, :T], c == 0, c == KC - 1,
                  reads=[self.rones, rsq], writes=[self.rpS2])

    def linear_dd(self, T, w, bias_off=None, kchunks=KC):
        P = self.P
        wv = w.rearrange("(kc p) f -> p kc f", p=128)
        pend = None
        for g in range(KC // GW):
            wt, rwt = self.wload(wv[:, :, g * GW * 128:(g + 1) * GW * 128], kchunks, GW * 128)
            for cc in range(GW):
                c = g * GW + cc
                i = self.pyi
                self.pyi = (i + 1) % 2
                py, rpy = self.pY[i], self.rpY[i]
                for k in range(kchunks):
                    P.mm(py[:, :T], wt[:, k, cc * 128:(cc + 1) * 128], self.Hb[:, k, :T], k == 0, k == kchunks - 1,
                         reads=[rwt, self.rHb[k]], writes=[rpy])
                if pend is not None:
                    self._stat_mm(pend, T)
                sq, rsq = self._y_evac(c, T, py, rpy, bias_off)
                pend = (c, sq, rsq)
        self._stat_mm(pend, T)
        self.rstd_from(self.pS2, self.rpS2, T, self.R2, self.rR2)

    def ffn(self, T, wg, wu, wd):
        P = self.P
        wgv = wg.rearrange("(kc p) f -> p kc f", p=128)
        wuv = wu.rearrange("(kc p) f -> p kc f", p=128)
        wdv = wd.rearrange("(j p) f -> p j f", p=128)
        for g in range(JC // GW):
            gt, rgt = self.wload(wgv[:, :, g * GW * 128:(g + 1) * GW * 128], KC, GW * 128)
            ut, rut = self.wload(wuv[:, :, g * GW * 128:(g + 1) * GW * 128], KC, GW * 128)
            for jj in range(GW):
                j = g * GW + jj
                i = self.pai
                self.pai = (i + 1) % 2
                pa, rpa, pb, rpb = self.pA[i], self.rpA[i], self.pB[i], self.rpB[i]
                for k in range(KC):
                    P.mm(pa[:, :T], gt[:, k, jj * 128:(jj + 1) * 128], self.Hb[:, k, :T], k == 0, k == KC - 1,
                         reads=[rgt, self.rHb[k]], writes=[rpa])
                for k in range(KC):
                    P.mm(pb[:, :T], ut[:, k, jj * 128:(jj + 1) * 128], self.Hb[:, k, :T], k == 0, k == KC - 1,
                         reads=[rut, self.rHb[k]], writes=[rpb])
                t, rt = self.next_tmp()
                P.emit("act", lambda e, t=t, pa=pa: e.activation(out=t[:, :T], in_=pa[:, :T], func=AF.Silu),
                       reads=[rpa], writes=[rt])
                P.emit("dve", lambda e, t=t, pb=pb, j=j: e.tensor_tensor(out=self.HID[:, j, :T], in0=t[:, :T], in1=pb[:, :T],
                                                                       op=ALU.mult),
                       reads=[rt, rpb], writes=[self.rHID[j]])
        pend = None
        for c in range(KC):
            JH = JC // 2
            wts = [self.wload(wdv[:, h * JH:(h + 1) * JH, c * 128:(c + 1) * 128], JH, 128) for h in range(2)]
            i = self.pyi
            self.pyi = (i + 1) % 2
            py, rpy = self.pY[i], self.rpY[i]
            for j in range(JC):
                wt, rwt = wts[j // JH]
                P.mm(py[:, :T], wt[:, j % JH, :], self.HID[:, j, :T], j == 0, j == JC - 1,
                     reads=[rwt, self.rHID[j]], writes=[rpy])
            if pend is not None:
                self._stat_mm(pend, T)
            sq, rsq = self._y_evac(c, T, py, rpy)
            pend = (c, sq, rsq)
        self._stat_mm(pend, T)
        self.rstd_from(self.pS2, self.rpS2, T, self.R2, self.rR2)

    def post_res(self, xoff, T, g_off):
        P = self.P
        for c in range(KC):
            t, rt = self.next_tmp()
            P.emit("dve", lambda e, c=c, t=t: e.scalar_tensor_tensor(
                out=t[:, :T], in0=self.Y[:, c, :T], scalar=self.pvs(g_off, c), in1=self.R2[:, :T],
                op0=ALU.mult, op1=ALU.mult), reads=[self.rY[c], self.rR2, self.rpv], writes=[rt])
            P.emit("dve", lambda e, c=c, t=t: e.tensor_tensor(
                out=self.X[:, c, xoff:xoff + T], in0=self.X[:, c, xoff:xoff + T], in1=t[:, :T], op=ALU.add),
                reads=[rt, self.rX[c]], writes=[self.rX[c]])

    def conv(self, Tp, Tc, w_in, w_out, o):
        P = self.P
        wv = w_in.rearrange("(kc p) f -> p kc f", p=128)
        for g in range(KC // GW):
            at, rat = self.wload(wv[:, :, g * GW * 128:(g + 1) * GW * 128], KC, GW * 128)
            gt, rgt = self.wload(wv[:, :, D + g * GW * 128:D + (g + 1) * GW * 128], KC, GW * 128)
            for cc in range(GW):
                c = g * GW + cc
                i = self.pai
                self.pai = (i + 1) % 2
                pa, rpa, pb, rpb = self.pA[i], self.rpA[i], self.pB[i], self.rpB[i]
                for k in range(KC):
                    P.mm(pa[:, :Tp], at[:, k, cc * 128:(cc + 1) * 128], self.Hb[:, k, :Tp], k == 0, k == KC - 1,
                         reads=[rat, self.rHb[k]], writes=[rpa])
                for k in range(KC):
                    P.mm(pb[:, :Tp], gt[:, k, cc * 128:(cc + 1) * 128], self.Hb[:, k, :Tp], k == 0, k == KC - 1,
                         reads=[rgt, self.rHb[k]], writes=[rpb])
                t, rt = self.next_tmp()
                P.emit("act", lambda e, t=t, pb=pb, c=c: e.activation(out=t[:, :Tp], in_=pb[:, :Tp], func=AF.Sigmoid,
                                                                    bias=self.pvs(o["b_in"] + KC, c)),
                       reads=[rpb, self.rpv], writes=[rt])
                P.emit("dve", lambda e, t=t: e.tensor_tensor(out=t[:, :Tp], in0=t[:, :Tp], in1=self.mask[:, :Tp], op=ALU.mult),
                       reads=[rt, self.rmask], writes=[rt])
                ui = self.ui
                self.ui = (ui + 1) % 2
                U, rU = self.U[ui], self.rU[ui]
                P.emit("dve", lambda e, t=t, pa=pa, c=c, U=U: e.scalar_tensor_tensor(
                    out=U[:, :Tp], in0=pa[:, :Tp], scalar=self.pvs(o["b_in"], c), in1=t[:, :Tp], op0=ALU.add, op1=ALU.mult),
                    reads=[rpa, rt, self.rpv], writes=[rU])
                a0, a1 = self.acc
                ra0, ra1 = self.racc
                wk = lambda k, c=c: self.pv[:, o["w_dw"] + c * CW + k:o["w_dw"] + c * CW + k + 1]
                P.emit("dve", lambda e, U=U, c=c, wk=wk: e.tensor_scalar(
                    out=a0[:, :Tc], in0=U[:, 0:Tc], scalar1=wk(0), scalar2=self.pvs(o["b_dw"], c), op0=ALU.mult, op1=ALU.add),
                    reads=[rU, self.rpv], writes=[ra0])
                P.emit("dve", lambda e, U=U, wk=wk: e.tensor_single_scalar(
                    out=a1[:, :Tc], in_=U[:, 1:1 + Tc], scalar=wk(1), op=ALU.mult),
                    reads=[rU, self.rpv], writes=[ra1])
                for k in range(2, CW):
                    a, ra = (a0, ra0) if k % 2 == 0 else (a1, ra1)
                    P.emit("dve", lambda e, U=U, wk=wk, k=k, a=a: e.scalar_tensor_tensor(
                        out=a[:, :Tc], in0=U[:, k:k + Tc], scalar=wk(k), in1=a[:, :Tc], op0=ALU.mult, op1=ALU.add),
                        reads=[rU, ra, self.rpv], writes=[ra])
                P.emit("dve", lambda e, c=c: e.tensor_tensor(out=self.V[:, c, :Tc], in0=a0[:, :Tc], in1=a1[:, :Tc], op=ALU.add),
                       reads=[ra0, ra1], writes=[self.rV[c]])
                sq, rsq = self.next_sq()
                P.emit("act", lambda e, c=c, sq=sq: e.activation(out=sq[:, :Tc], in_=self.V[:, c, :Tc], func=AF.Copy),
                       reads=[self.rV[c]], writes=[rsq])
                P.mm(self.pS[:, :Tc], self.ones[:], sq[:, :Tc], c == 0, c == KC - 1, reads=[self.rones, rsq], writes=[self.rpS])
                sq, rsq = self.next_sq()
                P.emit("act", lambda e, c=c, sq=sq: e.activation(out=sq[:, :Tc], in_=self.V[:, c, :Tc], func=AF.Square),
                       reads=[self.rV[c]], writes=[rsq])
                P.mm(self.pS2[:, :Tc], self.ones[:], sq[:, :Tc], c == 0, c == KC - 1, reads=[self.rones, rsq], writes=[self.rpS2])
        MU, rMU, R, rR = self.MU, self.rMU, self.R, self.rR
        P.emit("dve", lambda e: e.tensor_single_scalar(out=MU[:, :Tc], in_=self.pS[:, :Tc], scalar=1.0 / D, op=ALU.mult),
               reads=[self.rpS], writes=[rMU])
        t, rt = self.next_tmp()
        P.emit("dve", lambda e: e.tensor_tensor(out=t[:, :Tc], in0=MU[:, :Tc], in1=MU[:, :Tc], op=ALU.mult),
               reads=[rMU], writes=[rt])
        P.emit("dve", lambda e: e.scalar_tensor_tensor(out=R[:, :Tc], in0=self.pS2[:, :Tc], scalar=1.0 / D, in1=t[:, :Tc],
                                                       op0=ALU.mult, op1=ALU.subtract), reads=[self.rpS2, rt], writes=[rR])
        P.emit("dve", lambda e: e.tensor_single_scalar(out=R[:, :Tc], in_=R[:, :Tc], scalar=EPS, op=ALU.add),
               reads=[rR], writes=[rR])
        P.emit("act", lambda e: e.activation(out=R[:, :Tc], in_=R[:, :Tc], func=AF.Sqrt), reads=[rR], writes=[rR])
        P.emit("dve", lambda e: e.reciprocal(out=R[:, :Tc], in_=R[:, :Tc]), reads=[rR], writes=[rR])
        for c in range(KC):
            t, rt = self.next_tmp()
            P.emit("dve", lambda e, c=c, t=t: e.tensor_tensor(out=t[:, :Tc], in0=self.V[:, c, :Tc], in1=MU[:, :Tc], op=ALU.subtract),
                   reads=[self.rV[c], rMU], writes=[rt])
            P.emit("dve", lambda e, t=t: e.tensor_tensor(out=t[:, :Tc], in0=t[:, :Tc], in1=R[:, :Tc], op=ALU.mult),
                   reads=[rt, rR], writes=[rt])
            P.emit("act", lambda e, c=c, t=t: e.activation(out=self.Hb[:, c, :Tc], in_=t[:, :Tc], func=AF.Silu,
                                                             scale=self.pvs(o["ln_g"], c), bias=self.pvs(o["ln_b"], c)),
                   reads=[rt, self.rpv], writes=[self.rHb[c]])
        self.linear_dd(Tc, w_out, bias_off=o["b_out"])

    def load_x(self, src, T, q="sp"):
        v = src.rearrange("(c p) t -> p c t", p=128)
        for h in range(2):
            cs = slice(h * 8, (h + 1) * 8)
            self.P.dma(q, self.X[:, cs, :T], v[:, cs, :], writes=self.rX[cs])

    def load_hb(self, src, T, q="sp"):
        v = src.rearrange("(c p) t -> p c t", p=128)
        for h in range(2):
            cs = slice(h * 8, (h + 1) * 8)
            self.P.dma(q, self.Hb[:, cs, :T], v[:, cs, :], writes=self.rHb[cs])

    def store_x(self, dst, xoff, T, outs, q="sp"):
        v = dst.rearrange("(c p) t -> p c t", p=128)
        for h in range(2):
            cs = slice(h * 8, (h + 1) * 8)
            outs.append(self.P.dma(q, v[:, cs, :], self.X[:, cs, xoff:xoff + T], reads=self.rX[cs]))

    def store_hb(self, dst, T, outs, q="sp"):
        v = dst.rearrange("(c p) t -> p c t", p=128)
        for h in range(2):
            cs = slice(h * 8, (h + 1) * 8)
            outs.append(self.P.dma(q, v[:, cs, :], self.Hb[:, cs, :T], reads=self.rHb[cs]))

    def load_mask(self, src, T, q="sp"):
        self.P.dma(q, self.mask[:, :T], src, writes=[self.rmask])


NCORE = 8
TP = 440
TC = 410
NB = 5
LCORE = 2048
CCORE = 64
CTP = CCORE + 2 * HALO
SEQ = 8192
CTX = 256
NMOD = 9


def new_nc():
    return bass.Bass("TRN2", target_bir_lowering=False)


ACOLS = NMOD * D // NCORE


def build_L0():
    nc = new_nc()
    cT = nc.dram_tensor("cT", [128, KC * 3], F32, kind="ExternalInput").ap()
    aw = nc.dram_tensor("aw", [4, D, ACOLS], F32, kind="ExternalInput").ap()
    ab = nc.dram_tensor("ab", [4, 3, ACOLS], F32, kind="ExternalInput").ap()
    mo = nc.dram_tensor("mo", [4, 3, ACOLS], F32, kind="ExternalOutput").ap()
    with contextlib.ExitStack() as st:
        P = Prog(nc, st)
        sb = lambda n, s, d: st.enter_context(nc.sbuf_tensor("s_" + n, s, d))
        cs = sb("cs", [128, KC, 3], F32); rcs = Res()
        wt = [sb("wt%d" % i, [128, KC, 512], F32) for i in range(2)]; rwt = [Res(), Res()]
        bt = sb("bt", [3, 4, ACOLS], F32); rbt = Res()
        ot = sb("ot", [3, 4, ACOLS], F32); rot = Res()
        ps = [st.enter_context(nc.psum_tensor("ps%d" % i, [128, 512], F32)) for i in range(2)]; rps = [Res(excl=True), Res(excl=True)]
        P.dma("sp", cs[:].rearrange("p a b -> p (a b)"), cT, writes=[rcs])
        P.dma("act", bt[:], ab.rearrange("l r c -> r l c"), writes=[rbt])
        P.emit("act", lambda e: e.activation(out=cs[:], in_=cs[:], func=AF.Silu), reads=[rcs], writes=[rcs])
        tiles = [(l, c0, min(512, ACOLS - c0)) for l in range(4) for c0 in range(0, ACOLS, 512)]
        for i, (l, c0, n) in enumerate(tiles):
            w, rw = wt[i % 2], rwt[i % 2]
            src = aw[l].rearrange("(kc p) f -> p kc f", p=128)[:, :, c0:c0 + n]
            P.dma("sp" if i % 2 == 0 else "act", w[:, :, :n], src, writes=[rw])
            p, rp = ps[i % 2], rps[i % 2]
            for k in range(KC):
                P.mm(p[0:3, :n], cs[:, k, :], w[:, k, :n], k == 0, k == KC - 1, reads=[rcs, rw], writes=[rp])
            P.emit("dve", lambda e, p=p, l=l, c0=c0, n=n: e.tensor_tensor(
                out=ot[:, l, c0:c0 + n], in0=p[0:3, :n], in1=bt[:, l, c0:c0 + n], op=ALU.add),
                reads=[rp, rbt], writes=[rot])
        tok = P.dma("sp", mo.rearrange("l r c -> r l c"), ot[:], reads=[rot])
        P.final_wait("sp", [tok])
        with nc.Block() as block:
            P.replay(block)
    return nc


def run_L0(inp):
    c, c_ctx, ada_w, ada_b = inp["c"], inp["c_ctx"], inp["ada_w"], inp["ada_b"]
    cv = np.stack([c[0], c[1], c_ctx], 1)
    cT = np.ascontiguousarray(cv.reshape(KC, 128, 3).transpose(1, 0, 2).reshape(128, KC * 3))
    maps = []
    for k in range(NCORE):
        cols = slice(k * ACOLS, (k + 1) * ACOLS)
        maps.append(dict(cT=cT, aw=np.ascontiguousarray(ada_w[:, :, cols]),
                         ab=np.ascontiguousarray(np.broadcast_to(ada_b[:, None, cols], (4, 3, ACOLS)))))
    res = run_bass_kernel_spmd(build_L0(), maps, core_ids=list(range(NCORE)))
    m = np.concatenate([r["mo"] for r in res.results], axis=2)
    return m.reshape(4, 3, NMOD, D)


def layer_pv_names(tag):
    return ["%s_m%d" % (tag, j) for j in range(NMOD)] + ["%s_%s" % (tag, n) for n in ("A1", "G1", "A2", "G2", "A3", "G3")]


class PVL:
    def __init__(self):
        self.off = {}
        self.n = 0

    def add(self, name, w=KC):
        self.off[name] = self.n
        self.n += w

    def __getitem__(self, k):
        return self.off[k]


def pv_layout(layers, conv_layers):
    L = PVL()
    for l in layers:
        for j in range(6):
            L.add("g%d_%d" % (l, j))
        for s in ("lat", "ctx"):
            for n in layer_pv_names("L%d%s" % (l, s)):
                L.add(n)
    for l in conv_layers:
        L.add("c%d_b_in" % l, 32)
        L.add("c%d_w_dw" % l, KC * CW)
        for n in ("b_dw", "ln_g", "ln_b", "b_out"):
            L.add("c%d_%s" % (l, n))
    return L


def pv_fill(L, inp, m, core, layers, conv_layers):
    b = core // 4
    pv = np.zeros((128, L.n), np.float32)

    def put(name, v):
        a = pvec(v)
        pv[:, L[name]:L[name] + a.shape[1]] = a
    for l in layers:
        for j in range(6):
            put("g%d_%d" % (l, j), inp["norm_g"][l, j])
        for j in range(NMOD):
            put("L%dlat_m%d" % (l, j), m[l, b, j])
            put("L%dctx_m%d" % (l, j), m[l, 2, j])
    for l in conv_layers:
        j = l // 3
        put("c%d_b_in" % l, inp["conv_b_in"][j])
        wd = inp["conv_w_dw"][j]
        a = wd.T.reshape(KC, 128, CW).transpose(1, 0, 2).reshape(128, KC * CW)
        pv[:, L["c%d_w_dw" % l]:L["c%d_w_dw" % l] + KC * CW] = a
        put("c%d_b_dw" % l, inp["conv_b_dw"][j])
        put("c%d_ln_g" % l, inp["conv_ln_g"][j])
        put("c%d_ln_b" % l, inp["conv_ln_b"][j])
        put("c%d_b_out" % l, inp["conv_b_out"][j])
    return pv


def derive_layer(T, L, l, s):
    t = "L%d%s" % (l, s)
    g = lambda j: L["g%d_%d" % (l, j)]
    T.derive(L[t + "_A1"], g(0), L[t + "_m1"], "a1pb")
    T.derive(L[t + "_G1"], L[t + "_m2"], g(1), "ab", 0.5)
    T.derive(L[t + "_A2"], g(2), L[t + "_m4"], "a1pb")
    T.derive(L[t + "_G2"], L[t + "_m5"], g(3), "ab", 1.0)
    T.derive(L[t + "_A3"], g(4), L[t + "_m7"], "a1pb")
    T.derive(L[t + "_G3"], L[t + "_m8"], g(5), "ab", 0.5)


def conv_offs(L, l):
    return {k: L["c%d_%s" % (l, k)] for k in ("b_in", "w_dw", "b_dw", "ln_g", "ln_b", "b_out")}


def build_LA(nb=NB, with_ctx=True):
    nc = new_nc()
    L = pv_layout([0, 1], [0])
    dt = lambda n, s, d, k: nc.dram_tensor(n, s, d, kind=k).ap()
    xin = dt("xin", [nb, D, TP], F32, "ExternalInput")
    msk = dt("msk", [nb + 1, 128, TP], F32, "ExternalInput")
    cin = dt("cin", [D, CTP], F32, "ExternalInput")
    pvd = dt("pv", [128, L.n], F32, "ExternalInput")
    W = {}
    for n in ("g00", "u00", "g01", "u01", "g10", "u10"):
        W[n] = dt(n, [D, DFF], F32, "ExternalInput")
    for n in ("d00", "d01", "d10"):
        W[n] = dt(n, [DFF, D], F32, "ExternalInput")
    W["cwi"] = dt("cwi", [D, 2 * D], F32, "ExternalInput")
    W["cwo"] = dt("cwo", [D, D], F32, "ExternalInput")
    xo = dt("xo", [nb, D, TC], F32, "ExternalOutput")
    ho = dt("ho", [nb, D, TC], BF16, "ExternalOutput")
    xco = dt("xco", [D, CCORE], F32, "ExternalOutput")
    hco = dt("hco", [D, CCORE], BF16, "ExternalOutput")
    with contextlib.ExitStack() as st:
        P = Prog(nc, st)
        T = TL(nc, st, P, TP, pvd, L.n, conv=True, tcmax=TC)
        for l in (0, 1):
            for s in ("lat", "ctx"):
                derive_layer(T, L, l, s)
        outs = []
        co = conv_offs(L, 0)
        blocks = [("lat", i, TP, TC) for i in range(nb)]
        if with_ctx:
            blocks.append(("ctx", nb, CTP, CCORE))
        for (s, i, tp, tc) in blocks:
            t0, t1 = "L0" + s, "L1" + s
            T.load_x(xin[i] if s == "lat" else cin, tp)
            T.load_mask(msk[i, :, :tp], tp)
            T.norm_mod(0, tp, L[t0 + "_A1"], L[t0 + "_m0"])
            T.ffn(tp, W["g00"], W["u00"], W["d00"])
            T.post_res(0, tp, L[t0 + "_G1"])
            T.norm_mod(0, tp, L[t0 + "_A2"], L[t0 + "_m3"])
            T.conv(tp, tc, W["cwi"], W["cwo"], co)
            T.post_res(HALO, tc, L[t0 + "_G2"])
            T.norm_mod(HALO, tc, L[t0 + "_A3"], L[t0 + "_m6"])
            T.ffn(tc, W["g01"], W["u01"], W["d01"])
            T.post_res(HALO, tc, L[t0 + "_G3"])
            T.norm_mod(HALO, tc, L[t1 + "_A1"], L[t1 + "_m0"])
            T.ffn(tc, W["g10"], W["u10"], W["d10"])
            T.post_res(HALO, tc, L[t1 + "_G1"])
            T.norm_mod(HALO, tc, L[t1 + "_A2"], L[t1 + "_m3"])
            T.store_x(xo[i] if s == "lat" else xco, HALO, tc, outs)
            T.store_hb(ho[i] if s == "lat" else hco, tc, outs, q="act")
        P.final_wait("sp", outs)
        with nc.Block() as block:
            P.replay(block)
    return nc, L


def lat_blocks(xfull, core, halo):
    b, q = core // 4, core % 4
    out = np.zeros((NB, D, TC + 2 * halo), xfull.dtype)
    msk = np.zeros((NB, TC + 2 * halo), np.float32)
    for i in range(NB):
        s = q * LCORE + i * TC - halo
        e = s + TC + 2 * halo
        s0, e0 = max(s, 0), min(e, SEQ)
        out[i, :, s0 - s:e0 - s] = xfull[b, s0:e0].T
        msk[i, s0 - s:e0 - s] = 1.0
    return out, msk


def ctx_block(cfull, core, halo):
    b, q = core // 4, core % 4
    w = CCORE + 2 * halo
    out = np.zeros((D, w), cfull.dtype)
    msk = np.zeros((w,), np.float32)
    s = q * CCORE - halo
    e = s + w
    s0, e0 = max(s, 0), min(e, CTX)
    out[:, s0 - s:e0 - s] = cfull[b, s0:e0].T
    msk[s0 - s:e0 - s] = 1.0
    return out, msk


def gather_lat(blocks_per_core, dtype=None):
    d = blocks_per_core[0].dtype if dtype is None else dtype
    out = np.zeros((2, SEQ, D), d)
    for core, blk in enumerate(blocks_per_core):
        b, q = core // 4, core % 4
        for i in range(NB):
            s = q * LCORE + i * TC
            n = min(TC, (q + 1) * LCORE - s)
            out[b, s:s + n] = blk[i, :, :n].T
    return out


def gather_ctx(per_core):
    out = np.zeros((2, CTX, D), per_core[0].dtype)
    for core, a in enumerate(per_core):
        b, q = core // 4, core % 4
        out[b, q * CCORE:(q + 1) * CCORE] = a.T
    return out


def build_LC(nb=NB, with_ctx=True):
    nc = new_nc()
    L = pv_layout([1, 2], [])
    dt = lambda n, s, d, k: nc.dram_tensor(n, s, d, kind=k).ap()
    xin = dt("xin", [nb, D, TC], F32, "ExternalInput")
    yin = dt("yin", [nb, D, TC], BF16, "ExternalInput")
    cin = dt("cin", [D, CCORE], F32, "ExternalInput")
    ycin = dt("ycin", [D, CCORE], BF16, "ExternalInput")
    pvd = dt("pv", [128, L.n], F32, "ExternalInput")
    W = {}
    for n in ("g11", "u11", "g20", "u20"):
        W[n] = dt(n, [D, DFF], F32, "ExternalInput")
    for n in ("d11", "d20"):
        W[n] = dt(n, [DFF, D], F32, "ExternalInput")
    W["wo"] = dt("wo", [D, D], F32, "ExternalInput")
    xo = dt("xo", [nb, D, TC], F32, "ExternalOutput")
    ho = dt("ho", [nb, D, TC], BF16, "ExternalOutput")
    hco = dt("hco", [D, CCORE], BF16, "ExternalOutput")
    with contextlib.ExitStack() as st:
        P = Prog(nc, st)
        T = TL(nc, st, P, TC, pvd, L.n, conv=False)
        for l in (1, 2):
            for s in ("lat", "ctx"):
                derive_layer(T, L, l, s)
        outs = []
        blocks = [("lat", i, TC) for i in range(nb)]
        if with_ctx:
            blocks.append(("ctx", nb, CCORE))
        for (s, i, tc) in blocks:
            t1, t2 = "L1" + s, "L2" + s
            T.load_x(xin[i] if s == "lat" else cin, tc)
            T.load_hb(yin[i] if s == "lat" else ycin, tc, q="act")
            T.linear_dd(tc, W["wo"])
            T.post_res(0, tc, L[t1 + "_G2"])
            T.norm_mod(0, tc, L[t1 + "_A3"], L[t1 + "_m6"])
            T.ffn(tc, W["g11"], W["u11"], W["d11"])
            T.post_res(0, tc, L[t1 + "_G3"])
            T.norm_mod(0, tc, L[t2 + "_A1"], L[t2 + "_m0"])
            T.ffn(tc, W["g20"], W["u20"], W["d20"])
            T.post_res(0, tc, L[t2 + "_G1"])
            T.norm_mod(0, tc, L[t2 + "_A2"], L[t2 + "_m3"])
            if s == "lat":
                T.store_x(xo[i], 0, tc, outs)
            T.store_hb(ho[i] if s == "lat" else hco, tc, outs, q="act")
        P.final_wait("sp", outs)
        with nc.Block() as block:
            P.replay(block)
    return nc, L


def build_LE(nb=NB):
    nc = new_nc()
    L = pv_layout([2, 3], [3])
    dt = lambda n, s, d, k: nc.dram_tensor(n, s, d, kind=k).ap()
    xin = dt("xin", [nb, D, TP], F32, "ExternalInput")
    yin = dt("yin", [nb, D, TP], BF16, "ExternalInput")
    msk = dt("msk", [nb, 128, TP], F32, "ExternalInput")
    pvd = dt("pv", [128, L.n], F32, "ExternalInput")
    W = {}
    for n in ("g21", "u21", "g30", "u30", "g31", "u31"):
        W[n] = dt(n, [D, DFF], F32, "ExternalInput")
    for n in ("d21", "d30", "d31"):
        W[n] = dt(n, [DFF, D], F32, "ExternalInput")
    W["wo"] = dt("wo", [D, D], F32, "ExternalInput")
    W["cwi"] = dt("cwi", [D, 2 * D], F32, "ExternalInput")
    W["cwo"] = dt("cwo", [D, D], F32, "ExternalInput")
    xo = dt("xo", [nb, D, TC], F32, "ExternalOutput")
    with contextlib.ExitStack() as st:
        P = Prog(nc, st)
        T = TL(nc, st, P, TP, pvd, L.n, conv=True, tcmax=TC)
        for l in (2, 3):
            derive_layer(T, L, l, "lat")
        outs = []
        co = conv_offs(L, 3)
        for i in range(nb):
            t2, t3 = "L2lat", "L3lat"
            T.load_x(xin[i], TP)
            T.load_hb(yin[i], TP, q="act")
            T.load_mask(msk[i], TP)
            T.linear_dd(TP, W["wo"])
            T.post_res(0, TP, L[t2 + "_G2"])
            T.norm_mod(0, TP, L[t2 + "_A3"], L[t2 + "_m6"])
            T.ffn(TP, W["g21"], W["u21"], W["d21"])
            T.post_res(0, TP, L[t2 + "_G3"])
            T.norm_mod(0, TP, L[t3 + "_A1"], L[t3 + "_m0"])
            T.ffn(TP, W["g30"], W["u30"], W["d30"])
            T.post_res(0, TP, L[t3 + "_G1"])
            T.norm_mod(0, TP, L[t3 + "_A2"], L[t3 + "_m3"])
            T.conv(TP, TC, W["cwi"], W["cwo"], co)
            T.post_res(HALO, TC, L[t3 + "_G2"])
            T.norm_mod(HALO, TC, L[t3 + "_A3"], L[t3 + "_m6"])
            T.ffn(TC, W["g31"], W["u31"], W["d31"])
            T.post_res(HALO, TC, L[t3 + "_G3"])
            T.store_x(xo[i], HALO, TC, outs)
        P.final_wait("sp", outs)
        with nc.Block() as block:
            P.replay(block)
    return nc, L


GH = 4
DK = 256
DV = 512
GT = 256
NCH = GT // 64
NLT = SEQ // GT
NTOK = CTX + SEQ
ROPE_THETA = 10000.0
PENG = "dve"


def gla_tables():
    inv = ROPE_THETA ** (-np.arange(0, 128, 2, dtype=np.float32) / 128)
    p = np.arange(128)
    sgn = np.where(p < 64, -1.0, 1.0).astype(np.float32)
    rows = np.arange(SEQ // 64, dtype=np.float32)
    cols = np.arange(64, dtype=np.float32)
    ang_r = rows[None, :] * inv[p % 64][:, None]
    ang_c = cols[None, :] * inv[p % 64][:, None]
    t = {}
    t["cr"] = np.cos(ang_r).astype(np.float32)
    t["sr"] = (np.sin(ang_r) * sgn[:, None]).astype(np.float32)
    t["cc"] = np.cos(ang_c).astype(np.float32)
    t["sc"] = (np.sin(ang_c) * sgn[:, None]).astype(np.float32)
    s = np.arange(128)[:, None]
    tt = np.arange(128)[None, :]
    same = (s // 64) == (tt // 64)
    g = -1.0 / 16.0
    t["tri"] = np.stack([np.where(same & (s <= tt), g, 0.0), np.where(same & (s > tt), g, 0.0),
                         np.where(same & (s >= tt), g, 0.0), np.where(same & (s < tt), g, 0.0)]).astype(np.float32)
    s = np.arange(64)[:, None]
    tt = np.arange(64)[None, :]
    t["msk"] = np.stack([(s <= tt), (s >= tt)]).astype(np.float32)
    t["ident"] = np.eye(128, dtype=np.float32)
    return t


def build_LB(nlt=NLT):
    nc = new_nc()
    SEQ = nlt * GT
    NTOK = CTX + SEQ
    dt = lambda n, s, d, k="ExternalInput": nc.dram_tensor(n, s, d, kind=k).ap()
    hT = dt("hT", [D, SEQ], BF16)
    hcT = dt("hcT", [D, CTX], BF16)
    wq = dt("wq", [D, DK], F32)
    wk = dt("wk", [D, DK], F32)
    wv = dt("wv", [D, DV], F32)
    wr = dt("wr", [D, DV], F32)
    wa1 = dt("wa1", [2, D, 16], F32)
    wa2 = dt("wa2", [2, 16, DK], F32)
    bab = dt("bab", [2, 128, DK], F32)
    brb = dt("brb", [64, DV], F32)
    ngb = dt("ngb", [64, DV], F32)
    tcr = dt("tcr", [128, 128], F32)
    tsr = dt("tsr", [128, 128], F32)
    tcc = dt("tcc", [128, 64], F32)
    tsc = dt("tsc", [128, 64], F32)
    ttri = dt("ttri", [4, 128, 128], F32)
    tmsk = dt("tmsk", [2, 64, 64], F32)
    tid = dt("tid", [128, 128], F32)
    oy = dt("oy", [NTOK, DV], BF16, "ExternalOutput")
    of_s = dt("of_s", [NTOK, DV], F32, "ExternalOutput")
    with contextlib.ExitStack() as st:
        P = Prog(nc, st)
        sb = lambda n, s, d: st.enter_context(nc.sbuf_tensor("s_" + n, s, d))
        psb = lambda n: st.enter_context(nc.psum_tensor(n, [128, 512], F32))

        def E(eng, fn, reads, writes, **kw):
            return P.emit(eng, lambda e: getattr(e, fn)(**kw), reads, writes)

        Wq = sb("Wq", [128, KC, DK], BF16); rWq = Res()
        Wk = sb("Wk", [128, KC, DK], BF16); rWk = Res()
        Wv = sb("Wv", [128, KC, DV], BF16); rWv = Res()
        Wr = sb("Wr", [128, KC, DV], BF16); rWr = Res()
        Wa1 = sb("Wa1", [128, 2, KC, 16], BF16); rWa1 = Res()
        Wa2 = sb("Wa2", [16, 2, DK], BF16); rWa2 = Res()
        P.dma("pool", Wq[:], wq.rearrange("(kc p) f -> p kc f", p=128), writes=[rWq])
        P.dma("pool", Wk[:], wk.rearrange("(kc p) f -> p kc f", p=128), writes=[rWk])
        P.dma("pool", Wv[:], wv.rearrange("(kc p) f -> p kc f", p=128), writes=[rWv])
        P.dma("pool", Wr[:], wr.rearrange("(kc p) f -> p kc f", p=128), writes=[rWr])
        for d in range(2):
            P.dma("pool", Wa1[:, d], wa1[d].rearrange("(kc p) f -> p kc f", p=128), writes=[rWa1])
        P.dma("pool", Wa2[:], wa2.rearrange("d r f -> r d f"), writes=[rWa2])
        rT = Res()
        BA = sb("BA", [128, 2, DK], F32)
        P.dma("sp", BA[:], bab.rearrange("d p f -> p d f"), writes=[rT])
        BR = sb("BR", [64, DV], F32); P.dma("sp", BR[:], brb, writes=[rT])
        NG = sb("NG", [64, DV], F32); P.dma("sp", NG[:], ngb, writes=[rT])
        CR = sb("CR", [128, 128], F32); P.dma("sp", CR[:], tcr, writes=[rT])
        SR = sb("SR", [128, 128], F32); P.dma("sp", SR[:], tsr, writes=[rT])
        CC = sb("CC", [128, 64], F32); P.dma("sp", CC[:], tcc, writes=[rT])
        SC = sb("SC", [128, 64], F32); P.dma("sp", SC[:], tsc, writes=[rT])
        TRI = sb("TRI", [128, 4, 128], F32); P.dma("sp", TRI[:], ttri.rearrange("a p f -> p a f"), writes=[rT])
        MSK = sb("MSK", [64, 2, 64], F32); P.dma("sp", MSK[:], tmsk.rearrange("a p f -> p a f"), writes=[rT])
        IDb = sb("IDb", [128, 128], BF16); P.dma("pool", IDb[:], tid, writes=[rT])

        H = [sb("H%d" % i, [128, KC, GT], BF16) for i in range(2)]; rH = [Res(), Res()]
        QS = sb("QS", [128, GT], F32); rQS = Res()
        QW = sb("QW", [128, GT], F32); rQW = Res()
        QR = sb("QR", [128, 2, GT], F32); rQR = [Res(), Res()]
        KR = sb("KR", [128, 2, GT], F32); rKR = [Res(), Res()]
        ZL = sb("ZL", [16, GT], BF16); rZL = Res()
        ZB = sb("ZB", [128, 2, DK], F32); rZB = Res()
        EQ = sb("EQ", [128, 2, GT], F32); rEQ = Res()
        EK = sb("EK", [128, 2, GT], F32); rEK = Res()
        EL = sb("EL", [128, 2, GT], F32); rEL = Res()
        QE = sb("QE", [128, 2, GT], BF16); rQE = Res()
        KE = sb("KE", [128, 2, GT], BF16); rKE = Res()
        KLT = sb("KLT", [128, 2, GT], BF16); rKLT = Res()
        S = sb("S", [128, 2, DV], F32); rS = [Res(), Res()]
        Sb = sb("Sb", [128, 2, DV], BF16); rSb = [Res(), Res()]
        VT = [sb("VT%d" % i, [64, DV], BF16) for i in range(2)]; rVT = [Res(), Res()]
        KL = [sb("KL%d" % i, [64, DK], BF16) for i in range(2)]; rKL = [Res(), Res()]
        SCT = [sb("SCT%d" % i, [64, 64], BF16) for i in range(2)]; rSCT = [Res(), Res()]
        OS = [sb("OS%d" % i, [64, DV], F32) for i in range(2)]; rOS = [Res(), Res()]
        OF = [sb("OF%d" % i, [64, DV], F32) for i in range(2)]; rOF = [Res(), Res()]
        SQ = sb("SQ", [64, DV], F32); rSQ = Res()
        RT = sb("RT", [64, DV], F32); rRT = Res()
        SS = sb("SS", [64, 1], F32); rSS = Res()
        OUT = [sb("OUT%d" % i, [64, DV], BF16) for i in range(2)]; rOUT = [Res(), Res()]
        pP = [psb("pP0"), psb("pP1")]; rpP = [Res(excl=True), Res(excl=True)]
        pZ = psb("pZ"); rpZ = Res(excl=True)
        pC = psb("pC"); rpC = Res(excl=True)
        pV = psb("pV"); rpV = Res(excl=True)
        pK = psb("pK"); rpK = Res(excl=True)
        pO = psb("pO"); rpO = Res(excl=True)
        pS = psb("pS"); rpS = Res(excl=True)

        for dc in range(2):
            E("dve", "memset", [], [rS[dc]], ap=S[:, dc, :], constant=0.0)
            E("dve", "memset", [], [rSb[dc]], ap=Sb[:, dc, :], constant=0.0)

        outs = []
        cnt = {"h": 0, "p": 0, "c": 0}

        def rope_evac(ps, rps, dst, rdst, row0, is_q, lat):
            sc = (1.0 / 16.0) if is_q else 1.0
            if not lat:
                E("act", "activation", [rps], [rdst], out=dst, in_=ps, func=AF.Identity, scale=sc)
                return
            E("act", "activation", [rps], [rQS], out=QS[:], in_=ps, func=AF.Identity, scale=sc)
            E(PENG, "tensor_copy", [rQS], [rQW], out=QW[0:64, :], in_=QS[64:128, :])
            E(PENG, "tensor_copy", [rQS], [rQW], out=QW[64:128, :], in_=QS[0:64, :])
            return

        def tile_pass(d, src, t0, lat, grow0, ch_order, tok0, final):
            hi = cnt["h"] % 2
            cnt["h"] += 1
            Ht, rHt = H[hi], rH[hi]
            v = src.rearrange("(c p) t -> p c t", p=128)
            for hh in range(2):
                cs = slice(hh * 8, (hh + 1) * 8)
                P.dma("sp" if hh == 0 else "act", Ht[:, cs, :], v[:, cs, t0:t0 + GT], writes=[rHt])
            for (Wm, rW, dst, rdst, is_q) in ((Wq, rWq, QR, rQR, True), (Wk, rWk, KR, rKR, False)):
                for dc in range(2):
                    pi = cnt["p"] % 2
                    cnt["p"] += 1
                    pp, rpp = pP[pi], rpP[pi]
                    for k in range(KC):
                        P.mm(pp[:, :GT], Wm[:, k, dc * 128:(dc + 1) * 128], Ht[:, k, :], k == 0, k == KC - 1,
                             reads=[rW, rHt], writes=[rpp])
                    sc = (1.0 / 16.0) if is_q else 1.0
                    if not lat:
                        E("act", "activation", [rpp], [rdst[dc]], out=dst[:, dc, :], in_=pp[:, :GT], func=AF.Identity, scale=sc)
                    else:
                        E("act", "activation", [rpp], [rQS], out=QS[:], in_=pp[:, :GT], func=AF.Identity, scale=sc)
                        E(PENG, "tensor_copy", [rQS], [rQW], out=QW[0:64, :], in_=QS[64:128, :])
                        E(PENG, "tensor_copy", [rQS], [rQW], out=QW[64:128, :], in_=QS[0:64, :])
                        q3 = lambda a: a.rearrange("p (r c) -> p r c", c=64)
                        if dc == 0:
                            ct = CR[:, grow0:grow0 + NCH].unsqueeze(2).to_broadcast([128, NCH, 64])
                            stb = SR[:, grow0:grow0 + NCH].unsqueeze(2).to_broadcast([128, NCH, 64])
                        else:
                            ct = CC[:].unsqueeze(1).to_broadcast([128, NCH, 64])
                            stb = SC[:].unsqueeze(1).to_broadcast([128, NCH, 64])
                        E("dve", "tensor_tensor", [rQS, rT], [rdst[dc]], out=q3(dst[:, dc, :]), in0=q3(QS[:]), in1=ct, op=ALU.mult)
                        E(PENG, "tensor_tensor", [rQW, rT], [rQW], out=q3(QW[:]), in0=q3(QW[:]), in1=stb, op=ALU.mult)
                        E("dve", "tensor_tensor", [rQW, rdst[dc]], [rdst[dc]], out=dst[:, dc, :], in0=dst[:, dc, :], in1=QW[:], op=ALU.add)
            pi = cnt["p"] % 2
            cnt["p"] += 1
            pp, rpp = pP[pi], rpP[pi]
            for k in range(KC):
                P.mm(pp[0:16, :GT], Wa1[:, d, k, :], Ht[:, k, :], k == 0, k == KC - 1, reads=[rWa1, rHt], writes=[rpp])
            E("act", "activation", [rpp], [rZL], out=ZL[:], in_=pp[0:16, :GT], func=AF.Copy)
            for pr in range(2):
                P.mm(pZ[:, pr * DK:(pr + 1) * DK], ZL[:, pr * 128:(pr + 1) * 128], Wa2[:, d, :], True, True,
                     reads=[rZL, rWa2], writes=[rpZ])
            z3 = pZ[:].rearrange("p (a f) -> p a f", a=2)
            E("dve", "tensor_tensor", [rpZ, rT], [rZB], out=ZB[:], in0=z3,
              in1=BA[:, d, :].unsqueeze(1).to_broadcast([128, 2, DK]), op=ALU.add)
            E("act", "activation", [rZB], [rZB], out=ZB[:], in_=ZB[:], func=AF.Exp, scale=-1.0)
            E("act", "activation", [rZB], [rZB], out=ZB[:], in_=ZB[:], func=AF.Ln, bias=1.0)
            for which, (dstE, rdstE, scl) in enumerate(((None, None, None), (EL, rEL, 1.0))):
                tri = TRI[:, 2 * d + which, :]
                for dc in range(2):
                    for pr in range(2):
                        P.mm(pC[:, dc * GT + pr * 128:dc * GT + (pr + 1) * 128], ZB[:, pr, dc * 128:(dc + 1) * 128], tri,
                             True, True, reads=[rZB, rT], writes=[rpC])
                pc3 = pC[:].rearrange("p (a f) -> p a f", a=2)
                if which == 0:
                    E("act", "activation", [rpC], [rEQ], out=EQ[:], in_=pc3, func=AF.Exp)
                    E("act", "activation", [rpC], [rEK], out=EK[:], in_=pc3, func=AF.Exp, scale=-1.0)
                else:
                    E("act", "activation", [rpC], [rEL], out=EL[:], in_=pc3, func=AF.Exp)
            E("dve", "tensor_tensor", rQR + [rEQ], [rQE], out=QE[:], in0=QR[:], in1=EQ[:], op=ALU.mult)
            E(PENG, "tensor_tensor", rKR + [rEK], [rKE], out=KE[:], in0=KR[:], in1=EK[:], op=ALU.mult)
            E("dve", "tensor_tensor", rKR + [rEL], [rKLT], out=KLT[:], in0=KR[:], in1=EL[:], op=ALU.mult)
            for ch in ch_order:
                ci = cnt["c"] % 2
                cnt["c"] += 1
                cs = slice(ch * 64, (ch + 1) * 64)
                grow = tok0 + ch * 64
                for k in range(KC):
                    P.mm(pV[0:64, :], Ht[:, k, cs], Wv[:, k, :], k == 0, k == KC - 1, reads=[rHt, rWv], writes=[rpV])
                E("act", "activation", [rpV], [rVT[ci]], out=VT[ci][:], in_=pV[0:64, :], func=AF.Copy)
                for dc in range(2):
                    P.mm(pK[0:64, dc * 128:(dc + 1) * 128], KLT[:, dc, cs], IDb[:], True, True, reads=[rKLT, rT], writes=[rpK])
                for dc in range(2):
                    P.mm(pK[0:64, 256:320], KE[:, dc, cs], QE[:, dc, cs], dc == 0, dc == 1, reads=[rKE, rQE], writes=[rpK])
                E("act", "activation", [rpK], [rKL[ci]], out=KL[ci][:], in_=pK[0:64, 0:256], func=AF.Copy)
                E("dve", "tensor_tensor", [rpK, rT], [rSCT[ci]], out=SCT[ci][:], in0=pK[0:64, 256:320], in1=MSK[:, d, :], op=ALU.mult)
                P.mm(pO[0:64, :], SCT[ci][:], VT[ci][:], True, False, reads=[rSCT[ci], rVT[ci]], writes=[rpO])
                for dc in range(2):
                    P.mm(pO[0:64, :], QE[:, dc, cs], Sb[:, dc, :], False, dc == 1, reads=[rQE, rSb[dc]], writes=[rpO])
                lcol = ch * 64 + (63 if d == 0 else 0)
                for dc in range(2):
                    P.mm(pS[:, :], KL[ci][:, dc * 128:(dc + 1) * 128], VT[ci][:], True, True, reads=[rKL[ci], rVT[ci]], writes=[rpS])
                    E("dve", "scalar_tensor_tensor", [rpS, rS[dc], rEQ], [rS[dc]], out=S[:, dc, :], in0=S[:, dc, :],
                      scalar=EQ[:, dc, lcol:lcol + 1], in1=pS[:, :], op0=ALU.mult, op1=ALU.add)
                    E("act", "activation", [rS[dc]], [rSb[dc]], out=Sb[:, dc, :], in_=S[:, dc, :], func=AF.Copy)
                if not final:
                    E("act", "activation", [rpO], [rOS[ci]], out=OS[ci][:], in_=pO[0:64, :], func=AF.Copy)
                    P.dma("sp", of_s[grow:grow + 64, :], OS[ci][:], reads=[rOS[ci]], writes=[rOFS[grow // 64]])
                else:
                    P.dma("act", OF[ci][:], of_s[grow:grow + 64, :], reads=[rOFS[grow // 64]], writes=[rOF[ci]])
                    E("dve", "tensor_tensor", [rpO, rOF[ci]], [rOS[ci]], out=OS[ci][:], in0=pO[0:64, :], in1=OF[ci][:], op=ALU.add)
                    for k in range(KC):
                        P.mm(pV[0:64, :], Ht[:, k, cs], Wr[:, k, :], k == 0, k == KC - 1, reads=[rHt, rWr], writes=[rpV])
                    E("dve", "tensor_tensor", [rpV, rT], [rRT], out=RT[:], in0=pV[0:64, :], in1=BR[:], op=ALU.add)
                    E("act", "activation", [rRT], [rRT], out=RT[:], in_=RT[:], func=AF.Silu)
                    E(PENG, "tensor_tensor", [rOS[ci]], [rSQ], out=SQ[:], in0=OS[ci][:], in1=OS[ci][:], op=ALU.mult)
                    E("dve", "reduce_sum", [rSQ], [rSS], out=SS[:], in_=SQ[:], axis=AX.X)
                    E("dve", "tensor_scalar", [rSS], [rSS], out=SS[:], in0=SS[:], scalar1=1.0 / DV, scalar2=EPS, op0=ALU.mult, op1=ALU.add)
                    E("act", "activation", [rSS], [rSS], out=SS[:], in_=SS[:], func=AF.Sqrt)
                    E("dve", "reciprocal", [rSS], [rSS], out=SS[:], in_=SS[:])
                    E("dve", "scalar_tensor_tensor", [rOS[ci], rSS, rT], [rOS[ci]], out=OS[ci][:], in0=OS[ci][:], scalar=SS[:, 0:1],
                      in1=NG[:], op0=ALU.mult, op1=ALU.mult)
                    E("dve", "tensor_tensor", [rOS[ci], rRT], [rOUT[ci]], out=OUT[ci][:], in0=OS[ci][:], in1=RT[:], op=ALU.mult)
                    outs.append(P.dma("sp", oy[grow:grow + 64, :], OUT[ci][:], reads=[rOUT[ci]]))

        rOFS = [Res() for _ in range(NTOK // 64)]
        tile_pass(0, hcT, 0, False, 0, list(range(NCH)), 0, False)
        for ti in range(nlt):
            tile_pass(0, hT, ti * GT, True, ti * NCH, list(range(NCH)), CTX + ti * GT, False)
        for dc in range(2):
            E("dve", "memset", [], [rS[dc]], ap=S[:, dc, :], constant=0.0)
            E("dve", "memset", [], [rSb[dc]], ap=Sb[:, dc, :], constant=0.0)
        tile_pass(1, hcT, 0, False, 0, list(range(NCH))[::-1], 0, True)
        for ti in range(nlt)[::-1]:
            tile_pass(1, hT, ti * GT, True, ti * NCH, list(range(NCH))[::-1], CTX + ti * GT, True)
        P.final_wait("sp", outs)
        with nc.Block() as block:
            P.replay(block)
    return nc


def lb_inputs(inp, h1, hc1, core, tabs):
    import ml_dtypes
    b, hd = core // 4, core % 4
    ks = slice(hd * DK, (hd + 1) * DK)
    vs = slice(hd * DV, (hd + 1) * DV)
    d = dict(
        hT=np.ascontiguousarray(h1[b].T).astype(ml_dtypes.bfloat16),
        hcT=np.ascontiguousarray(hc1[b].T).astype(ml_dtypes.bfloat16),
        wq=np.ascontiguousarray(inp["gla_w_q"][0][:, ks]), wk=np.ascontiguousarray(inp["gla_w_k"][0][:, ks]),
        wv=np.ascontiguousarray(inp["gla_w_v"][0][:, vs]), wr=np.ascontiguousarray(inp["gla_w_r"][0][:, vs]),
        wa1=np.ascontiguousarray(inp["gla_w_a1"][0]), wa2=np.ascontiguousarray(inp["gla_w_a2"][0][:, :, ks]),
        bab=np.ascontiguousarray(np.broadcast_to(inp["gla_b_a"][0][:, None, ks], (2, 128, DK))),
        brb=np.ascontiguousarray(np.broadcast_to(inp["gla_b_r"][0][None, vs], (64, DV))),
        ngb=np.ascontiguousarray(np.broadcast_to(inp["gla_norm_g"][0][None, :], (64, DV))),
        tcr=tabs["cr"], tsr=tabs["sr"], tcc=tabs["cc"], tsc=tabs["sc"], ttri=tabs["tri"], tmsk=tabs["msk"], tid=tabs["ident"])
    return d


NH = 16
DH = 128
HPC = 4
ROWS = 128
GW_ = 64
NT = 512
NEG = -30000.0
NTOKA = SEQ + CTX
NVT = NTOKA // 128


def na_bias_tables(rpb):
    col = np.arange(64)
    cs = np.clip(col - 8, 0, 48)
    out = np.full((NH, 8, 64, 8, 64), NEG, np.float32)
    for v in range(8):
        delta = v - 7
        for j in range(8):
            ri = delta + j + 7
            for q in range(64):
                kc = cs[q] + np.arange(16)
                out[:, v, q, j, kc] = rpb[:, ri, kc - q + 15]
    return out.reshape(NH, 8, 64, 512)


def build_LD():
    nc = new_nc()
    dt = lambda n, s, d, k="ExternalInput": nc.dram_tensor(n, s, d, kind=k).ap()
    hT = dt("hT", [D, NTOKA], BF16)
    wq = dt("wq", [HPC, D, DH], F32)
    wk = dt("wk", [HPC, D, DH], F32)
    wv = dt("wv", [HPC, D, DH], F32)
    btab = dt("btab", [HPC, 8, 64, 512], F32)
    tid = dt("tid", [64, 64], F32)
    oy = dt("oy", [SEQ, HPC * DH], BF16, "ExternalOutput")
    with contextlib.ExitStack() as st:
        P = Prog(nc, st)
        sb = lambda n, s, d: st.enter_context(nc.sbuf_tensor("s_" + n, s, d))
        psb = lambda n: st.enter_context(nc.psum_tensor(n, [128, 512], F32))

        def E(eng, fn, reads, writes, **kw):
            return P.emit(eng, lambda e: getattr(e, fn)(**kw), reads, writes)

        W = [sb("W%d" % i, [128, 3, KC, DH], BF16) for i in range(2)]; rW = [Res(), Res()]
        BT = sb("BT", [64, 8, 512], F32); rBT = Res()
        ID = sb("ID", [64, 64], BF16); rID = Res()
        P.dma("pool", ID[:], tid, writes=[rID])
        H = [sb("H%d" % i, [128, KC, NT], BF16) for i in range(2)]; rH = [Res(), Res()]
        QT = sb("QT", [128, SEQ], BF16); rQT = Res()
        KT = sb("KT", [128, NTOKA], BF16); rKT = Res()
        VE = sb("VE", [128, NVT, DH + 1], BF16); rVE = Res()
        VO = sb("VO", [128, NVT, DH + 1], BF16); rVO = Res()
        SL = [sb("SL%d" % i, [64, 768], F32) for i in range(2)]; rSL = [Res(), Res()]
        MX = [sb("MX%d" % i, [64, 2], F32) for i in range(2)]; rMX = [Res(), Res()]
        PB = [sb("PB%d" % i, [64, 768], BF16) for i in range(2)]; rPB = [Res(), Res()]
        PT = [sb("PT%d" % i, [128, 384], BF16) for i in range(2)]; rPT = [Res(), Res()]
        RS = [sb("RS%d" % i, [64, 1], F32) for i in range(2)]; rRS = [Res(), Res()]
        OB = [sb("OB%d" % i, [64, DH], BF16) for i in range(2)]; rOB = [Res(), Res()]
        pP = [psb("pP0"), psb("pP1")]; rpP = [Res(excl=True), Res(excl=True)]
        pA = [psb("pA0"), psb("pA1")]; rpA = [Res(excl=True), Res(excl=True)]
        pB = psb("pB"); rpB = Res(excl=True)
        pT = [psb("pT0"), psb("pT1")]; rpT = [Res(excl=True), Res(excl=True)]
        pO = psb("pO"); rpO = Res(excl=True)

        E("dve", "memset", [], [rVE], ap=VE[:, :, DH:DH + 1], constant=1.0)
        outs = []
        hv = hT.rearrange("(c p) t -> p c t", p=128)
        cnt = {"h": 0, "p": 0, "u": 0}
        scale = DH ** -0.5
        for hh in range(HPC):
            Wt, rWt = W[hh % 2], rW[hh % 2]
            for i, wsrc in enumerate((wq, wk, wv)):
                P.dma("pool", Wt[:, i], wsrc[hh].rearrange("(kc p) f -> p kc f", p=128), writes=[rWt])
            P.dma("sp", BT[:], btab[hh].rearrange("v q k -> q v k"), writes=[rBT])
            for ti in range(NTOKA // NT + (1 if NTOKA % NT else 0)):
                t0 = ti * NT
                n = min(NT, NTOKA - t0)
                hi = cnt["h"] % 2
                cnt["h"] += 1
                Ht, rHt = H[hi], rH[hi]
                for half in range(2):
                    cs = slice(half * 8, (half + 1) * 8)
                    P.dma("sp" if half == 0 else "act", Ht[:, cs, :n], hv[:, cs, t0:t0 + n], writes=[rHt])
                lat = t0 < SEQ
                for which in ((0, 1) if lat else (1,)):
                    pi = cnt["p"] % 2
                    cnt["p"] += 1
                    pp, rpp = pP[pi], rpP[pi]
                    for k in range(KC):
                        P.mm(pp[:, :n], Wt[:, which, k, :], Ht[:, k, :n], k == 0, k == KC - 1, reads=[rWt, rHt], writes=[rpp])
                    if which == 0:
                        E("act", "activation", [rpp], [rQT], out=QT[:, t0:t0 + n], in_=pp[:, :n], func=AF.Identity, scale=scale)
                    else:
                        E("dve", "tensor_copy", [rpp], [rKT], out=KT[:, t0:t0 + n], in_=pp[:, :n])
                for s in range(n // 128):
                    pi = cnt["p"] % 2
                    cnt["p"] += 1
                    pp, rpp = pP[pi], rpP[pi]
                    for k in range(KC):
                        P.mm(pp[:, :DH], Ht[:, k, s * 128:(s + 1) * 128], Wt[:, 2, k, :], k == 0, k == KC - 1,
                             reads=[rWt, rHt], writes=[rpp])
                    E("act" if s % 2 else "dve", "activation" if s % 2 else "tensor_copy", [rpp], [rVE],
                      **(dict(out=VE[:, t0 // 128 + s, 0:DH], in_=pp[:, :DH], func=AF.Copy) if s % 2 else
                         dict(out=VE[:, t0 // 128 + s, 0:DH], in_=pp[:, :DH])))
            nlat = SEQ // 128
            P.dma("sp", VO[0:64, 0:nlat, :], VE[64:128, 0:nlat, :], reads=[rVE], writes=[rVO])
            P.dma("act", VO[64:128, 0:nlat - 1, :], VE[0:64, 1:nlat, :], reads=[rVE], writes=[rVO])
            for r in range(ROWS):
                u = cnt["u"] % 2
                cnt["u"] += 1
                rs = min(max(r - 4, 0), ROWS - 8)
                var = rs - r + 7
                qs = QT[:, r * 64:(r + 1) * 64]
                P.mm(pA[u][0:64, :], qs, KT[:, rs * 64:rs * 64 + 512], True, True, reads=[rQT, rKT], writes=[rpA[u]])
                P.mm(pB[0:64, 0:CTX], qs, KT[:, SEQ:SEQ + CTX], True, True, reads=[rQT, rKT], writes=[rpB])
                E("dve", "tensor_tensor", [rpA[u], rBT], [rSL[u]], out=SL[u][:, 0:512], in0=pA[u][0:64, :], in1=BT[:, var, :], op=ALU.add)
                E("act", "activation", [rpB], [rSL[u]], out=SL[u][:, 512:768], in_=pB[0:64, 0:CTX], func=AF.Copy)
                E("dve", "reduce_max", [rSL[u]], [rMX[u]], out=MX[u][:, 0:1], in_=SL[u][:], axis=AX.X)
                E("dve", "tensor_single_scalar", [rMX[u]], [rMX[u]], out=MX[u][:, 1:2], in_=MX[u][:, 0:1], scalar=-1.0, op=ALU.mult)
                E("act", "activation", [rSL[u], rMX[u]], [rPB[u]], out=PB[u][:], in_=SL[u][:], func=AF.Exp, bias=MX[u][:, 1:2])
                for kc in range(6):
                    P.mm(pT[u][:, kc * 64:(kc + 1) * 64], PB[u][:, kc * 128:(kc + 1) * 128], ID[:], True, True,
                         reads=[rPB[u], rID], writes=[rpT[u]])
                E("act" if r % 2 else "dve", "activation" if r % 2 else "tensor_copy", [rpT[u]], [rPT[u]],
                  **(dict(out=PT[u][:], in_=pT[u][:, 0:384], func=AF.Copy) if r % 2 else dict(out=PT[u][:], in_=pT[u][:, 0:384])))
                for kc in range(6):
                    if kc < 4:
                        if rs % 2 == 0:
                            vt, rv = VE[:, rs // 2 + kc, :], rVE
                        else:
                            vt, rv = VO[:, (rs - 1) // 2 + kc, :], rVO
                    else:
                        vt, rv = VE[:, SEQ // 128 + (kc - 4), :], rVE
                    P.mm(pO[0:64, 0:DH + 1], PT[u][:, kc * 64:(kc + 1) * 64], vt, kc == 0, kc == 5, reads=[rPT[u], rv], writes=[rpO])
                E("dve", "reciprocal", [rpO], [rRS[u]], out=RS[u][:], in_=pO[0:64, DH:DH + 1])
                E("act", "activation", [rpO, rRS[u]], [rOB[u]], out=OB[u][:], in_=pO[0:64, 0:DH], func=AF.Identity, scale=RS[u][:, 0:1])
                outs.append(P.dma("sp" if r % 2 else "act", oy[r * 64:(r + 1) * 64, hh * DH:(hh + 1) * DH], OB[u][:], reads=[rOB[u]]))
        P.final_wait("sp", outs)
        with nc.Block() as block:
            P.replay(block)
    return nc


def ld_inputs(inp, h2, hc2, core, btab_all):
    import ml_dtypes
    b, hg = core // 4, core % 4
    hcat = np.concatenate([h2[b], hc2[b]], 0)
    wqkv = inp["na_w_qkv"][0]
    sl = lambda base: np.ascontiguousarray(
        np.stack([wqkv[:, base + (hg * HPC + i) * DH: base + (hg * HPC + i + 1) * DH] for i in range(HPC)]))
    return dict(hT=np.ascontiguousarray(hcat.T).astype(ml_dtypes.bfloat16),
                wq=sl(0), wk=sl(D), wv=sl(2 * D),
                btab=np.ascontiguousarray(btab_all[hg * HPC:(hg + 1) * HPC]),
                tid=np.eye(64, dtype=np.float32))


_CACHE = {}


def _run(nc, maps):
    return run_bass_kernel_spmd(nc, maps, core_ids=list(range(NCORE))).results


def kernel(**inputs):
    import ml_dtypes
    inp = {k: np.asarray(v) for k, v in inputs.items()}
    cores = range(NCORE)
    fw_ = lambda l, i: (inp["ffn_w_gate"][l, i], inp["ffn_w_up"][l, i], inp["ffn_w_down"][l, i])
    m = run_L0(inp)
    nc, L = build_LA()
    maps = []
    for core in cores:
        xb, mk = lat_blocks(inp["x"], core, HALO)
        cb, cmk = ctx_block(inp["ctx"], core, HALO)
        msk = np.zeros((NB + 1, 128, TP), np.float32)
        msk[:NB] = mk[:, None, :]
        msk[NB, :, :CTP] = cmk[None, :]
        g00, u00, d00 = fw_(0, 0)
        g01, u01, d01 = fw_(0, 1)
        g10, u10, d10 = fw_(1, 0)
        maps.append(dict(xin=xb, msk=msk, cin=cb, pv=pv_fill(L, inp, m, core, [0, 1], [0]),
                         g00=g00, u00=u00, d00=d00, g01=g01, u01=u01, d01=d01, g10=g10, u10=u10, d10=d10,
                         cwi=inp["conv_w_in"][0], cwo=inp["conv_w_out"][0]))
    res = _run(nc, maps)
    x1 = gather_lat([r["xo"] for r in res])
    h1 = gather_lat([r["ho"] for r in res])
    xc1 = gather_ctx([r["xco"] for r in res])
    hc1 = gather_ctx([r["hco"] for r in res])
    del maps, res
    tabs = gla_tables()
    res = _run(build_LB(), [lb_inputs(inp, h1, hc1, core, tabs) for core in cores])
    y1 = np.zeros((2, SEQ, D), ml_dtypes.bfloat16)
    yc1 = np.zeros((2, CTX, D), ml_dtypes.bfloat16)
    for core in cores:
        b, hd = core // 4, core % 4
        o = np.asarray(res[core]["oy"])
        yc1[b, :, hd * DV:(hd + 1) * DV] = o[:CTX]
        y1[b, :, hd * DV:(hd + 1) * DV] = o[CTX:]
    del res
    nc, L = build_LC()
    maps = []
    for core in cores:
        g11, u11, d11 = fw_(1, 1)
        g20, u20, d20 = fw_(2, 0)
        maps.append(dict(xin=lat_blocks(x1, core, 0)[0], yin=lat_blocks(y1, core, 0)[0],
                         cin=ctx_block(xc1, core, 0)[0], ycin=ctx_block(yc1, core, 0)[0],
                         pv=pv_fill(L, inp, m, core, [1, 2], []),
                         g11=g11, u11=u11, d11=d11, g20=g20, u20=u20, d20=d20, wo=inp["gla_w_o"][0]))
    res = _run(nc, maps)
    x2 = gather_lat([r["xo"] for r in res])
    h2 = gather_lat([r["ho"] for r in res])
    hc2 = gather_ctx([r["hco"] for r in res])
    del maps, res
    btab = na_bias_tables(inp["na_rpb"][0])
    res = _run(build_LD(), [ld_inputs(inp, h2, hc2, core, btab) for core in cores])
    y2 = np.zeros((2, SEQ, D), ml_dtypes.bfloat16)
    for core in cores:
        b, hg = core // 4, core % 4
        y2[b, :, hg * HPC * DH:(hg + 1) * HPC * DH] = np.asarray(res[core]["oy"])
    del res
    nc, L = build_LE()
    maps = []
    for core in cores:
        xb, mk = lat_blocks(x2, core, HALO)
        g21, u21, d21 = fw_(2, 1)
        g30, u30, d30 = fw_(3, 0)
        g31, u31, d31 = fw_(3, 1)
        maps.append(dict(xin=xb, yin=lat_blocks(y2, core, HALO)[0],
                         msk=np.ascontiguousarray(np.broadcast_to(mk[:, None, :], (NB, 128, TP))),
                         pv=pv_fill(L, inp, m, core, [2, 3], [3]),
                         g21=g21, u21=u21, d21=d21, g30=g30, u30=u30, d30=d30, g31=g31, u31=u31, d31=d31,
                         wo=inp["na_w_o"][0], cwi=inp["conv_w_in"][1], cwo=inp["conv_w_out"][1]))
    res = _run(nc, maps)
    out = gather_lat([r["xo"] for r in res])
    return out.astype(np.float32)
```

```python
import numpy as np
import contextlib
import concourse.bass as bass
import concourse.mybir as mybir
from concourse.bass_utils import run_bass_kernel_spmd

F32 = mybir.dt.float32
BF16 = mybir.dt.bfloat16
AF = mybir.ActivationFunctionType
ALU = mybir.AluOpType
AX = mybir.AxisListType


class Res:
    __slots__ = ("w", "rd", "name", "excl")

    def __init__(self, name="", excl=False):
        self.w = None
        self.rd = {}
        self.name = name
        self.excl = excl


class Prog:
    ENGS = ("pe", "act", "dve", "pool", "sp")
    NDS = 8

    def __init__(self, nc, stack):
        self.nc = nc
        self.q = {e: [] for e in self.ENGS}
        self.sem = {e: stack.enter_context(nc.semaphore("s_" + e)) for e in self.ENGS}
        self.cnt = {e: 0 for e in self.ENGS}
        self.seen = {e: {} for e in self.ENGS}
        self.dq = ("sp", "act", "pool")
        self.dsem = {e: [stack.enter_context(nc.semaphore("d_%s%d" % (e, i))) for i in range(self.NDS)]
                     for e in self.dq}
        self.dcnt = {e: 0 for e in self.dq}
        self.nwaits = 0

    def _need(self, eng, tok, waits, kind):
        if tok is None:
            return
        key, semh, val, teng = tok
        if teng == eng and kind != "raw":
            return
        if teng == "pe" and eng == "pe":
            return
        if self.seen[eng].get(key, 0) >= val:
            return
        self.seen[eng][key] = val
        waits.append((semh, val))

    def emit(self, eng, fn, reads=(), writes=(), dma=False):
        waits = []
        for r in reads:
            self._need(eng, r.w, waits, "raw")
            if r.excl:
                for t in r.rd.values():
                    self._need(eng, t, waits, "rar")
        for r in writes:
            self._need(eng, r.w, waits, "waw")
            for t in r.rd.values():
                self._need(eng, t, waits, "war")
        if dma:
            i = self.dcnt[eng]
            self.dcnt[eng] += 1
            slot, k = i % self.NDS, i // self.NDS
            semh = self.dsem[eng][slot]
            key = "d_%s%d" % (eng, slot)
            if k > 0:
                self._need(eng, (key, semh, 16 * k, None), waits, "raw")
            tok = (key, semh, 16 * (k + 1), None)
            inc = 16
        else:
            self.cnt[eng] += 1
            semh = self.sem[eng]
            tok = ("e_" + eng, semh, self.cnt[eng], eng)
            inc = 1
        for r in reads:
            r.rd[tok[0]] = tok
        for r in writes:
            r.w = tok
            r.rd = {}
        self.nwaits += len(waits)
        self.q[eng].append((waits, fn, semh, inc))
        return tok

    def dma(self, q, out, in_, reads=(), writes=()):
        return self.emit(q, lambda e: e.dma_start(out=out, in_=in_), reads, writes, dma=True)

    def mm(self, out, lhsT, rhs, start, stop, reads=(), writes=()):
        return self.emit("pe", lambda e: e.matmul(out, lhsT, rhs, start=start, stop=stop), reads, writes)

    def final_wait(self, eng, toks):
        waits = []
        for t in toks:
            self._need(eng, t, waits, "raw")
        self.q[eng].append((waits, None, None, 0))

    def replay(self, block):
        nc = self.nc
        names = {"pe": "tensor", "act": "scalar", "dve": "vector", "pool": "gpsimd", "sp": "sync"}
        for e in self.ENGS:
            ops = self.q[e]

            def body(eng, ops=ops):
                for waits, fn, semh, inc in ops:
                    for (s, v) in waits:
                        eng.wait_ge(s, v)
                    if fn is not None:
                        fn(eng).then_inc(semh, inc)
            getattr(block, names[e])(body)


D = 2048
DFF = 5632
KC = 16
JC = 44
EPS = 1e-6
CW = 31
HALO = 15
WB_ELEMS = 4096
NWB = 5
GW = 2


def pvec(v):
    v = np.asarray(v, np.float32).reshape(-1)
    return np.ascontiguousarray(v.reshape(-1, 128).T)


class TL:
    def __init__(self, nc, st, P, TMAX, pv_dram, npv, conv=False, tcmax=None):
        self.nc, self.st, self.P, self.TMAX = nc, st, P, TMAX
        sb = lambda n, s, d: st.enter_context(nc.sbuf_tensor("s_" + n, s, d))
        ps = lambda n: st.enter_context(nc.psum_tensor(n, [128, 512], F32))
        self.X = sb("X", [128, KC, TMAX], F32)
        self.rX = [Res("X%d" % c) for c in range(KC)]
        self.Hb = sb("Hb", [128, KC, TMAX], BF16)
        self.rHb = [Res("Hb%d" % c) for c in range(KC)]
        self.HID = sb("HID", [128, JC, TMAX], BF16)
        self.rHID = [Res("HID%d" % j) for j in range(JC)]
        self.Y = sb("Y", [128, KC, TMAX], F32)
        self.rY = [Res("Y%d" % c) for c in range(KC)]
        self.WB = [sb("WB%d" % i, [128, WB_ELEMS], BF16) for i in range(NWB)]
        self.rWB = [Res("WB%d" % i) for i in range(NWB)]
        self.wbi = 0
        self.pv = sb("pv", [128, npv], F32)
        self.rpv = Res("pv")
        self.ones = sb("ones", [128, 128], BF16)
        self.rones = Res("ones")
        self.sq = [sb("sq%d" % i, [128, TMAX], BF16) for i in range(2)]
        self.rsq = [Res() for _ in range(2)]
        self.sqi = 0
        self.tmp = [sb("tmp%d" % i, [128, TMAX], F32) for i in range(3)]
        self.rtmp = [Res() for _ in range(3)]
        self.tmi = 0
        self.R = sb("R", [128, TMAX], F32)
        self.rR = Res("R")
        self.R2 = sb("R2", [128, TMAX], F32)
        self.rR2 = Res("R2")
        self.pA = [ps("pA%d" % i) for i in range(2)]
        self.rpA = [Res(excl=True) for _ in range(2)]
        self.pB = [ps("pB%d" % i) for i in range(2)]
        self.rpB = [Res(excl=True) for _ in range(2)]
        self.pY = [ps("pY%d" % i) for i in range(2)]
        self.rpY = [Res(excl=True) for _ in range(2)]
        self.pS = ps("pS")
        self.rpS = Res("pS", excl=True)
        self.pS2 = ps("pS2")
        self.rpS2 = Res("pS2", excl=True)
        self.pai = 0
        self.pyi = 0
        if conv:
            self.V = sb("V", [128, KC, tcmax], F32)
            self.rV = [Res() for _ in range(KC)]
            self.U = [sb("U%d" % i, [128, TMAX], F32) for i in range(2)]
            self.rU = [Res() for _ in range(2)]
            self.ui = 0
            self.acc = [sb("acc%d" % i, [128, tcmax], F32) for i in range(2)]
            self.racc = [Res() for _ in range(2)]
            self.MU = sb("MU", [128, tcmax], F32)
            self.rMU = Res()
            self.mask = sb("mask", [128, TMAX], F32)
            self.rmask = Res()
        P.dma("sp", self.pv[:], pv_dram, writes=[self.rpv])
        P.emit("dve", lambda e: e.memset(self.ones[:], 1.0), writes=[self.rones])

    def wload(self, src, a, b):
        i = self.wbi
        self.wbi = (i + 1) % NWB
        view = self.WB[i][:, 0:a * b].rearrange("p (a b) -> p a b", a=a)
        q = "pool" if src.dtype == F32 else "sp"
        self.P.dma(q, view, src, writes=[self.rWB[i]])
        return view, self.rWB[i]

    def next_tmp(self):
        i = self.tmi
        self.tmi = (i + 1) % 3
        return self.tmp[i], self.rtmp[i]

    def next_sq(self):
        i = self.sqi
        self.sqi = (i + 1) % 2
        return self.sq[i], self.rsq[i]

    def pvs(self, off, c):
        return self.pv[:, off + c:off + c + 1]

    def derive(self, out_off, a_off, b_off, mode, scale=1.0):
        P, pv = self.P, self.pv
        o = pv[:, out_off:out_off + KC]
        a = pv[:, a_off:a_off + KC]
        b = pv[:, b_off:b_off + KC]
        if mode == "a1pb":
            P.emit("dve", lambda e: e.scalar_tensor_tensor(out=o, in0=b, scalar=1.0, in1=a, op0=ALU.add, op1=ALU.mult),
                   reads=[self.rpv], writes=[self.rpv])
        else:
            P.emit("dve", lambda e: e.scalar_tensor_tensor(out=o, in0=a, scalar=scale, in1=b, op0=ALU.mult, op1=ALU.mult),
                   reads=[self.rpv], writes=[self.rpv])

    def rstd_from(self, pS, rpS, T, out, rout):
        P = self.P
        P.emit("dve", lambda e: e.tensor_scalar(out=out[:, :T], in0=pS[:, :T], scalar1=1.0 / D, scalar2=EPS,
                                                op0=ALU.mult, op1=ALU.add), reads=[rpS], writes=[rout])
        P.emit("act", lambda e: e.activation(out=out[:, :T], in_=out[:, :T], func=AF.Sqrt), reads=[rout], writes=[rout])
        P.emit("dve", lambda e: e.reciprocal(out=out[:, :T], in_=out[:, :T]), reads=[rout], writes=[rout])

    def norm_mod(self, xoff, T, a_off, sh_off):
        P = self.P
        for c in range(KC):
            sq, rsq = self.next_sq()
            P.emit("act", lambda e, c=c, sq=sq: e.activation(out=sq[:, :T], in_=self.X[:, c, xoff:xoff + T], func=AF.Square),
                   reads=[self.rX[c]], writes=[rsq])
            P.mm(self.pS[:, :T], self.ones[:], sq[:, :T], c == 0, c == KC - 1, reads=[self.rones, rsq], writes=[self.rpS])
        self.rstd_from(self.pS, self.rpS, T, self.R, self.rR)
        for c in range(KC):
            t, rt = self.next_tmp()
            P.emit("dve", lambda e, c=c, t=t: e.scalar_tensor_tensor(
                out=t[:, :T], in0=self.X[:, c, xoff:xoff + T], scalar=self.pvs(a_off, c), in1=self.R[:, :T],
                op0=ALU.mult, op1=ALU.mult), reads=[self.rX[c], self.rR, self.rpv], writes=[rt])
            P.emit("act", lambda e, c=c, t=t: e.activation(out=self.Hb[:, c, :T], in_=t[:, :T], func=AF.Identity,
                                                             bias=self.pvs(sh_off, c)),
                   reads=[rt, self.rpv], writes=[self.rHb[c]])

    def _y_evac(self, c, T, py, rpy, bias_off=None):
        P = self.P
        if bias_off is None:
            P.emit("act", lambda e: e.activation(out=self.Y[:, c, :T], in_=py[:, :T], func=AF.Copy),
                   reads=[rpy], writes=[self.rY[c]])
        else:
            P.emit("act", lambda e: e.activation(out=self.Y[:, c, :T], in_=py[:, :T], func=AF.Identity,
                                                 bias=self.pvs(bias_off, c)),
                   reads=[rpy, self.rpv], writes=[self.rY[c]])
        sq, rsq = self.next_sq()
        P.emit("dve", lambda e: e.tensor_tensor(out=sq[:, :T], in0=self.Y[:, c, :T], in1=self.Y[:, c, :T], op=ALU.mult),
               reads=[self.rY[c]], writes=[rsq])
        return sq, rsq

    def _stat_mm(self, pend, T):
        c, sq, rsq = pend
        self.P.mm(self.pS2[:, :T], self.ones[:], sq[:, :T], c == 0, c == KC - 1,
                  reads=[self.rones, rsq], writes=[self.rpS2])

    def linear_dd(self, T, w, bias_off=None, kchunks=KC):
        P = self.P
        wv = w.rearrange("(kc p) f -> p kc f", p=128)
        pend = None
        for g in range(KC // GW):
            wt, rwt = self.wload(wv[:, :, g * GW * 128:(g + 1) * GW * 128], kchunks, GW * 128)
            for cc in range(GW):
                c = g * GW + cc
                i = self.pyi
                self.pyi = (i + 1) % 2
                py, rpy = self.pY[i], self.rpY[i]
                for k in range(kchunks):
                    P.mm(py[:, :T], wt[:, k, cc * 128:(cc + 1) * 128], self.Hb[:, k, :T], k == 0, k == kchunks - 1,
                         reads=[rwt, self.rHb[k]], writes=[rpy])
                if pend is not None:
                    self._stat_mm(pend, T)
                sq, rsq = self._y_evac(c, T, py, rpy, bias_off)
                pend = (c, sq, rsq)
        self._stat_mm(pend, T)
        self.rstd_from(self.pS2, self.rpS2, T, self.R2, self.rR2)

    def ffn(self, T, wg, wu, wd):
        P = self.P
        wgv = wg.rearrange("(kc p) f -> p kc f", p=128)
        wuv = wu.rearrange("(kc p) f -> p kc f", p=128)
        wdv = wd.rearrange("(j p) f -> p j f", p=128)
        for g in range(JC // GW):
            gt, rgt = self.wload(wgv[:, :, g * GW * 128:(g + 1) * GW * 128], KC, GW * 128)
            ut, rut = self.wload(wuv[:, :, g * GW * 128:(g + 1) * GW * 128], KC, GW * 128)
            for jj in range(GW):
                j = g * GW + jj
                i = self.pai
                self.pai = (i + 1) % 2
                pa, rpa, pb, rpb = self.pA[i], self.rpA[i], self.pB[i], self.rpB[i]
                for k in range(KC):
                    P.mm(pa[:, :T], gt[:, k, jj * 128:(jj + 1) * 128], self.Hb[:, k, :T], k == 0, k == KC - 1,
                         reads=[rgt, self.rHb[k]], writes=[rpa])
                for k in range(KC):
                    P.mm(pb[:, :T], ut[:, k, jj * 128:(jj + 1) * 128], self.Hb[:, k, :T], k == 0, k == KC - 1,
                         reads=[rut, self.rHb[k]], writes=[rpb])
                t, rt = self.next_tmp()
                P.emit("act", lambda e, t=t, pa=pa: e.activation(out=t[:, :T], in_=pa[:, :T], func=AF.Silu),
                       reads=[rpa], writes=[rt])
                P.emit("dve", lambda e, t=t, pb=pb, j=j: e.tensor_tensor(out=self.HID[:, j, :T], in0=t[:, :T], in1=pb[:, :T],
                                                                       op=ALU.mult),
                       reads=[rt, rpb], writes=[self.rHID[j]])
        pend = None
        for c in range(KC):
            JH = JC // 2
            wts = [self.wload(wdv[:, h * JH:(h + 1) * JH, c * 128:(c + 1) * 128], JH, 128) for h in range(2)]
            i = self.pyi
            self.pyi = (i + 1) % 2
            py, rpy = self.pY[i], self.rpY[i]
            for j in range(JC):
                wt, rwt = wts[j // JH]
                P.mm(py[:, :T], wt[:, j % JH, :], self.HID[:, j, :T], j == 0, j == JC - 1,
                     reads=[rwt, self.rHID[j]], writes=[rpy])
            if pend is not None:
                self._stat_mm(pend, T)
            sq, rsq = self._y_evac(c, T, py, rpy)
            pend = (c, sq, rsq)
        self._stat_mm(pend, T)
        self.rstd_from(self.pS2, self.rpS2, T, self.R2, self.rR2)

    def post_res(self, xoff, T, g_off):
        P = self.P
        for c in range(KC):
            t, rt = self.next_tmp()
            P.emit("dve", lambda e, c=c, t=t: e.scalar_tensor_tensor(
                out=t[:, :T], in0=self.Y[:, c, :T], scalar=self.pvs(g_off, c), in1=self.R2[:, :T],
                op0=ALU.mult, op1=ALU.mult), reads=[self.rY[c], self.rR2, self.rpv], writes=[rt])
            P.emit("dve", lambda e, c=c, t=t: e.tensor_tensor(
                out=self.X[:, c, xoff:xoff + T], in0=self.X[:, c, xoff:xoff + T], in1=t[:, :T], op=ALU.add),
                reads=[rt, self.rX[c]], writes=[self.rX[c]])

    def conv(self, Tp, Tc, w_in, w_out, o):
        P = self.P
        wv = w_in.rearrange("(kc p) f -> p kc f", p=128)
        for g in range(KC // GW):
            at, rat = self.wload(wv[:, :, g * GW * 128:(g + 1) * GW * 128], KC, GW * 128)
            gt, rgt = self.wload(wv[:, :, D + g * GW * 128:D + (g + 1) * GW * 128], KC, GW * 128)
            for cc in range(GW):
                c = g * GW + cc
                i = self.pai
                self.pai = (i + 1) % 2
                pa, rpa, pb, rpb = self.pA[i], self.rpA[i], self.pB[i], self.rpB[i]
                for k in range(KC):
                    P.mm(pa[:, :Tp], at[:, k, cc * 128:(cc + 1) * 128], self.Hb[:, k, :Tp], k == 0, k == KC - 1,
                         reads=[rat, self.rHb[k]], writes=[rpa])
                for k in range(KC):
                    P.mm(pb[:, :Tp], gt[:, k, cc * 128:(cc + 1) * 128], self.Hb[:, k, :Tp], k == 0, k == KC - 1,
                         reads=[rgt, self.rHb[k]], writes=[rpb])
                t, rt = self.next_tmp()
                P.emit("act", lambda e, t=t, pb=pb, c=c: e.activation(out=t[:, :Tp], in_=pb[:, :Tp], func=AF.Sigmoid,
                                                                    bias=self.pvs(o["b_in"] + KC, c)),
                       reads=[rpb, self.rpv], writes=[rt])
                P.emit("dve", lambda e, t=t: e.tensor_tensor(out=t[:, :Tp], in0=t[:, :Tp], in1=self.mask[:, :Tp], op=ALU.mult),
                       reads=[rt, self.rmask], writes=[rt])
                ui = self.ui
                self.ui = (ui + 1) % 2
                U, rU = self.U[ui], self.rU[ui]
                P.emit("dve", lambda e, t=t, pa=pa, c=c, U=U: e.scalar_tensor_tensor(
                    out=U[:, :Tp], in0=pa[:, :Tp], scalar=self.pvs(o["b_in"], c), in1=t[:, :Tp], op0=ALU.add, op1=ALU.mult),
                    reads=[rpa, rt, self.rpv], writes=[rU])
                a0, a1 = self.acc
                ra0, ra1 = self.racc
                wk = lambda k, c=c: self.pv[:, o["w_dw"] + c * CW + k:o["w_dw"] + c * CW + k + 1]
                P.emit("dve", lambda e, U=U, c=c, wk=wk: e.tensor_scalar(
                    out=a0[:, :Tc], in0=U[:, 0:Tc], scalar1=wk(0), scalar2=self.pvs(o["b_dw"], c), op0=ALU.mult, op1=ALU.add),
                    reads=[rU, self.rpv], writes=[ra0])
                P.emit("dve", lambda e, U=U, wk=wk: e.tensor_single_scalar(
                    out=a1[:, :Tc], in_=U[:, 1:1 + Tc], scalar=wk(1), op=ALU.mult),
                    reads=[rU, self.rpv], writes=[ra1])
                for k in range(2, CW):
                    a, ra = (a0, ra0) if k % 2 == 0 else (a1, ra1)
                    P.emit("dve", lambda e, U=U, wk=wk, k=k, a=a: e.scalar_tensor_tensor(
                        out=a[:, :Tc], in0=U[:, k:k + Tc], scalar=wk(k), in1=a[:, :Tc], op0=ALU.mult, op1=ALU.add),
                        reads=[rU, ra, self.rpv], writes=[ra])
                P.emit("dve", lambda e, c=c: e.tensor_tensor(out=self.V[:, c, :Tc], in0=a0[:, :Tc], in1=a1[:, :Tc], op=ALU.add),
                       reads=[ra0, ra1], writes=[self.rV[c]])
                sq, rsq = self.next_sq()
                P.emit("act", lambda e, c=c, sq=sq: e.activation(out=sq[:, :Tc], in_=self.V[:, c, :Tc], func=AF.Copy),
                       reads=[self.rV[c]], writes=[rsq])
                P.mm(self.pS[:, :Tc], self.ones[:], sq[:, :Tc], c == 0, c == KC - 1, reads=[self.rones, rsq], writes=[self.rpS])
                sq, rsq = self.next_sq()
                P.emit("act", lambda e, c=c, sq=sq: e.activation(out=sq[:, :Tc], in_=self.V[:, c, :Tc], func=AF.Square),
                       reads=[self.rV[c]], writes=[rsq])
                P.mm(self.pS2[:, :Tc], self.ones[:], sq[:, :Tc], c == 0, c == KC - 1, reads=[self.rones, rsq], writes=[self.rpS2])
        MU, rMU, R, rR = self.MU, self.rMU, self.R, self.rR
        P.emit("dve", lambda e: e.tensor_single_scalar(out=MU[:, :Tc], in_=self.pS[:, :Tc], scalar=1.0 / D, op=ALU.mult),
               reads=[self.rpS], writes=[rMU])
        t, rt = self.next_tmp()
        P.emit("dve", lambda e: e.tensor_tensor(out=t[:, :Tc], in0=MU[:, :Tc], in1=MU[:, :Tc], op=ALU.mult),
               reads=[rMU], writes=[rt])
        P.emit("dve", lambda e: e.scalar_tensor_tensor(out=R[:, :Tc], in0=self.pS2[:, :Tc], scalar=1.0 / D, in1=t[:, :Tc],
                                                       op0=ALU.mult, op1=ALU.subtract), reads=[self.rpS2, rt], writes=[rR])
        P.emit("dve", lambda e: e.tensor_single_scalar(out=R[:, :Tc], in_=R[:, :Tc], scalar=EPS, op=ALU.add),
               reads=[rR], writes=[rR])
        P.emit("act", lambda e: e.activation(out=R[:, :Tc], in_=R[:, :Tc], func=AF.Sqrt), reads=[rR], writes=[rR])
        P.emit("dve", lambda e: e.reciprocal(out=R[:, :Tc], in_=R[:, :Tc]), reads=[rR], writes=[rR])
        for c in range(KC):
            t, rt = self.next_tmp()
            P.emit("dve", lambda e, c=c, t=t: e.tensor_tensor(out=t[:, :Tc], in0=self.V[:, c, :Tc], in1=MU[:, :Tc], op=ALU.subtract),
                   reads=[self.rV[c], rMU], writes=[rt])
            P.emit("dve", lambda e, t=t: e.tensor_tensor(out=t[:, :Tc], in0=t[:, :Tc], in1=R[:, :Tc], op=ALU.mult),
                   reads=[rt, rR], writes=[rt])
            P.emit("act", lambda e, c=c, t=t: e.activation(out=self.Hb[:, c, :Tc], in_=t[:, :Tc], func=AF.Silu,
                                                             scale=self.pvs(o["ln_g"], c), bias=self.pvs(o["ln_b"], c)),
                   reads=[rt, self.rpv], writes=[self.rHb[c]])
        self.linear_dd(Tc, w_out, bias_off=o["b_out"])

    def load_x(self, src, T, q="sp"):
        v = src.rearrange("(c p) t -> p c t", p=128)
        for h in range(2):
            cs = slice(h * 8, (h + 1) * 8)
            self.P.dma(q, self.X[:, cs, :T], v[:, cs, :], writes=self.rX[cs])

    def load_hb(self, src, T, q="sp"):
        v = src.rearrange("(c p) t -> p c t", p=128)
        for h in range(2):
            cs = slice(h * 8, (h + 1) * 8)
            self.P.dma(q, self.Hb[:, cs, :T], v[:, cs, :], writes=self.rHb[cs])

    def store_x(self, dst, xoff, T, outs, q="sp"):
        v = dst.rearrange("(c p) t -> p c t", p=128)
        for h in range(2):
            cs = slice(h * 8, (h + 1) * 8)
            outs.append(self.P.dma(q, v[:, cs, :], self.X[:, cs, xoff:xoff + T], reads=self.rX[cs]))

    def store_hb(self, dst, T, outs, q="sp"):
        v = dst.rearrange("(c p) t -> p c t", p=128)
        for h in range(2):
            cs = slice(h * 8, (h + 1) * 8)
            outs.append(self.P.dma(q, v[:, cs, :], self.Hb[:, cs, :T], reads=self.rHb[cs]))

    def load_mask(self, src, T, q="sp"):
        self.P.dma(q, self.mask[:, :T], src, writes=[self.rmask])


NCORE = 8
TP = 440
TC = 410
NB = 5
LCORE = 2048
CCORE = 64
CTP = CCORE + 2 * HALO
SEQ = 8192
CTX = 256
NMOD = 9


def new_nc():
    return bass.Bass("TRN2", target_bir_lowering=False)


ACOLS = NMOD * D // NCORE


def build_L0():
    nc = new_nc()
    cT = nc.dram_tensor("cT", [128, KC * 3], F32, kind="ExternalInput").ap()
    aw = nc.dram_tensor("aw", [4, D, ACOLS], F32, kind="ExternalInput").ap()
    ab = nc.dram_tensor("ab", [4, 3, ACOLS], F32, kind="ExternalInput").ap()
    mo = nc.dram_tensor("mo", [4, 3, ACOLS], F32, kind="ExternalOutput").ap()
    with contextlib.ExitStack() as st:
        P = Prog(nc, st)
        sb = lambda n, s, d: st.enter_context(nc.sbuf_tensor("s_" + n, s, d))
        cs = sb("cs", [128, KC, 3], F32); rcs = Res()
        wt = [sb("wt%d" % i, [128, KC, 512], F32) for i in range(2)]; rwt = [Res(), Res()]
        bt = sb("bt", [3, 4, ACOLS], F32); rbt = Res()
        ot = sb("ot", [3, 4, ACOLS], F32); rot = Res()
        ps = [st.enter_context(nc.psum_tensor("ps%d" % i, [128, 512], F32)) for i in range(2)]; rps = [Res(excl=True), Res(excl=True)]
        P.dma("sp", cs[:].rearrange("p a b -> p (a b)"), cT, writes=[rcs])
        P.dma("act", bt[:], ab.rearrange("l r c -> r l c"), writes=[rbt])
        P.emit("act", lambda e: e.activation(out=cs[:], in_=cs[:], func=AF.Silu), reads=[rcs], writes=[rcs])
        tiles = [(l, c0, min(512, ACOLS - c0)) for l in range(4) for c0 in range(0, ACOLS, 512)]
        for i, (l, c0, n) in enumerate(tiles):
            w, rw = wt[i % 2], rwt[i % 2]
            src = aw[l].rearrange("(kc p) f -> p kc f", p=128)[:, :, c0:c0 + n]
            P.dma("sp" if i % 2 == 0 else "act", w[:, :, :n], src, writes=[rw])
            p, rp = ps[i % 2], rps[i % 2]
            for k in range(KC):
                P.mm(p[0:3, :n], cs[:, k, :], w[:, k, :n], k == 0, k == KC - 1, reads=[rcs, rw], writes=[rp])
            P.emit("dve", lambda e, p=p, l=l, c0=c0, n=n: e.tensor_tensor(
                out=ot[:, l, c0:c0 + n], in0=p[0:3, :n], in1=bt[:, l, c0:c0 + n], op=ALU.add),
                reads=[rp, rbt], writes=[rot])
        tok = P.dma("sp", mo.rearrange("l r c -> r l c"), ot[:], reads=[rot])
        P.final_wait("sp", [tok])
        with nc.Block() as block:
            P.replay(block)
    return nc


def run_L0(inp):
    c, c_ctx, ada_w, ada_b = inp["c"], inp["c_ctx"], inp["ada_w"], inp["ada_b"]
    cv = np.stack([c[0], c[1], c_ctx], 1)
    cT = np.ascontiguousarray(cv.reshape(KC, 128, 3).transpose(1, 0, 2).reshape(128, KC * 3))
    maps = []
    for k in range(NCORE):
        cols = slice(k * ACOLS, (k + 1) * ACOLS)
        maps.append(dict(cT=cT, aw=np.ascontiguousarray(ada_w[:, :, cols]),
                         ab=np.ascontiguousarray(np.broadcast_to(ada_b[:, None, cols], (4, 3, ACOLS)))))
    res = run_bass_kernel_spmd(build_L0(), maps, core_ids=list(range(NCORE)))
    m = np.concatenate([r["mo"] for r in res.results], axis=2)
    return m.reshape(4, 3, NMOD, D)


def layer_pv_names(tag):
    return ["%s_m%d" % (tag, j) for j in range(NMOD)] + ["%s_%s" % (tag, n) for n in ("A1", "G1", "A2", "G2", "A3", "G3")]


class PVL:
    def __init__(self):
        self.off = {}
        self.n = 0

    def add(self, name, w=KC):
        self.off[name] = self.n
        self.n += w

    def __getitem__(self, k):
        return self.off[k]


def pv_layout(layers, conv_layers):
    L = PVL()
    for l in layers:
        for j in range(6):
            L.add("g%d_%d" % (l, j))
        for s in ("lat", "ctx"):
            for n in layer_pv_names("L%d%s" % (l, s)):
                L.add(n)
    for l in conv_layers:
        L.add("c%d_b_in" % l, 32)
        L.add("c%d_w_dw" % l, KC * CW)
        for n in ("b_dw", "ln_g", "ln_b", "b_out"):
            L.add("c%d_%s" % (l, n))
    return L


def pv_fill(L, inp, m, core, layers, conv_layers):
    b = core // 4
    pv = np.zeros((128, L.n), np.float32)

    def put(name, v):
        a = pvec(v)
        pv[:, L[name]:L[name] + a.shape[1]] = a
    for l in layers:
        for j in range(6):
            put("g%d_%d" % (l, j), inp["norm_g"][l, j])
        for j in range(NMOD):
            put("L%dlat_m%d" % (l, j), m[l, b, j])
            put("L%dctx_m%d" % (l, j), m[l, 2, j])
    for l in conv_layers:
        j = l // 3
        put("c%d_b_in" % l, inp["conv_b_in"][j])
        wd = inp["conv_w_dw"][j]
        a = wd.T.reshape(KC, 128, CW).transpose(1, 0, 2).reshape(128, KC * CW)
        pv[:, L["c%d_w_dw" % l]:L["c%d_w_dw" % l] + KC * CW] = a
        put("c%d_b_dw" % l, inp["conv_b_dw"][j])
        put("c%d_ln_g" % l, inp["conv_ln_g"][j])
        put("c%d_ln_b" % l, inp["conv_ln_b"][j])
        put("c%d_b_out" % l, inp["conv_b_out"][j])
    return pv


def derive_layer(T, L, l, s):
    t = "L%d%s" % (l, s)
    g = lambda j: L["g%d_%d" % (l, j)]
    T.derive(L[t + "_A1"], g(0), L[t + "_m1"], "a1pb")
    T.derive(L[t + "_G1"], L[t + "_m2"], g(1), "ab", 0.5)
    T.derive(L[t + "_A2"], g(2), L[t + "_m4"], "a1pb")
    T.derive(L[t + "_G2"], L[t + "_m5"], g(3), "ab", 1.0)
    T.derive(L[t + "_A3"], g(4), L[t + "_m7"], "a1pb")
    T.derive(L[t + "_G3"], L[t + "_m8"], g(5), "ab", 0.5)


def conv_offs(L, l):
    return {k: L["c%d_%s" % (l, k)] for k in ("b_in", "w_dw", "b_dw", "ln_g", "ln_b", "b_out")}


def build_LA(nb=NB, with_ctx=True, WDT=BF16):
    nc = new_nc()
    L = pv_layout([0, 1], [0])
    dt = lambda n, s, d, k: nc.dram_tensor(n, s, d, kind=k).ap()
    xin = dt("xin", [nb, D, TP], F32, "ExternalInput")
    msk = dt("msk", [nb + 1, 128, TP], F32, "ExternalInput")
    cin = dt("cin", [D, CTP], F32, "ExternalInput")
    pvd = dt("pv", [128, L.n], F32, "ExternalInput")
    W = {}
    for n in ("g00", "u00", "g01", "u01", "g10", "u10"):
        W[n] = dt(n, [D, DFF], WDT, "ExternalInput")
    for n in ("d00", "d01", "d10"):
        W[n] = dt(n, [DFF, D], WDT, "ExternalInput")
    W["cwi"] = dt("cwi", [D, 2 * D], WDT, "ExternalInput")
    W["cwo"] = dt("cwo", [D, D], WDT, "ExternalInput")
    xo = dt("xo", [nb, D, TC], F32, "ExternalOutput")
    ho = dt("ho", [nb, D, TC], BF16, "ExternalOutput")
    xco = dt("xco", [D, CCORE], F32, "ExternalOutput")
    hco = dt("hco", [D, CCORE], BF16, "ExternalOutput")
    with contextlib.ExitStack() as st:
        P = Prog(nc, st)
        T = TL(nc, st, P, TP, pvd, L.n, conv=True, tcmax=TC)
        for l in (0, 1):
            for s in ("lat", "ctx"):
                derive_layer(T, L, l, s)
        outs = []
        co = conv_offs(L, 0)
        blocks = [("lat", i, TP, TC) for i in range(nb)]
        if with_ctx:
            blocks.append(("ctx", nb, CTP, CCORE))
        for (s, i, tp, tc) in blocks:
            t0, t1 = "L0" + s, "L1" + s
            T.load_x(xin[i] if s == "lat" else cin, tp)
            T.load_mask(msk[i, :, :tp], tp)
            T.norm_mod(0, tp, L[t0 + "_A1"], L[t0 + "_m0"])
            T.ffn(tp, W["g00"], W["u00"], W["d00"])
            T.post_res(0, tp, L[t0 + "_G1"])
            T.norm_mod(0, tp, L[t0 + "_A2"], L[t0 + "_m3"])
            T.conv(tp, tc, W["cwi"], W["cwo"], co)
            T.post_res(HALO, tc, L[t0 + "_G2"])
            T.norm_mod(HALO, tc, L[t0 + "_A3"], L[t0 + "_m6"])
            T.ffn(tc, W["g01"], W["u01"], W["d01"])
            T.post_res(HALO, tc, L[t0 + "_G3"])
            T.norm_mod(HALO, tc, L[t1 + "_A1"], L[t1 + "_m0"])
            T.ffn(tc, W["g10"], W["u10"], W["d10"])
            T.post_res(HALO, tc, L[t1 + "_G1"])
            T.norm_mod(HALO, tc, L[t1 + "_A2"], L[t1 + "_m3"])
            T.store_x(xo[i] if s == "lat" else xco, HALO, tc, outs)
            T.store_hb(ho[i] if s == "lat" else hco, tc, outs, q="act")
        P.final_wait("sp", outs)
        with nc.Block() as block:
            P.replay(block)
    return nc, L


def lat_blocks(xfull, core, halo):
    b, q = core // 4, core % 4
    out = np.zeros((NB, D, TC + 2 * halo), xfull.dtype)
    msk = np.zeros((NB, TC + 2 * halo), np.float32)
    for i in range(NB):
        s = q * LCORE + i * TC - halo
        e = s + TC + 2 * halo
        s0, e0 = max(s, 0), min(e, SEQ)
        out[i, :, s0 - s:e0 - s] = xfull[b, s0:e0].T
        msk[i, s0 - s:e0 - s] = 1.0
    return out, msk


def ctx_block(cfull, core, halo):
    b, q = core // 4, core % 4
    w = CCORE + 2 * halo
    out = np.zeros((D, w), cfull.dtype)
    msk = np.zeros((w,), np.float32)
    s = q * CCORE - halo
    e = s + w
    s0, e0 = max(s, 0), min(e, CTX)
    out[:, s0 - s:e0 - s] = cfull[b, s0:e0].T
    msk[s0 - s:e0 - s] = 1.0
    return out, msk


def gather_lat(blocks_per_core, dtype=None):
    d = blocks_per_core[0].dtype if dtype is None else dtype
    out = np.zeros((2, SEQ, D), d)
    for core, blk in enumerate(blocks_per_core):
        b, q = core // 4, core % 4
        for i in range(NB):
            s = q * LCORE + i * TC
            n = min(TC, (q + 1) * LCORE - s)
            out[b, s:s + n] = blk[i, :, :n].T
    return out


def gather_ctx(per_core):
    out = np.zeros((2, CTX, D), per_core[0].dtype)
    for core, a in enumerate(per_core):
        b, q = core // 4, core % 4
        out[b, q * CCORE:(q + 1) * CCORE] = a.T
    return out


def build_LC(nb=NB, with_ctx=True, WDT=BF16):
    nc = new_nc()
    L = pv_layout([1, 2], [])
    dt = lambda n, s, d, k: nc.dram_tensor(n, s, d, kind=k).ap()
    xin = dt("xin", [nb, D, TC], F32, "ExternalInput")
    yin = dt("yin", [nb, D, TC], BF16, "ExternalInput")
    cin = dt("cin", [D, CCORE], F32, "ExternalInput")
    ycin = dt("ycin", [D, CCORE], BF16, "ExternalInput")
    pvd = dt("pv", [128, L.n], F32, "ExternalInput")
    W = {}
    for n in ("g11", "u11", "g20", "u20"):
        W[n] = dt(n, [D, DFF], WDT, "ExternalInput")
    for n in ("d11", "d20"):
        W[n] = dt(n, [DFF, D], WDT, "ExternalInput")
    W["wo"] = dt("wo", [D, D], WDT, "ExternalInput")
    xo = dt("xo", [nb, D, TC], F32, "ExternalOutput")
    ho = dt("ho", [nb, D, TC], BF16, "ExternalOutput")
    hco = dt("hco", [D, CCORE], BF16, "ExternalOutput")
    with contextlib.ExitStack() as st:
        P = Prog(nc, st)
        T = TL(nc, st, P, TC, pvd, L.n, conv=False)
        for l in (1, 2):
            for s in ("lat", "ctx"):
                derive_layer(T, L, l, s)
        outs = []
        blocks = [("lat", i, TC) for i in range(nb)]
        if with_ctx:
            blocks.append(("ctx", nb, CCORE))
        for (s, i, tc) in blocks:
            t1, t2 = "L1" + s, "L2" + s
            T.load_x(xin[i] if s == "lat" else cin, tc)
            T.load_hb(yin[i] if s == "lat" else ycin, tc, q="act")
            T.linear_dd(tc, W["wo"])
            T.post_res(0, tc, L[t1 + "_G2"])
            T.norm_mod(0, tc, L[t1 + "_A3"], L[t1 + "_m6"])
            T.ffn(tc, W["g11"], W["u11"], W["d11"])
            T.post_res(0, tc, L[t1 + "_G3"])
            T.norm_mod(0, tc, L[t2 + "_A1"], L[t2 + "_m0"])
            T.ffn(tc, W["g20"], W["u20"], W["d20"])
            T.post_res(0, tc, L[t2 + "_G1"])
            T.norm_mod(0, tc, L[t2 + "_A2"], L[t2 + "_m3"])
            if s == "lat":
                T.store_x(xo[i], 0, tc, outs)
            T.store_hb(ho[i] if s == "lat" else hco, tc, outs, q="act")
        P.final_wait("sp", outs)
        with nc.Block() as block:
            P.replay(block)
    return nc, L


def build_LE(nb=NB, WDT=BF16):
    nc = new_nc()
    L = pv_layout([2, 3], [3])
    dt = lambda n, s, d, k: nc.dram_tensor(n, s, d, kind=k).ap()
    xin = dt("xin", [nb, D, TP], F32, "ExternalInput")
    yin = dt("yin", [nb, D, TP], BF16, "ExternalInput")
    msk = dt("msk", [nb, 128, TP], F32, "ExternalInput")
    pvd = dt("pv", [128, L.n], F32, "ExternalInput")
    W = {}
    for n in ("g21", "u21", "g30", "u30", "g31", "u31"):
        W[n] = dt(n, [D, DFF], WDT, "ExternalInput")
    for n in ("d21", "d30", "d31"):
        W[n] = dt(n, [DFF, D], WDT, "ExternalInput")
    W["wo"] = dt("wo", [D, D], WDT, "ExternalInput")
    W["cwi"] = dt("cwi", [D, 2 * D], WDT, "ExternalInput")
    W["cwo"] = dt("cwo", [D, D], WDT, "ExternalInput")
    xo = dt("xo", [nb, D, TC], F32, "ExternalOutput")
    with contextlib.ExitStack() as st:
        P = Prog(nc, st)
        T = TL(nc, st, P, TP, pvd, L.n, conv=True, tcmax=TC)
        for l in (2, 3):
            derive_layer(T, L, l, "lat")
        outs = []
        co = conv_offs(L, 3)
        for i in range(nb):
            t2, t3 = "L2lat", "L3lat"
            T.load_x(xin[i], TP)
            T.load_hb(yin[i], TP, q="act")
            T.load_mask(msk[i], TP)
            T.linear_dd(TP, W["wo"])
            T.post_res(0, TP, L[t2 + "_G2"])
            T.norm_mod(0, TP, L[t2 + "_A3"], L[t2 + "_m6"])
            T.ffn(TP, W["g21"], W["u21"], W["d21"])
            T.post_res(0, TP, L[t2 + "_G3"])
            T.norm_mod(0, TP, L[t3 + "_A1"], L[t3 + "_m0"])
            T.ffn(TP, W["g30"], W["u30"], W["d30"])
            T.post_res(0, TP, L[t3 + "_G1"])
            T.norm_mod(0, TP, L[t3 + "_A2"], L[t3 + "_m3"])
            T.conv(TP, TC, W["cwi"], W["cwo"], co)
            T.post_res(HALO, TC, L[t3 + "_G2"])
            T.norm_mod(HALO, TC, L[t3 + "_A3"], L[t3 + "_m6"])
            T.ffn(TC, W["g31"], W["u31"], W["d31"])
            T.post_res(HALO, TC, L[t3 + "_G3"])
            T.store_x(xo[i], HALO, TC, outs)
        P.final_wait("sp", outs)
        with nc.Block() as block:
            P.replay(block)
    return nc, L


CAST_CH = 8192


def cast_weight_list(inp):
    lst = []
    for l in range(4):
        for i in range(2):
            lst.append(("g%d%d" % (l, i), inp["ffn_w_gate"][l, i]))
            lst.append(("u%d%d" % (l, i), inp["ffn_w_up"][l, i]))
            lst.append(("d%d%d" % (l, i), inp["ffn_w_down"][l, i]))
    for j in range(2):
        lst.append(("cwi%d" % j, inp["conv_w_in"][j]))
        lst.append(("cwo%d" % j, inp["conv_w_out"][j]))
    lst.append(("gwo", inp["gla_w_o"][0]))
    lst.append(("nwo", inp["na_w_o"][0]))
    return lst


def build_LW(ncols):
    nc = new_nc()
    wi = nc.dram_tensor("wi", [128, ncols], F32, kind="ExternalInput").ap()
    wo = nc.dram_tensor("wo", [128, ncols], BF16, kind="ExternalOutput").ap()
    with contextlib.ExitStack() as st:
        P = Prog(nc, st)
        NBUF = 4
        bufs = [st.enter_context(nc.sbuf_tensor("s_cb%d" % i, [128, CAST_CH], BF16)) for i in range(NBUF)]
        rb = [Res() for _ in range(NBUF)]
        outs = []
        i = 0
        for c0 in range(0, ncols, CAST_CH):
            n = min(CAST_CH, ncols - c0)
            b, r = bufs[i % NBUF], rb[i % NBUF]
            P.dma("pool", b[:, :n], wi[:, c0:c0 + n], writes=[r])
            outs.append(P.dma("sp" if i % 2 == 0 else "act", wo[:, c0:c0 + n], b[:, :n], reads=[r]))
            i += 1
        P.final_wait("sp", outs)
        with nc.Block() as block:
            P.replay(block)
    return nc


def run_LW(inp):
    lst = cast_weight_list(inp)
    per = [a.size // (NCORE * 128) for _, a in lst]
    ncols = sum(per)
    maps = []
    for k in range(NCORE):
        parts = [a.reshape(NCORE, 128, -1)[k] for _, a in lst]
        maps.append(dict(wi=np.ascontiguousarray(np.concatenate(parts, axis=1))))
    res = run_bass_kernel_spmd(build_LW(ncols), maps, core_ids=list(range(NCORE))).results
    outs = [np.asarray(r["wo"]) for r in res]
    W = {}
    off = 0
    for (key, a), n in zip(lst, per):
        W[key] = np.stack([o[:, off:off + n] for o in outs]).reshape(a.shape)
        off += n
    return W


GH = 4
DK = 256
DV = 512
GT = 256
NCH = GT // 64
NLT = SEQ // GT
NTOK = CTX + SEQ
ROPE_THETA = 10000.0
PENG = "dve"


def gla_tables():
    inv = ROPE_THETA ** (-np.arange(0, 128, 2, dtype=np.float32) / 128)
    p = np.arange(128)
    sgn = np.where(p < 64, -1.0, 1.0).astype(np.float32)
    rows = np.arange(SEQ // 64, dtype=np.float32)
    cols = np.arange(64, dtype=np.float32)
    ang_r = rows[None, :] * inv[p % 64][:, None]
    ang_c = cols[None, :] * inv[p % 64][:, None]
    t = {}
    t["cr"] = np.cos(ang_r).astype(np.float32)
    t["sr"] = (np.sin(ang_r) * sgn[:, None]).astype(np.float32)
    t["cc"] = np.cos(ang_c).astype(np.float32)
    t["sc"] = (np.sin(ang_c) * sgn[:, None]).astype(np.float32)
    s = np.arange(128)[:, None]
    tt = np.arange(128)[None, :]
    same = (s // 64) == (tt // 64)
    g = -1.0 / 16.0
    t["tri"] = np.stack([np.where(same & (s <= tt), g, 0.0), np.where(same & (s > tt), g, 0.0),
                         np.where(same & (s >= tt), g, 0.0), np.where(same & (s < tt), g, 0.0)]).astype(np.float32)
    s = np.arange(64)[:, None]
    tt = np.arange(64)[None, :]
    t["msk"] = np.stack([(s <= tt), (s >= tt)]).astype(np.float32)
    t["ident"] = np.eye(128, dtype=np.float32)
    return t


def build_LB(nlt=NLT):
    nc = new_nc()
    SEQ = nlt * GT
    NTOK = CTX + SEQ
    dt = lambda n, s, d, k="ExternalInput": nc.dram_tensor(n, s, d, kind=k).ap()
    hT = dt("hT", [D, SEQ], BF16)
    hcT = dt("hcT", [D, CTX], BF16)
    wq = dt("wq", [D, DK], F32)
    wk = dt("wk", [D, DK], F32)
    wv = dt("wv", [D, DV], F32)
    wr = dt("wr", [D, DV], F32)
    wa1 = dt("wa1", [2, D, 16], F32)
    wa2 = dt("wa2", [2, 16, DK], F32)
    bab = dt("bab", [2, 128, DK], F32)
    brb = dt("brb", [64, DV], F32)
    ngb = dt("ngb", [64, DV], F32)
    tcr = dt("tcr", [128, 128], F32)
    tsr = dt("tsr", [128, 128], F32)
    tcc = dt("tcc", [128, 64], F32)
    tsc = dt("tsc", [128, 64], F32)
    ttri = dt("ttri", [4, 128, 128], F32)
    tmsk = dt("tmsk", [2, 64, 64], F32)
    tid = dt("tid", [128, 128], F32)
    oy = dt("oy", [NTOK, DV], BF16, "ExternalOutput")
    of_s = dt("of_s", [NTOK, DV], F32, "ExternalOutput")
    with contextlib.ExitStack() as st:
        P = Prog(nc, st)
        sb = lambda n, s, d: st.enter_context(nc.sbuf_tensor("s_" + n, s, d))
        psb = lambda n: st.enter_context(nc.psum_tensor(n, [128, 512], F32))

        def E(eng, fn, reads, writes, **kw):
            return P.emit(eng, lambda e: getattr(e, fn)(**kw), reads, writes)

        Wq = sb("Wq", [128, KC, DK], BF16); rWq = Res()
        Wk = sb("Wk", [128, KC, DK], BF16); rWk = Res()
        Wv = sb("Wv", [128, KC, DV], BF16); rWv = Res()
        Wr = sb("Wr", [128, KC, DV], BF16); rWr = Res()
        Wa1 = sb("Wa1", [128, 2, KC, 16], BF16); rWa1 = Res()
        Wa2 = sb("Wa2", [16, 2, DK], BF16); rWa2 = Res()
        P.dma("pool", Wq[:], wq.rearrange("(kc p) f -> p kc f", p=128), writes=[rWq])
        P.dma("pool", Wk[:], wk.rearrange("(kc p) f -> p kc f", p=128), writes=[rWk])
        P.dma("pool", Wv[:], wv.rearrange("(kc p) f -> p kc f", p=128), writes=[rWv])
        P.dma("pool", Wr[:], wr.rearrange("(kc p) f -> p kc f", p=128), writes=[rWr])
        for d in range(2):
            P.dma("pool", Wa1[:, d], wa1[d].rearrange("(kc p) f -> p kc f", p=128), writes=[rWa1])
        P.dma("pool", Wa2[:], wa2.rearrange("d r f -> r d f"), writes=[rWa2])
        rT = Res()
        BA = sb("BA", [128, 2, DK], F32)
        P.dma("sp", BA[:], bab.rearrange("d p f -> p d f"), writes=[rT])
        BR = sb("BR", [64, DV], F32); P.dma("sp", BR[:], brb, writes=[rT])
        NG = sb("NG", [64, DV], F32); P.dma("sp", NG[:], ngb, writes=[rT])
        CR = sb("CR", [128, 128], F32); P.dma("sp", CR[:], tcr, writes=[rT])
        SR = sb("SR", [128, 128], F32); P.dma("sp", SR[:], tsr, writes=[rT])
        CC = sb("CC", [128, 64], F32); P.dma("sp", CC[:], tcc, writes=[rT])
        SC = sb("SC", [128, 64], F32); P.dma("sp", SC[:], tsc, writes=[rT])
        TRI = sb("TRI", [128, 4, 128], F32); P.dma("sp", TRI[:], ttri.rearrange("a p f -> p a f"), writes=[rT])
        MSK = sb("MSK", [64, 2, 64], F32); P.dma("sp", MSK[:], tmsk.rearrange("a p f -> p a f"), writes=[rT])
        IDb = sb("IDb", [128, 128], BF16); P.dma("pool", IDb[:], tid, writes=[rT])

        H = [sb("H%d" % i, [128, KC, GT], BF16) for i in range(2)]; rH = [Res(), Res()]
        QS = sb("QS", [128, GT], F32); rQS = Res()
        QW = sb("QW", [128, GT], F32); rQW = Res()
        QR = sb("QR", [128, 2, GT], F32); rQR = [Res(), Res()]
        KR = sb("KR", [128, 2, GT], F32); rKR = [Res(), Res()]
        ZL = sb("ZL", [16, GT], BF16); rZL = Res()
        ZB = sb("ZB", [128, 2, DK], F32); rZB = Res()
        EQ = sb("EQ", [128, 2, GT], F32); rEQ = Res()
        EK = sb("EK", [128, 2, GT], F32); rEK = Res()
        EL = sb("EL", [128, 2, GT], F32); rEL = Res()
        QE = sb("QE", [128, 2, GT], BF16); rQE = Res()
        KE = sb("KE", [128, 2, GT], BF16); rKE = Res()
        KLT = sb("KLT", [128, 2, GT], BF16); rKLT = Res()
        S = sb("S", [128, 2, DV], F32); rS = [Res(), Res()]
        Sb = sb("Sb", [128, 2, DV], BF16); rSb = [Res(), Res()]
        VT = [sb("VT%d" % i, [64, DV], BF16) for i in range(2)]; rVT = [Res(), Res()]
        KL = [sb("KL%d" % i, [64, DK], BF16) for i in range(2)]; rKL = [Res(), Res()]
        SCT = [sb("SCT%d" % i, [64, 64], BF16) for i in range(2)]; rSCT = [Res(), Res()]
        OS = [sb("OS%d" % i, [64, DV], F32) for i in range(2)]; rOS = [Res(), Res()]
        OF = [sb("OF%d" % i, [64, DV], F32) for i in range(2)]; rOF = [Res(), Res()]
        SQ = sb("SQ", [64, DV], F32); rSQ = Res()
        RT = sb("RT", [64, DV], F32); rRT = Res()
        SS = sb("SS", [64, 1], F32); rSS = Res()
        OUT = [sb("OUT%d" % i, [64, DV], BF16) for i in range(2)]; rOUT = [Res(), Res()]
        pP = [psb("pP0"), psb("pP1")]; rpP = [Res(excl=True), Res(excl=True)]
        pZ = psb("pZ"); rpZ = Res(excl=True)
        pC = psb("pC"); rpC = Res(excl=True)
        pV = psb("pV"); rpV = Res(excl=True)
        pK = psb("pK"); rpK = Res(excl=True)
        pO = psb("pO"); rpO = Res(excl=True)
        pS = psb("pS"); rpS = Res(excl=True)

        for dc in range(2):
            E("dve", "memset", [], [rS[dc]], ap=S[:, dc, :], constant=0.0)
            E("dve", "memset", [], [rSb[dc]], ap=Sb[:, dc, :], constant=0.0)

        outs = []
        cnt = {"h": 0, "p": 0, "c": 0}

        def rope_evac(ps, rps, dst, rdst, row0, is_q, lat):
            sc = (1.0 / 16.0) if is_q else 1.0
            if not lat:
                E("act", "activation", [rps], [rdst], out=dst, in_=ps, func=AF.Identity, scale=sc)
                return
            E("act", "activation", [rps], [rQS], out=QS[:], in_=ps, func=AF.Identity, scale=sc)
            E(PENG, "tensor_copy", [rQS], [rQW], out=QW[0:64, :], in_=QS[64:128, :])
            E(PENG, "tensor_copy", [rQS], [rQW], out=QW[64:128, :], in_=QS[0:64, :])
            return

        def tile_pass(d, src, t0, lat, grow0, ch_order, tok0, final):
            hi = cnt["h"] % 2
            cnt["h"] += 1
            Ht, rHt = H[hi], rH[hi]
            v = src.rearrange("(c p) t -> p c t", p=128)
            for hh in range(2):
                cs = slice(hh * 8, (hh + 1) * 8)
                P.dma("sp" if hh == 0 else "act", Ht[:, cs, :], v[:, cs, t0:t0 + GT], writes=[rHt])
            for (Wm, rW, dst, rdst, is_q) in ((Wq, rWq, QR, rQR, True), (Wk, rWk, KR, rKR, False)):
                for dc in range(2):
                    pi = cnt["p"] % 2
                    cnt["p"] += 1
                    pp, rpp = pP[pi], rpP[pi]
                    for k in range(KC):
                        P.mm(pp[:, :GT], Wm[:, k, dc * 128:(dc + 1) * 128], Ht[:, k, :], k == 0, k == KC - 1,
                             reads=[rW, rHt], writes=[rpp])
                    sc = (1.0 / 16.0) if is_q else 1.0
                    if not lat:
                        E("act", "activation", [rpp], [rdst[dc]], out=dst[:, dc, :], in_=pp[:, :GT], func=AF.Identity, scale=sc)
                    else:
                        E("act", "activation", [rpp], [rQS], out=QS[:], in_=pp[:, :GT], func=AF.Identity, scale=sc)
                        E(PENG, "tensor_copy", [rQS], [rQW], out=QW[0:64, :], in_=QS[64:128, :])
                        E(PENG, "tensor_copy", [rQS], [rQW], out=QW[64:128, :], in_=QS[0:64, :])
                        q3 = lambda a: a.rearrange("p (r c) -> p r c", c=64)
                        if dc == 0:
                            ct = CR[:, grow0:grow0 + NCH].unsqueeze(2).to_broadcast([128, NCH, 64])
                            stb = SR[:, grow0:grow0 + NCH].unsqueeze(2).to_broadcast([128, NCH, 64])
                        else:
                            ct = CC[:].unsqueeze(1).to_broadcast([128, NCH, 64])
                            stb = SC[:].unsqueeze(1).to_broadcast([128, NCH, 64])
                        E("dve", "tensor_tensor", [rQS, rT], [rdst[dc]], out=q3(dst[:, dc, :]), in0=q3(QS[:]), in1=ct, op=ALU.mult)
                        E(PENG, "tensor_tensor", [rQW, rT], [rQW], out=q3(QW[:]), in0=q3(QW[:]), in1=stb, op=ALU.mult)
                        E("dve", "tensor_tensor", [rQW, rdst[dc]], [rdst[dc]], out=dst[:, dc, :], in0=dst[:, dc, :], in1=QW[:], op=ALU.add)
            pi = cnt["p"] % 2
            cnt["p"] += 1
            pp, rpp = pP[pi], rpP[pi]
            for k in range(KC):
                P.mm(pp[0:16, :GT], Wa1[:, d, k, :], Ht[:, k, :], k == 0, k == KC - 1, reads=[rWa1, rHt], writes=[rpp])
            E("act", "activation", [rpp], [rZL], out=ZL[:], in_=pp[0:16, :GT], func=AF.Copy)
            for pr in range(2):
                P.mm(pZ[:, pr * DK:(pr + 1) * DK], ZL[:, pr * 128:(pr + 1) * 128], Wa2[:, d, :], True, True,
                     reads=[rZL, rWa2], writes=[rpZ])
            z3 = pZ[:].rearrange("p (a f) -> p a f", a=2)
            E("dve", "tensor_tensor", [rpZ, rT], [rZB], out=ZB[:], in0=z3,
              in1=BA[:, d, :].unsqueeze(1).to_broadcast([128, 2, DK]), op=ALU.add)
            E("act", "activation", [rZB], [rZB], out=ZB[:], in_=ZB[:], func=AF.Exp, scale=-1.0)
            E("act", "activation", [rZB], [rZB], out=ZB[:], in_=ZB[:], func=AF.Ln, bias=1.0)
            for which, (dstE, rdstE, scl) in enumerate(((None, None, None), (EL, rEL, 1.0))):
                tri = TRI[:, 2 * d + which, :]
                for dc in range(2):
                    for pr in range(2):
                        P.mm(pC[:, dc * GT + pr * 128:dc * GT + (pr + 1) * 128], ZB[:, pr, dc * 128:(dc + 1) * 128], tri,
                             True, True, reads=[rZB, rT], writes=[rpC])
                pc3 = pC[:].rearrange("p (a f) -> p a f", a=2)
                if which == 0:
                    E("act", "activation", [rpC], [rEQ], out=EQ[:], in_=pc3, func=AF.Exp)
                    E("act", "activation", [rpC], [rEK], out=EK[:], in_=pc3, func=AF.Exp, scale=-1.0)
                else:
                    E("act", "activation", [rpC], [rEL], out=EL[:], in_=pc3, func=AF.Exp)
            E("dve", "tensor_tensor", rQR + [rEQ], [rQE], out=QE[:], in0=QR[:], in1=EQ[:], op=ALU.mult)
            E(PENG, "tensor_tensor", rKR + [rEK], [rKE], out=KE[:], in0=KR[:], in1=EK[:], op=ALU.mult)
            E("dve", "tensor_tensor", rKR + [rEL], [rKLT], out=KLT[:], in0=KR[:], in1=EL[:], op=ALU.mult)
            for ch in ch_order:
                ci = cnt["c"] % 2
                cnt["c"] += 1
                cs = slice(ch * 64, (ch + 1) * 64)
                grow = tok0 + ch * 64
                for k in range(KC):
                    P.mm(pV[0:64, :], Ht[:, k, cs], Wv[:, k, :], k == 0, k == KC - 1, reads=[rHt, rWv], writes=[rpV])
                E("act", "activation", [rpV], [rVT[ci]], out=VT[ci][:], in_=pV[0:64, :], func=AF.Copy)
                for dc in range(2):
                    P.mm(pK[0:64, dc * 128:(dc + 1) * 128], KLT[:, dc, cs], IDb[:], True, True, reads=[rKLT, rT], writes=[rpK])
                for dc in range(2):
                    P.mm(pK[0:64, 256:320], KE[:, dc, cs], QE[:, dc, cs], dc == 0, dc == 1, reads=[rKE, rQE], writes=[rpK])
                E("act", "activation", [rpK], [rKL[ci]], out=KL[ci][:], in_=pK[0:64, 0:256], func=AF.Copy)
                E("dve", "tensor_tensor", [rpK, rT], [rSCT[ci]], out=SCT[ci][:], in0=pK[0:64, 256:320], in1=MSK[:, d, :], op=ALU.mult)
                P.mm(pO[0:64, :], SCT[ci][:], VT[ci][:], True, False, reads=[rSCT[ci], rVT[ci]], writes=[rpO])
                for dc in range(2):
                    P.mm(pO[0:64, :], QE[:, dc, cs], Sb[:, dc, :], False, dc == 1, reads=[rQE, rSb[dc]], writes=[rpO])
                lcol = ch * 64 + (63 if d == 0 else 0)
                for dc in range(2):
                    P.mm(pS[:, :], KL[ci][:, dc * 128:(dc + 1) * 128], VT[ci][:], True, True, reads=[rKL[ci], rVT[ci]], writes=[rpS])
                    E("dve", "scalar_tensor_tensor", [rpS, rS[dc], rEQ], [rS[dc]], out=S[:, dc, :], in0=S[:, dc, :],
                      scalar=EQ[:, dc, lcol:lcol + 1], in1=pS[:, :], op0=ALU.mult, op1=ALU.add)
                    E("act", "activation", [rS[dc]], [rSb[dc]], out=Sb[:, dc, :], in_=S[:, dc, :], func=AF.Copy)
                if not final:
                    E("act", "activation", [rpO], [rOS[ci]], out=OS[ci][:], in_=pO[0:64, :], func=AF.Copy)
                    P.dma("sp", of_s[grow:grow + 64, :], OS[ci][:], reads=[rOS[ci]], writes=[rOFS[grow // 64]])
                else:
                    P.dma("act", OF[ci][:], of_s[grow:grow + 64, :], reads=[rOFS[grow // 64]], writes=[rOF[ci]])
                    E("dve", "tensor_tensor", [rpO, rOF[ci]], [rOS[ci]], out=OS[ci][:], in0=pO[0:64, :], in1=OF[ci][:], op=ALU.add)
                    for k in range(KC):
                        P.mm(pV[0:64, :], Ht[:, k, cs], Wr[:, k, :], k == 0, k == KC - 1, reads=[rHt, rWr], writes=[rpV])
                    E("dve", "tensor_tensor", [rpV, rT], [rRT], out=RT[:], in0=pV[0:64, :], in1=BR[:], op=ALU.add)
                    E("act", "activation", [rRT], [rRT], out=RT[:], in_=RT[:], func=AF.Silu)
                    E(PENG, "tensor_tensor", [rOS[ci]], [rSQ], out=SQ[:], in0=OS[ci][:], in1=OS[ci][:], op=ALU.mult)
                    E("dve", "reduce_sum", [rSQ], [rSS], out=SS[:], in_=SQ[:], axis=AX.X)
                    E("dve", "tensor_scalar", [rSS], [rSS], out=SS[:], in0=SS[:], scalar1=1.0 / DV, scalar2=EPS, op0=ALU.mult, op1=ALU.add)
                    E("act", "activation", [rSS], [rSS], out=SS[:], in_=SS[:], func=AF.Sqrt)
                    E("dve", "reciprocal", [rSS], [rSS], out=SS[:], in_=SS[:])
                    E("dve", "scalar_tensor_tensor", [rOS[ci], rSS, rT], [rOS[ci]], out=OS[ci][:], in0=OS[ci][:], scalar=SS[:, 0:1],
                      in1=NG[:], op0=ALU.mult, op1=ALU.mult)
                    E("dve", "tensor_tensor", [rOS[ci], rRT], [rOUT[ci]], out=OUT[ci][:], in0=OS[ci][:], in1=RT[:], op=ALU.mult)
                    outs.append(P.dma("sp", oy[grow:grow + 64, :], OUT[ci][:], reads=[rOUT[ci]]))

        rOFS = [Res() for _ in range(NTOK // 64)]
        tile_pass(0, hcT, 0, False, 0, list(range(NCH)), 0, False)
        for ti in range(nlt):
            tile_pass(0, hT, ti * GT, True, ti * NCH, list(range(NCH)), CTX + ti * GT, False)
        for dc in range(2):
            E("dve", "memset", [], [rS[dc]], ap=S[:, dc, :], constant=0.0)
            E("dve", "memset", [], [rSb[dc]], ap=Sb[:, dc, :], constant=0.0)
        tile_pass(1, hcT, 0, False, 0, list(range(NCH))[::-1], 0, True)
        for ti in range(nlt)[::-1]:
            tile_pass(1, hT, ti * GT, True, ti * NCH, list(range(NCH))[::-1], CTX + ti * GT, True)
        P.final_wait("sp", outs)
        with nc.Block() as block:
            P.replay(block)
    return nc


def lb_inputs(inp, h1, hc1, core, tabs):
    import ml_dtypes
    b, hd = core // 4, core % 4
    ks = slice(hd * DK, (hd + 1) * DK)
    vs = slice(hd * DV, (hd + 1) * DV)
    d = dict(
        hT=np.ascontiguousarray(h1[b].T).astype(ml_dtypes.bfloat16),
        hcT=np.ascontiguousarray(hc1[b].T).astype(ml_dtypes.bfloat16),
        wq=np.ascontiguousarray(inp["gla_w_q"][0][:, ks]), wk=np.ascontiguousarray(inp["gla_w_k"][0][:, ks]),
        wv=np.ascontiguousarray(inp["gla_w_v"][0][:, vs]), wr=np.ascontiguousarray(inp["gla_w_r"][0][:, vs]),
        wa1=np.ascontiguousarray(inp["gla_w_a1"][0]), wa2=np.ascontiguousarray(inp["gla_w_a2"][0][:, :, ks]),
        bab=np.ascontiguousarray(np.broadcast_to(inp["gla_b_a"][0][:, None, ks], (2, 128, DK))),
        brb=np.ascontiguousarray(np.broadcast_to(inp["gla_b_r"][0][None, vs], (64, DV))),
        ngb=np.ascontiguousarray(np.broadcast_to(inp["gla_norm_g"][0][None, :], (64, DV))),
        tcr=tabs["cr"], tsr=tabs["sr"], tcc=tabs["cc"], tsc=tabs["sc"], ttri=tabs["tri"], tmsk=tabs["msk"], tid=tabs["ident"])
    return d


NH = 16
DH = 128
HPC = 4
ROWS = 128
GW_ = 64
NT = 512
NEG = -30000.0
NTOKA = SEQ + CTX
NVT = NTOKA // 128


def na_bias_tables(rpb):
    col = np.arange(64)
    cs = np.clip(col - 8, 0, 48)
    out = np.full((NH, 8, 64, 8, 64), NEG, np.float32)
    for v in range(8):
        delta = v - 7
        for j in range(8):
            ri = delta + j + 7
            for q in range(64):
                kc = cs[q] + np.arange(16)
                out[:, v, q, j, kc] = rpb[:, ri, kc - q + 15]
    return out.reshape(NH, 8, 64, 512)


def build_LD():
    nc = new_nc()
    dt = lambda n, s, d, k="ExternalInput": nc.dram_tensor(n, s, d, kind=k).ap()
    hT = dt("hT", [D, NTOKA], BF16)
    wq = dt("wq", [HPC, D, DH], F32)
    wk = dt("wk", [HPC, D, DH], F32)
    wv = dt("wv", [HPC, D, DH], F32)
    btab = dt("btab", [HPC, 8, 64, 512], F32)
    tid = dt("tid", [64, 64], F32)
    oy = dt("oy", [SEQ, HPC * DH], BF16, "ExternalOutput")
    with contextlib.ExitStack() as st:
        P = Prog(nc, st)
        sb = lambda n, s, d: st.enter_context(nc.sbuf_tensor("s_" + n, s, d))
        psb = lambda n: st.enter_context(nc.psum_tensor(n, [128, 512], F32))

        def E(eng, fn, reads, writes, **kw):
            return P.emit(eng, lambda e: getattr(e, fn)(**kw), reads, writes)

        W = [sb("W%d" % i, [128, 3, KC, DH], BF16) for i in range(2)]; rW = [Res(), Res()]
        BT = sb("BT", [64, 8, 512], F32); rBT = Res()
        ID = sb("ID", [64, 64], BF16); rID = Res()
        P.dma("pool", ID[:], tid, writes=[rID])
        H = [sb("H%d" % i, [128, KC, NT], BF16) for i in range(2)]; rH = [Res(), Res()]
        QT = sb("QT", [128, SEQ], BF16); rQT = Res()
        KT = sb("KT", [128, NTOKA], BF16); rKT = Res()
        VE = sb("VE", [128, NVT, DH + 1], BF16); rVE = Res()
        VO = sb("VO", [128, NVT, DH + 1], BF16); rVO = Res()
        SL = [sb("SL%d" % i, [64, 768], F32) for i in range(2)]; rSL = [Res(), Res()]
        MX = [sb("MX%d" % i, [64, 2], F32) for i in range(2)]; rMX = [Res(), Res()]
        PB = [sb("PB%d" % i, [64, 768], BF16) for i in range(2)]; rPB = [Res(), Res()]
        PT = [sb("PT%d" % i, [128, 384], BF16) for i in range(2)]; rPT = [Res(), Res()]
        RS = [sb("RS%d" % i, [64, 1], F32) for i in range(2)]; rRS = [Res(), Res()]
        OB = [sb("OB%d" % i, [64, DH], BF16) for i in range(2)]; rOB = [Res(), Res()]
        pP = [psb("pP0"), psb("pP1")]; rpP = [Res(excl=True), Res(excl=True)]
        pA = [psb("pA0"), psb("pA1")]; rpA = [Res(excl=True), Res(excl=True)]
        pB = psb("pB"); rpB = Res(excl=True)
        pT = [psb("pT0"), psb("pT1")]; rpT = [Res(excl=True), Res(excl=True)]
        pO = psb("pO"); rpO = Res(excl=True)

        E("dve", "memset", [], [rVE], ap=VE[:, :, DH:DH + 1], constant=1.0)
        outs = []
        hv = hT.rearrange("(c p) t -> p c t", p=128)
        cnt = {"h": 0, "p": 0, "u": 0}
        scale = DH ** -0.5
        for hh in range(HPC):
            Wt, rWt = W[hh % 2], rW[hh % 2]
            for i, wsrc in enumerate((wq, wk, wv)):
                P.dma("pool", Wt[:, i], wsrc[hh].rearrange("(kc p) f -> p kc f", p=128), writes=[rWt])
            P.dma("sp", BT[:], btab[hh].rearrange("v q k -> q v k"), writes=[rBT])
            for ti in range(NTOKA // NT + (1 if NTOKA % NT else 0)):
                t0 = ti * NT
                n = min(NT, NTOKA - t0)
                hi = cnt["h"] % 2
                cnt["h"] += 1
                Ht, rHt = H[hi], rH[hi]
                for half in range(2):
                    cs = slice(half * 8, (half + 1) * 8)
                    P.dma("sp" if half == 0 else "act", Ht[:, cs, :n], hv[:, cs, t0:t0 + n], writes=[rHt])
                lat = t0 < SEQ
                for which in ((0, 1) if lat else (1,)):
                    pi = cnt["p"] % 2
                    cnt["p"] += 1
                    pp, rpp = pP[pi], rpP[pi]
                    for k in range(KC):
                        P.mm(pp[:, :n], Wt[:, which, k, :], Ht[:, k, :n], k == 0, k == KC - 1, reads=[rWt, rHt], writes=[rpp])
                    if which == 0:
                        E("act", "activation", [rpp], [rQT], out=QT[:, t0:t0 + n], in_=pp[:, :n], func=AF.Identity, scale=scale)
                    else:
                        E("dve", "tensor_copy", [rpp], [rKT], out=KT[:, t0:t0 + n], in_=pp[:, :n])
                for s in range(n // 128):
                    pi = cnt["p"] % 2
                    cnt["p"] += 1
                    pp, rpp = pP[pi], rpP[pi]
                    for k in range(KC):
                        P.mm(pp[:, :DH], Ht[:, k, s * 128:(s + 1) * 128], Wt[:, 2, k, :], k == 0, k == KC - 1,
                             reads=[rWt, rHt], writes=[rpp])
                    E("act" if s % 2 else "dve", "activation" if s % 2 else "tensor_copy", [rpp], [rVE],
                      **(dict(out=VE[:, t0 // 128 + s, 0:DH], in_=pp[:, :DH], func=AF.Copy) if s % 2 else
                         dict(out=VE[:, t0 // 128 + s, 0:DH], in_=pp[:, :DH])))
            nlat = SEQ // 128
            P.dma("sp", VO[0:64, 0:nlat, :], VE[64:128, 0:nlat, :], reads=[rVE], writes=[rVO])
            P.dma("act", VO[64:128, 0:nlat - 1, :], VE[0:64, 1:nlat, :], reads=[rVE], writes=[rVO])
            for r in range(ROWS):
                u = cnt["u"] % 2
                cnt["u"] += 1
                rs = min(max(r - 4, 0), ROWS - 8)
                var = rs - r + 7
                qs = QT[:, r * 64:(r + 1) * 64]
                P.mm(pA[u][0:64, :], qs, KT[:, rs * 64:rs * 64 + 512], True, True, reads=[rQT, rKT], writes=[rpA[u]])
                P.mm(pB[0:64, 0:CTX], qs, KT[:, SEQ:SEQ + CTX], True, True, reads=[rQT, rKT], writes=[rpB])
                E("dve", "tensor_tensor", [rpA[u], rBT], [rSL[u]], out=SL[u][:, 0:512], in0=pA[u][0:64, :], in1=BT[:, var, :], op=ALU.add)
                E("act", "activation", [rpB], [rSL[u]], out=SL[u][:, 512:768], in_=pB[0:64, 0:CTX], func=AF.Copy)
                E("dve", "reduce_max", [rSL[u]], [rMX[u]], out=MX[u][:, 0:1], in_=SL[u][:], axis=AX.X)
                E("dve", "tensor_single_scalar", [rMX[u]], [rMX[u]], out=MX[u][:, 1:2], in_=MX[u][:, 0:1], scalar=-1.0, op=ALU.mult)
                E("act", "activation", [rSL[u], rMX[u]], [rPB[u]], out=PB[u][:], in_=SL[u][:], func=AF.Exp, bias=MX[u][:, 1:2])
                for kc in range(6):
                    P.mm(pT[u][:, kc * 64:(kc + 1) * 64], PB[u][:, kc * 128:(kc + 1) * 128], ID[:], True, True,
                         reads=[rPB[u], rID], writes=[rpT[u]])
                E("act" if r % 2 else "dve", "activation" if r % 2 else "tensor_copy", [rpT[u]], [rPT[u]],
                  **(dict(out=PT[u][:], in_=pT[u][:, 0:384], func=AF.Copy) if r % 2 else dict(out=PT[u][:], in_=pT[u][:, 0:384])))
                for kc in range(6):
                    if kc < 4:
                        if rs % 2 == 0:
                            vt, rv = VE[:, rs // 2 + kc, :], rVE
                        else:
                            vt, rv = VO[:, (rs - 1) // 2 + kc, :], rVO
                    else:
                        vt, rv = VE[:, SEQ // 128 + (kc - 4), :], rVE
                    P.mm(pO[0:64, 0:DH + 1], PT[u][:, kc * 64:(kc + 1) * 64], vt, kc == 0, kc == 5, reads=[rPT[u], rv], writes=[rpO])
                E("dve", "reciprocal", [rpO], [rRS[u]], out=RS[u][:], in_=pO[0:64, DH:DH + 1])
                E("act", "activation", [rpO, rRS[u]], [rOB[u]], out=OB[u][:], in_=pO[0:64, 0:DH], func=AF.Identity, scale=RS[u][:, 0:1])
                outs.append(P.dma("sp" if r % 2 else "act", oy[r * 64:(r + 1) * 64, hh * DH:(hh + 1) * DH], OB[u][:], reads=[rOB[u]]))
        P.final_wait("sp", outs)
        with nc.Block() as block:
            P.replay(block)
    return nc


def ld_inputs(inp, h2, hc2, core, btab_all):
    import ml_dtypes
    b, hg = core // 4, core % 4
    hcat = np.concatenate([h2[b], hc2[b]], 0)
    wqkv = inp["na_w_qkv"][0]
    sl = lambda base: np.ascontiguousarray(
        np.stack([wqkv[:, base + (hg * HPC + i) * DH: base + (hg * HPC + i + 1) * DH] for i in range(HPC)]))
    return dict(hT=np.ascontiguousarray(hcat.T).astype(ml_dtypes.bfloat16),
                wq=sl(0), wk=sl(D), wv=sl(2 * D),
                btab=np.ascontiguousarray(btab_all[hg * HPC:(hg + 1) * HPC]),
                tid=np.eye(64, dtype=np.float32))


_CACHE = {}


def _run(nc, maps):
    return run_bass_kernel_spmd(nc, maps, core_ids=list(range(NCORE))).results


def kernel(**inputs):
    import ml_dtypes
    inp = {k: np.asarray(v) for k, v in inputs.items()}
    cores = range(NCORE)
    WB = run_LW(inp)
    fw_ = lambda l, i: (WB["g%d%d" % (l, i)], WB["u%d%d" % (l, i)], WB["d%d%d" % (l, i)])
    m = run_L0(inp)
    nc, L = build_LA()
    maps = []
    for core in cores:
        xb, mk = lat_blocks(inp["x"], core, HALO)
        cb, cmk = ctx_block(inp["ctx"], core, HALO)
        msk = np.zeros((NB + 1, 128, TP), np.float32)
        msk[:NB] = mk[:, None, :]
        msk[NB, :, :CTP] = cmk[None, :]
        g00, u00, d00 = fw_(0, 0)
        g01, u01, d01 = fw_(0, 1)
        g10, u10, d10 = fw_(1, 0)
        maps.append(dict(xin=xb, msk=msk, cin=cb, pv=pv_fill(L, inp, m, core, [0, 1], [0]),
                         g00=g00, u00=u00, d00=d00, g01=g01, u01=u01, d01=d01, g10=g10, u10=u10, d10=d10,
                         cwi=WB["cwi0"], cwo=WB["cwo0"]))
    res = _run(nc, maps)
    x1 = gather_lat([r["xo"] for r in res])
    h1 = gather_lat([r["ho"] for r in res])
    xc1 = gather_ctx([r["xco"] for r in res])
    hc1 = gather_ctx([r["hco"] for r in res])
    del maps, res
    tabs = gla_tables()
    res = _run(build_LB(), [lb_inputs(inp, h1, hc1, core, tabs) for core in cores])
    y1 = np.zeros((2, SEQ, D), ml_dtypes.bfloat16)
    yc1 = np.zeros((2, CTX, D), ml_dtypes.bfloat16)
    for core in cores:
        b, hd = core // 4, core % 4
        o = np.asarray(res[core]["oy"])
        yc1[b, :, hd * DV:(hd + 1) * DV] = o[:CTX]
        y1[b, :, hd * DV:(hd + 1) * DV] = o[CTX:]
    del res
    nc, L = build_LC()
    maps = []
    for core in cores:
        g11, u11, d11 = fw_(1, 1)
        g20, u20, d20 = fw_(2, 0)
        maps.append(dict(xin=lat_blocks(x1, core, 0)[0], yin=lat_blocks(y1, core, 0)[0],
                         cin=ctx_block(xc1, core, 0)[0], ycin=ctx_block(yc1, core, 0)[0],
                         pv=pv_fill(L, inp, m, core, [1, 2], []),
                         g11=g11, u11=u11, d11=d11, g20=g20, u20=u20, d20=d20, wo=WB["gwo"]))
    res = _run(nc, maps)
    x2 = gather_lat([r["xo"] for r in res])
    h2 = gather_lat([r["ho"] for r in res])
    hc2 = gather_ctx([r["hco"] for r in res])
    del maps, res
    btab = na_bias_tables(inp["na_rpb"][0])
    res = _run(build_LD(), [ld_inputs(inp, h2, hc2, core, btab) for core in cores])
    y2 = np.zeros((2, SEQ, D), ml_dtypes.bfloat16)
    for core in cores:
        b, hg = core // 4, core % 4
        y2[b, :, hg * HPC * DH:(hg + 1) * HPC * DH] = np.asarray(res[core]["oy"])
    del res
    nc, L = build_LE()
    maps = []
    for core in cores:
        xb, mk = lat_blocks(x2, core, HALO)
        g21, u21, d21 = fw_(2, 1)
        g30, u30, d30 = fw_(3, 0)
        g31, u31, d31 = fw_(3, 1)
        maps.append(dict(xin=xb, yin=lat_blocks(y2, core, HALO)[0],
                         msk=np.ascontiguousarray(np.broadcast_to(mk[:, None, :], (NB, 128, TP))),
                         pv=pv_fill(L, inp, m, core, [2, 3], [3]),
                         g21=g21, u21=u21, d21=d21, g30=g30, u30=u30, d30=d30, g31=g31, u31=u31, d31=d31,
                         wo=WB["nwo"], cwi=WB["cwi1"], cwo=WB["cwo1"]))
    res = _run(nc, maps)
    out = gather_lat([r["xo"] for r in res])
    return out.astype(np.float32)
```

```python
import numpy as np
import contextlib
import concourse.bass as bass
import concourse.mybir as mybir
from concourse.bass_utils import run_bass_kernel_spmd

F32 = mybir.dt.float32
BF16 = mybir.dt.bfloat16
AF = mybir.ActivationFunctionType
ALU = mybir.AluOpType
AX = mybir.AxisListType


class Res:
    __slots__ = ("w", "rd", "name", "excl")

    def __init__(self, name="", excl=False):
        self.w = None
        self.rd = {}
        self.name = name
        self.excl = excl


class Prog:
    ENGS = ("pe", "act", "dve", "pool", "sp")
    NDS = 8

    def __init__(self, nc, stack):
        self.nc = nc
        self.q = {e: [] for e in self.ENGS}
        self.sem = {e: stack.enter_context(nc.semaphore("s_" + e)) for e in self.ENGS}
        self.cnt = {e: 0 for e in self.ENGS}
        self.seen = {e: {} for e in self.ENGS}
        self.dq = ("sp", "act", "pool")
        self.dsem = {e: [stack.enter_context(nc.semaphore("d_%s%d" % (e, i))) for i in range(self.NDS)]
                     for e in self.dq}
        self.dcnt = {e: 0 for e in self.dq}
        self.nwaits = 0

    def _need(self, eng, tok, waits, kind):
        if tok is None:
            return
        key, semh, val, teng = tok
        if teng == eng and kind != "raw":
            return
        if teng == "pe" and eng == "pe":
            return
        if self.seen[eng].get(key, 0) >= val:
            return
        self.seen[eng][key] = val
        waits.append((semh, val))

    def emit(self, eng, fn, reads=(), writes=(), dma=False):
        waits = []
        for r in reads:
            self._need(eng, r.w, waits, "raw")
            if r.excl:
                for t in r.rd.values():
                    self._need(eng, t, waits, "rar")
        for r in writes:
            self._need(eng, r.w, waits, "waw")
            for t in r.rd.values():
                self._need(eng, t, waits, "war")
        if dma:
            i = self.dcnt[eng]
            self.dcnt[eng] += 1
            slot, k = i % self.NDS, i // self.NDS
            semh = self.dsem[eng][slot]
            key = "d_%s%d" % (eng, slot)
            if k > 0:
                self._need(eng, (key, semh, 16 * k, None), waits, "raw")
            tok = (key, semh, 16 * (k + 1), None)
            inc = 16
        else:
            self.cnt[eng] += 1
            semh = self.sem[eng]
            tok = ("e_" + eng, semh, self.cnt[eng], eng)
            inc = 1
        for r in reads:
            r.rd[tok[0]] = tok
        for r in writes:
            r.w = tok
            r.rd = {}
        self.nwaits += len(waits)
        self.q[eng].append((waits, fn, semh, inc))
        return tok

    def dma(self, q, out, in_, reads=(), writes=()):
        return self.emit(q, lambda e: e.dma_start(out=out, in_=in_), reads, writes, dma=True)

    def mm(self, out, lhsT, rhs, start, stop, reads=(), writes=()):
        return self.emit("pe", lambda e: e.matmul(out, lhsT, rhs, start=start, stop=stop), reads, writes)

    def final_wait(self, eng, toks):
        waits = []
        for t in toks:
            self._need(eng, t, waits, "raw")
        self.q[eng].append((waits, None, None, 0))

    def replay(self, block):
        nc = self.nc
        names = {"pe": "tensor", "act": "scalar", "dve": "vector", "pool": "gpsimd", "sp": "sync"}
        for e in self.ENGS:
            ops = self.q[e]

            def body(eng, ops=ops):
                for waits, fn, semh, inc in ops:
                    for (s, v) in waits:
                        eng.wait_ge(s, v)
                    if fn is not None:
                        fn(eng).then_inc(semh, inc)
            getattr(block, names[e])(body)


D = 2048
DFF = 5632
KC = 16
JC = 44
EPS = 1e-6
CW = 31
HALO = 15
WB_ELEMS = 4096
NWB = 4
GW = 2


def pvec(v):
    v = np.asarray(v, np.float32).reshape(-1)
    return np.ascontiguousarray(v.reshape(-1, 128).T)


class TL:
    def __init__(self, nc, st, P, TMAX, pv_dram, L, conv=False, tcmax=None, ident_dram=None):
        self.nc, self.st, self.P, self.TMAX = nc, st, P, TMAX
        self.L = L
        npv = L.n
        sb = lambda n, s, d: st.enter_context(nc.sbuf_tensor("s_" + n, s, d))
        ps = lambda n: st.enter_context(nc.psum_tensor(n, [128, 512], F32))
        self.X = sb("X", [128, KC, TMAX], F32)
        self.rX = [Res("X%d" % c) for c in range(KC)]
        self.Hb = sb("Hb", [128, KC, TMAX], BF16)
        self.rHb = [Res("Hb%d" % c) for c in range(KC)]
        self.HID = sb("HID", [128, JC, TMAX], BF16)
        self.rHID = [Res("HID%d" % j) for j in range(JC)]
        self.Y = sb("Y", [128, KC, TMAX], F32)
        self.rY = [Res("Y%d" % c) for c in range(KC)]
        self.WB = [sb("WB%d" % i, [128, WB_ELEMS], BF16) for i in range(NWB)]
        self.rWB = [Res("WB%d" % i) for i in range(NWB)]
        self.wbi = 0
        self.pv = sb("pv", [128, npv], F32)
        self.rpv = Res("pv")
        self.ones = sb("ones", [128, 128], BF16)
        self.rones = Res("ones")
        self.sq = [sb("sq%d" % i, [128, TMAX], BF16) for i in range(2)]
        self.rsq = [Res() for _ in range(2)]
        self.sqi = 0
        self.tmp = [sb("tmp%d" % i, [128, TMAX], F32) for i in range(3)]
        self.rtmp = [Res() for _ in range(3)]
        self.tmi = 0
        self.R = sb("R", [128, TMAX], F32)
        self.rR = Res("R")
        self.R2 = sb("R2", [128, TMAX], F32)
        self.rR2 = Res("R2")
        self.pA = [ps("pA%d" % i) for i in range(2)]
        self.rpA = [Res(excl=True) for _ in range(2)]
        self.pB = [ps("pB%d" % i) for i in range(2)]
        self.rpB = [Res(excl=True) for _ in range(2)]
        self.pY = [ps("pY%d" % i) for i in range(2)]
        self.rpY = [Res(excl=True) for _ in range(2)]
        self.pS = ps("pS")
        self.rpS = Res("pS", excl=True)
        self.pS2 = ps("pS2")
        self.rpS2 = Res("pS2", excl=True)
        self.pai = 0
        self.pyi = 0
        if conv:
            self.V = sb("V", [128, KC, tcmax], F32)
            self.rV = [Res() for _ in range(KC)]
            self.U = [sb("U%d" % i, [128, TMAX], BF16) for i in range(2)]
            self.rU = [Res() for _ in range(2)]
            self.ui = 0
            self.NDG = 8
            self.DG = [sb("DG%d" % i, [128, 128], BF16) for i in range(self.NDG)]
            self.rDG = [Res() for _ in range(self.NDG)]
            self.dgi = 0
            self.ident = sb("ident", [128, 128], F32)
            self.rident = Res()
            self.MU = sb("MU", [128, tcmax], F32)
            self.rMU = Res()
            self.mask = sb("mask", [128, TMAX], F32)
            self.rmask = Res()
            P.dma("act", self.ident[:], ident_dram, writes=[self.rident])
        P.dma("sp", self.pv[:], pv_dram, writes=[self.rpv])
        P.emit("dve", lambda e: e.memset(self.ones[:], 1.0), writes=[self.rones])

    def wload(self, src, a, b):
        i = self.wbi
        self.wbi = (i + 1) % NWB
        view = self.WB[i][:, 0:a * b].rearrange("p (a b) -> p a b", a=a)
        q = "pool" if src.dtype == F32 else "sp"
        self.P.dma(q, view, src, writes=[self.rWB[i]])
        return view, self.rWB[i]

    def next_tmp(self):
        i = self.tmi
        self.tmi = (i + 1) % 3
        return self.tmp[i], self.rtmp[i]

    def next_sq(self):
        i = self.sqi
        self.sqi = (i + 1) % 2
        return self.sq[i], self.rsq[i]

    def pvs(self, off, c):
        return self.pv[:, off + c:off + c + 1]

    def derive(self, out_off, a_off, b_off, mode, scale=1.0):
        P, pv = self.P, self.pv
        o = pv[:, out_off:out_off + KC]
        a = pv[:, a_off:a_off + KC]
        b = pv[:, b_off:b_off + KC]
        if mode == "a1pb":
            P.emit("dve", lambda e: e.scalar_tensor_tensor(out=o, in0=b, scalar=1.0, in1=a, op0=ALU.add, op1=ALU.mult),
                   reads=[self.rpv], writes=[self.rpv])
        else:
            P.emit("dve", lambda e: e.scalar_tensor_tensor(out=o, in0=a, scalar=scale, in1=b, op0=ALU.mult, op1=ALU.mult),
                   reads=[self.rpv], writes=[self.rpv])

    def rstd_from(self, pS, rpS, T, out, rout):
        P = self.P
        P.emit("dve", lambda e: e.tensor_scalar(out=out[:, :T], in0=pS[:, :T], scalar1=1.0 / D, scalar2=EPS,
                                                op0=ALU.mult, op1=ALU.add), reads=[rpS], writes=[rout])
        P.emit("act", lambda e: e.activation(out=out[:, :T], in_=out[:, :T], func=AF.Sqrt), reads=[rout], writes=[rout])
        P.emit("dve", lambda e: e.reciprocal(out=out[:, :T], in_=out[:, :T]), reads=[rout], writes=[rout])

    def norm_mod(self, segs, an, sn):
        P, L = self.P, self.L
        Tt = sum(sg[1] for sg in segs)
        for c in range(KC):
            sq, rsq = self.next_sq()
            for (xoff, T, dst, pf) in segs:
                P.emit("act", lambda e, c=c, sq=sq, xoff=xoff, T=T, dst=dst: e.activation(
                    out=sq[:, dst:dst + T], in_=self.X[:, c, xoff:xoff + T], func=AF.Square),
                    reads=[self.rX[c]], writes=[rsq])
            P.mm(self.pS[:, :Tt], self.ones[:], sq[:, :Tt], c == 0, c == KC - 1, reads=[self.rones, rsq], writes=[self.rpS])
        self.rstd_from(self.pS, self.rpS, Tt, self.R, self.rR)
        for c in range(KC):
            t, rt = self.next_tmp()
            for (xoff, T, dst, pf) in segs:
                a_off, sh_off = L[pf + "_" + an], L[pf + "_" + sn]
                P.emit("dve", lambda e, c=c, t=t, xoff=xoff, T=T, dst=dst, a_off=a_off: e.scalar_tensor_tensor(
                    out=t[:, dst:dst + T], in0=self.X[:, c, xoff:xoff + T], scalar=self.pvs(a_off, c), in1=self.R[:, dst:dst + T],
                    op0=ALU.mult, op1=ALU.mult), reads=[self.rX[c], self.rR, self.rpv], writes=[rt])
                P.emit("act", lambda e, c=c, t=t, T=T, dst=dst, sh_off=sh_off: e.activation(
                    out=self.Hb[:, c, dst:dst + T], in_=t[:, dst:dst + T], func=AF.Identity, bias=self.pvs(sh_off, c)),
                    reads=[rt, self.rpv], writes=[self.rHb[c]])

    def _y_evac(self, c, T, py, rpy, bias_off=None):
        P = self.P
        if bias_off is None:
            P.emit("act", lambda e: e.activation(out=self.Y[:, c, :T], in_=py[:, :T], func=AF.Copy),
                   reads=[rpy], writes=[self.rY[c]])
        else:
            P.emit("act", lambda e: e.activation(out=self.Y[:, c, :T], in_=py[:, :T], func=AF.Identity,
                                                 bias=self.pvs(bias_off, c)),
                   reads=[rpy, self.rpv], writes=[self.rY[c]])
        sq, rsq = self.next_sq()
        P.emit("dve", lambda e: e.tensor_tensor(out=sq[:, :T], in0=self.Y[:, c, :T], in1=self.Y[:, c, :T], op=ALU.mult),
               reads=[self.rY[c]], writes=[rsq])
        return sq, rsq

    def _stat_mm(self, pend, T):
        c, sq, rsq = pend
        self.P.mm(self.pS2[:, :T], self.ones[:], sq[:, :T], c == 0, c == KC - 1,
                  reads=[self.rones, rsq], writes=[self.rpS2])

    def linear_dd(self, T, w, bias_off=None, kchunks=KC):
        P = self.P
        wv = w.rearrange("(kc p) f -> p kc f", p=128)
        pend = None
        for g in range(KC // GW):
            wt, rwt = self.wload(wv[:, :, g * GW * 128:(g + 1) * GW * 128], kchunks, GW * 128)
            for cc in range(GW):
                c = g * GW + cc
                i = self.pyi
                self.pyi = (i + 1) % 2
                py, rpy = self.pY[i], self.rpY[i]
                for k in range(kchunks):
                    P.mm(py[:, :T], wt[:, k, cc * 128:(cc + 1) * 128], self.Hb[:, k, :T], k == 0, k == kchunks - 1,
                         reads=[rwt, self.rHb[k]], writes=[rpy])
                if pend is not None:
                    self._stat_mm(pend, T)
                sq, rsq = self._y_evac(c, T, py, rpy, bias_off)
                pend = (c, sq, rsq)
        self._stat_mm(pend, T)
        self.rstd_from(self.pS2, self.rpS2, T, self.R2, self.rR2)

    def ffn(self, T, wg, wu, wd):
        P = self.P
        wgv = wg.rearrange("(kc p) f -> p kc f", p=128)
        wuv = wu.rearrange("(kc p) f -> p kc f", p=128)
        wdv = wd.rearrange("(j p) f -> p j f", p=128)
        for g in range(JC // GW):
            gt, rgt = self.wload(wgv[:, :, g * GW * 128:(g + 1) * GW * 128], KC, GW * 128)
            ut, rut = self.wload(wuv[:, :, g * GW * 128:(g + 1) * GW * 128], KC, GW * 128)
            for jj in range(GW):
                j = g * GW + jj
                i = self.pai
                self.pai = (i + 1) % 2
                pa, rpa, pb, rpb = self.pA[i], self.rpA[i], self.pB[i], self.rpB[i]
                for k in range(KC):
                    P.mm(pa[:, :T], gt[:, k, jj * 128:(jj + 1) * 128], self.Hb[:, k, :T], k == 0, k == KC - 1,
                         reads=[rgt, self.rHb[k]], writes=[rpa])
                for k in range(KC):
                    P.mm(pb[:, :T], ut[:, k, jj * 128:(jj + 1) * 128], self.Hb[:, k, :T], k == 0, k == KC - 1,
                         reads=[rut, self.rHb[k]], writes=[rpb])
                t, rt = self.next_tmp()
                P.emit("act", lambda e, t=t, pa=pa: e.activation(out=t[:, :T], in_=pa[:, :T], func=AF.Silu),
                       reads=[rpa], writes=[rt])
                P.emit("dve", lambda e, t=t, pb=pb, j=j: e.tensor_tensor(out=self.HID[:, j, :T], in0=t[:, :T], in1=pb[:, :T],
                                                                       op=ALU.mult),
                       reads=[rt, rpb], writes=[self.rHID[j]])
        pend = None
        for c in range(KC):
            JH = JC // 2
            wts = [self.wload(wdv[:, h * JH:(h + 1) * JH, c * 128:(c + 1) * 128], JH, 128) for h in range(2)]
            i = self.pyi
            self.pyi = (i + 1) % 2
            py, rpy = self.pY[i], self.rpY[i]
            for j in range(JC):
                wt, rwt = wts[j // JH]
                P.mm(py[:, :T], wt[:, j % JH, :], self.HID[:, j, :T], j == 0, j == JC - 1,
                     reads=[rwt, self.rHID[j]], writes=[rpy])
            if pend is not None:
                self._stat_mm(pend, T)
            sq, rsq = self._y_evac(c, T, py, rpy)
            pend = (c, sq, rsq)
        self._stat_mm(pend, T)
        self.rstd_from(self.pS2, self.rpS2, T, self.R2, self.rR2)

    def post_res(self, segs, gn):
        P, L = self.P, self.L
        for c in range(KC):
            t, rt = self.next_tmp()
            for (xoff, T, dst, pf) in segs:
                g_off = L[pf + "_" + gn]
                P.emit("dve", lambda e, c=c, t=t, T=T, dst=dst, g_off=g_off: e.scalar_tensor_tensor(
                    out=t[:, dst:dst + T], in0=self.Y[:, c, dst:dst + T], scalar=self.pvs(g_off, c), in1=self.R2[:, dst:dst + T],
                    op0=ALU.mult, op1=ALU.mult), reads=[self.rY[c], self.rR2, self.rpv], writes=[rt])
                P.emit("dve", lambda e, c=c, t=t, xoff=xoff, T=T, dst=dst: e.tensor_tensor(
                    out=self.X[:, c, xoff:xoff + T], in0=self.X[:, c, xoff:xoff + T], in1=t[:, dst:dst + T], op=ALU.add),
                    reads=[rt, self.rX[c]], writes=[self.rX[c]])

    def conv(self, Tp, csegs, w_in, w_out, o):
        P = self.P
        Tc = sum(sg[1] for sg in csegs)
        wv = w_in.rearrange("(kc p) f -> p kc f", p=128)
        tiles = {}

        def emit_in(c):
            g, cc = divmod(c, GW)
            if cc == 0:
                tiles[g] = (self.wload(wv[:, :, g * GW * 128:(g + 1) * GW * 128], KC, GW * 128),
                            self.wload(wv[:, :, D + g * GW * 128:D + (g + 1) * GW * 128], KC, GW * 128))
            (at, rat), (gt, rgt) = tiles[g]
            i = self.pai
            self.pai = (i + 1) % 2
            pa, rpa, pb, rpb = self.pA[i], self.rpA[i], self.pB[i], self.rpB[i]
            for k in range(KC):
                P.mm(pa[:, :Tp], at[:, k, cc * 128:(cc + 1) * 128], self.Hb[:, k, :Tp], k == 0, k == KC - 1,
                     reads=[rat, self.rHb[k]], writes=[rpa])
            for k in range(KC):
                P.mm(pb[:, :Tp], gt[:, k, cc * 128:(cc + 1) * 128], self.Hb[:, k, :Tp], k == 0, k == KC - 1,
                     reads=[rgt, self.rHb[k]], writes=[rpb])
            t, rt = self.next_tmp()
            P.emit("act", lambda e: e.activation(out=t[:, :Tp], in_=pb[:, :Tp], func=AF.Sigmoid, bias=self.pvs(o["b_in"] + KC, c)),
                   reads=[rpb, self.rpv], writes=[rt])
            P.emit("dve", lambda e: e.tensor_tensor(out=t[:, :Tp], in0=t[:, :Tp], in1=self.mask[:, :Tp], op=ALU.mult),
                   reads=[rt, self.rmask], writes=[rt])
            ui = self.ui
            self.ui = (ui + 1) % 2
            U, rU = self.U[ui], self.rU[ui]
            P.emit("dve", lambda e: e.scalar_tensor_tensor(
                out=U[:, :Tp], in0=pa[:, :Tp], scalar=self.pvs(o["b_in"], c), in1=t[:, :Tp], op0=ALU.add, op1=ALU.mult),
                reads=[rpa, rt, self.rpv], writes=[rU])
            return U, rU

        def emit_taps(c, U, rU):
            for k in range(CW):
                di = self.dgi
                self.dgi = (di + 1) % self.NDG
                dg, rdg = self.DG[di], self.rDG[di]
                wk = self.pv[:, o["w_dw"] + c * CW + k:o["w_dw"] + c * CW + k + 1]
                P.emit("act", lambda e, dg=dg, wk=wk: e.activation(out=dg[:], in_=self.ident[:], func=AF.Identity, scale=wk),
                       reads=[self.rident, self.rpv], writes=[rdg])
                for si, (po, tci, dst) in enumerate(csegs):
                    P.mm(self.pY[si][:, :tci], dg[:], U[:, po + k:po + k + tci], k == 0, k == CW - 1,
                         reads=[rdg, rU], writes=[self.rpY[si]])
            for si, (po, tci, dst) in enumerate(csegs):
                P.emit("dve", lambda e, si=si, tci=tci, dst=dst: e.tensor_single_scalar(
                    out=self.V[:, c, dst:dst + tci], in_=self.pY[si][:, :tci], scalar=self.pvs(o["b_dw"], c), op=ALU.add),
                    reads=[self.rpY[si], self.rpv], writes=[self.rV[c]])
            sq, rsq = self.next_sq()
            P.emit("act", lambda e, sq=sq: e.activation(out=sq[:, :Tc], in_=self.V[:, c, :Tc], func=AF.Copy),
                   reads=[self.rV[c]], writes=[rsq])
            P.mm(self.pS[:, :Tc], self.ones[:], sq[:, :Tc], c == 0, c == KC - 1, reads=[self.rones, rsq], writes=[self.rpS])
            sq, rsq = self.next_sq()
            P.emit("act", lambda e, sq=sq: e.activation(out=sq[:, :Tc], in_=self.V[:, c, :Tc], func=AF.Square),
                   reads=[self.rV[c]], writes=[rsq])
            P.mm(self.pS2[:, :Tc], self.ones[:], sq[:, :Tc], c == 0, c == KC - 1, reads=[self.rones, rsq], writes=[self.rpS2])

        prev = None
        for c in range(KC):
            cur = emit_in(c)
            if prev is not None:
                emit_taps(c - 1, *prev)
            prev = cur
        emit_taps(KC - 1, *prev)
        MU, rMU, R, rR = self.MU, self.rMU, self.R, self.rR
        P.emit("dve", lambda e: e.tensor_single_scalar(out=MU[:, :Tc], in_=self.pS[:, :Tc], scalar=1.0 / D, op=ALU.mult),
               reads=[self.rpS], writes=[rMU])
        t, rt = self.next_tmp()
        P.emit("dve", lambda e: e.tensor_tensor(out=t[:, :Tc], in0=MU[:, :Tc], in1=MU[:, :Tc], op=ALU.mult),
               reads=[rMU], writes=[rt])
        P.emit("dve", lambda e: e.scalar_tensor_tensor(out=R[:, :Tc], in0=self.pS2[:, :Tc], scalar=1.0 / D, in1=t[:, :Tc],
                                                       op0=ALU.mult, op1=ALU.subtract), reads=[self.rpS2, rt], writes=[rR])
        P.emit("dve", lambda e: e.tensor_single_scalar(out=R[:, :Tc], in_=R[:, :Tc], scalar=EPS, op=ALU.add),
               reads=[rR], writes=[rR])
        P.emit("act", lambda e: e.activation(out=R[:, :Tc], in_=R[:, :Tc], func=AF.Sqrt), reads=[rR], writes=[rR])
        P.emit("dve", lambda e: e.reciprocal(out=R[:, :Tc], in_=R[:, :Tc]), reads=[rR], writes=[rR])
        for c in range(KC):
            t, rt = self.next_tmp()
            P.emit("dve", lambda e, c=c, t=t: e.tensor_tensor(out=t[:, :Tc], in0=self.V[:, c, :Tc], in1=MU[:, :Tc], op=ALU.subtract),
                   reads=[self.rV[c], rMU], writes=[rt])
            P.emit("dve", lambda e, t=t: e.tensor_tensor(out=t[:, :Tc], in0=t[:, :Tc], in1=R[:, :Tc], op=ALU.mult),
                   reads=[rt, rR], writes=[rt])
            P.emit("act", lambda e, c=c, t=t: e.activation(out=self.Hb[:, c, :Tc], in_=t[:, :Tc], func=AF.Silu,
                                                             scale=self.pvs(o["ln_g"], c), bias=self.pvs(o["ln_b"], c)),
                   reads=[rt, self.rpv], writes=[self.rHb[c]])
        self.linear_dd(Tc, w_out, bias_off=o["b_out"])

    def load_x(self, src, T, q="sp", col0=0):
        v = src.rearrange("(c p) t -> p c t", p=128)
        for h in range(2):
            cs = slice(h * 8, (h + 1) * 8)
            self.P.dma(q, self.X[:, cs, col0:col0 + T], v[:, cs, :], writes=self.rX[cs])

    def load_hb(self, src, T, q="sp", col0=0):
        v = src.rearrange("(c p) t -> p c t", p=128)
        for h in range(2):
            cs = slice(h * 8, (h + 1) * 8)
            self.P.dma(q, self.Hb[:, cs, col0:col0 + T], v[:, cs, :], writes=self.rHb[cs])

    def store_x(self, dst, xoff, T, outs, q="sp"):
        v = dst.rearrange("(c p) t -> p c t", p=128)
        for h in range(2):
            cs = slice(h * 8, (h + 1) * 8)
            outs.append(self.P.dma(q, v[:, cs, :], self.X[:, cs, xoff:xoff + T], reads=self.rX[cs]))

    def store_hb(self, dst, T, outs, q="sp", col0=0):
        v = dst.rearrange("(c p) t -> p c t", p=128)
        for h in range(2):
            cs = slice(h * 8, (h + 1) * 8)
            outs.append(self.P.dma(q, v[:, cs, :], self.Hb[:, cs, col0:col0 + T], reads=self.rHb[cs]))

    def load_mask(self, src, T, q="sp"):
        self.P.dma(q, self.mask[:, :T], src, writes=[self.rmask])


NCORE = 8
TP = 440
TC = 410
NB = 5
LCORE = 2048
CCORE = 64
CTP = CCORE + 2 * HALO
SEQ = 8192
CTX = 256
NMOD = 9


def new_nc():
    return bass.Bass("TRN2", target_bir_lowering=False)


ACOLS = NMOD * D // NCORE


def build_L0():
    nc = new_nc()
    cT = nc.dram_tensor("cT", [128, KC * 3], F32, kind="ExternalInput").ap()
    aw = nc.dram_tensor("aw", [4, D, ACOLS], F32, kind="ExternalInput").ap()
    ab = nc.dram_tensor("ab", [4, 3, ACOLS], F32, kind="ExternalInput").ap()
    mo = nc.dram_tensor("mo", [4, 3, ACOLS], F32, kind="ExternalOutput").ap()
    with contextlib.ExitStack() as st:
        P = Prog(nc, st)
        sb = lambda n, s, d: st.enter_context(nc.sbuf_tensor("s_" + n, s, d))
        cs = sb("cs", [128, KC, 3], F32); rcs = Res()
        wt = [sb("wt%d" % i, [128, KC, 512], F32) for i in range(2)]; rwt = [Res(), Res()]
        bt = sb("bt", [3, 4, ACOLS], F32); rbt = Res()
        ot = sb("ot", [3, 4, ACOLS], F32); rot = Res()
        ps = [st.enter_context(nc.psum_tensor("ps%d" % i, [128, 512], F32)) for i in range(2)]; rps = [Res(excl=True), Res(excl=True)]
        P.dma("sp", cs[:].rearrange("p a b -> p (a b)"), cT, writes=[rcs])
        P.dma("act", bt[:], ab.rearrange("l r c -> r l c"), writes=[rbt])
        P.emit("act", lambda e: e.activation(out=cs[:], in_=cs[:], func=AF.Silu), reads=[rcs], writes=[rcs])
        tiles = [(l, c0, min(512, ACOLS - c0)) for l in range(4) for c0 in range(0, ACOLS, 512)]
        for i, (l, c0, n) in enumerate(tiles):
            w, rw = wt[i % 2], rwt[i % 2]
            src = aw[l].rearrange("(kc p) f -> p kc f", p=128)[:, :, c0:c0 + n]
            P.dma("sp" if i % 2 == 0 else "act", w[:, :, :n], src, writes=[rw])
            p, rp = ps[i % 2], rps[i % 2]
            for k in range(KC):
                P.mm(p[0:3, :n], cs[:, k, :], w[:, k, :n], k == 0, k == KC - 1, reads=[rcs, rw], writes=[rp])
            P.emit("dve", lambda e, p=p, l=l, c0=c0, n=n: e.tensor_tensor(
                out=ot[:, l, c0:c0 + n], in0=p[0:3, :n], in1=bt[:, l, c0:c0 + n], op=ALU.add),
                reads=[rp, rbt], writes=[rot])
        tok = P.dma("sp", mo.rearrange("l r c -> r l c"), ot[:], reads=[rot])
        P.final_wait("sp", [tok])
        with nc.Block() as block:
            P.replay(block)
    return nc


def run_L0(inp):
    c, c_ctx, ada_w, ada_b = inp["c"], inp["c_ctx"], inp["ada_w"], inp["ada_b"]
    cv = np.stack([c[0], c[1], c_ctx], 1)
    cT = np.ascontiguousarray(cv.reshape(KC, 128, 3).transpose(1, 0, 2).reshape(128, KC * 3))
    maps = []
    for k in range(NCORE):
        cols = slice(k * ACOLS, (k + 1) * ACOLS)
        maps.append(dict(cT=cT, aw=np.ascontiguousarray(ada_w[:, :, cols]),
                         ab=np.ascontiguousarray(np.broadcast_to(ada_b[:, None, cols], (4, 3, ACOLS)))))
    res = run_bass_kernel_spmd(build_L0(), maps, core_ids=list(range(NCORE)))
    m = np.concatenate([r["mo"] for r in res.results], axis=2)
    return m.reshape(4, 3, NMOD, D)


def layer_pv_names(tag):
    return ["%s_m%d" % (tag, j) for j in range(NMOD)] + ["%s_%s" % (tag, n) for n in ("A1", "G1", "A2", "G2", "A3", "G3")]


class PVL:
    def __init__(self):
        self.off = {}
        self.n = 0

    def add(self, name, w=KC):
        self.off[name] = self.n
        self.n += w

    def __getitem__(self, k):
        return self.off[k]


def pv_layout(layers, conv_layers):
    L = PVL()
    for l in layers:
        for j in range(6):
            L.add("g%d_%d" % (l, j))
        for s in ("lat", "ctx"):
            for n in layer_pv_names("L%d%s" % (l, s)):
                L.add(n)
    for l in conv_layers:
        L.add("c%d_b_in" % l, 32)
        L.add("c%d_w_dw" % l, KC * CW)
        for n in ("b_dw", "ln_g", "ln_b", "b_out"):
            L.add("c%d_%s" % (l, n))
    return L


def pv_fill(L, inp, m, core, layers, conv_layers):
    b = core // 4
    pv = np.zeros((128, L.n), np.float32)

    def put(name, v):
        a = pvec(v)
        pv[:, L[name]:L[name] + a.shape[1]] = a
    for l in layers:
        for j in range(6):
            put("g%d_%d" % (l, j), inp["norm_g"][l, j])
        for j in range(NMOD):
            put("L%dlat_m%d" % (l, j), m[l, b, j])
            put("L%dctx_m%d" % (l, j), m[l, 2, j])
    for l in conv_layers:
        j = l // 3
        put("c%d_b_in" % l, inp["conv_b_in"][j])
        wd = inp["conv_w_dw"][j]
        a = wd.T.reshape(KC, 128, CW).transpose(1, 0, 2).reshape(128, KC * CW)
        pv[:, L["c%d_w_dw" % l]:L["c%d_w_dw" % l] + KC * CW] = a
        put("c%d_b_dw" % l, inp["conv_b_dw"][j])
        put("c%d_ln_g" % l, inp["conv_ln_g"][j])
        put("c%d_ln_b" % l, inp["conv_ln_b"][j])
        put("c%d_b_out" % l, inp["conv_b_out"][j])
    return pv


def derive_layer(T, L, l, s):
    t = "L%d%s" % (l, s)
    g = lambda j: L["g%d_%d" % (l, j)]
    T.derive(L[t + "_A1"], g(0), L[t + "_m1"], "a1pb")
    T.derive(L[t + "_G1"], L[t + "_m2"], g(1), "ab", 0.5)
    T.derive(L[t + "_A2"], g(2), L[t + "_m4"], "a1pb")
    T.derive(L[t + "_G2"], L[t + "_m5"], g(3), "ab", 1.0)
    T.derive(L[t + "_A3"], g(4), L[t + "_m7"], "a1pb")
    T.derive(L[t + "_G3"], L[t + "_m8"], g(5), "ab", 0.5)


def conv_offs(L, l):
    return {k: L["c%d_%s" % (l, k)] for k in ("b_in", "w_dw", "b_dw", "ln_g", "ln_b", "b_out")}


NBA, TCA = 6, 342
TPA = TCA + 2 * HALO


def _wdecl(dt, names_gu, names_d, extra, WDT):
    W = {}
    for n in names_gu:
        W[n] = dt(n, [D, DFF], WDT, "ExternalInput")
    for n in names_d:
        W[n] = dt(n, [DFF, D], WDT, "ExternalInput")
    for n, shp in extra:
        W[n] = dt(n, shp, WDT, "ExternalInput")
    return W


def build_LA(WDT=BF16):
    nc = new_nc()
    L = pv_layout([0, 1], [0])
    dt = lambda n, s, d, k: nc.dram_tensor(n, s, d, kind=k).ap()
    TM = TPA + CTP
    xin = dt("xin", [NBA, D, TPA], F32, "ExternalInput")
    msk = dt("msk", [NBA, 128, TM], F32, "ExternalInput")
    cin = dt("cin", [D, CTP], F32, "ExternalInput")
    pvd = dt("pv", [128, L.n], F32, "ExternalInput")
    idn = dt("idn", [128, 128], F32, "ExternalInput")
    W = _wdecl(dt, ("g00", "u00", "g01", "u01", "g10", "u10"), ("d00", "d01", "d10"),
               (("cwi", [D, 2 * D]), ("cwo", [D, D])), WDT)
    xo = dt("xo", [NBA, D, TCA], F32, "ExternalOutput")
    ho = dt("ho", [NBA, D, TCA], BF16, "ExternalOutput")
    xco = dt("xco", [D, CCORE], F32, "ExternalOutput")
    hco = dt("hco", [D, CCORE], BF16, "ExternalOutput")
    with contextlib.ExitStack() as st:
        P = Prog(nc, st)
        T = TL(nc, st, P, TM, pvd, L, conv=True, tcmax=TCA + CCORE, ident_dram=idn)
        for l in (0, 1):
            for s in ("lat", "ctx"):
                derive_layer(T, L, l, s)
        outs = []
        co = conv_offs(L, 0)
        for i in range(NBA):
            two = (i == 0)
            tp = TM if two else TPA
            tc = TCA + CCORE if two else TCA
            pre = lambda l: [(0, TPA, 0, "L%dlat" % l)] + ([(TPA, CTP, TPA, "L%dctx" % l)] if two else [])
            post = lambda l: [(HALO, TCA, 0, "L%dlat" % l)] + ([(TPA + HALO, CCORE, TCA, "L%dctx" % l)] if two else [])
            csegs = [(0, TCA, 0)] + ([(TPA, CCORE, TCA)] if two else [])
            T.load_x(xin[i], TPA)
            if two:
                T.load_x(cin, CTP, col0=TPA, q="act")
            T.load_mask(msk[i, :, :tp], tp)
            T.norm_mod(pre(0), "A1", "m0")
            T.ffn(tp, W["g00"], W["u00"], W["d00"])
            T.post_res(pre(0), "G1")
            T.norm_mod(pre(0), "A2", "m3")
            T.conv(tp, csegs, W["cwi"], W["cwo"], co)
            T.post_res(post(0), "G2")
            T.norm_mod(post(0), "A3", "m6")
            T.ffn(tc, W["g01"], W["u01"], W["d01"])
            T.post_res(post(0), "G3")
            T.norm_mod(post(1), "A1", "m0")
            T.ffn(tc, W["g10"], W["u10"], W["d10"])
            T.post_res(post(1), "G1")
            T.norm_mod(post(1), "A2", "m3")
            T.store_x(xo[i], HALO, TCA, outs)
            T.store_hb(ho[i], TCA, outs, q="act")
            if two:
                T.store_x(xco, TPA + HALO, CCORE, outs)
                T.store_hb(hco, CCORE, outs, q="act", col0=TCA)
        P.final_wait("sp", outs)
        with nc.Block() as block:
            P.replay(block)
    return nc, L


def build_LC(WDT=BF16):
    nc = new_nc()
    L = pv_layout([1, 2], [])
    dt = lambda n, s, d, k: nc.dram_tensor(n, s, d, kind=k).ap()
    TM = TC + CCORE
    xin = dt("xin", [NB, D, TC], F32, "ExternalInput")
    yin = dt("yin", [NB, D, TC], BF16, "ExternalInput")
    cin = dt("cin", [D, CCORE], F32, "ExternalInput")
    ycin = dt("ycin", [D, CCORE], BF16, "ExternalInput")
    pvd = dt("pv", [128, L.n], F32, "ExternalInput")
    W = _wdecl(dt, ("g11", "u11", "g20", "u20"), ("d11", "d20"), (("wo", [D, D]),), WDT)
    xo = dt("xo", [NB, D, TC], F32, "ExternalOutput")
    ho = dt("ho", [NB, D, TC], BF16, "ExternalOutput")
    hco = dt("hco", [D, CCORE], BF16, "ExternalOutput")
    with contextlib.ExitStack() as st:
        P = Prog(nc, st)
        T = TL(nc, st, P, TM, pvd, L, conv=False)
        for l in (1, 2):
            for s in ("lat", "ctx"):
                derive_layer(T, L, l, s)
        outs = []
        for i in range(NB):
            two = (i == 0)
            tc = TM if two else TC
            sg = lambda l: [(0, TC, 0, "L%dlat" % l)] + ([(TC, CCORE, TC, "L%dctx" % l)] if two else [])
            T.load_x(xin[i], TC)
            T.load_hb(yin[i], TC, q="act")
            if two:
                T.load_x(cin, CCORE, col0=TC)
                T.load_hb(ycin, CCORE, q="act", col0=TC)
            T.linear_dd(tc, W["wo"])
            T.post_res(sg(1), "G2")
            T.norm_mod(sg(1), "A3", "m6")
            T.ffn(tc, W["g11"], W["u11"], W["d11"])
            T.post_res(sg(1), "G3")
            T.norm_mod(sg(2), "A1", "m0")
            T.ffn(tc, W["g20"], W["u20"], W["d20"])
            T.post_res(sg(2), "G1")
            T.norm_mod(sg(2), "A2", "m3")
            T.store_x(xo[i], 0, TC, outs)
            T.store_hb(ho[i], TC, outs, q="act")
            if two:
                T.store_hb(hco, CCORE, outs, q="act", col0=TC)
        P.final_wait("sp", outs)
        with nc.Block() as block:
            P.replay(block)
    return nc, L


def build_LE(WDT=BF16):
    nc = new_nc()
    L = pv_layout([2, 3], [3])
    dt = lambda n, s, d, k: nc.dram_tensor(n, s, d, kind=k).ap()
    xin = dt("xin", [NB, D, TP], F32, "ExternalInput")
    yin = dt("yin", [NB, D, TP], BF16, "ExternalInput")
    msk = dt("msk", [NB, 128, TP], F32, "ExternalInput")
    pvd = dt("pv", [128, L.n], F32, "ExternalInput")
    idn = dt("idn", [128, 128], F32, "ExternalInput")
    W = _wdecl(dt, ("g21", "u21", "g30", "u30", "g31", "u31"), ("d21", "d30", "d31"),
               (("wo", [D, D]), ("cwi", [D, 2 * D]), ("cwo", [D, D])), WDT)
    xo = dt("xo", [NB, D, TC], F32, "ExternalOutput")
    with contextlib.ExitStack() as st:
        P = Prog(nc, st)
        T = TL(nc, st, P, TP, pvd, L, conv=True, tcmax=TC, ident_dram=idn)
        for l in (2, 3):
            derive_layer(T, L, l, "lat")
        outs = []
        co = conv_offs(L, 3)
        for i in range(NB):
            pre = lambda l: [(0, TP, 0, "L%dlat" % l)]
            post = lambda l: [(HALO, TC, 0, "L%dlat" % l)]
            T.load_x(xin[i], TP)
            T.load_hb(yin[i], TP, q="act")
            T.load_mask(msk[i], TP)
            T.linear_dd(TP, W["wo"])
            T.post_res(pre(2), "G2")
            T.norm_mod(pre(2), "A3", "m6")
            T.ffn(TP, W["g21"], W["u21"], W["d21"])
            T.post_res(pre(2), "G3")
            T.norm_mod(pre(3), "A1", "m0")
            T.ffn(TP, W["g30"], W["u30"], W["d30"])
            T.post_res(pre(3), "G1")
            T.norm_mod(pre(3), "A2", "m3")
            T.conv(TP, [(0, TC, 0)], W["cwi"], W["cwo"], co)
            T.post_res(post(3), "G2")
            T.norm_mod(post(3), "A3", "m6")
            T.ffn(TC, W["g31"], W["u31"], W["d31"])
            T.post_res(post(3), "G3")
            T.store_x(xo[i], HALO, TC, outs)
        P.final_wait("sp", outs)
        with nc.Block() as block:
            P.replay(block)
    return nc, L


def lat_blocks(xfull, core, halo, nb=NB, tc=TC):
    b, q = core // 4, core % 4
    out = np.zeros((nb, D, tc + 2 * halo), xfull.dtype)
    msk = np.zeros((nb, tc + 2 * halo), np.float32)
    for i in range(nb):
        s = q * LCORE + i * tc - halo
        e = s + tc + 2 * halo
        s0, e0 = max(s, 0), min(e, SEQ)
        out[i, :, s0 - s:e0 - s] = xfull[b, s0:e0].T
        msk[i, s0 - s:e0 - s] = 1.0
    return out, msk


def ctx_block(cfull, core, halo):
    b, q = core // 4, core % 4
    w = CCORE + 2 * halo
    out = np.zeros((D, w), cfull.dtype)
    msk = np.zeros((w,), np.float32)
    s = q * CCORE - halo
    e = s + w
    s0, e0 = max(s, 0), min(e, CTX)
    out[:, s0 - s:e0 - s] = cfull[b, s0:e0].T
    msk[s0 - s:e0 - s] = 1.0
    return out, msk


def gather_lat(blocks_per_core, dtype=None, nb=NB, tc=TC):
    d = blocks_per_core[0].dtype if dtype is None else dtype
    out = np.zeros((2, SEQ, D), d)
    for core, blk in enumerate(blocks_per_core):
        b, q = core // 4, core % 4
        for i in range(nb):
            s = q * LCORE + i * tc
            n = min(tc, (q + 1) * LCORE - s)
            out[b, s:s + n] = blk[i, :, :n].T
    return out


def gather_ctx(per_core):
    out = np.zeros((2, CTX, D), per_core[0].dtype)
    for core, a in enumerate(per_core):
        b, q = core // 4, core % 4
        out[b, q * CCORE:(q + 1) * CCORE] = a.T
    return out


CAST_CH = 8192


def cast_weight_list(inp):
    lst = []
    for l in range(4):
        for i in range(2):
            lst.append(("g%d%d" % (l, i), inp["ffn_w_gate"][l, i]))
            lst.append(("u%d%d" % (l, i), inp["ffn_w_up"][l, i]))
            lst.append(("d%d%d" % (l, i), inp["ffn_w_down"][l, i]))
    for j in range(2):
        lst.append(("cwi%d" % j, inp["conv_w_in"][j]))
        lst.append(("cwo%d" % j, inp["conv_w_out"][j]))
    lst.append(("gwo", inp["gla_w_o"][0]))
    lst.append(("nwo", inp["na_w_o"][0]))
    return lst


def build_LW(ncols):
    nc = new_nc()
    wi = nc.dram_tensor("wi", [128, ncols], F32, kind="ExternalInput").ap()
    wo = nc.dram_tensor("wo", [128, ncols], BF16, kind="ExternalOutput").ap()
    with contextlib.ExitStack() as st:
        P = Prog(nc, st)
        NBUF = 4
        bufs = [st.enter_context(nc.sbuf_tensor("s_cb%d" % i, [128, CAST_CH], BF16)) for i in range(NBUF)]
        rb = [Res() for _ in range(NBUF)]
        outs = []
        i = 0
        for c0 in range(0, ncols, CAST_CH):
            n = min(CAST_CH, ncols - c0)
            b, r = bufs[i % NBUF], rb[i % NBUF]
            P.dma("pool", b[:, :n], wi[:, c0:c0 + n], writes=[r])
            outs.append(P.dma("sp" if i % 2 == 0 else "act", wo[:, c0:c0 + n], b[:, :n], reads=[r]))
            i += 1
        P.final_wait("sp", outs)
        with nc.Block() as block:
            P.replay(block)
    return nc


def run_LW(inp):
    lst = cast_weight_list(inp)
    per = [a.size // (NCORE * 128) for _, a in lst]
    ncols = sum(per)
    maps = []
    for k in range(NCORE):
        parts = [a.reshape(NCORE, 128, -1)[k] for _, a in lst]
        maps.append(dict(wi=np.ascontiguousarray(np.concatenate(parts, axis=1))))
    res = run_bass_kernel_spmd(build_LW(ncols), maps, core_ids=list(range(NCORE))).results
    outs = [np.asarray(r["wo"]) for r in res]
    W = {}
    off = 0
    for (key, a), n in zip(lst, per):
        W[key] = np.stack([o[:, off:off + n] for o in outs]).reshape(a.shape)
        off += n
    return W


GH = 4
DK = 256
DV = 512
GT = 256
NCH = GT // 64
NLT = SEQ // GT
NTOK = CTX + SEQ
ROPE_THETA = 10000.0
PENG = "dve"


def gla_tables():
    inv = ROPE_THETA ** (-np.arange(0, 128, 2, dtype=np.float32) / 128)
    p = np.arange(128)
    sgn = np.where(p < 64, -1.0, 1.0).astype(np.float32)
    rows = np.arange(SEQ // 64, dtype=np.float32)
    cols = np.arange(64, dtype=np.float32)
    ang_r = rows[None, :] * inv[p % 64][:, None]
    ang_c = cols[None, :] * inv[p % 64][:, None]
    t = {}
    t["cr"] = np.cos(ang_r).astype(np.float32)
    t["sr"] = (np.sin(ang_r) * sgn[:, None]).astype(np.float32)
    t["cc"] = np.cos(ang_c).astype(np.float32)
    t["sc"] = (np.sin(ang_c) * sgn[:, None]).astype(np.float32)
    s = np.arange(128)[:, None]
    tt = np.arange(128)[None, :]
    same = (s // 64) == (tt // 64)
    g = -1.0 / 16.0
    t["tri"] = np.stack([np.where(same & (s <= tt), g, 0.0), np.where(same & (s > tt), g, 0.0),
                         np.where(same & (s >= tt), g, 0.0), np.where(same & (s < tt), g, 0.0)]).astype(np.float32)
    s = np.arange(64)[:, None]
    tt = np.arange(64)[None, :]
    t["msk"] = np.stack([(s <= tt), (s >= tt)]).astype(np.float32)
    t["ident"] = np.eye(128, dtype=np.float32)
    return t


def build_LB(nlt=NLT):
    nc = new_nc()
    SEQ = nlt * GT
    NTOK = CTX + SEQ
    dt = lambda n, s, d, k="ExternalInput": nc.dram_tensor(n, s, d, kind=k).ap()
    hT = dt("hT", [D, SEQ], BF16)
    hcT = dt("hcT", [D, CTX], BF16)
    wq = dt("wq", [D, DK], F32)
    wk = dt("wk", [D, DK], F32)
    wv = dt("wv", [D, DV], F32)
    wr = dt("wr", [D, DV], F32)
    wa1 = dt("wa1", [2, D, 16], F32)
    wa2 = dt("wa2", [2, 16, DK], F32)
    bab = dt("bab", [2, 128, DK], F32)
    brb = dt("brb", [64, DV], F32)
    ngb = dt("ngb", [64, DV], F32)
    tcr = dt("tcr", [128, 128], F32)
    tsr = dt("tsr", [128, 128], F32)
    tcc = dt("tcc", [128, 64], F32)
    tsc = dt("tsc", [128, 64], F32)
    ttri = dt("ttri", [4, 128, 128], F32)
    tmsk = dt("tmsk", [2, 64, 64], F32)
    tid = dt("tid", [128, 128], F32)
    oy = dt("oy", [NTOK, DV], BF16, "ExternalOutput")
    of_s = dt("of_s", [NTOK, DV], F32, "ExternalOutput")
    with contextlib.ExitStack() as st:
        P = Prog(nc, st)
        sb = lambda n, s, d: st.enter_context(nc.sbuf_tensor("s_" + n, s, d))
        psb = lambda n: st.enter_context(nc.psum_tensor(n, [128, 512], F32))

        def E(eng, fn, reads, writes, **kw):
            return P.emit(eng, lambda e: getattr(e, fn)(**kw), reads, writes)

        Wq = sb("Wq", [128, KC, DK], BF16); rWq = Res()
        Wk = sb("Wk", [128, KC, DK], BF16); rWk = Res()
        Wv = sb("Wv", [128, KC, DV], BF16); rWv = Res()
        Wr = sb("Wr", [128, KC, DV], BF16); rWr = Res()
        Wa1 = sb("Wa1", [128, 2, KC, 16], BF16); rWa1 = Res()
        Wa2 = sb("Wa2", [16, 2, DK], BF16); rWa2 = Res()
        P.dma("pool", Wq[:], wq.rearrange("(kc p) f -> p kc f", p=128), writes=[rWq])
        P.dma("pool", Wk[:], wk.rearrange("(kc p) f -> p kc f", p=128), writes=[rWk])
        P.dma("pool", Wv[:], wv.rearrange("(kc p) f -> p kc f", p=128), writes=[rWv])
        P.dma("pool", Wr[:], wr.rearrange("(kc p) f -> p kc f", p=128), writes=[rWr])
        for d in range(2):
            P.dma("pool", Wa1[:, d], wa1[d].rearrange("(kc p) f -> p kc f", p=128), writes=[rWa1])
        P.dma("pool", Wa2[:], wa2.rearrange("d r f -> r d f"), writes=[rWa2])
        rT = Res()
        BA = sb("BA", [128, 2, DK], F32)
        P.dma("sp", BA[:], bab.rearrange("d p f -> p d f"), writes=[rT])
        BR = sb("BR", [64, DV], F32); P.dma("sp", BR[:], brb, writes=[rT])
        NG = sb("NG", [64, DV], F32); P.dma("sp", NG[:], ngb, writes=[rT])
        CR = sb("CR", [128, 128], F32); P.dma("sp", CR[:], tcr, writes=[rT])
        SR = sb("SR", [128, 128], F32); P.dma("sp", SR[:], tsr, writes=[rT])
        CC = sb("CC", [128, 64], F32); P.dma("sp", CC[:], tcc, writes=[rT])
        SC = sb("SC", [128, 64], F32); P.dma("sp", SC[:], tsc, writes=[rT])
        TRI = sb("TRI", [128, 4, 128], F32); P.dma("sp", TRI[:], ttri.rearrange("a p f -> p a f"), writes=[rT])
        MSK = sb("MSK", [64, 2, 64], F32); P.dma("sp", MSK[:], tmsk.rearrange("a p f -> p a f"), writes=[rT])
        IDb = sb("IDb", [128, 128], BF16); P.dma("pool", IDb[:], tid, writes=[rT])

        H = [sb("H%d" % i, [128, KC, GT], BF16) for i in range(2)]; rH = [Res(), Res()]
        QS = sb("QS", [128, GT], F32); rQS = Res()
        QW = sb("QW", [128, GT], F32); rQW = Res()
        QR = sb("QR", [128, 2, GT], F32); rQR = [Res(), Res()]
        KR = sb("KR", [128, 2, GT], F32); rKR = [Res(), Res()]
        ZL = sb("ZL", [16, GT], BF16); rZL = Res()
        ZB = sb("ZB", [128, 2, DK], F32); rZB = Res()
        EQ = sb("EQ", [128, 2, GT], F32); rEQ = Res()
        EK = sb("EK", [128, 2, GT], F32); rEK = Res()
        EL = sb("EL", [128, 2, GT], F32); rEL = Res()
        QE = sb("QE", [128, 2, GT], BF16); rQE = Res()
        KE = sb("KE", [128, 2, GT], BF16); rKE = Res()
        KLT = sb("KLT", [128, 2, GT], BF16); rKLT = Res()
        S = sb("S", [128, 2, DV], F32); rS = [Res(), Res()]
        Sb = sb("Sb", [128, 2, DV], BF16); rSb = [Res(), Res()]
        VT = [sb("VT%d" % i, [64, DV], BF16) for i in range(2)]; rVT = [Res(), Res()]
        KL = [sb("KL%d" % i, [64, DK], BF16) for i in range(2)]; rKL = [Res(), Res()]
        SCT = [sb("SCT%d" % i, [64, 64], BF16) for i in range(2)]; rSCT = [Res(), Res()]
        OS = [sb("OS%d" % i, [64, DV], F32) for i in range(2)]; rOS = [Res(), Res()]
        OF = [sb("OF%d" % i, [64, DV], F32) for i in range(2)]; rOF = [Res(), Res()]
        SQ = sb("SQ", [64, DV], F32); rSQ = Res()
        RT = [sb("RT%d" % i, [64, DV], F32) for i in range(2)]; rRT = [Res(), Res()]
        SS = sb("SS", [64, 1], F32); rSS = Res()
        OUT = [sb("OUT%d" % i, [64, DV], BF16) for i in range(2)]; rOUT = [Res(), Res()]
        pP = [psb("pP0"), psb("pP1")]; rpP = [Res(excl=True), Res(excl=True)]
        pZ = psb("pZ"); rpZ = Res(excl=True)
        pC = psb("pC"); rpC = Res(excl=True)
        pV = psb("pV"); rpV = Res(excl=True)
        pK = psb("pK"); rpK = Res(excl=True)
        pO = psb("pO"); rpO = Res(excl=True)
        pS = psb("pS"); rpS = Res(excl=True)

        for dc in range(2):
            E("dve", "memset", [], [rS[dc]], ap=S[:, dc, :], constant=0.0)
            E("dve", "memset", [], [rSb[dc]], ap=Sb[:, dc, :], constant=0.0)

        outs = []
        cnt = {"h": 0, "p": 0, "c": 0}

        def rope_evac(ps, rps, dst, rdst, row0, is_q, lat):
            sc = (1.0 / 16.0) if is_q else 1.0
            if not lat:
                E("act", "activation", [rps], [rdst], out=dst, in_=ps, func=AF.Identity, scale=sc)
                return
            E("act", "activation", [rps], [rQS], out=QS[:], in_=ps, func=AF.Identity, scale=sc)
            E(PENG, "tensor_copy", [rQS], [rQW], out=QW[0:64, :], in_=QS[64:128, :])
            E(PENG, "tensor_copy", [rQS], [rQW], out=QW[64:128, :], in_=QS[0:64, :])
            return

        def tile_pass(d, src, t0, lat, grow0, ch_order, tok0, final):
            hi = cnt["h"] % 2
            cnt["h"] += 1
            Ht, rHt = H[hi], rH[hi]
            v = src.rearrange("(c p) t -> p c t", p=128)
            for hh in range(2):
                cs = slice(hh * 8, (hh + 1) * 8)
                P.dma("sp" if hh == 0 else "act", Ht[:, cs, :], v[:, cs, t0:t0 + GT], writes=[rHt])
            for (Wm, rW, dst, rdst, is_q) in ((Wq, rWq, QR, rQR, True), (Wk, rWk, KR, rKR, False)):
                for dc in range(2):
                    pi = cnt["p"] % 2
                    cnt["p"] += 1
                    pp, rpp = pP[pi], rpP[pi]
                    for k in range(KC):
                        P.mm(pp[:, :GT], Wm[:, k, dc * 128:(dc + 1) * 128], Ht[:, k, :], k == 0, k == KC - 1,
                             reads=[rW, rHt], writes=[rpp])
                    sc = (1.0 / 16.0) if is_q else 1.0
                    if not lat:
                        E("act", "activation", [rpp], [rdst[dc]], out=dst[:, dc, :], in_=pp[:, :GT], func=AF.Identity, scale=sc)
                    else:
                        E("act", "activation", [rpp], [rQS], out=QS[:], in_=pp[:, :GT], func=AF.Identity, scale=sc)
                        E(PENG, "tensor_copy", [rQS], [rQW], out=QW[0:64, :], in_=QS[64:128, :])
                        E(PENG, "tensor_copy", [rQS], [rQW], out=QW[64:128, :], in_=QS[0:64, :])
                        q3 = lambda a: a.rearrange("p (r c) -> p r c", c=64)
                        if dc == 0:
                            ct = CR[:, grow0:grow0 + NCH].unsqueeze(2).to_broadcast([128, NCH, 64])
                            stb = SR[:, grow0:grow0 + NCH].unsqueeze(2).to_broadcast([128, NCH, 64])
                        else:
                            ct = CC[:].unsqueeze(1).to_broadcast([128, NCH, 64])
                            stb = SC[:].unsqueeze(1).to_broadcast([128, NCH, 64])
                        E("dve", "tensor_tensor", [rQS, rT], [rdst[dc]], out=q3(dst[:, dc, :]), in0=q3(QS[:]), in1=ct, op=ALU.mult)
                        E(PENG, "tensor_tensor", [rQW, rT], [rQW], out=q3(QW[:]), in0=q3(QW[:]), in1=stb, op=ALU.mult)
                        E("dve", "tensor_tensor", [rQW, rdst[dc]], [rdst[dc]], out=dst[:, dc, :], in0=dst[:, dc, :], in1=QW[:], op=ALU.add)
            pi = cnt["p"] % 2
            cnt["p"] += 1
            pp, rpp = pP[pi], rpP[pi]
            for k in range(KC):
                P.mm(pp[0:16, :GT], Wa1[:, d, k, :], Ht[:, k, :], k == 0, k == KC - 1, reads=[rWa1, rHt], writes=[rpp])
            E("act", "activation", [rpp], [rZL], out=ZL[:], in_=pp[0:16, :GT], func=AF.Copy)
            for pr in range(2):
                P.mm(pZ[:, pr * DK:(pr + 1) * DK], ZL[:, pr * 128:(pr + 1) * 128], Wa2[:, d, :], True, True,
                     reads=[rZL, rWa2], writes=[rpZ])
            z3 = pZ[:].rearrange("p (a f) -> p a f", a=2)
            E("dve", "tensor_tensor", [rpZ, rT], [rZB], out=ZB[:], in0=z3,
              in1=BA[:, d, :].unsqueeze(1).to_broadcast([128, 2, DK]), op=ALU.add)
            E("act", "activation", [rZB], [rZB], out=ZB[:], in_=ZB[:], func=AF.Exp, scale=-1.0)
            E("act", "activation", [rZB], [rZB], out=ZB[:], in_=ZB[:], func=AF.Ln, bias=1.0)
            for which, (dstE, rdstE, scl) in enumerate(((None, None, None), (EL, rEL, 1.0))):
                tri = TRI[:, 2 * d + which, :]
                for dc in range(2):
                    for pr in range(2):
                        P.mm(pC[:, dc * GT + pr * 128:dc * GT + (pr + 1) * 128], ZB[:, pr, dc * 128:(dc + 1) * 128], tri,
                             True, True, reads=[rZB, rT], writes=[rpC])
                pc3 = pC[:].rearrange("p (a f) -> p a f", a=2)
                if which == 0:
                    E("act", "activation", [rpC], [rEQ], out=EQ[:], in_=pc3, func=AF.Exp)
                    E("act", "activation", [rpC], [rEK], out=EK[:], in_=pc3, func=AF.Exp, scale=-1.0)
                else:
                    E("act", "activation", [rpC], [rEL], out=EL[:], in_=pc3, func=AF.Exp)
            E("dve", "tensor_tensor", rQR + [rEQ], [rQE], out=QE[:], in0=QR[:], in1=EQ[:], op=ALU.mult)
            E(PENG, "tensor_tensor", rKR + [rEK], [rKE], out=KE[:], in0=KR[:], in1=EK[:], op=ALU.mult)
            E("dve", "tensor_tensor", rKR + [rEL], [rKLT], out=KLT[:], in0=KR[:], in1=EL[:], op=ALU.mult)
            def stage_a(ch, ci):
                cs = slice(ch * 64, (ch + 1) * 64)
                for k in range(KC):
                    P.mm(pV[0:64, :], Ht[:, k, cs], Wv[:, k, :], k == 0, k == KC - 1, reads=[rHt, rWv], writes=[rpV])
                E("act", "activation", [rpV], [rVT[ci]], out=VT[ci][:], in_=pV[0:64, :], func=AF.Copy)
                for dc in range(2):
                    P.mm(pK[0:64, dc * 128:(dc + 1) * 128], KLT[:, dc, cs], IDb[:], True, True, reads=[rKLT, rT], writes=[rpK])
                for dc in range(2):
                    P.mm(pK[0:64, 256:320], KE[:, dc, cs], QE[:, dc, cs], dc == 0, dc == 1, reads=[rKE, rQE], writes=[rpK])
                E("act", "activation", [rpK], [rKL[ci]], out=KL[ci][:], in_=pK[0:64, 0:256], func=AF.Copy)
                E("dve", "tensor_tensor", [rpK, rT], [rSCT[ci]], out=SCT[ci][:], in0=pK[0:64, 256:320], in1=MSK[:, d, :], op=ALU.mult)
                if final:
                    for k in range(KC):
                        P.mm(pZ[0:64, :], Ht[:, k, cs], Wr[:, k, :], k == 0, k == KC - 1, reads=[rHt, rWr], writes=[rpZ])
                    E("dve", "tensor_tensor", [rpZ, rT], [rRT[ci]], out=RT[ci][:], in0=pZ[0:64, :], in1=BR[:], op=ALU.add)
                    E("act", "activation", [rRT[ci]], [rRT[ci]], out=RT[ci][:], in_=RT[ci][:], func=AF.Silu)

            def stage_b(ch, ci):
                cs = slice(ch * 64, (ch + 1) * 64)
                grow = tok0 + ch * 64
                P.mm(pO[0:64, :], SCT[ci][:], VT[ci][:], True, False, reads=[rSCT[ci], rVT[ci]], writes=[rpO])
                for dc in range(2):
                    P.mm(pO[0:64, :], QE[:, dc, cs], Sb[:, dc, :], False, dc == 1, reads=[rQE, rSb[dc]], writes=[rpO])
                lcol = ch * 64 + (63 if d == 0 else 0)
                for dc in range(2):
                    P.mm(pS[:, :], KL[ci][:, dc * 128:(dc + 1) * 128], VT[ci][:], True, True, reads=[rKL[ci], rVT[ci]], writes=[rpS])
                    E("dve", "scalar_tensor_tensor", [rpS, rS[dc], rEQ], [rS[dc]], out=S[:, dc, :], in0=S[:, dc, :],
                      scalar=EQ[:, dc, lcol:lcol + 1], in1=pS[:, :], op0=ALU.mult, op1=ALU.add)
                    E("act", "activation", [rS[dc]], [rSb[dc]], out=Sb[:, dc, :], in_=S[:, dc, :], func=AF.Copy)
                if not final:
                    E("act", "activation", [rpO], [rOS[ci]], out=OS[ci][:], in_=pO[0:64, :], func=AF.Copy)
                    P.dma("sp", of_s[grow:grow + 64, :], OS[ci][:], reads=[rOS[ci]], writes=[rOFS[grow // 64]])
                else:
                    P.dma("sp", OF[ci][:], of_s[grow:grow + 64, :], reads=[rOFS[grow // 64]], writes=[rOF[ci]])
                    E("dve", "tensor_tensor", [rpO, rOF[ci]], [rOS[ci]], out=OS[ci][:], in0=pO[0:64, :], in1=OF[ci][:], op=ALU.add)
                    E("dve", "tensor_tensor", [rOS[ci]], [rSQ], out=SQ[:], in0=OS[ci][:], in1=OS[ci][:], op=ALU.mult)
                    E("dve", "reduce_sum", [rSQ], [rSS], out=SS[:], in_=SQ[:], axis=AX.X)
                    E("dve", "tensor_scalar", [rSS], [rSS], out=SS[:], in0=SS[:], scalar1=1.0 / DV, scalar2=EPS, op0=ALU.mult, op1=ALU.add)
                    E("act", "activation", [rSS], [rSS], out=SS[:], in_=SS[:], func=AF.Sqrt)
                    E("dve", "reciprocal", [rSS], [rSS], out=SS[:], in_=SS[:])
                    E("dve", "scalar_tensor_tensor", [rOS[ci], rSS, rT], [rOS[ci]], out=OS[ci][:], in0=OS[ci][:], scalar=SS[:, 0:1],
                      in1=NG[:], op0=ALU.mult, op1=ALU.mult)
                    E("dve", "tensor_tensor", [rOS[ci], rRT[ci]], [rOUT[ci]], out=OUT[ci][:], in0=OS[ci][:], in1=RT[ci][:], op=ALU.mult)
                    outs.append(P.dma("sp", oy[grow:grow + 64, :], OUT[ci][:], reads=[rOUT[ci]]))

            stage_a(ch_order[0], 0)
            for i_, ch in enumerate(ch_order):
                if i_ + 1 < len(ch_order):
                    stage_a(ch_order[i_ + 1], (i_ + 1) % 2)
                stage_b(ch, i_ % 2)

        rOFS = [Res() for _ in range(NTOK // 64)]
        tile_pass(0, hcT, 0, False, 0, list(range(NCH)), 0, False)
        for ti in range(nlt):
            tile_pass(0, hT, ti * GT, True, ti * NCH, list(range(NCH)), CTX + ti * GT, False)
        for dc in range(2):
            E("dve", "memset", [], [rS[dc]], ap=S[:, dc, :], constant=0.0)
            E("dve", "memset", [], [rSb[dc]], ap=Sb[:, dc, :], constant=0.0)
        tile_pass(1, hcT, 0, False, 0, list(range(NCH))[::-1], 0, True)
        for ti in range(nlt)[::-1]:
            tile_pass(1, hT, ti * GT, True, ti * NCH, list(range(NCH))[::-1], CTX + ti * GT, True)
        P.final_wait("sp", outs)
        with nc.Block() as block:
            P.replay(block)
    return nc


def lb_inputs(inp, h1, hc1, core, tabs):
    import ml_dtypes
    b, hd = core // 4, core % 4
    ks = slice(hd * DK, (hd + 1) * DK)
    vs = slice(hd * DV, (hd + 1) * DV)
    d = dict(
        hT=np.ascontiguousarray(h1[b].T).astype(ml_dtypes.bfloat16),
        hcT=np.ascontiguousarray(hc1[b].T).astype(ml_dtypes.bfloat16),
        wq=np.ascontiguousarray(inp["gla_w_q"][0][:, ks]), wk=np.ascontiguousarray(inp["gla_w_k"][0][:, ks]),
        wv=np.ascontiguousarray(inp["gla_w_v"][0][:, vs]), wr=np.ascontiguousarray(inp["gla_w_r"][0][:, vs]),
        wa1=np.ascontiguousarray(inp["gla_w_a1"][0]), wa2=np.ascontiguousarray(inp["gla_w_a2"][0][:, :, ks]),
        bab=np.ascontiguousarray(np.broadcast_to(inp["gla_b_a"][0][:, None, ks], (2, 128, DK))),
        brb=np.ascontiguousarray(np.broadcast_to(inp["gla_b_r"][0][None, vs], (64, DV))),
        ngb=np.ascontiguousarray(np.broadcast_to(inp["gla_norm_g"][0][None, :], (64, DV))),
        tcr=tabs["cr"], tsr=tabs["sr"], tcc=tabs["cc"], tsc=tabs["sc"], ttri=tabs["tri"], tmsk=tabs["msk"], tid=tabs["ident"])
    return d


NH = 16
DH = 128
HPC = 4
ROWS = 128
GW_ = 64
NT = 512
NEG = -30000.0
NTOKA = SEQ + CTX
NVT = NTOKA // 128


def na_bias_tables(rpb):
    col = np.arange(64)
    cs = np.clip(col - 8, 0, 48)
    out = np.full((NH, 8, 64, 8, 64), NEG, np.float32)
    for v in range(8):
        delta = v - 7
        for j in range(8):
            ri = delta + j + 7
            for q in range(64):
                kc = cs[q] + np.arange(16)
                out[:, v, q, j, kc] = rpb[:, ri, kc - q + 15]
    return out.reshape(NH, 8, 64, 512)


def build_LD():
    nc = new_nc()
    dt = lambda n, s, d, k="ExternalInput": nc.dram_tensor(n, s, d, kind=k).ap()
    hT = dt("hT", [D, NTOKA], BF16)
    wq = dt("wq", [HPC, D, DH], F32)
    wk = dt("wk", [HPC, D, DH], F32)
    wv = dt("wv", [HPC, D, DH], F32)
    btab = dt("btab", [HPC, 8, 64, 512], F32)
    tid = dt("tid", [64, 64], F32)
    oy = dt("oy", [SEQ, HPC * DH], BF16, "ExternalOutput")
    with contextlib.ExitStack() as st:
        P = Prog(nc, st)
        sb = lambda n, s, d: st.enter_context(nc.sbuf_tensor("s_" + n, s, d))
        psb = lambda n: st.enter_context(nc.psum_tensor(n, [128, 512], F32))

        def E(eng, fn, reads, writes, **kw):
            return P.emit(eng, lambda e: getattr(e, fn)(**kw), reads, writes)

        W = [sb("W%d" % i, [128, 3, KC, DH], BF16) for i in range(2)]; rW = [Res(), Res()]
        BT = sb("BT", [64, 8, 512], F32); rBT = Res()
        ID = sb("ID", [64, 64], BF16); rID = Res()
        P.dma("pool", ID[:], tid, writes=[rID])
        H = [sb("H%d" % i, [128, KC, NT], BF16) for i in range(2)]; rH = [Res(), Res()]
        QT = sb("QT", [128, SEQ], BF16); rQT = Res()
        KT = sb("KT", [128, NTOKA], BF16); rKT = Res()
        VE = sb("VE", [128, NVT, DH + 1], BF16); rVE = Res()
        VO = sb("VO", [128, NVT, DH + 1], BF16); rVO = Res()
        SL = [sb("SL%d" % i, [64, 768], F32) for i in range(2)]; rSL = [Res(), Res()]
        MX = [sb("MX%d" % i, [64, 2], F32) for i in range(2)]; rMX = [Res(), Res()]
        PB = [sb("PB%d" % i, [64, 768], BF16) for i in range(2)]; rPB = [Res(), Res()]
        PT = [sb("PT%d" % i, [128, 384], BF16) for i in range(2)]; rPT = [Res(), Res()]
        RS = [sb("RS%d" % i, [64, 1], F32) for i in range(2)]; rRS = [Res(), Res()]
        OB = [sb("OB%d" % i, [64, DH], BF16) for i in range(2)]; rOB = [Res(), Res()]
        pP = [psb("pP0"), psb("pP1")]; rpP = [Res(excl=True), Res(excl=True)]
        pA = [psb("pA0"), psb("pA1")]; rpA = [Res(excl=True), Res(excl=True)]
        pB = psb("pB"); rpB = Res(excl=True)
        pT = [psb("pT0"), psb("pT1")]; rpT = [Res(excl=True), Res(excl=True)]
        pO = psb("pO"); rpO = Res(excl=True)

        E("dve", "memset", [], [rVE], ap=VE[:, :, DH:DH + 1], constant=1.0)
        outs = []
        hv = hT.rearrange("(c p) t -> p c t", p=128)
        cnt = {"h": 0, "p": 0, "u": 0}
        scale = DH ** -0.5
        for hh in range(HPC):
            Wt, rWt = W[hh % 2], rW[hh % 2]
            for i, wsrc in enumerate((wq, wk, wv)):
                P.dma("pool", Wt[:, i], wsrc[hh].rearrange("(kc p) f -> p kc f", p=128), writes=[rWt])
            P.dma("sp", BT[:], btab[hh].rearrange("v q k -> q v k"), writes=[rBT])
            for ti in range(NTOKA // NT + (1 if NTOKA % NT else 0)):
                t0 = ti * NT
                n = min(NT, NTOKA - t0)
                hi = cnt["h"] % 2
                cnt["h"] += 1
                Ht, rHt = H[hi], rH[hi]
                for half in range(2):
                    cs = slice(half * 8, (half + 1) * 8)
                    P.dma("sp" if half == 0 else "act", Ht[:, cs, :n], hv[:, cs, t0:t0 + n], writes=[rHt])
                lat = t0 < SEQ
                for which in ((0, 1) if lat else (1,)):
                    pi = cnt["p"] % 2
                    cnt["p"] += 1
                    pp, rpp = pP[pi], rpP[pi]
                    for k in range(KC):
                        P.mm(pp[:, :n], Wt[:, which, k, :], Ht[:, k, :n], k == 0, k == KC - 1, reads=[rWt, rHt], writes=[rpp])
                    if which == 0:
                        E("act", "activation", [rpp], [rQT], out=QT[:, t0:t0 + n], in_=pp[:, :n], func=AF.Identity, scale=scale)
                    else:
                        E("dve", "tensor_copy", [rpp], [rKT], out=KT[:, t0:t0 + n], in_=pp[:, :n])
                for s in range(n // 128):
                    pi = cnt["p"] % 2
                    cnt["p"] += 1
                    pp, rpp = pP[pi], rpP[pi]
                    for k in range(KC):
                        P.mm(pp[:, :DH], Ht[:, k, s * 128:(s + 1) * 128], Wt[:, 2, k, :], k == 0, k == KC - 1,
                             reads=[rWt, rHt], writes=[rpp])
                    E("act" if s % 2 else "dve", "activation" if s % 2 else "tensor_copy", [rpp], [rVE],
                      **(dict(out=VE[:, t0 // 128 + s, 0:DH], in_=pp[:, :DH], func=AF.Copy) if s % 2 else
                         dict(out=VE[:, t0 // 128 + s, 0:DH], in_=pp[:, :DH])))
            nlat = SEQ // 128
            P.dma("sp", VO[0:64, 0:nlat, :], VE[64:128, 0:nlat, :], reads=[rVE], writes=[rVO])
            P.dma("act", VO[64:128, 0:nlat - 1, :], VE[0:64, 1:nlat, :], reads=[rVE], writes=[rVO])
            def s1(r):
                u = r % 2
                rs = min(max(r - 4, 0), ROWS - 8)
                var = rs - r + 7
                qs = QT[:, r * 64:(r + 1) * 64]
                P.mm(pA[u][0:64, :], qs, KT[:, rs * 64:rs * 64 + 512], True, True, reads=[rQT, rKT], writes=[rpA[u]])
                P.mm(pB[0:64, 0:CTX], qs, KT[:, SEQ:SEQ + CTX], True, True, reads=[rQT, rKT], writes=[rpB])
                E("dve", "tensor_tensor", [rpA[u], rBT], [rSL[u]], out=SL[u][:, 0:512], in0=pA[u][0:64, :], in1=BT[:, var, :], op=ALU.add)
                E("act", "activation", [rpB], [rSL[u]], out=SL[u][:, 512:768], in_=pB[0:64, 0:CTX], func=AF.Copy)
                E("dve", "reduce_max", [rSL[u]], [rMX[u]], out=MX[u][:, 0:1], in_=SL[u][:], axis=AX.X)
                E("dve", "tensor_single_scalar", [rMX[u]], [rMX[u]], out=MX[u][:, 1:2], in_=MX[u][:, 0:1], scalar=-1.0, op=ALU.mult)
                E("act", "activation", [rSL[u], rMX[u]], [rPB[u]], out=PB[u][:], in_=SL[u][:], func=AF.Exp, bias=MX[u][:, 1:2])

            def s2(r):
                u = r % 2
                for kc in range(6):
                    P.mm(pT[u][:, kc * 64:(kc + 1) * 64], PB[u][:, kc * 128:(kc + 1) * 128], ID[:], True, True,
                         reads=[rPB[u], rID], writes=[rpT[u]])
                E("act" if r % 4 < 2 else "dve", "activation" if r % 4 < 2 else "tensor_copy", [rpT[u]], [rPT[u]],
                  **(dict(out=PT[u][:], in_=pT[u][:, 0:384], func=AF.Copy) if r % 4 < 2 else dict(out=PT[u][:], in_=pT[u][:, 0:384])))

            def s3(r):
                u = r % 2
                rs = min(max(r - 4, 0), ROWS - 8)
                for kc in range(6):
                    if kc < 4:
                        if rs % 2 == 0:
                            vt, rv = VE[:, rs // 2 + kc, :], rVE
                        else:
                            vt, rv = VO[:, (rs - 1) // 2 + kc, :], rVO
                    else:
                        vt, rv = VE[:, SEQ // 128 + (kc - 4), :], rVE
                    P.mm(pO[0:64, 0:DH + 1], PT[u][:, kc * 64:(kc + 1) * 64], vt, kc == 0, kc == 5, reads=[rPT[u], rv], writes=[rpO])
                E("dve", "reciprocal", [rpO], [rRS[u]], out=RS[u][:], in_=pO[0:64, DH:DH + 1])
                E("act", "activation", [rpO, rRS[u]], [rOB[u]], out=OB[u][:], in_=pO[0:64, 0:DH], func=AF.Identity, scale=RS[u][:, 0:1])
                outs.append(P.dma("sp", oy[r * 64:(r + 1) * 64, hh * DH:(hh + 1) * DH], OB[u][:], reads=[rOB[u]]))

            for it in range(ROWS + 2):
                if it < ROWS:
                    s1(it)
                if 0 <= it - 1 < ROWS:
                    s2(it - 1)
                if 0 <= it - 2 < ROWS:
                    s3(it - 2)
        P.final_wait("sp", outs)
        with nc.Block() as block:
            P.replay(block)
    return nc


def ld_inputs(inp, h2, hc2, core, btab_all):
    import ml_dtypes
    b, hg = core // 4, core % 4
    hcat = np.concatenate([h2[b], hc2[b]], 0)
    wqkv = inp["na_w_qkv"][0]
    sl = lambda base: np.ascontiguousarray(
        np.stack([wqkv[:, base + (hg * HPC + i) * DH: base + (hg * HPC + i + 1) * DH] for i in range(HPC)]))
    return dict(hT=np.ascontiguousarray(hcat.T).astype(ml_dtypes.bfloat16),
                wq=sl(0), wk=sl(D), wv=sl(2 * D),
                btab=np.ascontiguousarray(btab_all[hg * HPC:(hg + 1) * HPC]),
                tid=np.eye(64, dtype=np.float32))


_CACHE = {}


def _run(nc, maps):
    return run_bass_kernel_spmd(nc, maps, core_ids=list(range(NCORE))).results


def kernel(**inputs):
    import ml_dtypes
    inp = {k: np.asarray(v) for k, v in inputs.items()}
    cores = range(NCORE)
    WB = run_LW(inp)
    fw_ = lambda l, i: (WB["g%d%d" % (l, i)], WB["u%d%d" % (l, i)], WB["d%d%d" % (l, i)])
    m = run_L0(inp)
    nc, L = build_LA()
    eye = np.eye(128, dtype=np.float32)
    maps = []
    for core in cores:
        xb, mk = lat_blocks(inp["x"], core, HALO, NBA, TCA)
        cb, cmk = ctx_block(inp["ctx"], core, HALO)
        msk = np.zeros((NBA, 128, TPA + CTP), np.float32)
        msk[:, :, :TPA] = mk[:, None, :]
        msk[0, :, TPA:] = cmk[None, :]
        g00, u00, d00 = fw_(0, 0)
        g01, u01, d01 = fw_(0, 1)
        g10, u10, d10 = fw_(1, 0)
        maps.append(dict(xin=xb, msk=msk, cin=cb, pv=pv_fill(L, inp, m, core, [0, 1], [0]), idn=eye,
                         g00=g00, u00=u00, d00=d00, g01=g01, u01=u01, d01=d01, g10=g10, u10=u10, d10=d10,
                         cwi=WB["cwi0"], cwo=WB["cwo0"]))
    res = _run(nc, maps)
    x1 = gather_lat([r["xo"] for r in res], nb=NBA, tc=TCA)
    h1 = gather_lat([r["ho"] for r in res], nb=NBA, tc=TCA)
    xc1 = gather_ctx([r["xco"] for r in res])
    hc1 = gather_ctx([r["hco"] for r in res])
    del maps, res
    tabs = gla_tables()
    res = _run(build_LB(), [lb_inputs(inp, h1, hc1, core, tabs) for core in cores])
    y1 = np.zeros((2, SEQ, D), ml_dtypes.bfloat16)
    yc1 = np.zeros((2, CTX, D), ml_dtypes.bfloat16)
    for core in cores:
        b, hd = core // 4, core % 4
        o = np.asarray(res[core]["oy"])
        yc1[b, :, hd * DV:(hd + 1) * DV] = o[:CTX]
        y1[b, :, hd * DV:(hd + 1) * DV] = o[CTX:]
    del res
    nc, L = build_LC()
    maps = []
    for core in cores:
        g11, u11, d11 = fw_(1, 1)
        g20, u20, d20 = fw_(2, 0)
        maps.append(dict(xin=lat_blocks(x1, core, 0)[0], yin=lat_blocks(y1, core, 0)[0],
                         cin=ctx_block(xc1, core, 0)[0], ycin=ctx_block(yc1, core, 0)[0],
                         pv=pv_fill(L, inp, m, core, [1, 2], []),
                         g11=g11, u11=u11, d11=d11, g20=g20, u20=u20, d20=d20, wo=WB["gwo"]))
    res = _run(nc, maps)
    x2 = gather_lat([r["xo"] for r in res])
    h2 = gather_lat([r["ho"] for r in res])
    hc2 = gather_ctx([r["hco"] for r in res])
    del maps, res
    btab = na_bias_tables(inp["na_rpb"][0])
    res = _run(build_LD(), [ld_inputs(inp, h2, hc2, core, btab) for core in cores])
    y2 = np.zeros((2, SEQ, D), ml_dtypes.bfloat16)
    for core in cores:
        b, hg = core // 4, core % 4
        y2[b, :, hg * HPC * DH:(hg + 1) * HPC * DH] = np.asarray(res[core]["oy"])
    del res
    nc, L = build_LE()
    maps = []
    for core in cores:
        xb, mk = lat_blocks(x2, core, HALO)
        g21, u21, d21 = fw_(2, 1)
        g30, u30, d30 = fw_(3, 0)
        g31, u31, d31 = fw_(3, 1)
        maps.append(dict(xin=xb, yin=lat_blocks(y2, core, HALO)[0], idn=eye,
                         msk=np.ascontiguousarray(np.broadcast_to(mk[:, None, :], (NB, 128, TP))),
                         pv=pv_fill(L, inp, m, core, [2, 3], [3]),
                         g21=g21, u21=u21, d21=d21, g30=g30, u30=u30, d30=d30, g31=g31, u31=u31, d31=d31,
                         wo=WB["nwo"], cwi=WB["cwi1"], cwo=WB["cwo1"]))
    res = _run(nc, maps)
    out = gather_lat([r["xo"] for r in res])
    return out.astype(np.float32)
```

```python
import numpy as np
import contextlib
import concourse.bass as bass
import concourse.mybir as mybir
from concourse.bass_utils import run_bass_kernel_spmd

F32 = mybir.dt.float32
BF16 = mybir.dt.bfloat16
AF = mybir.ActivationFunctionType
ALU = mybir.AluOpType
AX = mybir.AxisListType


class Res:
    __slots__ = ("w", "rd", "name", "excl")

    def __init__(self, name="", excl=False):
        self.w = None
        self.rd = {}
        self.name = name
        self.excl = excl


class Prog:
    ENGS = ("pe", "act", "dve", "pool", "sp")
    NDS = 8

    def __init__(self, nc, stack):
        self.nc = nc
        self.q = {e: [] for e in self.ENGS}
        self.sem = {e: stack.enter_context(nc.semaphore("s_" + e)) for e in self.ENGS}
        self.cnt = {e: 0 for e in self.ENGS}
        self.seen = {e: {} for e in self.ENGS}
        self.dq = ("sp", "act", "pool")
        self.dsem = {e: [stack.enter_context(nc.semaphore("d_%s%d" % (e, i))) for i in range(self.NDS)]
                     for e in self.dq}
        self.dcnt = {e: 0 for e in self.dq}
        self.nwaits = 0

    def _need(self, eng, tok, waits, kind):
        if tok is None:
            return
        key, semh, val, teng = tok
        if teng == eng and kind != "raw":
            return
        if teng == "pe" and eng == "pe":
            return
        if self.seen[eng].get(key, 0) >= val:
            return
        self.seen[eng][key] = val
        waits.append((semh, val))

    def emit(self, eng, fn, reads=(), writes=(), dma=False):
        waits = []
        for r in reads:
            self._need(eng, r.w, waits, "raw")
            if r.excl:
                for t in r.rd.values():
                    self._need(eng, t, waits, "rar")
        for r in writes:
            self._need(eng, r.w, waits, "waw")
            for t in r.rd.values():
                self._need(eng, t, waits, "war")
        if dma:
            i = self.dcnt[eng]
            self.dcnt[eng] += 1
            slot, k = i % self.NDS, i // self.NDS
            semh = self.dsem[eng][slot]
            key = "d_%s%d" % (eng, slot)
            if k > 0:
                self._need(eng, (key, semh, 16 * k, None), waits, "raw")
            tok = (key, semh, 16 * (k + 1), None)
            inc = 16
        else:
            self.cnt[eng] += 1
            semh = self.sem[eng]
            tok = ("e_" + eng, semh, self.cnt[eng], eng)
            inc = 1
        for r in reads:
            r.rd[tok[0]] = tok
        for r in writes:
            r.w = tok
            r.rd = {}
        self.nwaits += len(waits)
        self.q[eng].append((waits, fn, semh, inc))
        return tok

    def dma(self, q, out, in_, reads=(), writes=()):
        return self.emit(q, lambda e: e.dma_start(out=out, in_=in_), reads, writes, dma=True)

    def mm(self, out, lhsT, rhs, start, stop, reads=(), writes=()):
        return self.emit("pe", lambda e: e.matmul(out, lhsT, rhs, start=start, stop=stop), reads, writes)

    def final_wait(self, eng, toks):
        waits = []
        for t in toks:
            self._need(eng, t, waits, "raw")
        self.q[eng].append((waits, None, None, 0))

    def replay(self, block):
        nc = self.nc
        names = {"pe": "tensor", "act": "scalar", "dve": "vector", "pool": "gpsimd", "sp": "sync"}
        for e in self.ENGS:
            ops = self.q[e]

            def body(eng, ops=ops):
                for waits, fn, semh, inc in ops:
                    for (s, v) in waits:
                        eng.wait_ge(s, v)
                    if fn is not None:
                        fn(eng).then_inc(semh, inc)
            getattr(block, names[e])(body)


D = 2048
DFF = 5632
KC = 16
JC = 44
EPS = 1e-6
CW = 31
HALO = 15
WB_ELEMS = 4096
NWB = 4
GW = 2


def pvec(v):
    v = np.asarray(v, np.float32).reshape(-1)
    return np.ascontiguousarray(v.reshape(-1, 128).T)


class TL:
    def __init__(self, nc, st, P, TMAX, pv_dram, L, conv=False, tcmax=None, ident_dram=None, nwb=NWB):
        self.nc, self.st, self.P, self.TMAX = nc, st, P, TMAX
        self.L = L
        npv = L.n
        sb = lambda n, s, d: st.enter_context(nc.sbuf_tensor("s_" + n, s, d))
        ps = lambda n: st.enter_context(nc.psum_tensor(n, [128, 512], F32))
        self.X = sb("X", [128, KC, TMAX], F32)
        self.rX = [Res("X%d" % c) for c in range(KC)]
        self.Hb = sb("Hb", [128, KC, TMAX], BF16)
        self.rHb = [Res("Hb%d" % c) for c in range(KC)]
        self.HID = sb("HID", [128, JC, TMAX], BF16)
        self.rHID = [Res("HID%d" % j) for j in range(JC)]
        self.Y = sb("Y", [128, KC, TMAX], F32)
        self.rY = [Res("Y%d" % c) for c in range(KC)]
        self.NWB = nwb
        self.WB = [sb("WB%d" % i, [128, WB_ELEMS], BF16) for i in range(nwb)]
        self.rWB = [Res("WB%d" % i) for i in range(nwb)]
        self.wbi = 0
        self.pv = sb("pv", [128, npv], F32)
        self.rpv = Res("pv")
        self.ones = sb("ones", [128, 128], BF16)
        self.rones = Res("ones")
        self.sq = [sb("sq%d" % i, [128, TMAX], BF16) for i in range(2)]
        self.rsq = [Res() for _ in range(2)]
        self.sqi = 0
        self.tmp = [sb("tmp%d" % i, [128, TMAX], F32) for i in range(3)]
        self.rtmp = [Res() for _ in range(3)]
        self.tmi = 0
        self.R = sb("R", [128, TMAX], F32)
        self.rR = Res("R")
        self.R2 = sb("R2", [128, TMAX], F32)
        self.rR2 = Res("R2")
        self.pA = [ps("pA%d" % i) for i in range(2)]
        self.rpA = [Res(excl=True) for _ in range(2)]
        self.pB = [ps("pB%d" % i) for i in range(2)]
        self.rpB = [Res(excl=True) for _ in range(2)]
        self.pY = [ps("pY%d" % i) for i in range(2)]
        self.rpY = [Res(excl=True) for _ in range(2)]
        self.pS = ps("pS")
        self.rpS = Res("pS", excl=True)
        self.pS2 = ps("pS2")
        self.rpS2 = Res("pS2", excl=True)
        self.pai = 0
        self.pyi = 0
        if conv:
            self.V = sb("V", [128, KC, tcmax], F32)
            self.rV = [Res() for _ in range(KC)]
            self.U = [sb("U%d" % i, [128, TMAX], BF16) for i in range(2)]
            self.rU = [Res() for _ in range(2)]
            self.ui = 0
            self.NDG = 8
            self.DG = [sb("DG%d" % i, [128, 128], BF16) for i in range(self.NDG)]
            self.rDG = [Res() for _ in range(self.NDG)]
            self.dgi = 0
            self.ident = sb("ident", [128, 128], F32)
            self.rident = Res()
            self.MU = sb("MU", [128, tcmax], F32)
            self.rMU = Res()
            self.mask = sb("mask", [128, TMAX], F32)
            self.rmask = Res()
            P.dma("act", self.ident[:], ident_dram, writes=[self.rident])
        P.dma("sp", self.pv[:], pv_dram, writes=[self.rpv])
        P.emit("dve", lambda e: e.memset(self.ones[:], 1.0), writes=[self.rones])

    def wload(self, src, a, b):
        i = self.wbi
        self.wbi = (i + 1) % self.NWB
        view = self.WB[i][:, 0:a * b].rearrange("p (a b) -> p a b", a=a)
        q = "pool" if src.dtype == F32 else "sp"
        self.P.dma(q, view, src, writes=[self.rWB[i]])
        return view, self.rWB[i]

    def next_tmp(self):
        i = self.tmi
        self.tmi = (i + 1) % 3
        return self.tmp[i], self.rtmp[i]

    def next_sq(self):
        i = self.sqi
        self.sqi = (i + 1) % 2
        return self.sq[i], self.rsq[i]

    def pvs(self, off, c):
        return self.pv[:, off + c:off + c + 1]

    def derive(self, out_off, a_off, b_off, mode, scale=1.0):
        P, pv = self.P, self.pv
        o = pv[:, out_off:out_off + KC]
        a = pv[:, a_off:a_off + KC]
        b = pv[:, b_off:b_off + KC]
        if mode == "a1pb":
            P.emit("dve", lambda e: e.scalar_tensor_tensor(out=o, in0=b, scalar=1.0, in1=a, op0=ALU.add, op1=ALU.mult),
                   reads=[self.rpv], writes=[self.rpv])
        else:
            P.emit("dve", lambda e: e.scalar_tensor_tensor(out=o, in0=a, scalar=scale, in1=b, op0=ALU.mult, op1=ALU.mult),
                   reads=[self.rpv], writes=[self.rpv])

    def rstd_from(self, pS, rpS, T, out, rout):
        P = self.P
        P.emit("dve", lambda e: e.tensor_scalar(out=out[:, :T], in0=pS[:, :T], scalar1=1.0 / D, scalar2=EPS,
                                                op0=ALU.mult, op1=ALU.add), reads=[rpS], writes=[rout])
        P.emit("act", lambda e: e.activation(out=out[:, :T], in_=out[:, :T], func=AF.Sqrt), reads=[rout], writes=[rout])
        P.emit("dve", lambda e: e.reciprocal(out=out[:, :T], in_=out[:, :T]), reads=[rout], writes=[rout])

    def norm_mod(self, segs, an, sn):
        P, L = self.P, self.L
        Tt = sum(sg[1] for sg in segs)
        for c in range(KC):
            sq, rsq = self.next_sq()
            for (xoff, T, dst, pf) in segs:
                P.emit("act", lambda e, c=c, sq=sq, xoff=xoff, T=T, dst=dst: e.activation(
                    out=sq[:, dst:dst + T], in_=self.X[:, c, xoff:xoff + T], func=AF.Square),
                    reads=[self.rX[c]], writes=[rsq])
            P.mm(self.pS[:, :Tt], self.ones[:], sq[:, :Tt], c == 0, c == KC - 1, reads=[self.rones, rsq], writes=[self.rpS])
        self.rstd_from(self.pS, self.rpS, Tt, self.R, self.rR)
        for c in range(KC):
            t, rt = self.next_tmp()
            for (xoff, T, dst, pf) in segs:
                a_off, sh_off = L[pf + "_" + an], L[pf + "_" + sn]
                P.emit("dve", lambda e, c=c, t=t, xoff=xoff, T=T, dst=dst, a_off=a_off: e.scalar_tensor_tensor(
                    out=t[:, dst:dst + T], in0=self.X[:, c, xoff:xoff + T], scalar=self.pvs(a_off, c), in1=self.R[:, dst:dst + T],
                    op0=ALU.mult, op1=ALU.mult), reads=[self.rX[c], self.rR, self.rpv], writes=[rt])
                P.emit("act", lambda e, c=c, t=t, T=T, dst=dst, sh_off=sh_off: e.activation(
                    out=self.Hb[:, c, dst:dst + T], in_=t[:, dst:dst + T], func=AF.Identity, bias=self.pvs(sh_off, c)),
                    reads=[rt, self.rpv], writes=[self.rHb[c]])

    def _y_evac(self, c, T, py, rpy, bias_off=None):
        P = self.P
        if bias_off is None:
            P.emit("act", lambda e: e.activation(out=self.Y[:, c, :T], in_=py[:, :T], func=AF.Copy),
                   reads=[rpy], writes=[self.rY[c]])
        else:
            P.emit("act", lambda e: e.activation(out=self.Y[:, c, :T], in_=py[:, :T], func=AF.Identity,
                                                 bias=self.pvs(bias_off, c)),
                   reads=[rpy, self.rpv], writes=[self.rY[c]])
        sq, rsq = self.next_sq()
        P.emit("dve", lambda e: e.tensor_tensor(out=sq[:, :T], in0=self.Y[:, c, :T], in1=self.Y[:, c, :T], op=ALU.mult),
               reads=[self.rY[c]], writes=[rsq])
        return sq, rsq

    def _stat_mm(self, pend, T):
        c, sq, rsq = pend
        self.P.mm(self.pS2[:, :T], self.ones[:], sq[:, :T], c == 0, c == KC - 1,
                  reads=[self.rones, rsq], writes=[self.rpS2])

    def linear_dd(self, T, w, bias_off=None, kchunks=KC):
        P = self.P
        wv = w.rearrange("(kc p) f -> p kc f", p=128)
        pend = None
        for g in range(KC // GW):
            wt, rwt = self.wload(wv[:, :, g * GW * 128:(g + 1) * GW * 128], kchunks, GW * 128)
            for cc in range(GW):
                c = g * GW + cc
                i = self.pyi
                self.pyi = (i + 1) % 2
                py, rpy = self.pY[i], self.rpY[i]
                for k in range(kchunks):
                    P.mm(py[:, :T], wt[:, k, cc * 128:(cc + 1) * 128], self.Hb[:, k, :T], k == 0, k == kchunks - 1,
                         reads=[rwt, self.rHb[k]], writes=[rpy])
                if pend is not None:
                    self._stat_mm(pend, T)
                sq, rsq = self._y_evac(c, T, py, rpy, bias_off)
                pend = (c, sq, rsq)
        self._stat_mm(pend, T)
        self.rstd_from(self.pS2, self.rpS2, T, self.R2, self.rR2)

    def ffn(self, T, wg, wu, wd):
        P = self.P
        wgv = wg.rearrange("(kc p) f -> p kc f", p=128)
        wuv = wu.rearrange("(kc p) f -> p kc f", p=128)
        wdv = wd.rearrange("(j p) f -> p j f", p=128)
        for g in range(JC // GW):
            gt, rgt = self.wload(wgv[:, :, g * GW * 128:(g + 1) * GW * 128], KC, GW * 128)
            ut, rut = self.wload(wuv[:, :, g * GW * 128:(g + 1) * GW * 128], KC, GW * 128)
            for jj in range(GW):
                j = g * GW + jj
                i = self.pai
                self.pai = (i + 1) % 2
                pa, rpa, pb, rpb = self.pA[i], self.rpA[i], self.pB[i], self.rpB[i]
                for k in range(KC):
                    P.mm(pa[:, :T], gt[:, k, jj * 128:(jj + 1) * 128], self.Hb[:, k, :T], k == 0, k == KC - 1,
                         reads=[rgt, self.rHb[k]], writes=[rpa])
                for k in range(KC):
                    P.mm(pb[:, :T], ut[:, k, jj * 128:(jj + 1) * 128], self.Hb[:, k, :T], k == 0, k == KC - 1,
                         reads=[rut, self.rHb[k]], writes=[rpb])
                t, rt = self.next_tmp()
                P.emit("act", lambda e, t=t, pa=pa: e.activation(out=t[:, :T], in_=pa[:, :T], func=AF.Silu),
                       reads=[rpa], writes=[rt])
                P.emit("dve", lambda e, t=t, pb=pb, j=j: e.tensor_tensor(out=self.HID[:, j, :T], in0=t[:, :T], in1=pb[:, :T],
                                                                       op=ALU.mult),
                       reads=[rt, rpb], writes=[self.rHID[j]])
        pend = []
        JQ = JC // 4
        for cp in range(KC // 2):
            wts = [self.wload(wdv[:, q * JQ:(q + 1) * JQ, cp * 256:(cp + 1) * 256], JQ, 256) for q in range(4)]
            for j in range(JC):
                wt, rwt = wts[j // JQ]
                for h in range(2):
                    P.mm(self.pY[h][:, :T], wt[:, j % JQ, h * 128:(h + 1) * 128], self.HID[:, j, :T], j == 0, j == JC - 1,
                         reads=[rwt, self.rHID[j]], writes=[self.rpY[h]])
            for pd in pend:
                self._stat_mm(pd, T)
            pend = []
            for h in range(2):
                c = cp * 2 + h
                sq, rsq = self._y_evac(c, T, self.pY[h], self.rpY[h])
                pend.append((c, sq, rsq))
        for pd in pend:
            self._stat_mm(pd, T)
        self.rstd_from(self.pS2, self.rpS2, T, self.R2, self.rR2)

    def post_res(self, segs, gn):
        P, L = self.P, self.L
        for c in range(KC):
            t, rt = self.next_tmp()
            for (xoff, T, dst, pf) in segs:
                g_off = L[pf + "_" + gn]
                P.emit("dve", lambda e, c=c, t=t, T=T, dst=dst, g_off=g_off: e.scalar_tensor_tensor(
                    out=t[:, dst:dst + T], in0=self.Y[:, c, dst:dst + T], scalar=self.pvs(g_off, c), in1=self.R2[:, dst:dst + T],
                    op0=ALU.mult, op1=ALU.mult), reads=[self.rY[c], self.rR2, self.rpv], writes=[rt])
                P.emit("dve", lambda e, c=c, t=t, xoff=xoff, T=T, dst=dst: e.tensor_tensor(
                    out=self.X[:, c, xoff:xoff + T], in0=self.X[:, c, xoff:xoff + T], in1=t[:, dst:dst + T], op=ALU.add),
                    reads=[rt, self.rX[c]], writes=[self.rX[c]])

    def conv(self, Tp, csegs, w_in, w_out, o):
        P = self.P
        Tc = sum(sg[1] for sg in csegs)
        wv = w_in.rearrange("(kc p) f -> p kc f", p=128)
        tiles = {}

        def emit_in(c):
            g, cc = divmod(c, GW)
            if cc == 0:
                tiles[g] = (self.wload(wv[:, :, g * GW * 128:(g + 1) * GW * 128], KC, GW * 128),
                            self.wload(wv[:, :, D + g * GW * 128:D + (g + 1) * GW * 128], KC, GW * 128))
            (at, rat), (gt, rgt) = tiles[g]
            i = self.pai
            self.pai = (i + 1) % 2
            pa, rpa, pb, rpb = self.pA[i], self.rpA[i], self.pB[i], self.rpB[i]
            for k in range(KC):
                P.mm(pa[:, :Tp], at[:, k, cc * 128:(cc + 1) * 128], self.Hb[:, k, :Tp], k == 0, k == KC - 1,
                     reads=[rat, self.rHb[k]], writes=[rpa])
            for k in range(KC):
                P.mm(pb[:, :Tp], gt[:, k, cc * 128:(cc + 1) * 128], self.Hb[:, k, :Tp], k == 0, k == KC - 1,
                     reads=[rgt, self.rHb[k]], writes=[rpb])
            t, rt = self.next_tmp()
            P.emit("act", lambda e: e.activation(out=t[:, :Tp], in_=pb[:, :Tp], func=AF.Sigmoid, bias=self.pvs(o["b_in"] + KC, c)),
                   reads=[rpb, self.rpv], writes=[rt])
            P.emit("dve", lambda e: e.tensor_tensor(out=t[:, :Tp], in0=t[:, :Tp], in1=self.mask[:, :Tp], op=ALU.mult),
                   reads=[rt, self.rmask], writes=[rt])
            ui = self.ui
            self.ui = (ui + 1) % 2
            U, rU = self.U[ui], self.rU[ui]
            P.emit("dve", lambda e: e.scalar_tensor_tensor(
                out=U[:, :Tp], in0=pa[:, :Tp], scalar=self.pvs(o["b_in"], c), in1=t[:, :Tp], op0=ALU.add, op1=ALU.mult),
                reads=[rpa, rt, self.rpv], writes=[rU])
            return U, rU

        def emit_taps(c, U, rU):
            for k in range(CW):
                di = self.dgi
                self.dgi = (di + 1) % self.NDG
                dg, rdg = self.DG[di], self.rDG[di]
                wk = self.pv[:, o["w_dw"] + c * CW + k:o["w_dw"] + c * CW + k + 1]
                P.emit("act", lambda e, dg=dg, wk=wk: e.activation(out=dg[:], in_=self.ident[:], func=AF.Identity, scale=wk),
                       reads=[self.rident, self.rpv], writes=[rdg])
                for si, (po, tci, dst) in enumerate(csegs):
                    P.mm(self.pY[si][:, :tci], dg[:], U[:, po + k:po + k + tci], k == 0, k == CW - 1,
                         reads=[rdg, rU], writes=[self.rpY[si]])
            for si, (po, tci, dst) in enumerate(csegs):
                P.emit("dve", lambda e, si=si, tci=tci, dst=dst: e.tensor_single_scalar(
                    out=self.V[:, c, dst:dst + tci], in_=self.pY[si][:, :tci], scalar=self.pvs(o["b_dw"], c), op=ALU.add),
                    reads=[self.rpY[si], self.rpv], writes=[self.rV[c]])
            sq, rsq = self.next_sq()
            P.emit("act", lambda e, sq=sq: e.activation(out=sq[:, :Tc], in_=self.V[:, c, :Tc], func=AF.Copy),
                   reads=[self.rV[c]], writes=[rsq])
            P.mm(self.pS[:, :Tc], self.ones[:], sq[:, :Tc], c == 0, c == KC - 1, reads=[self.rones, rsq], writes=[self.rpS])
            sq, rsq = self.next_sq()
            P.emit("act", lambda e, sq=sq: e.activation(out=sq[:, :Tc], in_=self.V[:, c, :Tc], func=AF.Square),
                   reads=[self.rV[c]], writes=[rsq])
            P.mm(self.pS2[:, :Tc], self.ones[:], sq[:, :Tc], c == 0, c == KC - 1, reads=[self.rones, rsq], writes=[self.rpS2])

        prev = None
        for c in range(KC):
            cur = emit_in(c)
            if prev is not None:
                emit_taps(c - 1, *prev)
            prev = cur
        emit_taps(KC - 1, *prev)
        MU, rMU, R, rR = self.MU, self.rMU, self.R, self.rR
        P.emit("dve", lambda e: e.tensor_single_scalar(out=MU[:, :Tc], in_=self.pS[:, :Tc], scalar=1.0 / D, op=ALU.mult),
               reads=[self.rpS], writes=[rMU])
        t, rt = self.next_tmp()
        P.emit("dve", lambda e: e.tensor_tensor(out=t[:, :Tc], in0=MU[:, :Tc], in1=MU[:, :Tc], op=ALU.mult),
               reads=[rMU], writes=[rt])
        P.emit("dve", lambda e: e.scalar_tensor_tensor(out=R[:, :Tc], in0=self.pS2[:, :Tc], scalar=1.0 / D, in1=t[:, :Tc],
                                                       op0=ALU.mult, op1=ALU.subtract), reads=[self.rpS2, rt], writes=[rR])
        P.emit("dve", lambda e: e.tensor_single_scalar(out=R[:, :Tc], in_=R[:, :Tc], scalar=EPS, op=ALU.add),
               reads=[rR], writes=[rR])
        P.emit("act", lambda e: e.activation(out=R[:, :Tc], in_=R[:, :Tc], func=AF.Sqrt), reads=[rR], writes=[rR])
        P.emit("dve", lambda e: e.reciprocal(out=R[:, :Tc], in_=R[:, :Tc]), reads=[rR], writes=[rR])
        for c in range(KC):
            t, rt = self.next_tmp()
            P.emit("dve", lambda e, c=c, t=t: e.tensor_tensor(out=t[:, :Tc], in0=self.V[:, c, :Tc], in1=MU[:, :Tc], op=ALU.subtract),
                   reads=[self.rV[c], rMU], writes=[rt])
            P.emit("dve", lambda e, t=t: e.tensor_tensor(out=t[:, :Tc], in0=t[:, :Tc], in1=R[:, :Tc], op=ALU.mult),
                   reads=[rt, rR], writes=[rt])
            P.emit("act", lambda e, c=c, t=t: e.activation(out=self.Hb[:, c, :Tc], in_=t[:, :Tc], func=AF.Silu,
                                                             scale=self.pvs(o["ln_g"], c), bias=self.pvs(o["ln_b"], c)),
                   reads=[rt, self.rpv], writes=[self.rHb[c]])
        self.linear_dd(Tc, w_out, bias_off=o["b_out"])

    def load_x(self, src, T, q="sp", col0=0):
        v = src.rearrange("(c p) t -> p c t", p=128)
        for h in range(2):
            cs = slice(h * 8, (h + 1) * 8)
            self.P.dma(q, self.X[:, cs, col0:col0 + T], v[:, cs, :], writes=self.rX[cs])

    def load_hb(self, src, T, q="sp", col0=0):
        v = src.rearrange("(c p) t -> p c t", p=128)
        for h in range(2):
            cs = slice(h * 8, (h + 1) * 8)
            self.P.dma(q, self.Hb[:, cs, col0:col0 + T], v[:, cs, :], writes=self.rHb[cs])

    def store_x(self, dst, xoff, T, outs, q="sp"):
        v = dst.rearrange("(c p) t -> p c t", p=128)
        for h in range(2):
            cs = slice(h * 8, (h + 1) * 8)
            outs.append(self.P.dma(q, v[:, cs, :], self.X[:, cs, xoff:xoff + T], reads=self.rX[cs]))

    def store_hb(self, dst, T, outs, q="sp", col0=0):
        v = dst.rearrange("(c p) t -> p c t", p=128)
        for h in range(2):
            cs = slice(h * 8, (h + 1) * 8)
            outs.append(self.P.dma(q, v[:, cs, :], self.Hb[:, cs, col0:col0 + T], reads=self.rHb[cs]))

    def load_mask(self, src, T, q="sp"):
        self.P.dma(q, self.mask[:, :T], src, writes=[self.rmask])


NCORE = 8
TP = 440
TC = 410
NB = 5
LCORE = 2048
CCORE = 64
CTP = CCORE + 2 * HALO
SEQ = 8192
CTX = 256
NMOD = 9


def new_nc():
    return bass.Bass("TRN2", target_bir_lowering=False)


ACOLS = NMOD * D // NCORE


def build_L0():
    nc = new_nc()
    cT = nc.dram_tensor("cT", [128, KC * 3], F32, kind="ExternalInput").ap()
    aw = nc.dram_tensor("aw", [4, D, ACOLS], F32, kind="ExternalInput").ap()
    ab = nc.dram_tensor("ab", [4, 3, ACOLS], F32, kind="ExternalInput").ap()
    mo = nc.dram_tensor("mo", [4, 3, ACOLS], F32, kind="ExternalOutput").ap()
    with contextlib.ExitStack() as st:
        P = Prog(nc, st)
        sb = lambda n, s, d: st.enter_context(nc.sbuf_tensor("s_" + n, s, d))
        cs = sb("cs", [128, KC, 3], F32); rcs = Res()
        wt = [sb("wt%d" % i, [128, KC, 512], F32) for i in range(2)]; rwt = [Res(), Res()]
        bt = sb("bt", [3, 4, ACOLS], F32); rbt = Res()
        ot = sb("ot", [3, 4, ACOLS], F32); rot = Res()
        ps = [st.enter_context(nc.psum_tensor("ps%d" % i, [128, 512], F32)) for i in range(2)]; rps = [Res(excl=True), Res(excl=True)]
        P.dma("sp", cs[:].rearrange("p a b -> p (a b)"), cT, writes=[rcs])
        P.dma("act", bt[:], ab.rearrange("l r c -> r l c"), writes=[rbt])
        P.emit("act", lambda e: e.activation(out=cs[:], in_=cs[:], func=AF.Silu), reads=[rcs], writes=[rcs])
        tiles = [(l, c0, min(512, ACOLS - c0)) for l in range(4) for c0 in range(0, ACOLS, 512)]
        for i, (l, c0, n) in enumerate(tiles):
            w, rw = wt[i % 2], rwt[i % 2]
            src = aw[l].rearrange("(kc p) f -> p kc f", p=128)[:, :, c0:c0 + n]
            P.dma("sp" if i % 2 == 0 else "act", w[:, :, :n], src, writes=[rw])
            p, rp = ps[i % 2], rps[i % 2]
            for k in range(KC):
                P.mm(p[0:3, :n], cs[:, k, :], w[:, k, :n], k == 0, k == KC - 1, reads=[rcs, rw], writes=[rp])
            P.emit("dve", lambda e, p=p, l=l, c0=c0, n=n: e.tensor_tensor(
                out=ot[:, l, c0:c0 + n], in0=p[0:3, :n], in1=bt[:, l, c0:c0 + n], op=ALU.add),
                reads=[rp, rbt], writes=[rot])
        tok = P.dma("sp", mo.rearrange("l r c -> r l c"), ot[:], reads=[rot])
        P.final_wait("sp", [tok])
        with nc.Block() as block:
            P.replay(block)
    return nc


def run_L0(inp):
    c, c_ctx, ada_w, ada_b = inp["c"], inp["c_ctx"], inp["ada_w"], inp["ada_b"]
    cv = np.stack([c[0], c[1], c_ctx], 1)
    cT = np.ascontiguousarray(cv.reshape(KC, 128, 3).transpose(1, 0, 2).reshape(128, KC * 3))
    maps = []
    for k in range(NCORE):
        cols = slice(k * ACOLS, (k + 1) * ACOLS)
        maps.append(dict(cT=cT, aw=np.ascontiguousarray(ada_w[:, :, cols]),
                         ab=np.ascontiguousarray(np.broadcast_to(ada_b[:, None, cols], (4, 3, ACOLS)))))
    res = run_bass_kernel_spmd(build_L0(), maps, core_ids=list(range(NCORE)))
    m = np.concatenate([r["mo"] for r in res.results], axis=2)
    return m.reshape(4, 3, NMOD, D)


def layer_pv_names(tag):
    return ["%s_m%d" % (tag, j) for j in range(NMOD)] + ["%s_%s" % (tag, n) for n in ("A1", "G1", "A2", "G2", "A3", "G3")]


class PVL:
    def __init__(self):
        self.off = {}
        self.n = 0

    def add(self, name, w=KC):
        self.off[name] = self.n
        self.n += w

    def __getitem__(self, k):
        return self.off[k]


def pv_layout(layers, conv_layers):
    L = PVL()
    for l in layers:
        for j in range(6):
            L.add("g%d_%d" % (l, j))
        for s in ("lat", "ctx"):
            for n in layer_pv_names("L%d%s" % (l, s)):
                L.add(n)
    for l in conv_layers:
        L.add("c%d_b_in" % l, 32)
        L.add("c%d_w_dw" % l, KC * CW)
        for n in ("b_dw", "ln_g", "ln_b", "b_out"):
            L.add("c%d_%s" % (l, n))
    return L


def pv_fill(L, inp, m, core, layers, conv_layers):
    b = core // 4
    pv = np.zeros((128, L.n), np.float32)

    def put(name, v):
        a = pvec(v)
        pv[:, L[name]:L[name] + a.shape[1]] = a
    for l in layers:
        for j in range(6):
            put("g%d_%d" % (l, j), inp["norm_g"][l, j])
        for j in range(NMOD):
            put("L%dlat_m%d" % (l, j), m[l, b, j])
            put("L%dctx_m%d" % (l, j), m[l, 2, j])
    for l in conv_layers:
        j = l // 3
        put("c%d_b_in" % l, inp["conv_b_in"][j])
        wd = inp["conv_w_dw"][j]
        a = wd.T.reshape(KC, 128, CW).transpose(1, 0, 2).reshape(128, KC * CW)
        pv[:, L["c%d_w_dw" % l]:L["c%d_w_dw" % l] + KC * CW] = a
        put("c%d_b_dw" % l, inp["conv_b_dw"][j])
        put("c%d_ln_g" % l, inp["conv_ln_g"][j])
        put("c%d_ln_b" % l, inp["conv_ln_b"][j])
        put("c%d_b_out" % l, inp["conv_b_out"][j])
    return pv


def derive_layer(T, L, l, s):
    t = "L%d%s" % (l, s)
    g = lambda j: L["g%d_%d" % (l, j)]
    T.derive(L[t + "_A1"], g(0), L[t + "_m1"], "a1pb")
    T.derive(L[t + "_G1"], L[t + "_m2"], g(1), "ab", 0.5)
    T.derive(L[t + "_A2"], g(2), L[t + "_m4"], "a1pb")
    T.derive(L[t + "_G2"], L[t + "_m5"], g(3), "ab", 1.0)
    T.derive(L[t + "_A3"], g(4), L[t + "_m7"], "a1pb")
    T.derive(L[t + "_G3"], L[t + "_m8"], g(5), "ab", 0.5)


def conv_offs(L, l):
    return {k: L["c%d_%s" % (l, k)] for k in ("b_in", "w_dw", "b_dw", "ln_g", "ln_b", "b_out")}


NBA, TCA = 6, 342
TPA = TCA + 2 * HALO


def _wdecl(dt, names_gu, names_d, extra, WDT):
    W = {}
    for n in names_gu:
        W[n] = dt(n, [D, DFF], WDT, "ExternalInput")
    for n in names_d:
        W[n] = dt(n, [DFF, D], WDT, "ExternalInput")
    for n, shp in extra:
        W[n] = dt(n, shp, WDT, "ExternalInput")
    return W


def build_LA(WDT=BF16):
    nc = new_nc()
    L = pv_layout([0, 1], [0])
    dt = lambda n, s, d, k: nc.dram_tensor(n, s, d, kind=k).ap()
    TM = TPA + CTP
    xin = dt("xin", [NBA, D, TPA], F32, "ExternalInput")
    msk = dt("msk", [NBA, 128, TM], F32, "ExternalInput")
    cin = dt("cin", [D, CTP], F32, "ExternalInput")
    pvd = dt("pv", [128, L.n], F32, "ExternalInput")
    idn = dt("idn", [128, 128], F32, "ExternalInput")
    W = _wdecl(dt, ("g00", "u00", "g01", "u01", "g10", "u10"), ("d00", "d01", "d10"),
               (("cwi", [D, 2 * D]), ("cwo", [D, D])), WDT)
    xo = dt("xo", [NBA, D, TCA], F32, "ExternalOutput")
    ho = dt("ho", [NBA, D, TCA], BF16, "ExternalOutput")
    xco = dt("xco", [D, CCORE], F32, "ExternalOutput")
    hco = dt("hco", [D, CCORE], BF16, "ExternalOutput")
    with contextlib.ExitStack() as st:
        P = Prog(nc, st)
        T = TL(nc, st, P, TM, pvd, L, conv=True, tcmax=TCA + CCORE, ident_dram=idn, nwb=5)
        for l in (0, 1):
            for s in ("lat", "ctx"):
                derive_layer(T, L, l, s)
        outs = []
        co = conv_offs(L, 0)
        for i in range(NBA):
            two = (i == 0)
            tp = TM if two else TPA
            tc = TCA + CCORE if two else TCA
            pre = lambda l: [(0, TPA, 0, "L%dlat" % l)] + ([(TPA, CTP, TPA, "L%dctx" % l)] if two else [])
            post = lambda l: [(HALO, TCA, 0, "L%dlat" % l)] + ([(TPA + HALO, CCORE, TCA, "L%dctx" % l)] if two else [])
            csegs = [(0, TCA, 0)] + ([(TPA, CCORE, TCA)] if two else [])
            T.load_x(xin[i], TPA)
            if two:
                T.load_x(cin, CTP, col0=TPA, q="act")
            T.load_mask(msk[i, :, :tp], tp)
            T.norm_mod(pre(0), "A1", "m0")
            T.ffn(tp, W["g00"], W["u00"], W["d00"])
            T.post_res(pre(0), "G1")
            T.norm_mod(pre(0), "A2", "m3")
            T.conv(tp, csegs, W["cwi"], W["cwo"], co)
            T.post_res(post(0), "G2")
            T.norm_mod(post(0), "A3", "m6")
            T.ffn(tc, W["g01"], W["u01"], W["d01"])
            T.post_res(post(0), "G3")
            T.norm_mod(post(1), "A1", "m0")
            T.ffn(tc, W["g10"], W["u10"], W["d10"])
            T.post_res(post(1), "G1")
            T.norm_mod(post(1), "A2", "m3")
            T.store_x(xo[i], HALO, TCA, outs)
            T.store_hb(ho[i], TCA, outs, q="act")
            if two:
                T.store_x(xco, TPA + HALO, CCORE, outs)
                T.store_hb(hco, CCORE, outs, q="act", col0=TCA)
        P.final_wait("sp", outs)
        with nc.Block() as block:
            P.replay(block)
    return nc, L


def build_LC(WDT=BF16):
    nc = new_nc()
    L = pv_layout([1, 2], [])
    dt = lambda n, s, d, k: nc.dram_tensor(n, s, d, kind=k).ap()
    TM = TC + CCORE
    xin = dt("xin", [NB, D, TC], F32, "ExternalInput")
    yin = dt("yin", [NB, D, TC], BF16, "ExternalInput")
    cin = dt("cin", [D, CCORE], F32, "ExternalInput")
    ycin = dt("ycin", [D, CCORE], BF16, "ExternalInput")
    pvd = dt("pv", [128, L.n], F32, "ExternalInput")
    W = _wdecl(dt, ("g11", "u11", "g20", "u20"), ("d11", "d20"), (("wo", [D, D]),), WDT)
    xo = dt("xo", [NB, D, TC], F32, "ExternalOutput")
    ho = dt("ho", [NB, D, TC], BF16, "ExternalOutput")
    hco = dt("hco", [D, CCORE], BF16, "ExternalOutput")
    with contextlib.ExitStack() as st:
        P = Prog(nc, st)
        T = TL(nc, st, P, TM, pvd, L, conv=False, nwb=8)
        for l in (1, 2):
            for s in ("lat", "ctx"):
                derive_layer(T, L, l, s)
        outs = []
        for i in range(NB):
            two = (i == 0)
            tc = TM if two else TC
            sg = lambda l: [(0, TC, 0, "L%dlat" % l)] + ([(TC, CCORE, TC, "L%dctx" % l)] if two else [])
            T.load_x(xin[i], TC)
            T.load_hb(yin[i], TC, q="act")
            if two:
                T.load_x(cin, CCORE, col0=TC)
                T.load_hb(ycin, CCORE, q="act", col0=TC)
            T.linear_dd(tc, W["wo"])
            T.post_res(sg(1), "G2")
            T.norm_mod(sg(1), "A3", "m6")
            T.ffn(tc, W["g11"], W["u11"], W["d11"])
            T.post_res(sg(1), "G3")
            T.norm_mod(sg(2), "A1", "m0")
            T.ffn(tc, W["g20"], W["u20"], W["d20"])
            T.post_res(sg(2), "G1")
            T.norm_mod(sg(2), "A2", "m3")
            T.store_x(xo[i], 0, TC, outs)
            T.store_hb(ho[i], TC, outs, q="act")
            if two:
                T.store_hb(hco, CCORE, outs, q="act", col0=TC)
        P.final_wait("sp", outs)
        with nc.Block() as block:
            P.replay(block)
    return nc, L


def build_LE(WDT=BF16):
    nc = new_nc()
    L = pv_layout([2, 3], [3])
    dt = lambda n, s, d, k: nc.dram_tensor(n, s, d, kind=k).ap()
    xin = dt("xin", [NB, D, TP], F32, "ExternalInput")
    yin = dt("yin", [NB, D, TP], BF16, "ExternalInput")
    msk = dt("msk", [NB, 128, TP], F32, "ExternalInput")
    pvd = dt("pv", [128, L.n], F32, "ExternalInput")
    idn = dt("idn", [128, 128], F32, "ExternalInput")
    W = _wdecl(dt, ("g21", "u21", "g30", "u30", "g31", "u31"), ("d21", "d30", "d31"),
               (("wo", [D, D]), ("cwi", [D, 2 * D]), ("cwo", [D, D])), WDT)
    xo = dt("xo", [NB, D, TC], F32, "ExternalOutput")
    with contextlib.ExitStack() as st:
        P = Prog(nc, st)
        T = TL(nc, st, P, TP, pvd, L, conv=True, tcmax=TC, ident_dram=idn, nwb=6)
        for l in (2, 3):
            derive_layer(T, L, l, "lat")
        outs = []
        co = conv_offs(L, 3)
        for i in range(NB):
            pre = lambda l: [(0, TP, 0, "L%dlat" % l)]
            post = lambda l: [(HALO, TC, 0, "L%dlat" % l)]
            T.load_x(xin[i], TP)
            T.load_hb(yin[i], TP, q="act")
            T.load_mask(msk[i], TP)
            T.linear_dd(TP, W["wo"])
            T.post_res(pre(2), "G2")
            T.norm_mod(pre(2), "A3", "m6")
            T.ffn(TP, W["g21"], W["u21"], W["d21"])
            T.post_res(pre(2), "G3")
            T.norm_mod(pre(3), "A1", "m0")
            T.ffn(TP, W["g30"], W["u30"], W["d30"])
            T.post_res(pre(3), "G1")
            T.norm_mod(pre(3), "A2", "m3")
            T.conv(TP, [(0, TC, 0)], W["cwi"], W["cwo"], co)
            T.post_res(post(3), "G2")
            T.norm_mod(post(3), "A3", "m6")
            T.ffn(TC, W["g31"], W["u31"], W["d31"])
            T.post_res(post(3), "G3")
            T.store_x(xo[i], HALO, TC, outs)
        P.final_wait("sp", outs)
        with nc.Block() as block:
            P.replay(block)
    return nc, L


def lat_blocks(xfull, core, halo, nb=NB, tc=TC):
    b, q = core // 4, core % 4
    out = np.zeros((nb, D, tc + 2 * halo), xfull.dtype)
    msk = np.zeros((nb, tc + 2 * halo), np.float32)
    for i in range(nb):
        s = q * LCORE + i * tc - halo
        e = s + tc + 2 * halo
        s0, e0 = max(s, 0), min(e, SEQ)
        out[i, :, s0 - s:e0 - s] = xfull[b, s0:e0].T
        msk[i, s0 - s:e0 - s] = 1.0
    return out, msk


def ctx_block(cfull, core, halo):
    b, q = core // 4, core % 4
    w = CCORE + 2 * halo
    out = np.zeros((D, w), cfull.dtype)
    msk = np.zeros((w,), np.float32)
    s = q * CCORE - halo
    e = s + w
    s0, e0 = max(s, 0), min(e, CTX)
    out[:, s0 - s:e0 - s] = cfull[b, s0:e0].T
    msk[s0 - s:e0 - s] = 1.0
    return out, msk


def gather_lat(blocks_per_core, dtype=None, nb=NB, tc=TC):
    d = blocks_per_core[0].dtype if dtype is None else dtype
    out = np.zeros((2, SEQ, D), d)
    for core, blk in enumerate(blocks_per_core):
        b, q = core // 4, core % 4
        for i in range(nb):
            s = q * LCORE + i * tc
            n = min(tc, (q + 1) * LCORE - s)
            out[b, s:s + n] = blk[i, :, :n].T
    return out


def gather_ctx(per_core):
    out = np.zeros((2, CTX, D), per_core[0].dtype)
    for core, a in enumerate(per_core):
        b, q = core // 4, core % 4
        out[b, q * CCORE:(q + 1) * CCORE] = a.T
    return out


CAST_CH = 8192


def cast_weight_list(inp):
    lst = []
    for l in range(4):
        for i in range(2):
            lst.append(("g%d%d" % (l, i), inp["ffn_w_gate"][l, i]))
            lst.append(("u%d%d" % (l, i), inp["ffn_w_up"][l, i]))
            lst.append(("d%d%d" % (l, i), inp["ffn_w_down"][l, i]))
    for j in range(2):
        lst.append(("cwi%d" % j, inp["conv_w_in"][j]))
        lst.append(("cwo%d" % j, inp["conv_w_out"][j]))
    lst.append(("gwo", inp["gla_w_o"][0]))
    lst.append(("nwo", inp["na_w_o"][0]))
    return lst


def build_LW(ncols):
    nc = new_nc()
    wi = nc.dram_tensor("wi", [128, ncols], F32, kind="ExternalInput").ap()
    wo = nc.dram_tensor("wo", [128, ncols], BF16, kind="ExternalOutput").ap()
    with contextlib.ExitStack() as st:
        P = Prog(nc, st)
        NBUF = 4
        bufs = [st.enter_context(nc.sbuf_tensor("s_cb%d" % i, [128, CAST_CH], BF16)) for i in range(NBUF)]
        rb = [Res() for _ in range(NBUF)]
        outs = []
        i = 0
        for c0 in range(0, ncols, CAST_CH):
            n = min(CAST_CH, ncols - c0)
            b, r = bufs[i % NBUF], rb[i % NBUF]
            P.dma("pool", b[:, :n], wi[:, c0:c0 + n], writes=[r])
            outs.append(P.dma("sp" if i % 2 == 0 else "act", wo[:, c0:c0 + n], b[:, :n], reads=[r]))
            i += 1
        P.final_wait("sp", outs)
        with nc.Block() as block:
            P.replay(block)
    return nc


def run_LW(inp):
    lst = cast_weight_list(inp)
    per = [a.size // (NCORE * 128) for _, a in lst]
    ncols = sum(per)
    maps = []
    for k in range(NCORE):
        parts = [a.reshape(NCORE, 128, -1)[k] for _, a in lst]
        maps.append(dict(wi=np.ascontiguousarray(np.concatenate(parts, axis=1))))
    res = run_bass_kernel_spmd(build_LW(ncols), maps, core_ids=list(range(NCORE))).results
    outs = [np.asarray(r["wo"]) for r in res]
    W = {}
    off = 0
    for (key, a), n in zip(lst, per):
        W[key] = np.stack([o[:, off:off + n] for o in outs]).reshape(a.shape)
        off += n
    return W


GH = 4
DK = 256
DV = 512
GT = 256
NCH = GT // 64
NLT = SEQ // GT
NTOK = CTX + SEQ
ROPE_THETA = 10000.0
PENG = "dve"


def gla_tables():
    inv = ROPE_THETA ** (-np.arange(0, 128, 2, dtype=np.float32) / 128)
    p = np.arange(128)
    sgn = np.where(p < 64, -1.0, 1.0).astype(np.float32)
    rows = np.arange(SEQ // 64, dtype=np.float32)
    cols = np.arange(64, dtype=np.float32)
    ang_r = rows[None, :] * inv[p % 64][:, None]
    ang_c = cols[None, :] * inv[p % 64][:, None]
    t = {}
    t["cr"] = np.cos(ang_r).astype(np.float32)
    t["sr"] = (np.sin(ang_r) * sgn[:, None]).astype(np.float32)
    t["cc"] = np.cos(ang_c).astype(np.float32)
    t["sc"] = (np.sin(ang_c) * sgn[:, None]).astype(np.float32)
    s = np.arange(128)[:, None]
    tt = np.arange(128)[None, :]
    same = (s // 64) == (tt // 64)
    g = -1.0 / 16.0
    t["tri"] = np.stack([np.where(same & (s <= tt), g, 0.0), np.where(same & (s > tt), g, 0.0),
                         np.where(same & (s >= tt), g, 0.0), np.where(same & (s < tt), g, 0.0)]).astype(np.float32)
    s = np.arange(64)[:, None]
    tt = np.arange(64)[None, :]
    t["msk"] = np.stack([(s <= tt), (s >= tt)]).astype(np.float32)
    t["ident"] = np.eye(128, dtype=np.float32)
    return t


def build_LB(nlt=NLT):
    nc = new_nc()
    SEQ = nlt * GT
    NTOK = CTX + SEQ
    dt = lambda n, s, d, k="ExternalInput": nc.dram_tensor(n, s, d, kind=k).ap()
    hT = dt("hT", [D, SEQ], BF16)
    hcT = dt("hcT", [D, CTX], BF16)
    wq = dt("wq", [D, DK], F32)
    wk = dt("wk", [D, DK], F32)
    wv = dt("wv", [D, DV], F32)
    wr = dt("wr", [D, DV], F32)
    wa1 = dt("wa1", [2, D, 16], F32)
    wa2 = dt("wa2", [2, 16, DK], F32)
    bab = dt("bab", [2, 128, DK], F32)
    brb = dt("brb", [64, DV], F32)
    ngb = dt("ngb", [64, DV], F32)
    tcr = dt("tcr", [128, 128], F32)
    tsr = dt("tsr", [128, 128], F32)
    tcc = dt("tcc", [128, 64], F32)
    tsc = dt("tsc", [128, 64], F32)
    ttri = dt("ttri", [4, 128, 128], F32)
    tmsk = dt("tmsk", [2, 64, 64], F32)
    tid = dt("tid", [128, 128], F32)
    oy = dt("oy", [NTOK, DV], BF16, "ExternalOutput")
    of_s = dt("of_s", [NTOK, DV], F32, "ExternalOutput")
    with contextlib.ExitStack() as st:
        P = Prog(nc, st)
        sb = lambda n, s, d: st.enter_context(nc.sbuf_tensor("s_" + n, s, d))
        psb = lambda n: st.enter_context(nc.psum_tensor(n, [128, 512], F32))

        def E(eng, fn, reads, writes, **kw):
            return P.emit(eng, lambda e: getattr(e, fn)(**kw), reads, writes)

        Wq = sb("Wq", [128, KC, DK], BF16); rWq = Res()
        Wk = sb("Wk", [128, KC, DK], BF16); rWk = Res()
        Wv = sb("Wv", [128, KC, DV], BF16); rWv = Res()
        Wr = sb("Wr", [128, KC, DV], BF16); rWr = Res()
        Wa1 = sb("Wa1", [128, 2, KC, 16], BF16); rWa1 = Res()
        Wa2 = sb("Wa2", [16, 2, DK], BF16); rWa2 = Res()
        P.dma("pool", Wq[:], wq.rearrange("(kc p) f -> p kc f", p=128), writes=[rWq])
        P.dma("pool", Wk[:], wk.rearrange("(kc p) f -> p kc f", p=128), writes=[rWk])
        P.dma("pool", Wv[:], wv.rearrange("(kc p) f -> p kc f", p=128), writes=[rWv])
        P.dma("pool", Wr[:], wr.rearrange("(kc p) f -> p kc f", p=128), writes=[rWr])
        for d in range(2):
            P.dma("pool", Wa1[:, d], wa1[d].rearrange("(kc p) f -> p kc f", p=128), writes=[rWa1])
        P.dma("pool", Wa2[:], wa2.rearrange("d r f -> r d f"), writes=[rWa2])
        rT = Res()
        BA = sb("BA", [128, 2, DK], F32)
        P.dma("sp", BA[:], bab.rearrange("d p f -> p d f"), writes=[rT])
        BR = sb("BR", [64, DV], F32); P.dma("sp", BR[:], brb, writes=[rT])
        NG = sb("NG", [64, DV], F32); P.dma("sp", NG[:], ngb, writes=[rT])
        CR = sb("CR", [128, 128], F32); P.dma("sp", CR[:], tcr, writes=[rT])
        SR = sb("SR", [128, 128], F32); P.dma("sp", SR[:], tsr, writes=[rT])
        CC = sb("CC", [128, 64], F32); P.dma("sp", CC[:], tcc, writes=[rT])
        SC = sb("SC", [128, 64], F32); P.dma("sp", SC[:], tsc, writes=[rT])
        TRI = sb("TRI", [128, 4, 128], F32); P.dma("sp", TRI[:], ttri.rearrange("a p f -> p a f"), writes=[rT])
        MSK = sb("MSK", [64, 2, 64], F32); P.dma("sp", MSK[:], tmsk.rearrange("a p f -> p a f"), writes=[rT])
        IDb = sb("IDb", [128, 128], BF16); P.dma("pool", IDb[:], tid, writes=[rT])

        H = [sb("H%d" % i, [128, KC, GT], BF16) for i in range(2)]; rH = [Res(), Res()]
        QS = sb("QS", [128, GT], F32); rQS = Res()
        QW = sb("QW", [128, GT], F32); rQW = Res()
        QR = sb("QR", [128, 2, GT], F32); rQR = [Res(), Res()]
        KR = sb("KR", [128, 2, GT], F32); rKR = [Res(), Res()]
        ZL = sb("ZL", [16, GT], BF16); rZL = Res()
        ZB = sb("ZB", [128, 2, DK], F32); rZB = Res()
        EQ = sb("EQ", [128, 2, GT], F32); rEQ = Res()
        EK = sb("EK", [128, 2, GT], F32); rEK = Res()
        EL = sb("EL", [128, 2, GT], F32); rEL = Res()
        QE = sb("QE", [128, 2, GT], BF16); rQE = Res()
        KE = sb("KE", [128, 2, GT], BF16); rKE = Res()
        KLT = sb("KLT", [128, 2, GT], BF16); rKLT = Res()
        S = sb("S", [128, 2, DV], F32); rS = [Res(), Res()]
        Sb = sb("Sb", [128, 2, DV], BF16); rSb = [Res(), Res()]
        VT = [sb("VT%d" % i, [64, DV], BF16) for i in range(2)]; rVT = [Res(), Res()]
        KL = [sb("KL%d" % i, [64, DK], BF16) for i in range(2)]; rKL = [Res(), Res()]
        SCT = [sb("SCT%d" % i, [64, 64], BF16) for i in range(2)]; rSCT = [Res(), Res()]
        OS = [sb("OS%d" % i, [64, DV], F32) for i in range(2)]; rOS = [Res(), Res()]
        OF = [sb("OF%d" % i, [64, DV], F32) for i in range(2)]; rOF = [Res(), Res()]
        SQ = sb("SQ", [64, DV], F32); rSQ = Res()
        RT = [sb("RT%d" % i, [64, DV], F32) for i in range(2)]; rRT = [Res(), Res()]
        SS = sb("SS", [64, 1], F32); rSS = Res()
        OUT = [sb("OUT%d" % i, [64, DV], BF16) for i in range(2)]; rOUT = [Res(), Res()]
        pP = [psb("pP0"), psb("pP1")]; rpP = [Res(excl=True), Res(excl=True)]
        pZ = psb("pZ"); rpZ = Res(excl=True)
        pC = psb("pC"); rpC = Res(excl=True)
        pV = psb("pV"); rpV = Res(excl=True)
        pK = psb("pK"); rpK = Res(excl=True)
        pO = psb("pO"); rpO = Res(excl=True)
        pS = psb("pS"); rpS = Res(excl=True)

        for dc in range(2):
            E("dve", "memset", [], [rS[dc]], ap=S[:, dc, :], constant=0.0)
            E("dve", "memset", [], [rSb[dc]], ap=Sb[:, dc, :], constant=0.0)

        outs = []
        cnt = {"h": 0, "p": 0, "c": 0}

        def rope_evac(ps, rps, dst, rdst, row0, is_q, lat):
            sc = (1.0 / 16.0) if is_q else 1.0
            if not lat:
                E("act", "activation", [rps], [rdst], out=dst, in_=ps, func=AF.Identity, scale=sc)
                return
            E("act", "activation", [rps], [rQS], out=QS[:], in_=ps, func=AF.Identity, scale=sc)
            E(PENG, "tensor_copy", [rQS], [rQW], out=QW[0:64, :], in_=QS[64:128, :])
            E(PENG, "tensor_copy", [rQS], [rQW], out=QW[64:128, :], in_=QS[0:64, :])
            return

        def tile_pass(d, src, t0, lat, grow0, ch_order, tok0, final):
            hi = cnt["h"] % 2
            cnt["h"] += 1
            Ht, rHt = H[hi], rH[hi]
            v = src.rearrange("(c p) t -> p c t", p=128)
            for hh in range(2):
                cs = slice(hh * 8, (hh + 1) * 8)
                P.dma("sp" if hh == 0 else "act", Ht[:, cs, :], v[:, cs, t0:t0 + GT], writes=[rHt])
            for (Wm, rW, dst, rdst, is_q) in ((Wq, rWq, QR, rQR, True), (Wk, rWk, KR, rKR, False)):
                for dc in range(2):
                    pi = cnt["p"] % 2
                    cnt["p"] += 1
                    pp, rpp = pP[pi], rpP[pi]
                    for k in range(KC):
                        P.mm(pp[:, :GT], Wm[:, k, dc * 128:(dc + 1) * 128], Ht[:, k, :], k == 0, k == KC - 1,
                             reads=[rW, rHt], writes=[rpp])
                    sc = (1.0 / 16.0) if is_q else 1.0
                    if not lat:
                        E("act", "activation", [rpp], [rdst[dc]], out=dst[:, dc, :], in_=pp[:, :GT], func=AF.Identity, scale=sc)
                    else:
                        E("act", "activation", [rpp], [rQS], out=QS[:], in_=pp[:, :GT], func=AF.Identity, scale=sc)
                        E(PENG, "tensor_copy", [rQS], [rQW], out=QW[0:64, :], in_=QS[64:128, :])
                        E(PENG, "tensor_copy", [rQS], [rQW], out=QW[64:128, :], in_=QS[0:64, :])
                        q3 = lambda a: a.rearrange("p (r c) -> p r c", c=64)
                        if dc == 0:
                            ct = CR[:, grow0:grow0 + NCH].unsqueeze(2).to_broadcast([128, NCH, 64])
                            stb = SR[:, grow0:grow0 + NCH].unsqueeze(2).to_broadcast([128, NCH, 64])
                        else:
                            ct = CC[:].unsqueeze(1).to_broadcast([128, NCH, 64])
                            stb = SC[:].unsqueeze(1).to_broadcast([128, NCH, 64])
                        E("dve", "tensor_tensor", [rQS, rT], [rdst[dc]], out=q3(dst[:, dc, :]), in0=q3(QS[:]), in1=ct, op=ALU.mult)
                        E(PENG, "tensor_tensor", [rQW, rT], [rQW], out=q3(QW[:]), in0=q3(QW[:]), in1=stb, op=ALU.mult)
                        E("dve", "tensor_tensor", [rQW, rdst[dc]], [rdst[dc]], out=dst[:, dc, :], in0=dst[:, dc, :], in1=QW[:], op=ALU.add)
            pi = cnt["p"] % 2
            cnt["p"] += 1
            pp, rpp = pP[pi], rpP[pi]
            for k in range(KC):
                P.mm(pp[0:16, :GT], Wa1[:, d, k, :], Ht[:, k, :], k == 0, k == KC - 1, reads=[rWa1, rHt], writes=[rpp])
            E("act", "activation", [rpp], [rZL], out=ZL[:], in_=pp[0:16, :GT], func=AF.Copy)
            for pr in range(2):
                P.mm(pZ[:, pr * DK:(pr + 1) * DK], ZL[:, pr * 128:(pr + 1) * 128], Wa2[:, d, :], True, True,
                     reads=[rZL, rWa2], writes=[rpZ])
            z3 = pZ[:].rearrange("p (a f) -> p a f", a=2)
            E("dve", "tensor_tensor", [rpZ, rT], [rZB], out=ZB[:], in0=z3,
              in1=BA[:, d, :].unsqueeze(1).to_broadcast([128, 2, DK]), op=ALU.add)
            E("act", "activation", [rZB], [rZB], out=ZB[:], in_=ZB[:], func=AF.Exp, scale=-1.0)
            E("act", "activation", [rZB], [rZB], out=ZB[:], in_=ZB[:], func=AF.Ln, bias=1.0)
            for which, (dstE, rdstE, scl) in enumerate(((None, None, None), (EL, rEL, 1.0))):
                tri = TRI[:, 2 * d + which, :]
                for dc in range(2):
                    for pr in range(2):
                        P.mm(pC[:, dc * GT + pr * 128:dc * GT + (pr + 1) * 128], ZB[:, pr, dc * 128:(dc + 1) * 128], tri,
                             True, True, reads=[rZB, rT], writes=[rpC])
                pc3 = pC[:].rearrange("p (a f) -> p a f", a=2)
                if which == 0:
                    E("act", "activation", [rpC], [rEQ], out=EQ[:], in_=pc3, func=AF.Exp)
                    E("act", "activation", [rpC], [rEK], out=EK[:], in_=pc3, func=AF.Exp, scale=-1.0)
                else:
                    E("act", "activation", [rpC], [rEL], out=EL[:], in_=pc3, func=AF.Exp)
            E("dve", "tensor_tensor", rQR + [rEQ], [rQE], out=QE[:], in0=QR[:], in1=EQ[:], op=ALU.mult)
            E(PENG, "tensor_tensor", rKR + [rEK], [rKE], out=KE[:], in0=KR[:], in1=EK[:], op=ALU.mult)
            E("dve", "tensor_tensor", rKR + [rEL], [rKLT], out=KLT[:], in0=KR[:], in1=EL[:], op=ALU.mult)
            def stage_a(ch, ci):
                cs = slice(ch * 64, (ch + 1) * 64)
                for k in range(KC):
                    P.mm(pV[0:64, :], Ht[:, k, cs], Wv[:, k, :], k == 0, k == KC - 1, reads=[rHt, rWv], writes=[rpV])
                E("act", "activation", [rpV], [rVT[ci]], out=VT[ci][:], in_=pV[0:64, :], func=AF.Copy)
                for dc in range(2):
                    P.mm(pK[0:64, dc * 128:(dc + 1) * 128], KLT[:, dc, cs], IDb[:], True, True, reads=[rKLT, rT], writes=[rpK])
                for dc in range(2):
                    P.mm(pK[0:64, 256:320], KE[:, dc, cs], QE[:, dc, cs], dc == 0, dc == 1, reads=[rKE, rQE], writes=[rpK])
                E("act", "activation", [rpK], [rKL[ci]], out=KL[ci][:], in_=pK[0:64, 0:256], func=AF.Copy)
                E("dve", "tensor_tensor", [rpK, rT], [rSCT[ci]], out=SCT[ci][:], in0=pK[0:64, 256:320], in1=MSK[:, d, :], op=ALU.mult)
                if final:
                    for k in range(KC):
                        P.mm(pZ[0:64, :], Ht[:, k, cs], Wr[:, k, :], k == 0, k == KC - 1, reads=[rHt, rWr], writes=[rpZ])
                    E("dve", "tensor_tensor", [rpZ, rT], [rRT[ci]], out=RT[ci][:], in0=pZ[0:64, :], in1=BR[:], op=ALU.add)
                    E("act", "activation", [rRT[ci]], [rRT[ci]], out=RT[ci][:], in_=RT[ci][:], func=AF.Silu)

            def stage_b(ch, ci):
                cs = slice(ch * 64, (ch + 1) * 64)
                grow = tok0 + ch * 64
                P.mm(pO[0:64, :], SCT[ci][:], VT[ci][:], True, False, reads=[rSCT[ci], rVT[ci]], writes=[rpO])
                for dc in range(2):
                    P.mm(pO[0:64, :], QE[:, dc, cs], Sb[:, dc, :], False, dc == 1, reads=[rQE, rSb[dc]], writes=[rpO])
                lcol = ch * 64 + (63 if d == 0 else 0)
                for dc in range(2):
                    P.mm(pS[:, :], KL[ci][:, dc * 128:(dc + 1) * 128], VT[ci][:], True, True, reads=[rKL[ci], rVT[ci]], writes=[rpS])
                    E("dve", "scalar_tensor_tensor", [rpS, rS[dc], rEQ], [rS[dc]], out=S[:, dc, :], in0=S[:, dc, :],
                      scalar=EQ[:, dc, lcol:lcol + 1], in1=pS[:, :], op0=ALU.mult, op1=ALU.add)
                    E("act", "activation", [rS[dc]], [rSb[dc]], out=Sb[:, dc, :], in_=S[:, dc, :], func=AF.Copy)
                if not final:
                    E("act", "activation", [rpO], [rOS[ci]], out=OS[ci][:], in_=pO[0:64, :], func=AF.Copy)
                    P.dma("sp", of_s[grow:grow + 64, :], OS[ci][:], reads=[rOS[ci]], writes=[rOFS[grow // 64]])
                else:
                    P.dma("sp", OF[ci][:], of_s[grow:grow + 64, :], reads=[rOFS[grow // 64]], writes=[rOF[ci]])
                    E("dve", "tensor_tensor", [rpO, rOF[ci]], [rOS[ci]], out=OS[ci][:], in0=pO[0:64, :], in1=OF[ci][:], op=ALU.add)
                    E("dve", "tensor_tensor", [rOS[ci]], [rSQ], out=SQ[:], in0=OS[ci][:], in1=OS[ci][:], op=ALU.mult)
                    E("dve", "reduce_sum", [rSQ], [rSS], out=SS[:], in_=SQ[:], axis=AX.X)
                    E("dve", "tensor_scalar", [rSS], [rSS], out=SS[:], in0=SS[:], scalar1=1.0 / DV, scalar2=EPS, op0=ALU.mult, op1=ALU.add)
                    E("act", "activation", [rSS], [rSS], out=SS[:], in_=SS[:], func=AF.Sqrt)
                    E("dve", "reciprocal", [rSS], [rSS], out=SS[:], in_=SS[:])
                    E("dve", "scalar_tensor_tensor", [rOS[ci], rSS, rT], [rOS[ci]], out=OS[ci][:], in0=OS[ci][:], scalar=SS[:, 0:1],
                      in1=NG[:], op0=ALU.mult, op1=ALU.mult)
                    E("dve", "tensor_tensor", [rOS[ci], rRT[ci]], [rOUT[ci]], out=OUT[ci][:], in0=OS[ci][:], in1=RT[ci][:], op=ALU.mult)
                    outs.append(P.dma("sp", oy[grow:grow + 64, :], OUT[ci][:], reads=[rOUT[ci]]))

            stage_a(ch_order[0], 0)
            for i_, ch in enumerate(ch_order):
                if i_ + 1 < len(ch_order):
                    stage_a(ch_order[i_ + 1], (i_ + 1) % 2)
                stage_b(ch, i_ % 2)

        rOFS = [Res() for _ in range(NTOK // 64)]
        tile_pass(0, hcT, 0, False, 0, list(range(NCH)), 0, False)
        for ti in range(nlt):
            tile_pass(0, hT, ti * GT, True, ti * NCH, list(range(NCH)), CTX + ti * GT, False)
        for dc in range(2):
            E("dve", "memset", [], [rS[dc]], ap=S[:, dc, :], constant=0.0)
            E("dve", "memset", [], [rSb[dc]], ap=Sb[:, dc, :], constant=0.0)
        tile_pass(1, hcT, 0, False, 0, list(range(NCH))[::-1], 0, True)
        for ti in range(nlt)[::-1]:
            tile_pass(1, hT, ti * GT, True, ti * NCH, list(range(NCH))[::-1], CTX + ti * GT, True)
        P.final_wait("sp", outs)
        with nc.Block() as block:
            P.replay(block)
    return nc


def lb_inputs(inp, h1, hc1, core, tabs):
    import ml_dtypes
    b, hd = core // 4, core % 4
    ks = slice(hd * DK, (hd + 1) * DK)
    vs = slice(hd * DV, (hd + 1) * DV)
    d = dict(
        hT=np.ascontiguousarray(h1[b].T).astype(ml_dtypes.bfloat16),
        hcT=np.ascontiguousarray(hc1[b].T).astype(ml_dtypes.bfloat16),
        wq=np.ascontiguousarray(inp["gla_w_q"][0][:, ks]), wk=np.ascontiguousarray(inp["gla_w_k"][0][:, ks]),
        wv=np.ascontiguousarray(inp["gla_w_v"][0][:, vs]), wr=np.ascontiguousarray(inp["gla_w_r"][0][:, vs]),
        wa1=np.ascontiguousarray(inp["gla_w_a1"][0]), wa2=np.ascontiguousarray(inp["gla_w_a2"][0][:, :, ks]),
        bab=np.ascontiguousarray(np.broadcast_to(inp["gla_b_a"][0][:, None, ks], (2, 128, DK))),
        brb=np.ascontiguousarray(np.broadcast_to(inp["gla_b_r"][0][None, vs], (64, DV))),
        ngb=np.ascontiguousarray(np.broadcast_to(inp["gla_norm_g"][0][None, :], (64, DV))),
        tcr=tabs["cr"], tsr=tabs["sr"], tcc=tabs["cc"], tsc=tabs["sc"], ttri=tabs["tri"], tmsk=tabs["msk"], tid=tabs["ident"])
    return d


NH = 16
DH = 128
HPC = 4
ROWS = 128
GW_ = 64
NT = 512
NEG = -30000.0
NTOKA = SEQ + CTX
NVT = NTOKA // 128


def na_bias_tables(rpb):
    col = np.arange(64)
    cs = np.clip(col - 8, 0, 48)
    out = np.full((NH, 8, 64, 8, 64), NEG, np.float32)
    for v in range(8):
        delta = v - 7
        for j in range(8):
            ri = delta + j + 7
            for q in range(64):
                kc = cs[q] + np.arange(16)
                out[:, v, q, j, kc] = rpb[:, ri, kc - q + 15]
    return out.reshape(NH, 8, 64, 512)


def build_LD():
    nc = new_nc()
    dt = lambda n, s, d, k="ExternalInput": nc.dram_tensor(n, s, d, kind=k).ap()
    hT = dt("hT", [D, NTOKA], BF16)
    wq = dt("wq", [HPC, D, DH], F32)
    wk = dt("wk", [HPC, D, DH], F32)
    wv = dt("wv", [HPC, D, DH], F32)
    btab = dt("btab", [HPC, 8, 64, 512], F32)
    tid = dt("tid", [64, 64], F32)
    oy = dt("oy", [SEQ, HPC * DH], BF16, "ExternalOutput")
    with contextlib.ExitStack() as st:
        P = Prog(nc, st)
        sb = lambda n, s, d: st.enter_context(nc.sbuf_tensor("s_" + n, s, d))
        psb = lambda n: st.enter_context(nc.psum_tensor(n, [128, 512], F32))

        def E(eng, fn, reads, writes, **kw):
            return P.emit(eng, lambda e: getattr(e, fn)(**kw), reads, writes)

        W = [sb("W%d" % i, [128, 3, KC, DH], BF16) for i in range(2)]; rW = [Res(), Res()]
        BT = sb("BT", [64, 8, 512], F32); rBT = Res()
        ID = sb("ID", [64, 64], BF16); rID = Res()
        P.dma("pool", ID[:], tid, writes=[rID])
        H = [sb("H%d" % i, [128, KC, NT], BF16) for i in range(2)]; rH = [Res(), Res()]
        QT = sb("QT", [128, SEQ], BF16); rQT = Res()
        KT = sb("KT", [128, NTOKA], BF16); rKT = Res()
        VE = sb("VE", [128, NVT, DH + 1], BF16); rVE = Res()
        VO = sb("VO", [128, NVT, DH + 1], BF16); rVO = Res()
        SL = [sb("SL%d" % i, [64, 768], F32) for i in range(2)]; rSL = [Res(), Res()]
        MX = [sb("MX%d" % i, [64, 2], F32) for i in range(2)]; rMX = [Res(), Res()]
        PB = [sb("PB%d" % i, [64, 768], BF16) for i in range(2)]; rPB = [Res(), Res()]
        PT = [sb("PT%d" % i, [128, 384], BF16) for i in range(2)]; rPT = [Res(), Res()]
        RS = [sb("RS%d" % i, [64, 1], F32) for i in range(2)]; rRS = [Res(), Res()]
        OB = [sb("OB%d" % i, [64, DH], BF16) for i in range(2)]; rOB = [Res(), Res()]
        pP = [psb("pP0"), psb("pP1")]; rpP = [Res(excl=True), Res(excl=True)]
        pA = [psb("pA0"), psb("pA1")]; rpA = [Res(excl=True), Res(excl=True)]
        pB = psb("pB"); rpB = Res(excl=True)
        pT = [psb("pT0"), psb("pT1")]; rpT = [Res(excl=True), Res(excl=True)]
        pO = psb("pO"); rpO = Res(excl=True)

        E("dve", "memset", [], [rVE], ap=VE[:, :, DH:DH + 1], constant=1.0)
        outs = []
        hv = hT.rearrange("(c p) t -> p c t", p=128)
        cnt = {"h": 0, "p": 0, "u": 0}
        scale = DH ** -0.5
        for hh in range(HPC):
            Wt, rWt = W[hh % 2], rW[hh % 2]
            for i, wsrc in enumerate((wq, wk, wv)):
                P.dma("pool", Wt[:, i], wsrc[hh].rearrange("(kc p) f -> p kc f", p=128), writes=[rWt])
            P.dma("sp", BT[:], btab[hh].rearrange("v q k -> q v k"), writes=[rBT])
            for ti in range(NTOKA // NT + (1 if NTOKA % NT else 0)):
                t0 = ti * NT
                n = min(NT, NTOKA - t0)
                hi = cnt["h"] % 2
                cnt["h"] += 1
                Ht, rHt = H[hi], rH[hi]
                for half in range(2):
                    cs = slice(half * 8, (half + 1) * 8)
                    P.dma("sp" if half == 0 else "act", Ht[:, cs, :n], hv[:, cs, t0:t0 + n], writes=[rHt])
                lat = t0 < SEQ
                for which in ((0, 1) if lat else (1,)):
                    pi = cnt["p"] % 2
                    cnt["p"] += 1
                    pp, rpp = pP[pi], rpP[pi]
                    for k in range(KC):
                        P.mm(pp[:, :n], Wt[:, which, k, :], Ht[:, k, :n], k == 0, k == KC - 1, reads=[rWt, rHt], writes=[rpp])
                    if which == 0:
                        E("act", "activation", [rpp], [rQT], out=QT[:, t0:t0 + n], in_=pp[:, :n], func=AF.Identity, scale=scale)
                    else:
                        E("dve", "tensor_copy", [rpp], [rKT], out=KT[:, t0:t0 + n], in_=pp[:, :n])
                for s in range(n // 128):
                    pi = cnt["p"] % 2
                    cnt["p"] += 1
                    pp, rpp = pP[pi], rpP[pi]
                    for k in range(KC):
                        P.mm(pp[:, :DH], Ht[:, k, s * 128:(s + 1) * 128], Wt[:, 2, k, :], k == 0, k == KC - 1,
                             reads=[rWt, rHt], writes=[rpp])
                    E("act" if s % 2 else "dve", "activation" if s % 2 else "tensor_copy", [rpp], [rVE],
                      **(dict(out=VE[:, t0 // 128 + s, 0:DH], in_=pp[:, :DH], func=AF.Copy) if s % 2 else
                         dict(out=VE[:, t0 // 128 + s, 0:DH], in_=pp[:, :DH])))
            nlat = SEQ // 128
            P.dma("sp", VO[0:64, 0:nlat, :], VE[64:128, 0:nlat, :], reads=[rVE], writes=[rVO])
            P.dma("act", VO[64:128, 0:nlat - 1, :], VE[0:64, 1:nlat, :], reads=[rVE], writes=[rVO])
            def s1(r):
                u = r % 2
                rs = min(max(r - 4, 0), ROWS - 8)
                var = rs - r + 7
                qs = QT[:, r * 64:(r + 1) * 64]
                P.mm(pA[u][0:64, :], qs, KT[:, rs * 64:rs * 64 + 512], True, True, reads=[rQT, rKT], writes=[rpA[u]])
                P.mm(pB[0:64, 0:CTX], qs, KT[:, SEQ:SEQ + CTX], True, True, reads=[rQT, rKT], writes=[rpB])
                E("dve", "tensor_tensor", [rpA[u], rBT], [rSL[u]], out=SL[u][:, 0:512], in0=pA[u][0:64, :], in1=BT[:, var, :], op=ALU.add)
                E("act", "activation", [rpB], [rSL[u]], out=SL[u][:, 512:768], in_=pB[0:64, 0:CTX], func=AF.Copy)
                E("dve", "reduce_max", [rSL[u]], [rMX[u]], out=MX[u][:, 0:1], in_=SL[u][:], axis=AX.X)
                E("dve", "tensor_single_scalar", [rMX[u]], [rMX[u]], out=MX[u][:, 1:2], in_=MX[u][:, 0:1], scalar=-1.0, op=ALU.mult)
                E("act", "activation", [rSL[u], rMX[u]], [rPB[u]], out=PB[u][:], in_=SL[u][:], func=AF.Exp, bias=MX[u][:, 1:2])

            def s2(r):
                u = r % 2
                for kc in range(6):
                    P.mm(pT[u][:, kc * 64:(kc + 1) * 64], PB[u][:, kc * 128:(kc + 1) * 128], ID[:], True, True,
                         reads=[rPB[u], rID], writes=[rpT[u]])
                E("act" if r % 4 < 2 else "dve", "activation" if r % 4 < 2 else "tensor_copy", [rpT[u]], [rPT[u]],
                  **(dict(out=PT[u][:], in_=pT[u][:, 0:384], func=AF.Copy) if r % 4 < 2 else dict(out=PT[u][:], in_=pT[u][:, 0:384])))

            def s3(r):
                u = r % 2
                rs = min(max(r - 4, 0), ROWS - 8)
                for kc in range(6):
                    if kc < 4:
                        if rs % 2 == 0:
                            vt, rv = VE[:, rs // 2 + kc, :], rVE
                        else:
                            vt, rv = VO[:, (rs - 1) // 2 + kc, :], rVO
                    else:
                        vt, rv = VE[:, SEQ // 128 + (kc - 4), :], rVE
                    P.mm(pO[0:64, 0:DH + 1], PT[u][:, kc * 64:(kc + 1) * 64], vt, kc == 0, kc == 5, reads=[rPT[u], rv], writes=[rpO])
                E("dve", "reciprocal", [rpO], [rRS[u]], out=RS[u][:], in_=pO[0:64, DH:DH + 1])
                E("act", "activation", [rpO, rRS[u]], [rOB[u]], out=OB[u][:], in_=pO[0:64, 0:DH], func=AF.Identity, scale=RS[u][:, 0:1])
                outs.append(P.dma("sp", oy[r * 64:(r + 1) * 64, hh * DH:(hh + 1) * DH], OB[u][:], reads=[rOB[u]]))

            for it in range(ROWS + 2):
                if it < ROWS:
                    s1(it)
                if 0 <= it - 1 < ROWS:
                    s2(it - 1)
                if 0 <= it - 2 < ROWS:
                    s3(it - 2)
        P.final_wait("sp", outs)
        with nc.Block() as block:
            P.replay(block)
    return nc


def ld_inputs(inp, h2, hc2, core, btab_all):
    import ml_dtypes
    b, hg = core // 4, core % 4
    hcat = np.concatenate([h2[b], hc2[b]], 0)
    wqkv = inp["na_w_qkv"][0]
    sl = lambda base: np.ascontiguousarray(
        np.stack([wqkv[:, base + (hg * HPC + i) * DH: base + (hg * HPC + i + 1) * DH] for i in range(HPC)]))
    return dict(hT=np.ascontiguousarray(hcat.T).astype(ml_dtypes.bfloat16),
                wq=sl(0), wk=sl(D), wv=sl(2 * D),
                btab=np.ascontiguousarray(btab_all[hg * HPC:(hg + 1) * HPC]),
                tid=np.eye(64, dtype=np.float32))


_CACHE = {}


def _run(nc, maps):
    return run_bass_kernel_spmd(nc, maps, core_ids=list(range(NCORE))).results


def kernel(**inputs):
    import ml_dtypes
    inp = {k: np.asarray(v) for k, v in inputs.items()}
    cores = range(NCORE)
    WB = run_LW(inp)
    fw_ = lambda l, i: (WB["g%d%d" % (l, i)], WB["u%d%d" % (l, i)], WB["d%d%d" % (l, i)])
    m = run_L0(inp)
    nc, L = build_LA()
    eye = np.eye(128, dtype=np.float32)
    maps = []
    for core in cores:
        xb, mk = lat_blocks(inp["x"], core, HALO, NBA, TCA)
        cb, cmk = ctx_block(inp["ctx"], core, HALO)
        msk = np.zeros((NBA, 128, TPA + CTP), np.float32)
        msk[:, :, :TPA] = mk[:, None, :]
        msk[0, :, TPA:] = cmk[None, :]
        g00, u00, d00 = fw_(0, 0)
        g01, u01, d01 = fw_(0, 1)
        g10, u10, d10 = fw_(1, 0)
        maps.append(dict(xin=xb, msk=msk, cin=cb, pv=pv_fill(L, inp, m, core, [0, 1], [0]), idn=eye,
                         g00=g00, u00=u00, d00=d00, g01=g01, u01=u01, d01=d01, g10=g10, u10=u10, d10=d10,
                         cwi=WB["cwi0"], cwo=WB["cwo0"]))
    res = _run(nc, maps)
    x1 = gather_lat([r["xo"] for r in res], nb=NBA, tc=TCA)
    h1 = gather_lat([r["ho"] for r in res], nb=NBA, tc=TCA)
    xc1 = gather_ctx([r["xco"] for r in res])
    hc1 = gather_ctx([r["hco"] for r in res])
    del maps, res
    tabs = gla_tables()
    res = _run(build_LB(), [lb_inputs(inp, h1, hc1, core, tabs) for core in cores])
    y1 = np.zeros((2, SEQ, D), ml_dtypes.bfloat16)
    yc1 = np.zeros((2, CTX, D), ml_dtypes.bfloat16)
    for core in cores:
        b, hd = core // 4, core % 4
        o = np.asarray(res[core]["oy"])
        yc1[b, :, hd * DV:(hd + 1) * DV] = o[:CTX]
        y1[b, :, hd * DV:(hd + 1) * DV] = o[CTX:]
    del res
    nc, L = build_LC()
    maps = []
    for core in cores:
        g11, u11, d11 = fw_(1, 1)
        g20, u20, d20 = fw_(2, 0)
        maps.append(dict(xin=lat_blocks(x1, core, 0)[0], yin=lat_blocks(y1, core, 0)[0],
                         cin=ctx_block(xc1, core, 0)[0], ycin=ctx_block(yc1, core, 0)[0],
                         pv=pv_fill(L, inp, m, core, [1, 2], []),
                         g11=g11, u11=u11, d11=d11, g20=g20, u20=u20, d20=d20, wo=WB["gwo"]))
    res = _run(nc, maps)
    x2 = gather_lat([r["xo"] for r in res])
    h2 = gather_lat([r["ho"] for r in res])
    hc2 = gather_ctx([r["hco"] for r in res])
    del maps, res
    btab = na_bias_tables(inp["na_rpb"][0])
    res = _run(build_LD(), [ld_inputs(inp, h2, hc2, core, btab) for core in cores])
    y2 = np.zeros((2, SEQ, D), ml_dtypes.bfloat16)
    for core in cores:
        b, hg = core // 4, core % 4
        y2[b, :, hg * HPC * DH:(hg + 1) * HPC * DH] = np.asarray(res[core]["oy"])
    del res
    nc, L = build_LE()
    maps = []
    for core in cores:
        xb, mk = lat_blocks(x2, core, HALO)
        g21, u21, d21 = fw_(2, 1)
        g30, u30, d30 = fw_(3, 0)
        g31, u31, d31 = fw_(3, 1)
        maps.append(dict(xin=xb, yin=lat_blocks(y2, core, HALO)[0], idn=eye,
                         msk=np.ascontiguousarray(np.broadcast_to(mk[:, None, :], (NB, 128, TP))),
                         pv=pv_fill(L, inp, m, core, [2, 3], [3]),
                         g21=g21, u21=u21, d21=d21, g30=g30, u30=u30, d30=d30, g31=g31, u31=u31, d31=d31,
                         wo=WB["nwo"], cwi=WB["cwi1"], cwo=WB["cwo1"]))
    res = _run(nc, maps)
    out = gather_lat([r["xo"] for r in res])
    return out.astype(np.float32)
```

```python
import numpy as np
import contextlib
import concourse.bass as bass
import concourse.mybir as mybir
from concourse.bass_utils import run_bass_kernel_spmd

F32 = mybir.dt.float32
BF16 = mybir.dt.bfloat16
AF = mybir.ActivationFunctionType
ALU = mybir.AluOpType
AX = mybir.AxisListType


class Res:
    __slots__ = ("w", "rd", "name", "excl")

    def __init__(self, name="", excl=False):
        self.w = None
        self.rd = {}
        self.name = name
        self.excl = excl


class Prog:
    ENGS = ("pe", "act", "dve", "pool", "sp")
    NDS = 8

    def __init__(self, nc, stack):
        self.nc = nc
        self.q = {e: [] for e in self.ENGS}
        self.sem = {e: stack.enter_context(nc.semaphore("s_" + e)) for e in self.ENGS}
        self.cnt = {e: 0 for e in self.ENGS}
        self.seen = {e: {} for e in self.ENGS}
        self.dq = ("sp", "act", "pool")
        self.dsem = {e: [stack.enter_context(nc.semaphore("d_%s%d" % (e, i))) for i in range(self.NDS)]
                     for e in self.dq}
        self.dcnt = {e: 0 for e in self.dq}
        self.nwaits = 0

    def _need(self, eng, tok, waits, kind):
        if tok is None:
            return
        key, semh, val, teng = tok
        if teng == eng and kind != "raw":
            return
        if teng == "pe" and eng == "pe":
            return
        if self.seen[eng].get(key, 0) >= val:
            return
        self.seen[eng][key] = val
        waits.append((semh, val))

    def emit(self, eng, fn, reads=(), writes=(), dma=False):
        waits = []
        for r in reads:
            self._need(eng, r.w, waits, "raw")
            if r.excl:
                for t in r.rd.values():
                    self._need(eng, t, waits, "rar")
        for r in writes:
            self._need(eng, r.w, waits, "waw")
            for t in r.rd.values():
                self._need(eng, t, waits, "war")
        if dma:
            i = self.dcnt[eng]
            self.dcnt[eng] += 1
            slot, k = i % self.NDS, i // self.NDS
            semh = self.dsem[eng][slot]
            key = "d_%s%d" % (eng, slot)
            if k > 0:
                self._need(eng, (key, semh, 16 * k, None), waits, "raw")
            tok = (key, semh, 16 * (k + 1), None)
            inc = 16
        else:
            self.cnt[eng] += 1
            semh = self.sem[eng]
            tok = ("e_" + eng, semh, self.cnt[eng], eng)
            inc = 1
        for r in reads:
            r.rd[tok[0]] = tok
        for r in writes:
            r.w = tok
            r.rd = {}
        self.nwaits += len(waits)
        self.q[eng].append((waits, fn, semh, inc))
        return tok

    def dma(self, q, out, in_, reads=(), writes=()):
        return self.emit(q, lambda e: e.dma_start(out=out, in_=in_), reads, writes, dma=True)

    def mm(self, out, lhsT, rhs, start, stop, reads=(), writes=()):
        return self.emit("pe", lambda e: e.matmul(out, lhsT, rhs, start=start, stop=stop), reads, writes)

    def final_wait(self, eng, toks):
        waits = []
        for t in toks:
            self._need(eng, t, waits, "raw")
        self.q[eng].append((waits, None, None, 0))

    def replay(self, block):
        nc = self.nc
        names = {"pe": "tensor", "act": "scalar", "dve": "vector", "pool": "gpsimd", "sp": "sync"}
        for e in self.ENGS:
            ops = self.q[e]

            def body(eng, ops=ops):
                for waits, fn, semh, inc in ops:
                    for (s, v) in waits:
                        eng.wait_ge(s, v)
                    if fn is not None:
                        fn(eng).then_inc(semh, inc)
            getattr(block, names[e])(body)


D = 2048
DFF = 5632
KC = 16
JC = 44
EPS = 1e-6
CW = 31
HALO = 15
WB_ELEMS = 4096
NWB = 4
GW = 2


def pvec(v):
    v = np.asarray(v, np.float32).reshape(-1)
    return np.ascontiguousarray(v.reshape(-1, 128).T)


class TL:
    def __init__(self, nc, st, P, TMAX, pv_dram, L, conv=False, tcmax=None, ident_dram=None, nwb=NWB):
        self.nc, self.st, self.P, self.TMAX = nc, st, P, TMAX
        self.L = L
        npv = L.n
        sb = lambda n, s, d: st.enter_context(nc.sbuf_tensor("s_" + n, s, d))
        ps = lambda n: st.enter_context(nc.psum_tensor(n, [128, 512], F32))
        self.X = sb("X", [128, KC, TMAX], F32)
        self.rX = [Res("X%d" % c) for c in range(KC)]
        self.Hb = sb("Hb", [128, KC, TMAX], BF16)
        self.rHb = [Res("Hb%d" % c) for c in range(KC)]
        self.HID = sb("HID", [128, JC, TMAX], BF16)
        self.rHID = [Res("HID%d" % j) for j in range(JC)]
        self.Y = sb("Y", [128, KC, TMAX], F32)
        self.rY = [Res("Y%d" % c) for c in range(KC)]
        self.NWB = nwb
        self.WB = [sb("WB%d" % i, [128, WB_ELEMS], BF16) for i in range(nwb)]
        self.rWB = [Res("WB%d" % i) for i in range(nwb)]
        self.wbi = 0
        self.pv = sb("pv", [128, npv], F32)
        self.rpv = Res("pv")
        self.ones = sb("ones", [128, 128], BF16)
        self.rones = Res("ones")
        self.sq = [sb("sq%d" % i, [128, TMAX], BF16) for i in range(2)]
        self.rsq = [Res() for _ in range(2)]
        self.sqi = 0
        self.tmp = [sb("tmp%d" % i, [128, TMAX], F32) for i in range(3)]
        self.rtmp = [Res() for _ in range(3)]
        self.tmi = 0
        self.R = sb("R", [128, TMAX], F32)
        self.rR = Res("R")
        self.R2 = sb("R2", [128, TMAX], F32)
        self.rR2 = Res("R2")
        self.pA = [ps("pA%d" % i) for i in range(2)]
        self.rpA = [Res(excl=True) for _ in range(2)]
        self.pB = [ps("pB%d" % i) for i in range(2)]
        self.rpB = [Res(excl=True) for _ in range(2)]
        self.pY = [ps("pY%d" % i) for i in range(2)]
        self.rpY = [Res(excl=True) for _ in range(2)]
        self.pS = ps("pS")
        self.rpS = Res("pS", excl=True)
        self.pS2 = ps("pS2")
        self.rpS2 = Res("pS2", excl=True)
        self.pai = 0
        self.pyi = 0
        if conv:
            self.V = sb("V", [128, KC, tcmax], F32)
            self.rV = [Res() for _ in range(KC)]
            self.U = [sb("U%d" % i, [128, TMAX], BF16) for i in range(2)]
            self.rU = [Res() for _ in range(2)]
            self.ui = 0
            self.NDG = 8
            self.DG = [sb("DG%d" % i, [128, 128], BF16) for i in range(self.NDG)]
            self.rDG = [Res() for _ in range(self.NDG)]
            self.dgi = 0
            self.ident = sb("ident", [128, 128], F32)
            self.rident = Res()
            self.MU = sb("MU", [128, tcmax], F32)
            self.rMU = Res()
            self.mask = sb("mask", [128, TMAX], F32)
            self.rmask = Res()
            P.dma("act", self.ident[:], ident_dram, writes=[self.rident])
        P.dma("sp", self.pv[:], pv_dram, writes=[self.rpv])
        P.emit("dve", lambda e: e.memset(self.ones[:], 1.0), writes=[self.rones])

    def wload(self, src, a, b):
        i = self.wbi
        self.wbi = (i + 1) % self.NWB
        view = self.WB[i][:, 0:a * b].rearrange("p (a b) -> p a b", a=a)
        q = "pool" if src.dtype == F32 else "sp"
        self.P.dma(q, view, src, writes=[self.rWB[i]])
        return view, self.rWB[i]

    def next_tmp(self):
        i = self.tmi
        self.tmi = (i + 1) % 3
        return self.tmp[i], self.rtmp[i]

    def next_sq(self):
        i = self.sqi
        self.sqi = (i + 1) % 2
        return self.sq[i], self.rsq[i]

    def pvs(self, off, c):
        return self.pv[:, off + c:off + c + 1]

    def derive(self, out_off, a_off, b_off, mode, scale=1.0):
        P, pv = self.P, self.pv
        o = pv[:, out_off:out_off + KC]
        a = pv[:, a_off:a_off + KC]
        b = pv[:, b_off:b_off + KC]
        if mode == "a1pb":
            P.emit("dve", lambda e: e.scalar_tensor_tensor(out=o, in0=b, scalar=1.0, in1=a, op0=ALU.add, op1=ALU.mult),
                   reads=[self.rpv], writes=[self.rpv])
        else:
            P.emit("dve", lambda e: e.scalar_tensor_tensor(out=o, in0=a, scalar=scale, in1=b, op0=ALU.mult, op1=ALU.mult),
                   reads=[self.rpv], writes=[self.rpv])

    def rstd_from(self, pS, rpS, T, out, rout):
        P = self.P
        P.emit("dve", lambda e: e.tensor_scalar(out=out[:, :T], in0=pS[:, :T], scalar1=1.0 / D, scalar2=EPS,
                                                op0=ALU.mult, op1=ALU.add), reads=[rpS], writes=[rout])
        P.emit("act", lambda e: e.activation(out=out[:, :T], in_=out[:, :T], func=AF.Sqrt), reads=[rout], writes=[rout])
        P.emit("dve", lambda e: e.reciprocal(out=out[:, :T], in_=out[:, :T]), reads=[rout], writes=[rout])

    def norm_mod(self, segs, an, sn):
        P, L = self.P, self.L
        Tt = sum(sg[1] for sg in segs)
        for c in range(KC):
            sq, rsq = self.next_sq()
            for (xoff, T, dst, pf) in segs:
                P.emit("act", lambda e, c=c, sq=sq, xoff=xoff, T=T, dst=dst: e.activation(
                    out=sq[:, dst:dst + T], in_=self.X[:, c, xoff:xoff + T], func=AF.Square),
                    reads=[self.rX[c]], writes=[rsq])
            P.mm(self.pS[:, :Tt], self.ones[:], sq[:, :Tt], c == 0, c == KC - 1, reads=[self.rones, rsq], writes=[self.rpS])
        self.rstd_from(self.pS, self.rpS, Tt, self.R, self.rR)
        for c in range(KC):
            t, rt = self.next_tmp()
            for (xoff, T, dst, pf) in segs:
                a_off, sh_off = L[pf + "_" + an], L[pf + "_" + sn]
                P.emit("dve", lambda e, c=c, t=t, xoff=xoff, T=T, dst=dst, a_off=a_off: e.scalar_tensor_tensor(
                    out=t[:, dst:dst + T], in0=self.X[:, c, xoff:xoff + T], scalar=self.pvs(a_off, c), in1=self.R[:, dst:dst + T],
                    op0=ALU.mult, op1=ALU.mult), reads=[self.rX[c], self.rR, self.rpv], writes=[rt])
                P.emit("act", lambda e, c=c, t=t, T=T, dst=dst, sh_off=sh_off: e.activation(
                    out=self.Hb[:, c, dst:dst + T], in_=t[:, dst:dst + T], func=AF.Identity, bias=self.pvs(sh_off, c)),
                    reads=[rt, self.rpv], writes=[self.rHb[c]])

    def _y_evac(self, c, T, py, rpy, bias_off=None):
        P = self.P
        if bias_off is None:
            P.emit("act", lambda e: e.activation(out=self.Y[:, c, :T], in_=py[:, :T], func=AF.Copy),
                   reads=[rpy], writes=[self.rY[c]])
        else:
            P.emit("act", lambda e: e.activation(out=self.Y[:, c, :T], in_=py[:, :T], func=AF.Identity,
                                                 bias=self.pvs(bias_off, c)),
                   reads=[rpy, self.rpv], writes=[self.rY[c]])
        sq, rsq = self.next_sq()
        P.emit("dve", lambda e: e.tensor_tensor(out=sq[:, :T], in0=self.Y[:, c, :T], in1=self.Y[:, c, :T], op=ALU.mult),
               reads=[self.rY[c]], writes=[rsq])
        return sq, rsq

    def _stat_mm(self, pend, T):
        c, sq, rsq = pend
        self.P.mm(self.pS2[:, :T], self.ones[:], sq[:, :T], c == 0, c == KC - 1,
                  reads=[self.rones, rsq], writes=[self.rpS2])

    def linear_dd(self, T, w, bias_off=None, kchunks=KC):
        P = self.P
        wv = w.rearrange("(kc p) f -> p kc f", p=128)
        pend = None
        for g in range(KC // GW):
            wt, rwt = self.wload(wv[:, :, g * GW * 128:(g + 1) * GW * 128], kchunks, GW * 128)
            for cc in range(GW):
                c = g * GW + cc
                i = self.pyi
                self.pyi = (i + 1) % 2
                py, rpy = self.pY[i], self.rpY[i]
                for k in range(kchunks):
                    P.mm(py[:, :T], wt[:, k, cc * 128:(cc + 1) * 128], self.Hb[:, k, :T], k == 0, k == kchunks - 1,
                         reads=[rwt, self.rHb[k]], writes=[rpy])
                if pend is not None:
                    self._stat_mm(pend, T)
                sq, rsq = self._y_evac(c, T, py, rpy, bias_off)
                pend = (c, sq, rsq)
        self._stat_mm(pend, T)
        self.rstd_from(self.pS2, self.rpS2, T, self.R2, self.rR2)

    def ffn(self, T, wg, wu, wd):
        P = self.P
        wgv = wg.rearrange("(kc p) f -> p kc f", p=128)
        wuv = wu.rearrange("(kc p) f -> p kc f", p=128)
        wdv = wd.rearrange("(j p) f -> p j f", p=128)
        for g in range(JC // GW):
            gt, rgt = self.wload(wgv[:, :, g * GW * 128:(g + 1) * GW * 128], KC, GW * 128)
            ut, rut = self.wload(wuv[:, :, g * GW * 128:(g + 1) * GW * 128], KC, GW * 128)
            for jj in range(GW):
                j = g * GW + jj
                i = self.pai
                self.pai = (i + 1) % 2
                pa, rpa, pb, rpb = self.pA[i], self.rpA[i], self.pB[i], self.rpB[i]
                for k in range(KC):
                    P.mm(pa[:, :T], gt[:, k, jj * 128:(jj + 1) * 128], self.Hb[:, k, :T], k == 0, k == KC - 1,
                         reads=[rgt, self.rHb[k]], writes=[rpa])
                for k in range(KC):
                    P.mm(pb[:, :T], ut[:, k, jj * 128:(jj + 1) * 128], self.Hb[:, k, :T], k == 0, k == KC - 1,
                         reads=[rut, self.rHb[k]], writes=[rpb])
                t, rt = self.next_tmp()
                P.emit("act", lambda e, t=t, pa=pa: e.activation(out=t[:, :T], in_=pa[:, :T], func=AF.Silu),
                       reads=[rpa], writes=[rt])
                P.emit("dve", lambda e, t=t, pb=pb, j=j: e.tensor_tensor(out=self.HID[:, j, :T], in0=t[:, :T], in1=pb[:, :T],
                                                                       op=ALU.mult),
                       reads=[rt, rpb], writes=[self.rHID[j]])
        pend = []
        JQ = JC // 4
        for cp in range(KC // 2):
            wts = [self.wload(wdv[:, q * JQ:(q + 1) * JQ, cp * 256:(cp + 1) * 256], JQ, 256) for q in range(4)]
            for j in range(JC):
                wt, rwt = wts[j // JQ]
                for h in range(2):
                    P.mm(self.pY[h][:, :T], wt[:, j % JQ, h * 128:(h + 1) * 128], self.HID[:, j, :T], j == 0, j == JC - 1,
                         reads=[rwt, self.rHID[j]], writes=[self.rpY[h]])
            for pd in pend:
                self._stat_mm(pd, T)
            pend = []
            for h in range(2):
                c = cp * 2 + h
                sq, rsq = self._y_evac(c, T, self.pY[h], self.rpY[h])
                pend.append((c, sq, rsq))
        for pd in pend:
            self._stat_mm(pd, T)
        self.rstd_from(self.pS2, self.rpS2, T, self.R2, self.rR2)

    def post_res(self, segs, gn):
        P, L = self.P, self.L
        for c in range(KC):
            t, rt = self.next_tmp()
            for (xoff, T, dst, pf) in segs:
                g_off = L[pf + "_" + gn]
                P.emit("dve", lambda e, c=c, t=t, T=T, dst=dst, g_off=g_off: e.scalar_tensor_tensor(
                    out=t[:, dst:dst + T], in0=self.Y[:, c, dst:dst + T], scalar=self.pvs(g_off, c), in1=self.R2[:, dst:dst + T],
                    op0=ALU.mult, op1=ALU.mult), reads=[self.rY[c], self.rR2, self.rpv], writes=[rt])
                P.emit("dve", lambda e, c=c, t=t, xoff=xoff, T=T, dst=dst: e.tensor_tensor(
                    out=self.X[:, c, xoff:xoff + T], in0=self.X[:, c, xoff:xoff + T], in1=t[:, dst:dst + T], op=ALU.add),
                    reads=[rt, self.rX[c]], writes=[self.rX[c]])

    def conv(self, Tp, csegs, w_in, w_out, o):
        P = self.P
        Tc = sum(sg[1] for sg in csegs)
        wv = w_in.rearrange("(kc p) f -> p kc f", p=128)
        tiles = {}

        def emit_in(c):
            g, cc = divmod(c, GW)
            if cc == 0:
                tiles[g] = (self.wload(wv[:, :, g * GW * 128:(g + 1) * GW * 128], KC, GW * 128),
                            self.wload(wv[:, :, D + g * GW * 128:D + (g + 1) * GW * 128], KC, GW * 128))
            (at, rat), (gt, rgt) = tiles[g]
            i = self.pai
            self.pai = (i + 1) % 2
            pa, rpa, pb, rpb = self.pA[i], self.rpA[i], self.pB[i], self.rpB[i]
            for k in range(KC):
                P.mm(pa[:, :Tp], at[:, k, cc * 128:(cc + 1) * 128], self.Hb[:, k, :Tp], k == 0, k == KC - 1,
                     reads=[rat, self.rHb[k]], writes=[rpa])
            for k in range(KC):
                P.mm(pb[:, :Tp], gt[:, k, cc * 128:(cc + 1) * 128], self.Hb[:, k, :Tp], k == 0, k == KC - 1,
                     reads=[rgt, self.rHb[k]], writes=[rpb])
            t, rt = self.next_tmp()
            P.emit("act", lambda e: e.activation(out=t[:, :Tp], in_=pb[:, :Tp], func=AF.Sigmoid, bias=self.pvs(o["b_in"] + KC, c)),
                   reads=[rpb, self.rpv], writes=[rt])
            P.emit("dve", lambda e: e.tensor_tensor(out=t[:, :Tp], in0=t[:, :Tp], in1=self.mask[:, :Tp], op=ALU.mult),
                   reads=[rt, self.rmask], writes=[rt])
            ui = self.ui
            self.ui = (ui + 1) % 2
            U, rU = self.U[ui], self.rU[ui]
            P.emit("dve", lambda e: e.scalar_tensor_tensor(
                out=U[:, :Tp], in0=pa[:, :Tp], scalar=self.pvs(o["b_in"], c), in1=t[:, :Tp], op0=ALU.add, op1=ALU.mult),
                reads=[rpa, rt, self.rpv], writes=[rU])
            return U, rU

        def emit_taps(c, U, rU):
            for k in range(CW):
                di = self.dgi
                self.dgi = (di + 1) % self.NDG
                dg, rdg = self.DG[di], self.rDG[di]
                wk = self.pv[:, o["w_dw"] + c * CW + k:o["w_dw"] + c * CW + k + 1]
                P.emit("act", lambda e, dg=dg, wk=wk: e.activation(out=dg[:], in_=self.ident[:], func=AF.Identity, scale=wk),
                       reads=[self.rident, self.rpv], writes=[rdg])
                for si, (po, tci, dst) in enumerate(csegs):
                    P.mm(self.pY[si][:, :tci], dg[:], U[:, po + k:po + k + tci], k == 0, k == CW - 1,
                         reads=[rdg, rU], writes=[self.rpY[si]])
            for si, (po, tci, dst) in enumerate(csegs):
                P.emit("dve", lambda e, si=si, tci=tci, dst=dst: e.tensor_single_scalar(
                    out=self.V[:, c, dst:dst + tci], in_=self.pY[si][:, :tci], scalar=self.pvs(o["b_dw"], c), op=ALU.add),
                    reads=[self.rpY[si], self.rpv], writes=[self.rV[c]])
            sq, rsq = self.next_sq()
            P.emit("act", lambda e, sq=sq: e.activation(out=sq[:, :Tc], in_=self.V[:, c, :Tc], func=AF.Copy),
                   reads=[self.rV[c]], writes=[rsq])
            P.mm(self.pS[:, :Tc], self.ones[:], sq[:, :Tc], c == 0, c == KC - 1, reads=[self.rones, rsq], writes=[self.rpS])
            sq, rsq = self.next_sq()
            P.emit("act", lambda e, sq=sq: e.activation(out=sq[:, :Tc], in_=self.V[:, c, :Tc], func=AF.Square),
                   reads=[self.rV[c]], writes=[rsq])
            P.mm(self.pS2[:, :Tc], self.ones[:], sq[:, :Tc], c == 0, c == KC - 1, reads=[self.rones, rsq], writes=[self.rpS2])

        prev = None
        for c in range(KC):
            cur = emit_in(c)
            if prev is not None:
                emit_taps(c - 1, *prev)
            prev = cur
        emit_taps(KC - 1, *prev)
        MU, rMU, R, rR = self.MU, self.rMU, self.R, self.rR
        P.emit("dve", lambda e: e.tensor_single_scalar(out=MU[:, :Tc], in_=self.pS[:, :Tc], scalar=1.0 / D, op=ALU.mult),
               reads=[self.rpS], writes=[rMU])
        t, rt = self.next_tmp()
        P.emit("dve", lambda e: e.tensor_tensor(out=t[:, :Tc], in0=MU[:, :Tc], in1=MU[:, :Tc], op=ALU.mult),
               reads=[rMU], writes=[rt])
        P.emit("dve", lambda e: e.scalar_tensor_tensor(out=R[:, :Tc], in0=self.pS2[:, :Tc], scalar=1.0 / D, in1=t[:, :Tc],
                                                       op0=ALU.mult, op1=ALU.subtract), reads=[self.rpS2, rt], writes=[rR])
        P.emit("dve", lambda e: e.tensor_single_scalar(out=R[:, :Tc], in_=R[:, :Tc], scalar=EPS, op=ALU.add),
               reads=[rR], writes=[rR])
        P.emit("act", lambda e: e.activation(out=R[:, :Tc], in_=R[:, :Tc], func=AF.Sqrt), reads=[rR], writes=[rR])
        P.emit("dve", lambda e: e.reciprocal(out=R[:, :Tc], in_=R[:, :Tc]), reads=[rR], writes=[rR])
        for c in range(KC):
            t, rt = self.next_tmp()
            P.emit("dve", lambda e, c=c, t=t: e.tensor_tensor(out=t[:, :Tc], in0=self.V[:, c, :Tc], in1=MU[:, :Tc], op=ALU.subtract),
                   reads=[self.rV[c], rMU], writes=[rt])
            P.emit("dve", lambda e, t=t: e.tensor_tensor(out=t[:, :Tc], in0=t[:, :Tc], in1=R[:, :Tc], op=ALU.mult),
                   reads=[rt, rR], writes=[rt])
            P.emit("act", lambda e, c=c, t=t: e.activation(out=self.Hb[:, c, :Tc], in_=t[:, :Tc], func=AF.Silu,
                                                             scale=self.pvs(o["ln_g"], c), bias=self.pvs(o["ln_b"], c)),
                   reads=[rt, self.rpv], writes=[self.rHb[c]])
        self.linear_dd(Tc, w_out, bias_off=o["b_out"])

    def load_x(self, src, T, q="sp", col0=0):
        v = src.rearrange("(c p) t -> p c t", p=128)
        for h in range(2):
            cs = slice(h * 8, (h + 1) * 8)
            self.P.dma(q, self.X[:, cs, col0:col0 + T], v[:, cs, :], writes=self.rX[cs])

    def load_hb(self, src, T, q="sp", col0=0):
        v = src.rearrange("(c p) t -> p c t", p=128)
        for h in range(2):
            cs = slice(h * 8, (h + 1) * 8)
            self.P.dma(q, self.Hb[:, cs, col0:col0 + T], v[:, cs, :], writes=self.rHb[cs])

    def store_x(self, dst, xoff, T, outs, q="sp"):
        v = dst.rearrange("(c p) t -> p c t", p=128)
        for h in range(2):
            cs = slice(h * 8, (h + 1) * 8)
            outs.append(self.P.dma(q, v[:, cs, :], self.X[:, cs, xoff:xoff + T], reads=self.rX[cs]))

    def store_hb(self, dst, T, outs, q="sp", col0=0):
        v = dst.rearrange("(c p) t -> p c t", p=128)
        for h in range(2):
            cs = slice(h * 8, (h + 1) * 8)
            outs.append(self.P.dma(q, v[:, cs, :], self.Hb[:, cs, col0:col0 + T], reads=self.rHb[cs]))

    def load_mask(self, src, T, q="sp"):
        self.P.dma(q, self.mask[:, :T], src, writes=[self.rmask])


NCORE = 8
TP = 440
TC = 410
NB = 5
LCORE = 2048
CCORE = 64
CTP = CCORE + 2 * HALO
SEQ = 8192
CTX = 256
NMOD = 9


def new_nc():
    return bass.Bass("TRN2", target_bir_lowering=False)


ACOLS = NMOD * D // NCORE


def build_L0():
    nc = new_nc()
    cT = nc.dram_tensor("cT", [128, KC * 3], F32, kind="ExternalInput").ap()
    aw = nc.dram_tensor("aw", [4, D, ACOLS], F32, kind="ExternalInput").ap()
    ab = nc.dram_tensor("ab", [4, 3, ACOLS], F32, kind="ExternalInput").ap()
    mo = nc.dram_tensor("mo", [4, 3, ACOLS], F32, kind="ExternalOutput").ap()
    with contextlib.ExitStack() as st:
        P = Prog(nc, st)
        sb = lambda n, s, d: st.enter_context(nc.sbuf_tensor("s_" + n, s, d))
        cs = sb("cs", [128, KC, 3], F32); rcs = Res()
        wt = [sb("wt%d" % i, [128, KC, 512], F32) for i in range(2)]; rwt = [Res(), Res()]
        bt = sb("bt", [3, 4, ACOLS], F32); rbt = Res()
        ot = sb("ot", [3, 4, ACOLS], F32); rot = Res()
        ps = [st.enter_context(nc.psum_tensor("ps%d" % i, [128, 512], F32)) for i in range(2)]; rps = [Res(excl=True), Res(excl=True)]
        P.dma("sp", cs[:].rearrange("p a b -> p (a b)"), cT, writes=[rcs])
        P.dma("act", bt[:], ab.rearrange("l r c -> r l c"), writes=[rbt])
        P.emit("act", lambda e: e.activation(out=cs[:], in_=cs[:], func=AF.Silu), reads=[rcs], writes=[rcs])
        tiles = [(l, c0, min(512, ACOLS - c0)) for l in range(4) for c0 in range(0, ACOLS, 512)]
        for i, (l, c0, n) in enumerate(tiles):
            w, rw = wt[i % 2], rwt[i % 2]
            src = aw[l].rearrange("(kc p) f -> p kc f", p=128)[:, :, c0:c0 + n]
            P.dma("sp" if i % 2 == 0 else "act", w[:, :, :n], src, writes=[rw])
            p, rp = ps[i % 2], rps[i % 2]
            for k in range(KC):
                P.mm(p[0:3, :n], cs[:, k, :], w[:, k, :n], k == 0, k == KC - 1, reads=[rcs, rw], writes=[rp])
            P.emit("dve", lambda e, p=p, l=l, c0=c0, n=n: e.tensor_tensor(
                out=ot[:, l, c0:c0 + n], in0=p[0:3, :n], in1=bt[:, l, c0:c0 + n], op=ALU.add),
                reads=[rp, rbt], writes=[rot])
        tok = P.dma("sp", mo.rearrange("l r c -> r l c"), ot[:], reads=[rot])
        P.final_wait("sp", [tok])
        with nc.Block() as block:
            P.replay(block)
    return nc


def run_L0(inp):
    c, c_ctx, ada_w, ada_b = inp["c"], inp["c_ctx"], inp["ada_w"], inp["ada_b"]
    cv = np.stack([c[0], c[1], c_ctx], 1)
    cT = np.ascontiguousarray(cv.reshape(KC, 128, 3).transpose(1, 0, 2).reshape(128, KC * 3))
    maps = []
    for k in range(NCORE):
        cols = slice(k * ACOLS, (k + 1) * ACOLS)
        maps.append(dict(cT=cT, aw=np.ascontiguousarray(ada_w[:, :, cols]),
                         ab=np.ascontiguousarray(np.broadcast_to(ada_b[:, None, cols], (4, 3, ACOLS)))))
    res = run_bass_kernel_spmd(build_L0(), maps, core_ids=list(range(NCORE)))
    m = np.concatenate([r["mo"] for r in res.results], axis=2)
    return m.reshape(4, 3, NMOD, D)


def layer_pv_names(tag):
    return ["%s_m%d" % (tag, j) for j in range(NMOD)] + ["%s_%s" % (tag, n) for n in ("A1", "G1", "A2", "G2", "A3", "G3")]


class PVL:
    def __init__(self):
        self.off = {}
        self.n = 0

    def add(self, name, w=KC):
        self.off[name] = self.n
        self.n += w

    def __getitem__(self, k):
        return self.off[k]


def pv_layout(layers, conv_layers):
    L = PVL()
    for l in layers:
        for j in range(6):
            L.add("g%d_%d" % (l, j))
        for s in ("lat", "ctx"):
            for n in layer_pv_names("L%d%s" % (l, s)):
                L.add(n)
    for l in conv_layers:
        L.add("c%d_b_in" % l, 32)
        L.add("c%d_w_dw" % l, KC * CW)
        for n in ("b_dw", "ln_g", "ln_b", "b_out"):
            L.add("c%d_%s" % (l, n))
    return L


def pv_fill(L, inp, m, core, layers, conv_layers):
    b = core // 4
    pv = np.zeros((128, L.n), np.float32)

    def put(name, v):
        a = pvec(v)
        pv[:, L[name]:L[name] + a.shape[1]] = a
    for l in layers:
        for j in range(6):
            put("g%d_%d" % (l, j), inp["norm_g"][l, j])
        for j in range(NMOD):
            put("L%dlat_m%d" % (l, j), m[l, b, j])
            put("L%dctx_m%d" % (l, j), m[l, 2, j])
    for l in conv_layers:
        j = l // 3
        put("c%d_b_in" % l, inp["conv_b_in"][j])
        wd = inp["conv_w_dw"][j]
        a = wd.T.reshape(KC, 128, CW).transpose(1, 0, 2).reshape(128, KC * CW)
        pv[:, L["c%d_w_dw" % l]:L["c%d_w_dw" % l] + KC * CW] = a
        put("c%d_b_dw" % l, inp["conv_b_dw"][j])
        put("c%d_ln_g" % l, inp["conv_ln_g"][j])
        put("c%d_ln_b" % l, inp["conv_ln_b"][j])
        put("c%d_b_out" % l, inp["conv_b_out"][j])
    return pv


def derive_layer(T, L, l, s):
    t = "L%d%s" % (l, s)
    g = lambda j: L["g%d_%d" % (l, j)]
    T.derive(L[t + "_A1"], g(0), L[t + "_m1"], "a1pb")
    T.derive(L[t + "_G1"], L[t + "_m2"], g(1), "ab", 0.5)
    T.derive(L[t + "_A2"], g(2), L[t + "_m4"], "a1pb")
    T.derive(L[t + "_G2"], L[t + "_m5"], g(3), "ab", 1.0)
    T.derive(L[t + "_A3"], g(4), L[t + "_m7"], "a1pb")
    T.derive(L[t + "_G3"], L[t + "_m8"], g(5), "ab", 0.5)


def conv_offs(L, l):
    return {k: L["c%d_%s" % (l, k)] for k in ("b_in", "w_dw", "b_dw", "ln_g", "ln_b", "b_out")}


NBA, TCA = 6, 342
TPA = TCA + 2 * HALO


def _wdecl(dt, names_gu, names_d, extra, WDT):
    W = {}
    for n in names_gu:
        W[n] = dt(n, [D, DFF], WDT, "ExternalInput")
    for n in names_d:
        W[n] = dt(n, [DFF, D], WDT, "ExternalInput")
    for n, shp in extra:
        W[n] = dt(n, shp, WDT, "ExternalInput")
    return W


def build_LA(WDT=BF16):
    nc = new_nc()
    L = pv_layout([0, 1], [0])
    dt = lambda n, s, d, k: nc.dram_tensor(n, s, d, kind=k).ap()
    TM = TPA + CTP
    xin = dt("xin", [NBA, D, TPA], F32, "ExternalInput")
    msk = dt("msk", [NBA, 128, TM], F32, "ExternalInput")
    cin = dt("cin", [D, CTP], F32, "ExternalInput")
    pvd = dt("pv", [128, L.n], F32, "ExternalInput")
    idn = dt("idn", [128, 128], F32, "ExternalInput")
    W = _wdecl(dt, ("g00", "u00", "g01", "u01", "g10", "u10"), ("d00", "d01", "d10"),
               (("cwi", [D, 2 * D]), ("cwo", [D, D])), WDT)
    xo = dt("xo", [NBA, D, TCA], F32, "ExternalOutput")
    ho = dt("ho", [NBA, D, TCA], BF16, "ExternalOutput")
    xco = dt("xco", [D, CCORE], F32, "ExternalOutput")
    hco = dt("hco", [D, CCORE], BF16, "ExternalOutput")
    with contextlib.ExitStack() as st:
        P = Prog(nc, st)
        T = TL(nc, st, P, TM, pvd, L, conv=True, tcmax=TCA + CCORE, ident_dram=idn, nwb=5)
        for l in (0, 1):
            for s in ("lat", "ctx"):
                derive_layer(T, L, l, s)
        outs = []
        co = conv_offs(L, 0)
        for i in range(NBA):
            two = (i == 0)
            tp = TM if two else TPA
            tc = TCA + CCORE if two else TCA
            pre = lambda l: [(0, TPA, 0, "L%dlat" % l)] + ([(TPA, CTP, TPA, "L%dctx" % l)] if two else [])
            post = lambda l: [(HALO, TCA, 0, "L%dlat" % l)] + ([(TPA + HALO, CCORE, TCA, "L%dctx" % l)] if two else [])
            csegs = [(0, TCA, 0)] + ([(TPA, CCORE, TCA)] if two else [])
            T.load_x(xin[i], TPA)
            if two:
                T.load_x(cin, CTP, col0=TPA, q="act")
            T.load_mask(msk[i, :, :tp], tp)
            T.norm_mod(pre(0), "A1", "m0")
            T.ffn(tp, W["g00"], W["u00"], W["d00"])
            T.post_res(pre(0), "G1")
            T.norm_mod(pre(0), "A2", "m3")
            T.conv(tp, csegs, W["cwi"], W["cwo"], co)
            T.post_res(post(0), "G2")
            T.norm_mod(post(0), "A3", "m6")
            T.ffn(tc, W["g01"], W["u01"], W["d01"])
            T.post_res(post(0), "G3")
            T.norm_mod(post(1), "A1", "m0")
            T.ffn(tc, W["g10"], W["u10"], W["d10"])
            T.post_res(post(1), "G1")
            T.norm_mod(post(1), "A2", "m3")
            T.store_x(xo[i], HALO, TCA, outs)
            T.store_hb(ho[i], TCA, outs, q="act")
            if two:
                T.store_x(xco, TPA + HALO, CCORE, outs)
                T.store_hb(hco, CCORE, outs, q="act", col0=TCA)
        P.final_wait("sp", outs)
        with nc.Block() as block:
            P.replay(block)
    return nc, L


def build_LC(WDT=BF16):
    nc = new_nc()
    L = pv_layout([1, 2], [])
    dt = lambda n, s, d, k: nc.dram_tensor(n, s, d, kind=k).ap()
    TM = TC + CCORE
    xin = dt("xin", [NB, D, TC], F32, "ExternalInput")
    yin = dt("yin", [NB, D, TC], BF16, "ExternalInput")
    cin = dt("cin", [D, CCORE], F32, "ExternalInput")
    ycin = dt("ycin", [D, CCORE], BF16, "ExternalInput")
    pvd = dt("pv", [128, L.n], F32, "ExternalInput")
    W = _wdecl(dt, ("g11", "u11", "g20", "u20"), ("d11", "d20"), (("wo", [D, D]),), WDT)
    xo = dt("xo", [NB, D, TC], F32, "ExternalOutput")
    ho = dt("ho", [NB, D, TC], BF16, "ExternalOutput")
    hco = dt("hco", [D, CCORE], BF16, "ExternalOutput")
    with contextlib.ExitStack() as st:
        P = Prog(nc, st)
        T = TL(nc, st, P, TM, pvd, L, conv=False, nwb=8)
        for l in (1, 2):
            for s in ("lat", "ctx"):
                derive_layer(T, L, l, s)
        outs = []
        for i in range(NB):
            two = (i == 0)
            tc = TM if two else TC
            sg = lambda l: [(0, TC, 0, "L%dlat" % l)] + ([(TC, CCORE, TC, "L%dctx" % l)] if two else [])
            T.load_x(xin[i], TC)
            T.load_hb(yin[i], TC, q="act")
            if two:
                T.load_x(cin, CCORE, col0=TC)
                T.load_hb(ycin, CCORE, q="act", col0=TC)
            T.linear_dd(tc, W["wo"])
            T.post_res(sg(1), "G2")
            T.norm_mod(sg(1), "A3", "m6")
            T.ffn(tc, W["g11"], W["u11"], W["d11"])
            T.post_res(sg(1), "G3")
            T.norm_mod(sg(2), "A1", "m0")
            T.ffn(tc, W["g20"], W["u20"], W["d20"])
            T.post_res(sg(2), "G1")
            T.norm_mod(sg(2), "A2", "m3")
            T.store_x(xo[i], 0, TC, outs)
            T.store_hb(ho[i], TC, outs, q="act")
            if two:
                T.store_hb(hco, CCORE, outs, q="act", col0=TC)
        P.final_wait("sp", outs)
        with nc.Block() as block:
            P.replay(block)
    return nc, L


def build_LE(WDT=BF16):
    nc = new_nc()
    L = pv_layout([2, 3], [3])
    dt = lambda n, s, d, k: nc.dram_tensor(n, s, d, kind=k).ap()
    xin = dt("xin", [NB, D, TP], F32, "ExternalInput")
    yin = dt("yin", [NB, D, TP], BF16, "ExternalInput")
    msk = dt("msk", [NB, 128, TP], F32, "ExternalInput")
    pvd = dt("pv", [128, L.n], F32, "ExternalInput")
    idn = dt("idn", [128, 128], F32, "ExternalInput")
    W = _wdecl(dt, ("g21", "u21", "g30", "u30", "g31", "u31"), ("d21", "d30", "d31"),
               (("wo", [D, D]), ("cwi", [D, 2 * D]), ("cwo", [D, D])), WDT)
    xo = dt("xo", [NB, D, TC], F32, "ExternalOutput")
    with contextlib.ExitStack() as st:
        P = Prog(nc, st)
        T = TL(nc, st, P, TP, pvd, L, conv=True, tcmax=TC, ident_dram=idn, nwb=6)
        for l in (2, 3):
            derive_layer(T, L, l, "lat")
        outs = []
        co = conv_offs(L, 3)
        for i in range(NB):
            pre = lambda l: [(0, TP, 0, "L%dlat" % l)]
            post = lambda l: [(HALO, TC, 0, "L%dlat" % l)]
            T.load_x(xin[i], TP)
            T.load_hb(yin[i], TP, q="act")
            T.load_mask(msk[i], TP)
            T.linear_dd(TP, W["wo"])
            T.post_res(pre(2), "G2")
            T.norm_mod(pre(2), "A3", "m6")
            T.ffn(TP, W["g21"], W["u21"], W["d21"])
            T.post_res(pre(2), "G3")
            T.norm_mod(pre(3), "A1", "m0")
            T.ffn(TP, W["g30"], W["u30"], W["d30"])
            T.post_res(pre(3), "G1")
            T.norm_mod(pre(3), "A2", "m3")
            T.conv(TP, [(0, TC, 0)], W["cwi"], W["cwo"], co)
            T.post_res(post(3), "G2")
            T.norm_mod(post(3), "A3", "m6")
            T.ffn(TC, W["g31"], W["u31"], W["d31"])
            T.post_res(post(3), "G3")
            T.store_x(xo[i], HALO, TC, outs)
        P.final_wait("sp", outs)
        with nc.Block() as block:
            P.replay(block)
    return nc, L


def lat_blocks(xfull, core, halo, nb=NB, tc=TC):
    b, q = core // 4, core % 4
    out = np.zeros((nb, D, tc + 2 * halo), xfull.dtype)
    msk = np.zeros((nb, tc + 2 * halo), np.float32)
    for i in range(nb):
        s = q * LCORE + i * tc - halo
        e = s + tc + 2 * halo
        s0, e0 = max(s, 0), min(e, SEQ)
        out[i, :, s0 - s:e0 - s] = xfull[b, s0:e0].T
        msk[i, s0 - s:e0 - s] = 1.0
    return out, msk


def ctx_block(cfull, core, halo):
    b, q = core // 4, core % 4
    w = CCORE + 2 * halo
    out = np.zeros((D, w), cfull.dtype)
    msk = np.zeros((w,), np.float32)
    s = q * CCORE - halo
    e = s + w
    s0, e0 = max(s, 0), min(e, CTX)
    out[:, s0 - s:e0 - s] = cfull[b, s0:e0].T
    msk[s0 - s:e0 - s] = 1.0
    return out, msk


def gather_lat(blocks_per_core, dtype=None, nb=NB, tc=TC):
    d = blocks_per_core[0].dtype if dtype is None else dtype
    out = np.zeros((2, SEQ, D), d)
    for core, blk in enumerate(blocks_per_core):
        b, q = core // 4, core % 4
        for i in range(nb):
            s = q * LCORE + i * tc
            n = min(tc, (q + 1) * LCORE - s)
            out[b, s:s + n] = blk[i, :, :n].T
    return out


def gather_ctx(per_core):
    out = np.zeros((2, CTX, D), per_core[0].dtype)
    for core, a in enumerate(per_core):
        b, q = core // 4, core % 4
        out[b, q * CCORE:(q + 1) * CCORE] = a.T
    return out


CAST_CH = 8192


def cast_weight_list(inp):
    lst = []
    for l in range(4):
        for i in range(2):
            lst.append(("g%d%d" % (l, i), inp["ffn_w_gate"][l, i]))
            lst.append(("u%d%d" % (l, i), inp["ffn_w_up"][l, i]))
            lst.append(("d%d%d" % (l, i), inp["ffn_w_down"][l, i]))
    for j in range(2):
        lst.append(("cwi%d" % j, inp["conv_w_in"][j]))
        lst.append(("cwo%d" % j, inp["conv_w_out"][j]))
    lst.append(("gwo", inp["gla_w_o"][0]))
    lst.append(("nwo", inp["na_w_o"][0]))
    return lst


def build_LW(ncols):
    nc = new_nc()
    wi = nc.dram_tensor("wi", [128, ncols], F32, kind="ExternalInput").ap()
    wo = nc.dram_tensor("wo", [128, ncols], BF16, kind="ExternalOutput").ap()
    with contextlib.ExitStack() as st:
        P = Prog(nc, st)
        NBUF = 4
        bufs = [st.enter_context(nc.sbuf_tensor("s_cb%d" % i, [128, CAST_CH], BF16)) for i in range(NBUF)]
        rb = [Res() for _ in range(NBUF)]
        outs = []
        i = 0
        for c0 in range(0, ncols, CAST_CH):
            n = min(CAST_CH, ncols - c0)
            b, r = bufs[i % NBUF], rb[i % NBUF]
            P.dma("pool", b[:, :n], wi[:, c0:c0 + n], writes=[r])
            outs.append(P.dma("sp" if i % 2 == 0 else "act", wo[:, c0:c0 + n], b[:, :n], reads=[r]))
            i += 1
        P.final_wait("sp", outs)
        with nc.Block() as block:
            P.replay(block)
    return nc


def run_LW(inp):
    lst = cast_weight_list(inp)
    per = [a.size // (NCORE * 128) for _, a in lst]
    ncols = sum(per)
    maps = []
    for k in range(NCORE):
        parts = [a.reshape(NCORE, 128, -1)[k] for _, a in lst]
        maps.append(dict(wi=np.ascontiguousarray(np.concatenate(parts, axis=1))))
    res = run_bass_kernel_spmd(build_LW(ncols), maps, core_ids=list(range(NCORE))).results
    outs = [np.asarray(r["wo"]) for r in res]
    W = {}
    off = 0
    for (key, a), n in zip(lst, per):
        W[key] = np.stack([o[:, off:off + n] for o in outs]).reshape(a.shape)
        off += n
    return W


GH = 4
DK = 256
DV = 512
GT = 256
NCH = GT // 64
NLT = SEQ // GT
NTOK = CTX + SEQ
ROPE_THETA = 10000.0
PENG = "dve"


def gla_tables():
    inv = ROPE_THETA ** (-np.arange(0, 128, 2, dtype=np.float32) / 128)
    p = np.arange(128)
    sgn = np.where(p < 64, -1.0, 1.0).astype(np.float32)
    rows = np.arange(SEQ // 64, dtype=np.float32)
    cols = np.arange(64, dtype=np.float32)
    ang_r = rows[None, :] * inv[p % 64][:, None]
    ang_c = cols[None, :] * inv[p % 64][:, None]
    t = {}
    t["cr"] = np.cos(ang_r).astype(np.float32)
    t["sr"] = (np.sin(ang_r) * sgn[:, None]).astype(np.float32)
    t["cc"] = np.cos(ang_c).astype(np.float32)
    t["sc"] = (np.sin(ang_c) * sgn[:, None]).astype(np.float32)
    s = np.arange(128)[:, None]
    tt = np.arange(128)[None, :]
    same = (s // 64) == (tt // 64)
    g = -1.0 / 16.0
    t["tri"] = np.stack([np.where(same & (s <= tt), g, 0.0), np.where(same & (s > tt), g, 0.0),
                         np.where(same & (s >= tt), g, 0.0), np.where(same & (s < tt), g, 0.0)]).astype(np.float32)
    s = np.arange(64)[:, None]
    tt = np.arange(64)[None, :]
    t["msk"] = np.stack([(s <= tt), (s >= tt)]).astype(np.float32)
    t["ident"] = np.eye(128, dtype=np.float32)
    return t


def build_LB(nlt=NLT):
    nc = new_nc()
    SEQ = nlt * GT
    NTOK = CTX + SEQ
    dt = lambda n, s, d, k="ExternalInput": nc.dram_tensor(n, s, d, kind=k).ap()
    hT = dt("hT", [D, SEQ], BF16)
    hcT = dt("hcT", [D, CTX], BF16)
    wq = dt("wq", [D, DK], F32)
    wk = dt("wk", [D, DK], F32)
    wv = dt("wv", [D, DV], F32)
    wr = dt("wr", [D, DV], F32)
    wa1 = dt("wa1", [2, D, 16], F32)
    wa2 = dt("wa2", [2, 16, DK], F32)
    bab = dt("bab", [2, 128, DK], F32)
    brb = dt("brb", [128, DV], F32)
    ngb = dt("ngb", [64, DV], F32)
    tcr = dt("tcr", [128, 128], F32)
    tsr = dt("tsr", [128, 128], F32)
    tcc = dt("tcc", [128, 64], F32)
    tsc = dt("tsc", [128, 64], F32)
    ttri = dt("ttri", [4, 128, 128], F32)
    tmsk = dt("tmsk", [2, 64, 64], F32)
    tid = dt("tid", [128, 128], F32)
    oy = dt("oy", [NTOK, DV], BF16, "ExternalOutput")
    of_s = dt("of_s", [NTOK, DV], F32, "ExternalOutput")
    with contextlib.ExitStack() as st:
        P = Prog(nc, st)
        sb = lambda n, s, d: st.enter_context(nc.sbuf_tensor("s_" + n, s, d))
        psb = lambda n: st.enter_context(nc.psum_tensor(n, [128, 512], F32))

        def E(eng, fn, reads, writes, **kw):
            return P.emit(eng, lambda e: getattr(e, fn)(**kw), reads, writes)

        Wq = sb("Wq", [128, KC, DK], BF16); rWq = Res()
        Wk = sb("Wk", [128, KC, DK], BF16); rWk = Res()
        Wv = sb("Wv", [128, KC, DV], BF16); rWv = Res()
        Wr = sb("Wr", [128, KC, DV], BF16); rWr = Res()
        Wa1 = sb("Wa1", [128, 2, KC, 16], BF16); rWa1 = Res()
        Wa2 = sb("Wa2", [16, 2, DK], BF16); rWa2 = Res()
        P.dma("pool", Wq[:], wq.rearrange("(kc p) f -> p kc f", p=128), writes=[rWq])
        P.dma("pool", Wk[:], wk.rearrange("(kc p) f -> p kc f", p=128), writes=[rWk])
        P.dma("pool", Wv[:], wv.rearrange("(kc p) f -> p kc f", p=128), writes=[rWv])
        P.dma("pool", Wr[:], wr.rearrange("(kc p) f -> p kc f", p=128), writes=[rWr])
        for d in range(2):
            P.dma("pool", Wa1[:, d], wa1[d].rearrange("(kc p) f -> p kc f", p=128), writes=[rWa1])
        P.dma("pool", Wa2[:], wa2.rearrange("d r f -> r d f"), writes=[rWa2])
        rT = Res()
        BA = sb("BA", [128, 2, DK], F32)
        P.dma("sp", BA[:], bab.rearrange("d p f -> p d f"), writes=[rT])
        BR = sb("BR", [128, DV], F32); P.dma("sp", BR[:], brb, writes=[rT])
        NG = sb("NG", [64, DV], F32); P.dma("sp", NG[:], ngb, writes=[rT])
        CR = sb("CR", [128, 128], F32); P.dma("sp", CR[:], tcr, writes=[rT])
        SR = sb("SR", [128, 128], F32); P.dma("sp", SR[:], tsr, writes=[rT])
        CC = sb("CC", [128, 64], F32); P.dma("sp", CC[:], tcc, writes=[rT])
        SC = sb("SC", [128, 64], F32); P.dma("sp", SC[:], tsc, writes=[rT])
        TRI = sb("TRI", [128, 4, 128], F32); P.dma("sp", TRI[:], ttri.rearrange("a p f -> p a f"), writes=[rT])
        MSK = sb("MSK", [64, 2, 64], F32); P.dma("sp", MSK[:], tmsk.rearrange("a p f -> p a f"), writes=[rT])
        IDb = sb("IDb", [128, 128], BF16); P.dma("pool", IDb[:], tid, writes=[rT])

        H = [sb("H%d" % i, [128, KC, GT], BF16) for i in range(2)]; rH = [Res(), Res()]
        QS = sb("QS", [128, GT], F32); rQS = Res()
        QW = sb("QW", [128, GT], F32); rQW = Res()
        QRs = [sb("QR%d" % i, [128, 2, GT], F32) for i in range(2)]; rQRs = [[Res(), Res()] for _ in range(2)]
        KRs = [sb("KR%d" % i, [128, 2, GT], F32) for i in range(2)]; rKRs = [[Res(), Res()] for _ in range(2)]
        ZL = sb("ZL", [16, GT], BF16); rZL = Res()
        ZB = sb("ZB", [128, 2, DK], F32); rZB = Res()
        EQs = [sb("EQ%d" % i, [128, 2, GT], F32) for i in range(2)]; rEQs = [Res(), Res()]
        EKs = [sb("EK%d" % i, [128, 2, GT], F32) for i in range(2)]; rEKs = [Res(), Res()]
        ELs = [sb("EL%d" % i, [128, 2, GT], F32) for i in range(2)]; rELs = [Res(), Res()]
        QEs = [sb("QE%d" % i, [128, 2, GT], BF16) for i in range(2)]; rQEs = [Res(), Res()]
        KEs = [sb("KE%d" % i, [128, 2, GT], BF16) for i in range(2)]; rKEs = [Res(), Res()]
        KLTs = [sb("KLT%d" % i, [128, 2, GT], BF16) for i in range(2)]; rKLTs = [Res(), Res()]
        S = sb("S", [128, 2, DV], F32); rS = [Res(), Res()]
        Sb = sb("Sb", [128, 2, DV], BF16); rSb = [Res(), Res()]
        VT = [sb("VT%d" % i, [64, DV], BF16) for i in range(4)]; rVT = [Res() for _ in range(4)]
        KL = [sb("KL%d" % i, [64, DK], BF16) for i in range(2)]; rKL = [Res(), Res()]
        SCT = [sb("SCT%d" % i, [64, 64], BF16) for i in range(2)]; rSCT = [Res(), Res()]
        OS = [sb("OS%d" % i, [64, DV], F32) for i in range(2)]; rOS = [Res(), Res()]
        OF = [sb("OF%d" % i, [64, DV], F32) for i in range(2)]; rOF = [Res(), Res()]
        SQ = sb("SQ", [64, DV], F32); rSQ = Res()
        RT = [sb("RT%d" % i, [64, DV], F32) for i in range(4)]; rRT = [Res() for _ in range(4)]
        SS = sb("SS", [64, 1], F32); rSS = Res()
        OUT = [sb("OUT%d" % i, [64, DV], BF16) for i in range(2)]; rOUT = [Res(), Res()]
        pP = [psb("pP0"), psb("pP1")]; rpP = [Res(excl=True), Res(excl=True)]
        pZ = psb("pZ"); rpZ = Res(excl=True)
        pC = psb("pC"); rpC = Res(excl=True)
        pV = psb("pV"); rpV = Res(excl=True)
        pK = psb("pK"); rpK = Res(excl=True)
        pO = psb("pO"); rpO = Res(excl=True)
        pS = psb("pS"); rpS = Res(excl=True)

        for dc in range(2):
            E("dve", "memset", [], [rS[dc]], ap=S[:, dc, :], constant=0.0)
            E("dve", "memset", [], [rSb[dc]], ap=Sb[:, dc, :], constant=0.0)

        outs = []
        cnt = {"h": 0, "p": 0, "c": 0}

        def rope_evac(ps, rps, dst, rdst, row0, is_q, lat):
            sc = (1.0 / 16.0) if is_q else 1.0
            if not lat:
                E("act", "activation", [rps], [rdst], out=dst, in_=ps, func=AF.Identity, scale=sc)
                return
            E("act", "activation", [rps], [rQS], out=QS[:], in_=ps, func=AF.Identity, scale=sc)
            E(PENG, "tensor_copy", [rQS], [rQW], out=QW[0:64, :], in_=QS[64:128, :])
            E(PENG, "tensor_copy", [rQS], [rQW], out=QW[64:128, :], in_=QS[0:64, :])
            return

        def make_tile(slot, d, src, t0, lat, grow0, ch_order, tok0, final):
            Ht, rHt = H[slot], rH[slot]
            QR, rQR, KR, rKR = QRs[slot], rQRs[slot], KRs[slot], rKRs[slot]
            EQ, rEQ, EK, rEK, EL, rEL = EQs[slot], rEQs[slot], EKs[slot], rEKs[slot], ELs[slot], rELs[slot]
            QE, rQE, KE, rKE, KLT, rKLT = QEs[slot], rQEs[slot], KEs[slot], rKEs[slot], KLTs[slot], rKLTs[slot]

            def pA():
                v = src.rearrange("(c p) t -> p c t", p=128)
                for hh in range(2):
                    cs = slice(hh * 8, (hh + 1) * 8)
                    P.dma("sp" if hh == 0 else "act", Ht[:, cs, :], v[:, cs, t0:t0 + GT], writes=[rHt])
                for (Wm, rW, dst, rdst, is_q) in ((Wq, rWq, QR, rQR, True), (Wk, rWk, KR, rKR, False)):
                    for dc in range(2):
                        pi = cnt["p"] % 2
                        cnt["p"] += 1
                        pp, rpp = pP[pi], rpP[pi]
                        for k in range(KC):
                            P.mm(pp[:, :GT], Wm[:, k, dc * 128:(dc + 1) * 128], Ht[:, k, :], k == 0, k == KC - 1,
                                 reads=[rW, rHt], writes=[rpp])
                        sc = (1.0 / 16.0) if is_q else 1.0
                        if not lat:
                            E("act", "activation", [rpp], [rdst[dc]], out=dst[:, dc, :], in_=pp[:, :GT], func=AF.Identity, scale=sc)
                        else:
                            E("act", "activation", [rpp], [rQS], out=QS[:], in_=pp[:, :GT], func=AF.Identity, scale=sc)
                            E(PENG, "tensor_copy", [rQS], [rQW], out=QW[0:64, :], in_=QS[64:128, :])
                            E(PENG, "tensor_copy", [rQS], [rQW], out=QW[64:128, :], in_=QS[0:64, :])
                            q3 = lambda a: a.rearrange("p (r c) -> p r c", c=64)
                            if dc == 0:
                                ct = CR[:, grow0:grow0 + NCH].unsqueeze(2).to_broadcast([128, NCH, 64])
                                stb = SR[:, grow0:grow0 + NCH].unsqueeze(2).to_broadcast([128, NCH, 64])
                            else:
                                ct = CC[:].unsqueeze(1).to_broadcast([128, NCH, 64])
                                stb = SC[:].unsqueeze(1).to_broadcast([128, NCH, 64])
                            E("dve", "tensor_tensor", [rQS, rT], [rdst[dc]], out=q3(dst[:, dc, :]), in0=q3(QS[:]), in1=ct, op=ALU.mult)
                            E(PENG, "tensor_tensor", [rQW, rT], [rQW], out=q3(QW[:]), in0=q3(QW[:]), in1=stb, op=ALU.mult)
                            E("dve", "tensor_tensor", [rQW, rdst[dc]], [rdst[dc]], out=dst[:, dc, :], in0=dst[:, dc, :], in1=QW[:], op=ALU.add)

            def pB():
                pi = cnt["p"] % 2
                cnt["p"] += 1
                pp, rpp = pP[pi], rpP[pi]
                for k in range(KC):
                    P.mm(pp[0:16, :GT], Wa1[:, d, k, :], Ht[:, k, :], k == 0, k == KC - 1, reads=[rWa1, rHt], writes=[rpp])
                E("act", "activation", [rpp], [rZL], out=ZL[:], in_=pp[0:16, :GT], func=AF.Copy)

            def pC_():
                for pr in range(2):
                    P.mm(pZ[:, pr * DK:(pr + 1) * DK], ZL[:, pr * 128:(pr + 1) * 128], Wa2[:, d, :], True, True,
                         reads=[rZL, rWa2], writes=[rpZ])
                z3 = pZ[:].rearrange("p (a f) -> p a f", a=2)
                E("dve", "tensor_tensor", [rpZ, rT], [rZB], out=ZB[:], in0=z3,
                  in1=BA[:, d, :].unsqueeze(1).to_broadcast([128, 2, DK]), op=ALU.add)
                E("act", "activation", [rZB], [rZB], out=ZB[:], in_=ZB[:], func=AF.Exp, scale=-1.0)
                E("act", "activation", [rZB], [rZB], out=ZB[:], in_=ZB[:], func=AF.Ln, bias=1.0)

            def pD(which):
                tri = TRI[:, 2 * d + which, :]
                for dc in range(2):
                    for pr in range(2):
                        P.mm(pC[:, dc * GT + pr * 128:dc * GT + (pr + 1) * 128], ZB[:, pr, dc * 128:(dc + 1) * 128], tri,
                             True, True, reads=[rZB, rT], writes=[rpC])
                pc3 = pC[:].rearrange("p (a f) -> p a f", a=2)
                if which == 0:
                    E("act", "activation", [rpC], [rEQ], out=EQ[:], in_=pc3, func=AF.Exp)
                    E("act", "activation", [rpC], [rEK], out=EK[:], in_=pc3, func=AF.Exp, scale=-1.0)
                    E("dve", "tensor_tensor", rQR + [rEQ], [rQE], out=QE[:], in0=QR[:], in1=EQ[:], op=ALU.mult)
                    E("dve", "tensor_tensor", rKR + [rEK], [rKE], out=KE[:], in0=KR[:], in1=EK[:], op=ALU.mult)
                else:
                    E("act", "activation", [rpC], [rEL], out=EL[:], in_=pc3, func=AF.Exp)
                    E("dve", "tensor_tensor", rKR + [rEL], [rKLT], out=KLT[:], in0=KR[:], in1=EL[:], op=ALU.mult)

            pieces = [pA, pB, pC_, lambda: pD(0), lambda: pD(1)]

            def stage_v(g):
                gs = slice(g * 128, (g + 1) * 128)
                for k in range(KC):
                    P.mm(pV[:, :], Ht[:, k, gs], Wv[:, k, :], k == 0, k == KC - 1, reads=[rHt, rWv], writes=[rpV])
                E("act", "activation", [rpV], [rVT[2 * g]], out=VT[2 * g][:], in_=pV[0:64, :], func=AF.Copy)
                E("dve", "tensor_copy", [rpV], [rVT[2 * g + 1]], out=VT[2 * g + 1][:], in_=pV[64:128, :])
                if final:
                    for k in range(KC):
                        P.mm(pZ[:, :], Ht[:, k, gs], Wr[:, k, :], k == 0, k == KC - 1, reads=[rHt, rWr], writes=[rpZ])
                    for hh_ in range(2):
                        c_ = 2 * g + hh_
                        ps_ = slice(hh_ * 64, (hh_ + 1) * 64)
                        E("dve", "tensor_tensor", [rpZ, rT], [rRT[c_]], out=RT[c_][:], in0=pZ[ps_, :], in1=BR[ps_, :], op=ALU.add)
                        E("act", "activation", [rRT[c_]], [rRT[c_]], out=RT[c_][:], in_=RT[c_][:], func=AF.Silu)

            def stage_a(ch, ci):
                cs = slice(ch * 64, (ch + 1) * 64)
                for dc in range(2):
                    P.mm(pK[0:64, dc * 128:(dc + 1) * 128], KLT[:, dc, cs], IDb[:], True, True, reads=[rKLT, rT], writes=[rpK])
                for dc in range(2):
                    P.mm(pK[0:64, 256:320], KE[:, dc, cs], QE[:, dc, cs], dc == 0, dc == 1, reads=[rKE, rQE], writes=[rpK])
                E("act", "activation", [rpK], [rKL[ci]], out=KL[ci][:], in_=pK[0:64, 0:256], func=AF.Copy)
                E("dve", "tensor_tensor", [rpK, rT], [rSCT[ci]], out=SCT[ci][:], in0=pK[0:64, 256:320], in1=MSK[:, d, :], op=ALU.mult)

            def stage_b(ch, ci):
                cs = slice(ch * 64, (ch + 1) * 64)
                grow = tok0 + ch * 64
                P.mm(pO[0:64, :], SCT[ci][:], VT[ch][:], True, False, reads=[rSCT[ci], rVT[ch]], writes=[rpO])
                for dc in range(2):
                    P.mm(pO[0:64, :], QE[:, dc, cs], Sb[:, dc, :], False, dc == 1, reads=[rQE, rSb[dc]], writes=[rpO])
                lcol = ch * 64 + (63 if d == 0 else 0)
                for dc in range(2):
                    P.mm(pS[:, :], KL[ci][:, dc * 128:(dc + 1) * 128], VT[ch][:], True, True, reads=[rKL[ci], rVT[ch]], writes=[rpS])
                    E("dve", "scalar_tensor_tensor", [rpS, rS[dc], rEQ], [rS[dc]], out=S[:, dc, :], in0=S[:, dc, :],
                      scalar=EQ[:, dc, lcol:lcol + 1], in1=pS[:, :], op0=ALU.mult, op1=ALU.add)
                    E("act", "activation", [rS[dc]], [rSb[dc]], out=Sb[:, dc, :], in_=S[:, dc, :], func=AF.Copy)
                if not final:
                    E("act", "activation", [rpO], [rOS[ci]], out=OS[ci][:], in_=pO[0:64, :], func=AF.Copy)
                    P.dma("sp", of_s[grow:grow + 64, :], OS[ci][:], reads=[rOS[ci]], writes=[rOFS[grow // 64]])
                else:
                    P.dma("sp", OF[ci][:], of_s[grow:grow + 64, :], reads=[rOFS[grow // 64]], writes=[rOF[ci]])
                    E("dve", "tensor_tensor", [rpO, rOF[ci]], [rOS[ci]], out=OS[ci][:], in0=pO[0:64, :], in1=OF[ci][:], op=ALU.add)
                    E("dve", "tensor_tensor", [rOS[ci]], [rSQ], out=SQ[:], in0=OS[ci][:], in1=OS[ci][:], op=ALU.mult)
                    E("dve", "reduce_sum", [rSQ], [rSS], out=SS[:], in_=SQ[:], axis=AX.X)
                    E("dve", "tensor_scalar", [rSS], [rSS], out=SS[:], in0=SS[:], scalar1=1.0 / DV, scalar2=EPS, op0=ALU.mult, op1=ALU.add)
                    E("act", "activation", [rSS], [rSS], out=SS[:], in_=SS[:], func=AF.Sqrt)
                    E("dve", "reciprocal", [rSS], [rSS], out=SS[:], in_=SS[:])
                    E("dve", "scalar_tensor_tensor", [rOS[ci], rSS, rT], [rOS[ci]], out=OS[ci][:], in0=OS[ci][:], scalar=SS[:, 0:1],
                      in1=NG[:], op0=ALU.mult, op1=ALU.mult)
                    E("dve", "tensor_tensor", [rOS[ci], rRT[ch]], [rOUT[ci]], out=OUT[ci][:], in0=OS[ci][:], in1=RT[ch][:], op=ALU.mult)
                    outs.append(P.dma("sp", oy[grow:grow + 64, :], OUT[ci][:], reads=[rOUT[ci]]))

            done_v = set()

            def ensure_v(ch):
                if ch // 2 not in done_v:
                    done_v.add(ch // 2)
                    stage_v(ch // 2)

            def step0():
                ensure_v(ch_order[0])
                stage_a(ch_order[0], 0)

            def mk_step(i_, ch):
                def f():
                    if i_ + 1 < len(ch_order):
                        ensure_v(ch_order[i_ + 1])
                        stage_a(ch_order[i_ + 1], (i_ + 1) % 2)
                    stage_b(ch, i_ % 2)
                return f
            steps = [step0] + [mk_step(i_, ch) for i_, ch in enumerate(ch_order)]
            return pieces, steps

        def run_pass(d, final):
            order = list(range(NCH)) if d == 0 else list(range(NCH))[::-1]
            tl_ = [(hcT, 0, False, 0, 0)] + [(hT, ti * GT, True, ti * NCH, CTX + ti * GT)
                                             for ti in (range(nlt) if d == 0 else range(nlt)[::-1])]
            mk = lambda i: make_tile(i % 2, d, tl_[i][0], tl_[i][1], tl_[i][2], tl_[i][3], order, tl_[i][4], final)
            cur = mk(0)
            for p_ in cur[0]:
                p_()
            for i in range(len(tl_)):
                nxt = mk(i + 1) if i + 1 < len(tl_) else ([], [])
                pend = list(nxt[0])
                for st_ in cur[1]:
                    st_()
                    if pend:
                        pend.pop(0)()
                for p_ in pend:
                    p_()
                cur = nxt

        rOFS = [Res() for _ in range(NTOK // 64)]
        run_pass(0, False)
        for dc in range(2):
            E("dve", "memset", [], [rS[dc]], ap=S[:, dc, :], constant=0.0)
            E("dve", "memset", [], [rSb[dc]], ap=Sb[:, dc, :], constant=0.0)
        run_pass(1, True)
        P.final_wait("sp", outs)
        with nc.Block() as block:
            P.replay(block)
    return nc


def lb_inputs(inp, h1, hc1, core, tabs):
    import ml_dtypes
    b, hd = core // 4, core % 4
    ks = slice(hd * DK, (hd + 1) * DK)
    vs = slice(hd * DV, (hd + 1) * DV)
    d = dict(
        hT=np.ascontiguousarray(h1[b].T).astype(ml_dtypes.bfloat16),
        hcT=np.ascontiguousarray(hc1[b].T).astype(ml_dtypes.bfloat16),
        wq=np.ascontiguousarray(inp["gla_w_q"][0][:, ks]), wk=np.ascontiguousarray(inp["gla_w_k"][0][:, ks]),
        wv=np.ascontiguousarray(inp["gla_w_v"][0][:, vs]), wr=np.ascontiguousarray(inp["gla_w_r"][0][:, vs]),
        wa1=np.ascontiguousarray(inp["gla_w_a1"][0]), wa2=np.ascontiguousarray(inp["gla_w_a2"][0][:, :, ks]),
        bab=np.ascontiguousarray(np.broadcast_to(inp["gla_b_a"][0][:, None, ks], (2, 128, DK))),
        brb=np.ascontiguousarray(np.broadcast_to(inp["gla_b_r"][0][None, vs], (128, DV))),
        ngb=np.ascontiguousarray(np.broadcast_to(inp["gla_norm_g"][0][None, :], (64, DV))),
        tcr=tabs["cr"], tsr=tabs["sr"], tcc=tabs["cc"], tsc=tabs["sc"], ttri=tabs["tri"], tmsk=tabs["msk"], tid=tabs["ident"])
    return d


NH = 16
DH = 128
HPC = 4
ROWS = 128
GW_ = 64
NT = 512
NEG = -30000.0
NTOKA = SEQ + CTX
NVT = NTOKA // 128


def na_bias_tables(rpb):
    col = np.arange(64)
    cs = np.clip(col - 8, 0, 48)
    out = np.full((NH, 8, 64, 8, 64), NEG, np.float32)
    for v in range(8):
        delta = v - 7
        for j in range(8):
            ri = delta + j + 7
            for q in range(64):
                kc = cs[q] + np.arange(16)
                out[:, v, q, j, kc] = rpb[:, ri, kc - q + 15]
    return out.reshape(NH, 8, 64, 512)


def build_LD():
    nc = new_nc()
    dt = lambda n, s, d, k="ExternalInput": nc.dram_tensor(n, s, d, kind=k).ap()
    hT = dt("hT", [D, NTOKA], BF16)
    wq = dt("wq", [HPC, D, DH], F32)
    wk = dt("wk", [HPC, D, DH], F32)
    wv = dt("wv", [HPC, D, DH], F32)
    btab = dt("btab", [HPC, 8, 64, 512], F32)
    tid = dt("tid", [64, 64], F32)
    oy = dt("oy", [SEQ, HPC * DH], BF16, "ExternalOutput")
    with contextlib.ExitStack() as st:
        P = Prog(nc, st)
        sb = lambda n, s, d: st.enter_context(nc.sbuf_tensor("s_" + n, s, d))
        psb = lambda n: st.enter_context(nc.psum_tensor(n, [128, 512], F32))

        def E(eng, fn, reads, writes, **kw):
            return P.emit(eng, lambda e: getattr(e, fn)(**kw), reads, writes)

        W = [sb("W%d" % i, [128, 3, KC, DH], BF16) for i in range(2)]; rW = [Res(), Res()]
        BT = sb("BT", [64, 8, 512], F32); rBT = Res()
        ID = sb("ID", [64, 64], BF16); rID = Res()
        P.dma("pool", ID[:], tid, writes=[rID])
        H = [sb("H%d" % i, [128, KC, NT], BF16) for i in range(2)]; rH = [Res(), Res()]
        QT = sb("QT", [128, SEQ], BF16); rQT = Res()
        KT = sb("KT", [128, NTOKA], BF16); rKT = Res()
        VE = sb("VE", [128, NVT, DH + 1], BF16); rVE = Res()
        VO = sb("VO", [128, NVT, DH + 1], BF16); rVO = Res()
        SL = [sb("SL%d" % i, [64, 768], F32) for i in range(2)]; rSL = [Res(), Res()]
        MX = [sb("MX%d" % i, [64, 2], F32) for i in range(2)]; rMX = [Res(), Res()]
        PB = [sb("PB%d" % i, [64, 768], BF16) for i in range(2)]; rPB = [Res(), Res()]
        PT = [sb("PT%d" % i, [128, 384], BF16) for i in range(2)]; rPT = [Res(), Res()]
        RS = [sb("RS%d" % i, [64, 1], F32) for i in range(2)]; rRS = [Res(), Res()]
        OB = [sb("OB%d" % i, [64, DH], BF16) for i in range(2)]; rOB = [Res(), Res()]
        pP = [psb("pP0"), psb("pP1")]; rpP = [Res(excl=True), Res(excl=True)]
        pA = [psb("pA0"), psb("pA1")]; rpA = [Res(excl=True), Res(excl=True)]
        pB = psb("pB"); rpB = Res(excl=True)
        pT = [psb("pT0"), psb("pT1")]; rpT = [Res(excl=True), Res(excl=True)]
        pO = psb("pO"); rpO = Res(excl=True)

        E("dve", "memset", [], [rVE], ap=VE[:, :, DH:DH + 1], constant=1.0)
        outs = []
        hv = hT.rearrange("(c p) t -> p c t", p=128)
        cnt = {"h": 0, "p": 0, "u": 0}
        scale = DH ** -0.5
        for hh in range(HPC):
            Wt, rWt = W[hh % 2], rW[hh % 2]
            for i, wsrc in enumerate((wq, wk, wv)):
                P.dma("pool", Wt[:, i], wsrc[hh].rearrange("(kc p) f -> p kc f", p=128), writes=[rWt])
            P.dma("sp", BT[:], btab[hh].rearrange("v q k -> q v k"), writes=[rBT])
            for ti in range(NTOKA // NT + (1 if NTOKA % NT else 0)):
                t0 = ti * NT
                n = min(NT, NTOKA - t0)
                hi = cnt["h"] % 2
                cnt["h"] += 1
                Ht, rHt = H[hi], rH[hi]
                for half in range(2):
                    cs = slice(half * 8, (half + 1) * 8)
                    P.dma("sp" if half == 0 else "act", Ht[:, cs, :n], hv[:, cs, t0:t0 + n], writes=[rHt])
                lat = t0 < SEQ
                for which in ((0, 1) if lat else (1,)):
                    pi = cnt["p"] % 2
                    cnt["p"] += 1
                    pp, rpp = pP[pi], rpP[pi]
                    for k in range(KC):
                        P.mm(pp[:, :n], Wt[:, which, k, :], Ht[:, k, :n], k == 0, k == KC - 1, reads=[rWt, rHt], writes=[rpp])
                    if which == 0:
                        E("act", "activation", [rpp], [rQT], out=QT[:, t0:t0 + n], in_=pp[:, :n], func=AF.Identity, scale=scale)
                    else:
                        E("dve", "tensor_copy", [rpp], [rKT], out=KT[:, t0:t0 + n], in_=pp[:, :n])
                for s in range(n // 128):
                    pi = cnt["p"] % 2
                    cnt["p"] += 1
                    pp, rpp = pP[pi], rpP[pi]
                    for k in range(KC):
                        P.mm(pp[:, :DH], Ht[:, k, s * 128:(s + 1) * 128], Wt[:, 2, k, :], k == 0, k == KC - 1,
                             reads=[rWt, rHt], writes=[rpp])
                    E("act" if s % 2 else "dve", "activation" if s % 2 else "tensor_copy", [rpp], [rVE],
                      **(dict(out=VE[:, t0 // 128 + s, 0:DH], in_=pp[:, :DH], func=AF.Copy) if s % 2 else
                         dict(out=VE[:, t0 // 128 + s, 0:DH], in_=pp[:, :DH])))
            nlat = SEQ // 128
            P.dma("sp", VO[0:64, 0:nlat, :], VE[64:128, 0:nlat, :], reads=[rVE], writes=[rVO])
            P.dma("act", VO[64:128, 0:nlat - 1, :], VE[0:64, 1:nlat, :], reads=[rVE], writes=[rVO])
            def s1(r):
                u = r % 2
                rs = min(max(r - 4, 0), ROWS - 8)
                var = rs - r + 7
                qs = QT[:, r * 64:(r + 1) * 64]
                P.mm(pA[u][0:64, :], qs, KT[:, rs * 64:rs * 64 + 512], True, True, reads=[rQT, rKT], writes=[rpA[u]])
                P.mm(pB[0:64, 0:CTX], qs, KT[:, SEQ:SEQ + CTX], True, True, reads=[rQT, rKT], writes=[rpB])
                E("dve", "tensor_tensor", [rpA[u], rBT], [rSL[u]], out=SL[u][:, 0:512], in0=pA[u][0:64, :], in1=BT[:, var, :], op=ALU.add)
                E("act", "activation", [rpB], [rSL[u]], out=SL[u][:, 512:768], in_=pB[0:64, 0:CTX], func=AF.Copy)
                E("dve", "reduce_max", [rSL[u]], [rMX[u]], out=MX[u][:, 0:1], in_=SL[u][:], axis=AX.X)
                E("dve", "tensor_single_scalar", [rMX[u]], [rMX[u]], out=MX[u][:, 1:2], in_=MX[u][:, 0:1], scalar=-1.0, op=ALU.mult)
                E("act", "activation", [rSL[u], rMX[u]], [rPB[u]], out=PB[u][:], in_=SL[u][:], func=AF.Exp, bias=MX[u][:, 1:2])

            def s2(r):
                u = r % 2
                for kc in range(6):
                    P.mm(pT[u][:, kc * 64:(kc + 1) * 64], PB[u][:, kc * 128:(kc + 1) * 128], ID[:], True, True,
                         reads=[rPB[u], rID], writes=[rpT[u]])
                E("act" if r % 4 < 2 else "dve", "activation" if r % 4 < 2 else "tensor_copy", [rpT[u]], [rPT[u]],
                  **(dict(out=PT[u][:], in_=pT[u][:, 0:384], func=AF.Copy) if r % 4 < 2 else dict(out=PT[u][:], in_=pT[u][:, 0:384])))

            def s3(r):
                u = r % 2
                rs = min(max(r - 4, 0), ROWS - 8)
                for kc in range(6):
                    if kc < 4:
                        if rs % 2 == 0:
                            vt, rv = VE[:, rs // 2 + kc, :], rVE
                        else:
                            vt, rv = VO[:, (rs - 1) // 2 + kc, :], rVO
                    else:
                        vt, rv = VE[:, SEQ // 128 + (kc - 4), :], rVE
                    P.mm(pO[0:64, 0:DH + 1], PT[u][:, kc * 64:(kc + 1) * 64], vt, kc == 0, kc == 5, reads=[rPT[u], rv], writes=[rpO])
                E("dve", "reciprocal", [rpO], [rRS[u]], out=RS[u][:], in_=pO[0:64, DH:DH + 1])
                E("act", "activation", [rpO, rRS[u]], [rOB[u]], out=OB[u][:], in_=pO[0:64, 0:DH], func=AF.Identity, scale=RS[u][:, 0:1])
                outs.append(P.dma("sp", oy[r * 64:(r + 1) * 64, hh * DH:(hh + 1) * DH], OB[u][:], reads=[rOB[u]]))

            for it in range(ROWS + 2):
                if it < ROWS:
                    s1(it)
                if 0 <= it - 1 < ROWS:
                    s2(it - 1)
                if 0 <= it - 2 < ROWS:
                    s3(it - 2)
        P.final_wait("sp", outs)
        with nc.Block() as block:
            P.replay(block)
    return nc


def ld_inputs(inp, h2, hc2, core, btab_all):
    import ml_dtypes
    b, hg = core // 4, core % 4
    hcat = np.concatenate([h2[b], hc2[b]], 0)
    wqkv = inp["na_w_qkv"][0]
    sl = lambda base: np.ascontiguousarray(
        np.stack([wqkv[:, base + (hg * HPC + i) * DH: base + (hg * HPC + i + 1) * DH] for i in range(HPC)]))
    return dict(hT=np.ascontiguousarray(hcat.T).astype(ml_dtypes.bfloat16),
                wq=sl(0), wk=sl(D), wv=sl(2 * D),
                btab=np.ascontiguousarray(btab_all[hg * HPC:(hg + 1) * HPC]),
                tid=np.eye(64, dtype=np.float32))


_CACHE = {}


def _run(nc, maps):
    return run_bass_kernel_spmd(nc, maps, core_ids=list(range(NCORE))).results


def kernel(**inputs):
    import ml_dtypes
    inp = {k: np.asarray(v) for k, v in inputs.items()}
    cores = range(NCORE)
    WB = run_LW(inp)
    fw_ = lambda l, i: (WB["g%d%d" % (l, i)], WB["u%d%d" % (l, i)], WB["d%d%d" % (l, i)])
    m = run_L0(inp)
    nc, L = build_LA()
    eye = np.eye(128, dtype=np.float32)
    maps = []
    for core in cores:
        xb, mk = lat_blocks(inp["x"], core, HALO, NBA, TCA)
        cb, cmk = ctx_block(inp["ctx"], core, HALO)
        msk = np.zeros((NBA, 128, TPA + CTP), np.float32)
        msk[:, :, :TPA] = mk[:, None, :]
        msk[0, :, TPA:] = cmk[None, :]
        g00, u00, d00 = fw_(0, 0)
        g01, u01, d01 = fw_(0, 1)
        g10, u10, d10 = fw_(1, 0)
        maps.append(dict(xin=xb, msk=msk, cin=cb, pv=pv_fill(L, inp, m, core, [0, 1], [0]), idn=eye,
                         g00=g00, u00=u00, d00=d00, g01=g01, u01=u01, d01=d01, g10=g10, u10=u10, d10=d10,
                         cwi=WB["cwi0"], cwo=WB["cwo0"]))
    res = _run(nc, maps)
    x1 = gather_lat([r["xo"] for r in res], nb=NBA, tc=TCA)
    h1 = gather_lat([r["ho"] for r in res], nb=NBA, tc=TCA)
    xc1 = gather_ctx([r["xco"] for r in res])
    hc1 = gather_ctx([r["hco"] for r in res])
    del maps, res
    tabs = gla_tables()
    res = _run(build_LB(), [lb_inputs(inp, h1, hc1, core, tabs) for core in cores])
    y1 = np.zeros((2, SEQ, D), ml_dtypes.bfloat16)
    yc1 = np.zeros((2, CTX, D), ml_dtypes.bfloat16)
    for core in cores:
        b, hd = core // 4, core % 4
        o = np.asarray(res[core]["oy"])
        yc1[b, :, hd * DV:(hd + 1) * DV] = o[:CTX]
        y1[b, :, hd * DV:(hd + 1) * DV] = o[CTX:]
    del res
    nc, L = build_LC()
    maps = []
    for core in cores:
        g11, u11, d11 = fw_(1, 1)
        g20, u20, d20 = fw_(2, 0)
        maps.append(dict(xin=lat_blocks(x1, core, 0)[0], yin=lat_blocks(y1, core, 0)[0],
                         cin=ctx_block(xc1, core, 0)[0], ycin=ctx_block(yc1, core, 0)[0],
                         pv=pv_fill(L, inp, m, core, [1, 2], []),
                         g11=g11, u11=u11, d11=d11, g20=g20, u20=u20, d20=d20, wo=WB["gwo"]))
    res = _run(nc, maps)
    x2 = gather_lat([r["xo"] for r in res])
    h2 = gather_lat([r["ho"] for r in res])
    hc2 = gather_ctx([r["hco"] for r in res])
    del maps, res
    btab = na_bias_tables(inp["na_rpb"][0])
    res = _run(build_LD(), [ld_inputs(inp, h2, hc2, core, btab) for core in cores])
    y2 = np.zeros((2, SEQ, D), ml_dtypes.bfloat16)
    for core in cores:
        b, hg = core // 4, core % 4
        y2[b, :, hg * HPC * DH:(hg + 1) * HPC * DH] = np.asarray(res[core]["oy"])
    del res
    nc, L = build_LE()
    maps = []
    for core in cores:
        xb, mk = lat_blocks(x2, core, HALO)
        g21, u21, d21 = fw_(2, 1)
        g30, u30, d30 = fw_(3, 0)
        g31, u31, d31 = fw_(3, 1)
        maps.append(dict(xin=xb, yin=lat_blocks(y2, core, HALO)[0], idn=eye,
                         msk=np.ascontiguousarray(np.broadcast_to(mk[:, None, :], (NB, 128, TP))),
                         pv=pv_fill(L, inp, m, core, [2, 3], [3]),
                         g21=g21, u21=u21, d21=d21, g30=g30, u30=u30, d30=d30, g31=g31, u31=u31, d31=d31,
                         wo=WB["nwo"], cwi=WB["cwi1"], cwo=WB["cwo1"]))
    res = _run(nc, maps)
    out = gather_lat([r["xo"] for r in res])
    return out.astype(np.float32)
```

```python
import numpy as np
import contextlib
import concourse.bass as bass
import concourse.mybir as mybir
from concourse.bass_utils import run_bass_kernel_spmd

F32 = mybir.dt.float32
BF16 = mybir.dt.bfloat16
AF = mybir.ActivationFunctionType
ALU = mybir.AluOpType
AX = mybir.AxisListType


class Res:
    __slots__ = ("w", "rd", "name", "excl")

    def __init__(self, name="", excl=False):
        self.w = None
        self.rd = {}
        self.name = name
        self.excl = excl


class Prog:
    ENGS = ("pe", "act", "dve", "pool", "sp")
    NDS = 8

    def __init__(self, nc, stack):
        self.nc = nc
        self.q = {e: [] for e in self.ENGS}
        self.sem = {e: stack.enter_context(nc.semaphore("s_" + e)) for e in self.ENGS}
        self.cnt = {e: 0 for e in self.ENGS}
        self.seen = {e: {} for e in self.ENGS}
        self.dq = ("sp", "act", "pool")
        self.dsem = {e: [stack.enter_context(nc.semaphore("d_%s%d" % (e, i))) for i in range(self.NDS)]
                     for e in self.dq}
        self.dcnt = {e: 0 for e in self.dq}
        self.nwaits = 0

    def _need(self, eng, tok, waits, kind):
        if tok is None:
            return
        key, semh, val, teng = tok
        if teng == eng and kind != "raw":
            return
        if teng == "pe" and eng == "pe":
            return
        if self.seen[eng].get(key, 0) >= val:
            return
        self.seen[eng][key] = val
        waits.append((semh, val))

    def emit(self, eng, fn, reads=(), writes=(), dma=False):
        waits = []
        for r in reads:
            self._need(eng, r.w, waits, "raw")
            if r.excl:
                for t in r.rd.values():
                    self._need(eng, t, waits, "rar")
        for r in writes:
            self._need(eng, r.w, waits, "waw")
            for t in r.rd.values():
                self._need(eng, t, waits, "war")
        if dma:
            i = self.dcnt[eng]
            self.dcnt[eng] += 1
            slot, k = i % self.NDS, i // self.NDS
            semh = self.dsem[eng][slot]
            key = "d_%s%d" % (eng, slot)
            if k > 0:
                self._need(eng, (key, semh, 16 * k, None), waits, "raw")
            tok = (key, semh, 16 * (k + 1), None)
            inc = 16
        else:
            self.cnt[eng] += 1
            semh = self.sem[eng]
            tok = ("e_" + eng, semh, self.cnt[eng], eng)
            inc = 1
        for r in reads:
            r.rd[tok[0]] = tok
        for r in writes:
            r.w = tok
            r.rd = {}
        self.nwaits += len(waits)
        self.q[eng].append((waits, fn, semh, inc))
        return tok

    def dma(self, q, out, in_, reads=(), writes=()):
        return self.emit(q, lambda e: e.dma_start(out=out, in_=in_), reads, writes, dma=True)

    def mm(self, out, lhsT, rhs, start, stop, reads=(), writes=()):
        return self.emit("pe", lambda e: e.matmul(out, lhsT, rhs, start=start, stop=stop), reads, writes)

    def final_wait(self, eng, toks):
        waits = []
        for t in toks:
            self._need(eng, t, waits, "raw")
        self.q[eng].append((waits, None, None, 0))

    def replay(self, block):
        nc = self.nc
        names = {"pe": "tensor", "act": "scalar", "dve": "vector", "pool": "gpsimd", "sp": "sync"}
        for e in self.ENGS:
            ops = self.q[e]

            def body(eng, ops=ops):
                for waits, fn, semh, inc in ops:
                    for (s, v) in waits:
                        eng.wait_ge(s, v)
                    if fn is not None:
                        fn(eng).then_inc(semh, inc)
            getattr(block, names[e])(body)


D = 2048
DFF = 5632
KC = 16
JC = 44
EPS = 1e-6
CW = 31
HALO = 15
WB_ELEMS = 4096
NWB = 4
GW = 2


def pvec(v):
    v = np.asarray(v, np.float32).reshape(-1)
    return np.ascontiguousarray(v.reshape(-1, 128).T)


class TL:
    def __init__(self, nc, st, P, TMAX, pv_dram, L, conv=False, tcmax=None, ident_dram=None, nwb=NWB):
        self.nc, self.st, self.P, self.TMAX = nc, st, P, TMAX
        self.L = L
        npv = L.n
        sb = lambda n, s, d: st.enter_context(nc.sbuf_tensor("s_" + n, s, d))
        ps = lambda n: st.enter_context(nc.psum_tensor(n, [128, 512], F32))
        self.X = sb("X", [128, KC, TMAX], F32)
        self.rX = [Res("X%d" % c) for c in range(KC)]
        self.Hb = sb("Hb", [128, KC, TMAX], BF16)
        self.rHb = [Res("Hb%d" % c) for c in range(KC)]
        self.HID = sb("HID", [128, JC, TMAX], BF16)
        self.rHID = [Res("HID%d" % j) for j in range(JC)]
        self.Y = sb("Y", [128, KC, TMAX], F32)
        self.rY = [Res("Y%d" % c) for c in range(KC)]
        self.NWB = nwb
        self.WB = [sb("WB%d" % i, [128, WB_ELEMS], BF16) for i in range(nwb)]
        self.rWB = [Res("WB%d" % i) for i in range(nwb)]
        self.wbi = 0
        self.pv = sb("pv", [128, npv], F32)
        self.rpv = Res("pv")
        self.ones = sb("ones", [128, 128], BF16)
        self.rones = Res("ones")
        self.sq = [sb("sq%d" % i, [128, TMAX], BF16) for i in range(2)]
        self.rsq = [Res() for _ in range(2)]
        self.sqi = 0
        self.tmp = [sb("tmp%d" % i, [128, TMAX], F32) for i in range(3)]
        self.rtmp = [Res() for _ in range(3)]
        self.tmi = 0
        self.R = sb("R", [128, TMAX], F32)
        self.rR = Res("R")
        self.R2 = sb("R2", [128, TMAX], F32)
        self.rR2 = Res("R2")
        self.pA = [ps("pA%d" % i) for i in range(2)]
        self.rpA = [Res(excl=True) for _ in range(2)]
        self.pB = [ps("pB%d" % i) for i in range(2)]
        self.rpB = [Res(excl=True) for _ in range(2)]
        self.pY = [ps("pY%d" % i) for i in range(2)]
        self.rpY = [Res(excl=True) for _ in range(2)]
        self.pS = ps("pS")
        self.rpS = Res("pS", excl=True)
        self.pS2 = ps("pS2")
        self.rpS2 = Res("pS2", excl=True)
        self.pai = 0
        self.pyi = 0
        if conv:
            self.V = sb("V", [128, KC, tcmax], F32)
            self.rV = [Res() for _ in range(KC)]
            self.U = [sb("U%d" % i, [128, TMAX], BF16) for i in range(2)]
            self.rU = [Res() for _ in range(2)]
            self.ui = 0
            self.NDG = 8
            self.DG = [sb("DG%d" % i, [128, 128], BF16) for i in range(self.NDG)]
            self.rDG = [Res() for _ in range(self.NDG)]
            self.dgi = 0
            self.ident = sb("ident", [128, 128], F32)
            self.rident = Res()
            self.MU = sb("MU", [128, tcmax], F32)
            self.rMU = Res()
            self.mask = sb("mask", [128, TMAX], F32)
            self.rmask = Res()
            P.dma("act", self.ident[:], ident_dram, writes=[self.rident])
        P.dma("sp", self.pv[:], pv_dram, writes=[self.rpv])
        P.emit("dve", lambda e: e.memset(self.ones[:], 1.0), writes=[self.rones])

    def wload(self, src, a, b):
        i = self.wbi
        self.wbi = (i + 1) % self.NWB
        view = self.WB[i][:, 0:a * b].rearrange("p (a b) -> p a b", a=a)
        q = "pool" if src.dtype == F32 else "sp"
        self.P.dma(q, view, src, writes=[self.rWB[i]])
        return view, self.rWB[i]

    def next_tmp(self):
        i = self.tmi
        self.tmi = (i + 1) % 3
        return self.tmp[i], self.rtmp[i]

    def next_sq(self):
        i = self.sqi
        self.sqi = (i + 1) % 2
        return self.sq[i], self.rsq[i]

    def pvs(self, off, c):
        return self.pv[:, off + c:off + c + 1]

    def derive(self, out_off, a_off, b_off, mode, scale=1.0):
        P, pv = self.P, self.pv
        o = pv[:, out_off:out_off + KC]
        a = pv[:, a_off:a_off + KC]
        b = pv[:, b_off:b_off + KC]
        if mode == "a1pb":
            P.emit("dve", lambda e: e.scalar_tensor_tensor(out=o, in0=b, scalar=1.0, in1=a, op0=ALU.add, op1=ALU.mult),
                   reads=[self.rpv], writes=[self.rpv])
        else:
            P.emit("dve", lambda e: e.scalar_tensor_tensor(out=o, in0=a, scalar=scale, in1=b, op0=ALU.mult, op1=ALU.mult),
                   reads=[self.rpv], writes=[self.rpv])

    def rstd_from(self, pS, rpS, T, out, rout):
        P = self.P
        P.emit("dve", lambda e: e.tensor_scalar(out=out[:, :T], in0=pS[:, :T], scalar1=1.0 / D, scalar2=EPS,
                                                op0=ALU.mult, op1=ALU.add), reads=[rpS], writes=[rout])
        P.emit("act", lambda e: e.activation(out=out[:, :T], in_=out[:, :T], func=AF.Sqrt), reads=[rout], writes=[rout])
        P.emit("dve", lambda e: e.reciprocal(out=out[:, :T], in_=out[:, :T]), reads=[rout], writes=[rout])

    def norm_mod(self, segs, an, sn):
        P, L = self.P, self.L
        Tt = sum(sg[1] for sg in segs)
        for c in range(KC):
            sq, rsq = self.next_sq()
            for (xoff, T, dst, pf) in segs:
                P.emit("act", lambda e, c=c, sq=sq, xoff=xoff, T=T, dst=dst: e.activation(
                    out=sq[:, dst:dst + T], in_=self.X[:, c, xoff:xoff + T], func=AF.Square),
                    reads=[self.rX[c]], writes=[rsq])
            P.mm(self.pS[:, :Tt], self.ones[:], sq[:, :Tt], c == 0, c == KC - 1, reads=[self.rones, rsq], writes=[self.rpS])
        self.rstd_from(self.pS, self.rpS, Tt, self.R, self.rR)
        for c in range(KC):
            t, rt = self.next_tmp()
            for (xoff, T, dst, pf) in segs:
                a_off, sh_off = L[pf + "_" + an], L[pf + "_" + sn]
                P.emit("dve", lambda e, c=c, t=t, xoff=xoff, T=T, dst=dst, a_off=a_off: e.scalar_tensor_tensor(
                    out=t[:, dst:dst + T], in0=self.X[:, c, xoff:xoff + T], scalar=self.pvs(a_off, c), in1=self.R[:, dst:dst + T],
                    op0=ALU.mult, op1=ALU.mult), reads=[self.rX[c], self.rR, self.rpv], writes=[rt])
                P.emit("act", lambda e, c=c, t=t, T=T, dst=dst, sh_off=sh_off: e.activation(
                    out=self.Hb[:, c, dst:dst + T], in_=t[:, dst:dst + T], func=AF.Identity, bias=self.pvs(sh_off, c)),
                    reads=[rt, self.rpv], writes=[self.rHb[c]])

    def _y_evac(self, c, T, py, rpy, bias_off=None):
        P = self.P
        if bias_off is None:
            P.emit("act", lambda e: e.activation(out=self.Y[:, c, :T], in_=py[:, :T], func=AF.Copy),
                   reads=[rpy], writes=[self.rY[c]])
        else:
            P.emit("act", lambda e: e.activation(out=self.Y[:, c, :T], in_=py[:, :T], func=AF.Identity,
                                                 bias=self.pvs(bias_off, c)),
                   reads=[rpy, self.rpv], writes=[self.rY[c]])
        sq, rsq = self.next_sq()
        P.emit("dve", lambda e: e.tensor_tensor(out=sq[:, :T], in0=self.Y[:, c, :T], in1=self.Y[:, c, :T], op=ALU.mult),
               reads=[self.rY[c]], writes=[rsq])
        return sq, rsq

    def _stat_mm(self, pend, T):
        c, sq, rsq = pend
        self.P.mm(self.pS2[:, :T], self.ones[:], sq[:, :T], c == 0, c == KC - 1,
                  reads=[self.rones, rsq], writes=[self.rpS2])

    def linear_dd(self, T, w, bias_off=None, kchunks=KC):
        P = self.P
        wv = w.rearrange("(kc p) f -> p kc f", p=128)
        pend = None
        for g in range(KC // GW):
            wt, rwt = self.wload(wv[:, :, g * GW * 128:(g + 1) * GW * 128], kchunks, GW * 128)
            for cc in range(GW):
                c = g * GW + cc
                i = self.pyi
                self.pyi = (i + 1) % 2
                py, rpy = self.pY[i], self.rpY[i]
                for k in range(kchunks):
                    P.mm(py[:, :T], wt[:, k, cc * 128:(cc + 1) * 128], self.Hb[:, k, :T], k == 0, k == kchunks - 1,
                         reads=[rwt, self.rHb[k]], writes=[rpy])
                if pend is not None:
                    self._stat_mm(pend, T)
                sq, rsq = self._y_evac(c, T, py, rpy, bias_off)
                pend = (c, sq, rsq)
        self._stat_mm(pend, T)
        self.rstd_from(self.pS2, self.rpS2, T, self.R2, self.rR2)

    def ffn(self, T, wg, wu, wd):
        P = self.P
        wgv = wg.rearrange("(kc p) f -> p kc f", p=128)
        wuv = wu.rearrange("(kc p) f -> p kc f", p=128)
        wdv = wd.rearrange("(j p) f -> p j f", p=128)
        for g in range(JC // GW):
            gt, rgt = self.wload(wgv[:, :, g * GW * 128:(g + 1) * GW * 128], KC, GW * 128)
            ut, rut = self.wload(wuv[:, :, g * GW * 128:(g + 1) * GW * 128], KC, GW * 128)
            for jj in range(GW):
                j = g * GW + jj
                i = self.pai
                self.pai = (i + 1) % 2
                pa, rpa, pb, rpb = self.pA[i], self.rpA[i], self.pB[i], self.rpB[i]
                for k in range(KC):
                    P.mm(pa[:, :T], gt[:, k, jj * 128:(jj + 1) * 128], self.Hb[:, k, :T], k == 0, k == KC - 1,
                         reads=[rgt, self.rHb[k]], writes=[rpa])
                for k in range(KC):
                    P.mm(pb[:, :T], ut[:, k, jj * 128:(jj + 1) * 128], self.Hb[:, k, :T], k == 0, k == KC - 1,
                         reads=[rut, self.rHb[k]], writes=[rpb])
                t, rt = self.next_tmp()
                P.emit("act", lambda e, t=t, pa=pa: e.activation(out=t[:, :T], in_=pa[:, :T], func=AF.Silu),
                       reads=[rpa], writes=[rt])
                P.emit("dve", lambda e, t=t, pb=pb, j=j: e.tensor_tensor(out=self.HID[:, j, :T], in0=t[:, :T], in1=pb[:, :T],
                                                                       op=ALU.mult),
                       reads=[rt, rpb], writes=[self.rHID[j]])
        pend = []
        JQ = JC // 4
        for cp in range(KC // 2):
            wts = [self.wload(wdv[:, q * JQ:(q + 1) * JQ, cp * 256:(cp + 1) * 256], JQ, 256) for q in range(4)]
            for j in range(JC):
                wt, rwt = wts[j // JQ]
                for h in range(2):
                    P.mm(self.pY[h][:, :T], wt[:, j % JQ, h * 128:(h + 1) * 128], self.HID[:, j, :T], j == 0, j == JC - 1,
                         reads=[rwt, self.rHID[j]], writes=[self.rpY[h]])
            for pd in pend:
                self._stat_mm(pd, T)
            pend = []
            for h in range(2):
                c = cp * 2 + h
                sq, rsq = self._y_evac(c, T, self.pY[h], self.rpY[h])
                pend.append((c, sq, rsq))
        for pd in pend:
            self._stat_mm(pd, T)
        self.rstd_from(self.pS2, self.rpS2, T, self.R2, self.rR2)

    def post_res(self, segs, gn):
        P, L = self.P, self.L
        for c in range(KC):
            t, rt = self.next_tmp()
            for (xoff, T, dst, pf) in segs:
                g_off = L[pf + "_" + gn]
                P.emit("dve", lambda e, c=c, t=t, T=T, dst=dst, g_off=g_off: e.scalar_tensor_tensor(
                    out=t[:, dst:dst + T], in0=self.Y[:, c, dst:dst + T], scalar=self.pvs(g_off, c), in1=self.R2[:, dst:dst + T],
                    op0=ALU.mult, op1=ALU.mult), reads=[self.rY[c], self.rR2, self.rpv], writes=[rt])
                P.emit("dve", lambda e, c=c, t=t, xoff=xoff, T=T, dst=dst: e.tensor_tensor(
                    out=self.X[:, c, xoff:xoff + T], in0=self.X[:, c, xoff:xoff + T], in1=t[:, dst:dst + T], op=ALU.add),
                    reads=[rt, self.rX[c]], writes=[self.rX[c]])

    def conv(self, Tp, csegs, w_in, w_out, o):
        P = self.P
        Tc = sum(sg[1] for sg in csegs)
        wv = w_in.rearrange("(kc p) f -> p kc f", p=128)
        tiles = {}

        def emit_in(c):
            g, cc = divmod(c, GW)
            if cc == 0:
                tiles[g] = (self.wload(wv[:, :, g * GW * 128:(g + 1) * GW * 128], KC, GW * 128),
                            self.wload(wv[:, :, D + g * GW * 128:D + (g + 1) * GW * 128], KC, GW * 128))
            (at, rat), (gt, rgt) = tiles[g]
            i = self.pai
            self.pai = (i + 1) % 2
            pa, rpa, pb, rpb = self.pA[i], self.rpA[i], self.pB[i], self.rpB[i]
            for k in range(KC):
                P.mm(pa[:, :Tp], at[:, k, cc * 128:(cc + 1) * 128], self.Hb[:, k, :Tp], k == 0, k == KC - 1,
                     reads=[rat, self.rHb[k]], writes=[rpa])
            for k in range(KC):
                P.mm(pb[:, :Tp], gt[:, k, cc * 128:(cc + 1) * 128], self.Hb[:, k, :Tp], k == 0, k == KC - 1,
                     reads=[rgt, self.rHb[k]], writes=[rpb])
            t, rt = self.next_tmp()
            P.emit("act", lambda e: e.activation(out=t[:, :Tp], in_=pb[:, :Tp], func=AF.Sigmoid, bias=self.pvs(o["b_in"] + KC, c)),
                   reads=[rpb, self.rpv], writes=[rt])
            P.emit("dve", lambda e: e.tensor_tensor(out=t[:, :Tp], in0=t[:, :Tp], in1=self.mask[:, :Tp], op=ALU.mult),
                   reads=[rt, self.rmask], writes=[rt])
            ui = self.ui
            self.ui = (ui + 1) % 2
            U, rU = self.U[ui], self.rU[ui]
            P.emit("dve", lambda e: e.scalar_tensor_tensor(
                out=U[:, :Tp], in0=pa[:, :Tp], scalar=self.pvs(o["b_in"], c), in1=t[:, :Tp], op0=ALU.add, op1=ALU.mult),
                reads=[rpa, rt, self.rpv], writes=[rU])
            return U, rU

        def emit_taps(c, U, rU):
            for k in range(CW):
                di = self.dgi
                self.dgi = (di + 1) % self.NDG
                dg, rdg = self.DG[di], self.rDG[di]
                wk = self.pv[:, o["w_dw"] + c * CW + k:o["w_dw"] + c * CW + k + 1]
                P.emit("act", lambda e, dg=dg, wk=wk: e.activation(out=dg[:], in_=self.ident[:], func=AF.Identity, scale=wk),
                       reads=[self.rident, self.rpv], writes=[rdg])
                for si, (po, tci, dst) in enumerate(csegs):
                    P.mm(self.pY[si][:, :tci], dg[:], U[:, po + k:po + k + tci], k == 0, k == CW - 1,
                         reads=[rdg, rU], writes=[self.rpY[si]])
            for si, (po, tci, dst) in enumerate(csegs):
                P.emit("dve", lambda e, si=si, tci=tci, dst=dst: e.tensor_single_scalar(
                    out=self.V[:, c, dst:dst + tci], in_=self.pY[si][:, :tci], scalar=self.pvs(o["b_dw"], c), op=ALU.add),
                    reads=[self.rpY[si], self.rpv], writes=[self.rV[c]])
            sq, rsq = self.next_sq()
            P.emit("act", lambda e, sq=sq: e.activation(out=sq[:, :Tc], in_=self.V[:, c, :Tc], func=AF.Copy),
                   reads=[self.rV[c]], writes=[rsq])
            P.mm(self.pS[:, :Tc], self.ones[:], sq[:, :Tc], c == 0, c == KC - 1, reads=[self.rones, rsq], writes=[self.rpS])
            sq, rsq = self.next_sq()
            P.emit("act", lambda e, sq=sq: e.activation(out=sq[:, :Tc], in_=self.V[:, c, :Tc], func=AF.Square),
                   reads=[self.rV[c]], writes=[rsq])
            P.mm(self.pS2[:, :Tc], self.ones[:], sq[:, :Tc], c == 0, c == KC - 1, reads=[self.rones, rsq], writes=[self.rpS2])

        prev = None
        for c in range(KC):
            cur = emit_in(c)
            if prev is not None:
                emit_taps(c - 1, *prev)
            prev = cur
        emit_taps(KC - 1, *prev)
        MU, rMU, R, rR = self.MU, self.rMU, self.R, self.rR
        P.emit("dve", lambda e: e.tensor_single_scalar(out=MU[:, :Tc], in_=self.pS[:, :Tc], scalar=1.0 / D, op=ALU.mult),
               reads=[self.rpS], writes=[rMU])
        t, rt = self.next_tmp()
        P.emit("dve", lambda e: e.tensor_tensor(out=t[:, :Tc], in0=MU[:, :Tc], in1=MU[:, :Tc], op=ALU.mult),
               reads=[rMU], writes=[rt])
        P.emit("dve", lambda e: e.scalar_tensor_tensor(out=R[:, :Tc], in0=self.pS2[:, :Tc], scalar=1.0 / D, in1=t[:, :Tc],
                                                       op0=ALU.mult, op1=ALU.subtract), reads=[self.rpS2, rt], writes=[rR])
        P.emit("dve", lambda e: e.tensor_single_scalar(out=R[:, :Tc], in_=R[:, :Tc], scalar=EPS, op=ALU.add),
               reads=[rR], writes=[rR])
        P.emit("act", lambda e: e.activation(out=R[:, :Tc], in_=R[:, :Tc], func=AF.Sqrt), reads=[rR], writes=[rR])
        P.emit("dve", lambda e: e.reciprocal(out=R[:, :Tc], in_=R[:, :Tc]), reads=[rR], writes=[rR])
        for c in range(KC):
            t, rt = self.next_tmp()
            P.emit("dve", lambda e, c=c, t=t: e.tensor_tensor(out=t[:, :Tc], in0=self.V[:, c, :Tc], in1=MU[:, :Tc], op=ALU.subtract),
                   reads=[self.rV[c], rMU], writes=[rt])
            P.emit("dve", lambda e, t=t: e.tensor_tensor(out=t[:, :Tc], in0=t[:, :Tc], in1=R[:, :Tc], op=ALU.mult),
                   reads=[rt, rR], writes=[rt])
            P.emit("act", lambda e, c=c, t=t: e.activation(out=self.Hb[:, c, :Tc], in_=t[:, :Tc], func=AF.Silu,
                                                             scale=self.pvs(o["ln_g"], c), bias=self.pvs(o["ln_b"], c)),
                   reads=[rt, self.rpv], writes=[self.rHb[c]])
        self.linear_dd(Tc, w_out, bias_off=o["b_out"])

    def load_x(self, src, T, q="sp", col0=0):
        v = src.rearrange("(c p) t -> p c t", p=128)
        for h in range(2):
            cs = slice(h * 8, (h + 1) * 8)
            self.P.dma(q, self.X[:, cs, col0:col0 + T], v[:, cs, :], writes=self.rX[cs])

    def load_hb(self, src, T, q="sp", col0=0):
        v = src.rearrange("(c p) t -> p c t", p=128)
        for h in range(2):
            cs = slice(h * 8, (h + 1) * 8)
            self.P.dma(q, self.Hb[:, cs, col0:col0 + T], v[:, cs, :], writes=self.rHb[cs])

    def store_x(self, dst, xoff, T, outs, q="sp"):
        v = dst.rearrange("(c p) t -> p c t", p=128)
        for h in range(2):
            cs = slice(h * 8, (h + 1) * 8)
            outs.append(self.P.dma(q, v[:, cs, :], self.X[:, cs, xoff:xoff + T], reads=self.rX[cs]))

    def store_hb(self, dst, T, outs, q="sp", col0=0):
        v = dst.rearrange("(c p) t -> p c t", p=128)
        for h in range(2):
            cs = slice(h * 8, (h + 1) * 8)
            outs.append(self.P.dma(q, v[:, cs, :], self.Hb[:, cs, col0:col0 + T], reads=self.rHb[cs]))

    def load_mask(self, src, T, q="sp"):
        self.P.dma(q, self.mask[:, :T], src, writes=[self.rmask])


NCORE = 8
TP = 440
TC = 410
NB = 5
LCORE = 2048
CCORE = 64
CTP = CCORE + 2 * HALO
SEQ = 8192
CTX = 256
NMOD = 9


def new_nc():
    return bass.Bass("TRN2", target_bir_lowering=False)


ACOLS = NMOD * D // NCORE


def build_L0():
    nc = new_nc()
    cT = nc.dram_tensor("cT", [128, KC * 3], F32, kind="ExternalInput").ap()
    aw = nc.dram_tensor("aw", [4, D, ACOLS], F32, kind="ExternalInput").ap()
    ab = nc.dram_tensor("ab", [4, 3, ACOLS], F32, kind="ExternalInput").ap()
    mo = nc.dram_tensor("mo", [4, 3, ACOLS], F32, kind="ExternalOutput").ap()
    with contextlib.ExitStack() as st:
        P = Prog(nc, st)
        sb = lambda n, s, d: st.enter_context(nc.sbuf_tensor("s_" + n, s, d))
        cs = sb("cs", [128, KC, 3], F32); rcs = Res()
        wt = [sb("wt%d" % i, [128, KC, 512], F32) for i in range(2)]; rwt = [Res(), Res()]
        bt = sb("bt", [3, 4, ACOLS], F32); rbt = Res()
        ot = sb("ot", [3, 4, ACOLS], F32); rot = Res()
        ps = [st.enter_context(nc.psum_tensor("ps%d" % i, [128, 512], F32)) for i in range(2)]; rps = [Res(excl=True), Res(excl=True)]
        P.dma("sp", cs[:].rearrange("p a b -> p (a b)"), cT, writes=[rcs])
        P.dma("act", bt[:], ab.rearrange("l r c -> r l c"), writes=[rbt])
        P.emit("act", lambda e: e.activation(out=cs[:], in_=cs[:], func=AF.Silu), reads=[rcs], writes=[rcs])
        tiles = [(l, c0, min(512, ACOLS - c0)) for l in range(4) for c0 in range(0, ACOLS, 512)]
        for i, (l, c0, n) in enumerate(tiles):
            w, rw = wt[i % 2], rwt[i % 2]
            src = aw[l].rearrange("(kc p) f -> p kc f", p=128)[:, :, c0:c0 + n]
            P.dma("sp" if i % 2 == 0 else "act", w[:, :, :n], src, writes=[rw])
            p, rp = ps[i % 2], rps[i % 2]
            for k in range(KC):
                P.mm(p[0:3, :n], cs[:, k, :], w[:, k, :n], k == 0, k == KC - 1, reads=[rcs, rw], writes=[rp])
            P.emit("dve", lambda e, p=p, l=l, c0=c0, n=n: e.tensor_tensor(
                out=ot[:, l, c0:c0 + n], in0=p[0:3, :n], in1=bt[:, l, c0:c0 + n], op=ALU.add),
                reads=[rp, rbt], writes=[rot])
        tok = P.dma("sp", mo.rearrange("l r c -> r l c"), ot[:], reads=[rot])
        P.final_wait("sp", [tok])
        with nc.Block() as block:
            P.replay(block)
    return nc


def run_L0(inp):
    c, c_ctx, ada_w, ada_b = inp["c"], inp["c_ctx"], inp["ada_w"], inp["ada_b"]
    cv = np.stack([c[0], c[1], c_ctx], 1)
    cT = np.ascontiguousarray(cv.reshape(KC, 128, 3).transpose(1, 0, 2).reshape(128, KC * 3))
    maps = []
    for k in range(NCORE):
        cols = slice(k * ACOLS, (k + 1) * ACOLS)
        maps.append(dict(cT=cT, aw=np.ascontiguousarray(ada_w[:, :, cols]),
                         ab=np.ascontiguousarray(np.broadcast_to(ada_b[:, None, cols], (4, 3, ACOLS)))))
    res = run_bass_kernel_spmd(build_L0(), maps, core_ids=list(range(NCORE)))
    m = np.concatenate([r["mo"] for r in res.results], axis=2)
    return m.reshape(4, 3, NMOD, D)


def layer_pv_names(tag):
    return ["%s_m%d" % (tag, j) for j in range(NMOD)] + ["%s_%s" % (tag, n) for n in ("A1", "G1", "A2", "G2", "A3", "G3")]


class PVL:
    def __init__(self):
        self.off = {}
        self.n = 0

    def add(self, name, w=KC):
        self.off[name] = self.n
        self.n += w

    def __getitem__(self, k):
        return self.off[k]


def pv_layout(layers, conv_layers):
    L = PVL()
    for l in layers:
        for j in range(6):
            L.add("g%d_%d" % (l, j))
        for s in ("lat", "ctx"):
            for n in layer_pv_names("L%d%s" % (l, s)):
                L.add(n)
    for l in conv_layers:
        L.add("c%d_b_in" % l, 32)
        L.add("c%d_w_dw" % l, KC * CW)
        for n in ("b_dw", "ln_g", "ln_b", "b_out"):
            L.add("c%d_%s" % (l, n))
    return L


def pv_fill(L, inp, m, core, layers, conv_layers):
    b = core // 4
    pv = np.zeros((128, L.n), np.float32)

    def put(name, v):
        a = pvec(v)
        pv[:, L[name]:L[name] + a.shape[1]] = a
    for l in layers:
        for j in range(6):
            put("g%d_%d" % (l, j), inp["norm_g"][l, j])
        for j in range(NMOD):
            put("L%dlat_m%d" % (l, j), m[l, b, j])
            put("L%dctx_m%d" % (l, j), m[l, 2, j])
    for l in conv_layers:
        j = l // 3
        put("c%d_b_in" % l, inp["conv_b_in"][j])
        wd = inp["conv_w_dw"][j]
        a = wd.T.reshape(KC, 128, CW).transpose(1, 0, 2).reshape(128, KC * CW)
        pv[:, L["c%d_w_dw" % l]:L["c%d_w_dw" % l] + KC * CW] = a
        put("c%d_b_dw" % l, inp["conv_b_dw"][j])
        put("c%d_ln_g" % l, inp["conv_ln_g"][j])
        put("c%d_ln_b" % l, inp["conv_ln_b"][j])
        put("c%d_b_out" % l, inp["conv_b_out"][j])
    return pv


def derive_layer(T, L, l, s):
    t = "L%d%s" % (l, s)
    g = lambda j: L["g%d_%d" % (l, j)]
    T.derive(L[t + "_A1"], g(0), L[t + "_m1"], "a1pb")
    T.derive(L[t + "_G1"], L[t + "_m2"], g(1), "ab", 0.5)
    T.derive(L[t + "_A2"], g(2), L[t + "_m4"], "a1pb")
    T.derive(L[t + "_G2"], L[t + "_m5"], g(3), "ab", 1.0)
    T.derive(L[t + "_A3"], g(4), L[t + "_m7"], "a1pb")
    T.derive(L[t + "_G3"], L[t + "_m8"], g(5), "ab", 0.5)


def conv_offs(L, l):
    return {k: L["c%d_%s" % (l, k)] for k in ("b_in", "w_dw", "b_dw", "ln_g", "ln_b", "b_out")}


NBA, TCA = 6, 342
TPA = TCA + 2 * HALO


def _wdecl(dt, names_gu, names_d, extra, WDT):
    W = {}
    for n in names_gu:
        W[n] = dt(n, [D, DFF], WDT, "ExternalInput")
    for n in names_d:
        W[n] = dt(n, [DFF, D], WDT, "ExternalInput")
    for n, shp in extra:
        W[n] = dt(n, shp, WDT, "ExternalInput")
    return W


def build_LA(WDT=BF16):
    nc = new_nc()
    L = pv_layout([0, 1], [0])
    dt = lambda n, s, d, k: nc.dram_tensor(n, s, d, kind=k).ap()
    TM = TPA + CTP
    xin = dt("xin", [NBA, D, TPA], F32, "ExternalInput")
    msk = dt("msk", [NBA, 128, TM], F32, "ExternalInput")
    cin = dt("cin", [D, CTP], F32, "ExternalInput")
    pvd = dt("pv", [128, L.n], F32, "ExternalInput")
    idn = dt("idn", [128, 128], F32, "ExternalInput")
    W = _wdecl(dt, ("g00", "u00", "g01", "u01", "g10", "u10"), ("d00", "d01", "d10"),
               (("cwi", [D, 2 * D]), ("cwo", [D, D])), WDT)
    xo = dt("xo", [NBA, D, TCA], F32, "ExternalOutput")
    ho = dt("ho", [NBA, D, TCA], BF16, "ExternalOutput")
    xco = dt("xco", [D, CCORE], F32, "ExternalOutput")
    hco = dt("hco", [D, CCORE], BF16, "ExternalOutput")
    with contextlib.ExitStack() as st:
        P = Prog(nc, st)
        T = TL(nc, st, P, TM, pvd, L, conv=True, tcmax=TCA + CCORE, ident_dram=idn, nwb=5)
        for l in (0, 1):
            for s in ("lat", "ctx"):
                derive_layer(T, L, l, s)
        outs = []
        co = conv_offs(L, 0)
        for i in range(NBA):
            two = (i == 0)
            tp = TM if two else TPA
            tc = TCA + CCORE if two else TCA
            pre = lambda l: [(0, TPA, 0, "L%dlat" % l)] + ([(TPA, CTP, TPA, "L%dctx" % l)] if two else [])
            post = lambda l: [(HALO, TCA, 0, "L%dlat" % l)] + ([(TPA + HALO, CCORE, TCA, "L%dctx" % l)] if two else [])
            csegs = [(0, TCA, 0)] + ([(TPA, CCORE, TCA)] if two else [])
            T.load_x(xin[i], TPA)
            if two:
                T.load_x(cin, CTP, col0=TPA, q="act")
            T.load_mask(msk[i, :, :tp], tp)
            T.norm_mod(pre(0), "A1", "m0")
            T.ffn(tp, W["g00"], W["u00"], W["d00"])
            T.post_res(pre(0), "G1")
            T.norm_mod(pre(0), "A2", "m3")
            T.conv(tp, csegs, W["cwi"], W["cwo"], co)
            T.post_res(post(0), "G2")
            T.norm_mod(post(0), "A3", "m6")
            T.ffn(tc, W["g01"], W["u01"], W["d01"])
            T.post_res(post(0), "G3")
            T.norm_mod(post(1), "A1", "m0")
            T.ffn(tc, W["g10"], W["u10"], W["d10"])
            T.post_res(post(1), "G1")
            T.norm_mod(post(1), "A2", "m3")
            T.store_x(xo[i], HALO, TCA, outs)
            T.store_hb(ho[i], TCA, outs, q="act")
            if two:
                T.store_x(xco, TPA + HALO, CCORE, outs)
                T.store_hb(hco, CCORE, outs, q="act", col0=TCA)
        P.final_wait("sp", outs)
        with nc.Block() as block:
            P.replay(block)
    return nc, L


def build_LC(WDT=BF16):
    nc = new_nc()
    L = pv_layout([1, 2], [])
    dt = lambda n, s, d, k: nc.dram_tensor(n, s, d, kind=k).ap()
    TM = TC + CCORE
    xin = dt("xin", [NB, D, TC], F32, "ExternalInput")
    yin = dt("yin", [NB, D, TC], BF16, "ExternalInput")
    cin = dt("cin", [D, CCORE], F32, "ExternalInput")
    ycin = dt("ycin", [D, CCORE], BF16, "ExternalInput")
    pvd = dt("pv", [128, L.n], F32, "ExternalInput")
    W = _wdecl(dt, ("g11", "u11", "g20", "u20"), ("d11", "d20"), (("wo", [D, D]),), WDT)
    xo = dt("xo", [NB, D, TC], F32, "ExternalOutput")
    ho = dt("ho", [NB, D, TC], BF16, "ExternalOutput")
    hco = dt("hco", [D, CCORE], BF16, "ExternalOutput")
    with contextlib.ExitStack() as st:
        P = Prog(nc, st)
        T = TL(nc, st, P, TM, pvd, L, conv=False, nwb=8)
        for l in (1, 2):
            for s in ("lat", "ctx"):
                derive_layer(T, L, l, s)
        outs = []
        for i in range(NB):
            two = (i == 0)
            tc = TM if two else TC
            sg = lambda l: [(0, TC, 0, "L%dlat" % l)] + ([(TC, CCORE, TC, "L%dctx" % l)] if two else [])
            T.load_x(xin[i], TC)
            T.load_hb(yin[i], TC, q="act")
            if two:
                T.load_x(cin, CCORE, col0=TC)
                T.load_hb(ycin, CCORE, q="act", col0=TC)
            T.linear_dd(tc, W["wo"])
            T.post_res(sg(1), "G2")
            T.norm_mod(sg(1), "A3", "m6")
            T.ffn(tc, W["g11"], W["u11"], W["d11"])
            T.post_res(sg(1), "G3")
            T.norm_mod(sg(2), "A1", "m0")
            T.ffn(tc, W["g20"], W["u20"], W["d20"])
            T.post_res(sg(2), "G1")
            T.norm_mod(sg(2), "A2", "m3")
            T.store_x(xo[i], 0, TC, outs)
            T.store_hb(ho[i], TC, outs, q="act")
            if two:
                T.store_hb(hco, CCORE, outs, q="act", col0=TC)
        P.final_wait("sp", outs)
        with nc.Block() as block:
            P.replay(block)
    return nc, L


def build_LE(WDT=BF16):
    nc = new_nc()
    L = pv_layout([2, 3], [3])
    dt = lambda n, s, d, k: nc.dram_tensor(n, s, d, kind=k).ap()
    xin = dt("xin", [NB, D, TP], F32, "ExternalInput")
    yin = dt("yin", [NB, D, TP], BF16, "ExternalInput")
    msk = dt("msk", [NB, 128, TP], F32, "ExternalInput")
    pvd = dt("pv", [128, L.n], F32, "ExternalInput")
    idn = dt("idn", [128, 128], F32, "ExternalInput")
    W = _wdecl(dt, ("g21", "u21", "g30", "u30", "g31", "u31"), ("d21", "d30", "d31"),
               (("wo", [D, D]), ("cwi", [D, 2 * D]), ("cwo", [D, D])), WDT)
    xo = dt("xo", [NB, D, TC], F32, "ExternalOutput")
    with contextlib.ExitStack() as st:
        P = Prog(nc, st)
        T = TL(nc, st, P, TP, pvd, L, conv=True, tcmax=TC, ident_dram=idn, nwb=6)
        for l in (2, 3):
            derive_layer(T, L, l, "lat")
        outs = []
        co = conv_offs(L, 3)
        for i in range(NB):
            pre = lambda l: [(0, TP, 0, "L%dlat" % l)]
            post = lambda l: [(HALO, TC, 0, "L%dlat" % l)]
            T.load_x(xin[i], TP)
            T.load_hb(yin[i], TP, q="act")
            T.load_mask(msk[i], TP)
            T.linear_dd(TP, W["wo"])
            T.post_res(pre(2), "G2")
            T.norm_mod(pre(2), "A3", "m6")
            T.ffn(TP, W["g21"], W["u21"], W["d21"])
            T.post_res(pre(2), "G3")
            T.norm_mod(pre(3), "A1", "m0")
            T.ffn(TP, W["g30"], W["u30"], W["d30"])
            T.post_res(pre(3), "G1")
            T.norm_mod(pre(3), "A2", "m3")
            T.conv(TP, [(0, TC, 0)], W["cwi"], W["cwo"], co)
            T.post_res(post(3), "G2")
            T.norm_mod(post(3), "A3", "m6")
            T.ffn(TC, W["g31"], W["u31"], W["d31"])
            T.post_res(post(3), "G3")
            T.store_x(xo[i], HALO, TC, outs)
        P.final_wait("sp", outs)
        with nc.Block() as block:
            P.replay(block)
    return nc, L


def lat_blocks(xfull, core, halo, nb=NB, tc=TC):
    b, q = core // 4, core % 4
    out = np.zeros((nb, D, tc + 2 * halo), xfull.dtype)
    msk = np.zeros((nb, tc + 2 * halo), np.float32)
    for i in range(nb):
        s = q * LCORE + i * tc - halo
        e = s + tc + 2 * halo
        s0, e0 = max(s, 0), min(e, SEQ)
        out[i, :, s0 - s:e0 - s] = xfull[b, s0:e0].T
        msk[i, s0 - s:e0 - s] = 1.0
    return out, msk


def ctx_block(cfull, core, halo):
    b, q = core // 4, core % 4
    w = CCORE + 2 * halo
    out = np.zeros((D, w), cfull.dtype)
    msk = np.zeros((w,), np.float32)
    s = q * CCORE - halo
    e = s + w
    s0, e0 = max(s, 0), min(e, CTX)
    out[:, s0 - s:e0 - s] = cfull[b, s0:e0].T
    msk[s0 - s:e0 - s] = 1.0
    return out, msk


def gather_lat(blocks_per_core, dtype=None, nb=NB, tc=TC):
    d = blocks_per_core[0].dtype if dtype is None else dtype
    out = np.zeros((2, SEQ, D), d)
    for core, blk in enumerate(blocks_per_core):
        b, q = core // 4, core % 4
        for i in range(nb):
            s = q * LCORE + i * tc
            n = min(tc, (q + 1) * LCORE - s)
            out[b, s:s + n] = blk[i, :, :n].T
    return out


def gather_ctx(per_core):
    out = np.zeros((2, CTX, D), per_core[0].dtype)
    for core, a in enumerate(per_core):
        b, q = core // 4, core % 4
        out[b, q * CCORE:(q + 1) * CCORE] = a.T
    return out


CAST_CH = 8192


def cast_weight_list(inp):
    lst = []
    for l in range(4):
        for i in range(2):
            lst.append(("g%d%d" % (l, i), inp["ffn_w_gate"][l, i]))
            lst.append(("u%d%d" % (l, i), inp["ffn_w_up"][l, i]))
            lst.append(("d%d%d" % (l, i), inp["ffn_w_down"][l, i]))
    for j in range(2):
        lst.append(("cwi%d" % j, inp["conv_w_in"][j]))
        lst.append(("cwo%d" % j, inp["conv_w_out"][j]))
    lst.append(("gwo", inp["gla_w_o"][0]))
    lst.append(("nwo", inp["na_w_o"][0]))
    return lst


def build_LW(ncols):
    nc = new_nc()
    wi = nc.dram_tensor("wi", [128, ncols], F32, kind="ExternalInput").ap()
    wo = nc.dram_tensor("wo", [128, ncols], BF16, kind="ExternalOutput").ap()
    with contextlib.ExitStack() as st:
        P = Prog(nc, st)
        NBUF = 4
        bufs = [st.enter_context(nc.sbuf_tensor("s_cb%d" % i, [128, CAST_CH], BF16)) for i in range(NBUF)]
        rb = [Res() for _ in range(NBUF)]
        outs = []
        i = 0
        for c0 in range(0, ncols, CAST_CH):
            n = min(CAST_CH, ncols - c0)
            b, r = bufs[i % NBUF], rb[i % NBUF]
            P.dma("pool", b[:, :n], wi[:, c0:c0 + n], writes=[r])
            outs.append(P.dma("sp" if i % 2 == 0 else "act", wo[:, c0:c0 + n], b[:, :n], reads=[r]))
            i += 1
        P.final_wait("sp", outs)
        with nc.Block() as block:
            P.replay(block)
    return nc


def run_LW(inp):
    lst = cast_weight_list(inp)
    per = [a.size // (NCORE * 128) for _, a in lst]
    ncols = sum(per)
    maps = []
    for k in range(NCORE):
        parts = [a.reshape(NCORE, 128, -1)[k] for _, a in lst]
        maps.append(dict(wi=np.ascontiguousarray(np.concatenate(parts, axis=1))))
    res = run_bass_kernel_spmd(build_LW(ncols), maps, core_ids=list(range(NCORE))).results
    outs = [np.asarray(r["wo"]) for r in res]
    W = {}
    off = 0
    for (key, a), n in zip(lst, per):
        W[key] = np.stack([o[:, off:off + n] for o in outs]).reshape(a.shape)
        off += n
    return W


def build_LP(ncols):
    nc = new_nc()
    cT = nc.dram_tensor("cT", [128, KC * 3], F32, kind="ExternalInput").ap()
    aw = nc.dram_tensor("aw", [4, D, ACOLS], F32, kind="ExternalInput").ap()
    ab = nc.dram_tensor("ab", [4, 3, ACOLS], F32, kind="ExternalInput").ap()
    mo = nc.dram_tensor("mo", [4, 3, ACOLS], F32, kind="ExternalOutput").ap()
    wi = nc.dram_tensor("wi", [128, ncols], F32, kind="ExternalInput").ap()
    wo = nc.dram_tensor("wo", [128, ncols], BF16, kind="ExternalOutput").ap()
    with contextlib.ExitStack() as st:
        P = Prog(nc, st)
        sb = lambda n, s, d: st.enter_context(nc.sbuf_tensor("s_" + n, s, d))
        outs = []
        cs = sb("cs", [128, KC, 3], F32); rcs = Res()
        wt = [sb("wt%d" % i, [128, KC, 512], F32) for i in range(2)]; rwt = [Res(), Res()]
        bt = sb("bt", [3, 4, ACOLS], F32); rbt = Res()
        ot = sb("ot", [3, 4, ACOLS], F32); rot = Res()
        ps = [st.enter_context(nc.psum_tensor("ps%d" % i, [128, 512], F32)) for i in range(2)]
        rps = [Res(excl=True), Res(excl=True)]
        P.dma("sp", cs[:].rearrange("p a b -> p (a b)"), cT, writes=[rcs])
        P.dma("act", bt[:], ab.rearrange("l r c -> r l c"), writes=[rbt])
        P.emit("act", lambda e: e.activation(out=cs[:], in_=cs[:], func=AF.Silu), reads=[rcs], writes=[rcs])
        tiles = [(l, c0, min(512, ACOLS - c0)) for l in range(4) for c0 in range(0, ACOLS, 512)]
        for i, (l, c0, n) in enumerate(tiles):
            w, rw = wt[i % 2], rwt[i % 2]
            src = aw[l].rearrange("(kc p) f -> p kc f", p=128)[:, :, c0:c0 + n]
            P.dma("sp" if i % 2 == 0 else "act", w[:, :, :n], src, writes=[rw])
            p, rp = ps[i % 2], rps[i % 2]
            for k in range(KC):
                P.mm(p[0:3, :n], cs[:, k, :], w[:, k, :n], k == 0, k == KC - 1, reads=[rcs, rw], writes=[rp])
            P.emit("dve", lambda e, p=p, l=l, c0=c0, n=n: e.tensor_tensor(
                out=ot[:, l, c0:c0 + n], in0=p[0:3, :n], in1=bt[:, l, c0:c0 + n], op=ALU.add),
                reads=[rp, rbt], writes=[rot])
        outs.append(P.dma("sp", mo.rearrange("l r c -> r l c"), ot[:], reads=[rot]))
        NBUF = 3
        bufs = [sb("cb%d" % i, [128, CAST_CH], BF16) for i in range(NBUF)]
        rb = [Res() for _ in range(NBUF)]
        i = 0
        for c0 in range(0, ncols, CAST_CH):
            n = min(CAST_CH, ncols - c0)
            b, r = bufs[i % NBUF], rb[i % NBUF]
            P.dma("pool", b[:, :n], wi[:, c0:c0 + n], writes=[r])
            outs.append(P.dma("sp" if i % 2 == 0 else "act", wo[:, c0:c0 + n], b[:, :n], reads=[r]))
            i += 1
        P.final_wait("sp", outs)
        with nc.Block() as block:
            P.replay(block)
    return nc


def run_LP(inp):
    c, c_ctx, ada_w, ada_b = inp["c"], inp["c_ctx"], inp["ada_w"], inp["ada_b"]
    cv = np.stack([c[0], c[1], c_ctx], 1)
    cT = np.ascontiguousarray(cv.reshape(KC, 128, 3).transpose(1, 0, 2).reshape(128, KC * 3))
    lst = cast_weight_list(inp)
    per = [a.size // (NCORE * 128) for _, a in lst]
    ncols = sum(per)
    maps = []
    for k in range(NCORE):
        cols = slice(k * ACOLS, (k + 1) * ACOLS)
        parts = [a.reshape(NCORE, 128, -1)[k] for _, a in lst]
        maps.append(dict(cT=cT, aw=np.ascontiguousarray(ada_w[:, :, cols]),
                         ab=np.ascontiguousarray(np.broadcast_to(ada_b[:, None, cols], (4, 3, ACOLS))),
                         wi=np.ascontiguousarray(np.concatenate(parts, axis=1))))
    res = run_bass_kernel_spmd(build_LP(ncols), maps, core_ids=list(range(NCORE))).results
    m = np.concatenate([r["mo"] for r in res], axis=2).reshape(4, 3, NMOD, D)
    outs = [np.asarray(r["wo"]) for r in res]
    W = {}
    off = 0
    for (key, a), n in zip(lst, per):
        W[key] = np.stack([o[:, off:off + n] for o in outs]).reshape(a.shape)
        off += n
    return m, W


GH = 4
DK = 256
DV = 512
GT = 256
NCH = GT // 64
NLT = SEQ // GT
NTOK = CTX + SEQ
ROPE_THETA = 10000.0
PENG = "dve"


def gla_tables():
    inv = ROPE_THETA ** (-np.arange(0, 128, 2, dtype=np.float32) / 128)
    p = np.arange(128)
    sgn = np.where(p < 64, -1.0, 1.0).astype(np.float32)
    rows = np.arange(SEQ // 64, dtype=np.float32)
    cols = np.arange(64, dtype=np.float32)
    ang_r = rows[None, :] * inv[p % 64][:, None]
    ang_c = cols[None, :] * inv[p % 64][:, None]
    t = {}
    t["cr"] = np.cos(ang_r).astype(np.float32)
    t["sr"] = (np.sin(ang_r) * sgn[:, None]).astype(np.float32)
    t["cc"] = np.cos(ang_c).astype(np.float32)
    t["sc"] = (np.sin(ang_c) * sgn[:, None]).astype(np.float32)
    s = np.arange(128)[:, None]
    tt = np.arange(128)[None, :]
    same = (s // 64) == (tt // 64)
    g = -1.0 / 16.0
    t["tri"] = np.stack([np.where(same & (s <= tt), g, 0.0), np.where(same & (s > tt), g, 0.0),
                         np.where(same & (s >= tt), g, 0.0), np.where(same & (s < tt), g, 0.0)]).astype(np.float32)
    s = np.arange(64)[:, None]
    tt = np.arange(64)[None, :]
    t["msk"] = np.stack([(s <= tt), (s >= tt)]).astype(np.float32)
    t["ident"] = np.eye(128, dtype=np.float32)
    return t


def build_LB(nlt=NLT):
    nc = new_nc()
    SEQ = nlt * GT
    NTOK = CTX + SEQ
    dt = lambda n, s, d, k="ExternalInput": nc.dram_tensor(n, s, d, kind=k).ap()
    hT = dt("hT", [D, SEQ], BF16)
    hcT = dt("hcT", [D, CTX], BF16)
    wq = dt("wq", [D, DK], F32)
    wk = dt("wk", [D, DK], F32)
    wv = dt("wv", [D, DV], F32)
    wr = dt("wr", [D, DV], F32)
    wa1 = dt("wa1", [2, D, 16], F32)
    wa2 = dt("wa2", [2, 16, DK], F32)
    bab = dt("bab", [2, 128, DK], F32)
    brb = dt("brb", [128, DV], F32)
    ngb = dt("ngb", [64, DV], F32)
    tcr = dt("tcr", [128, 128], F32)
    tsr = dt("tsr", [128, 128], F32)
    tcc = dt("tcc", [128, 64], F32)
    tsc = dt("tsc", [128, 64], F32)
    ttri = dt("ttri", [4, 128, 128], F32)
    tmsk = dt("tmsk", [2, 64, 64], F32)
    tid = dt("tid", [128, 128], F32)
    oy = dt("oy", [NTOK, DV], BF16, "ExternalOutput")
    of_s = dt("of_s", [NTOK, DV], F32, "ExternalOutput")
    with contextlib.ExitStack() as st:
        P = Prog(nc, st)
        sb = lambda n, s, d: st.enter_context(nc.sbuf_tensor("s_" + n, s, d))
        psb = lambda n: st.enter_context(nc.psum_tensor(n, [128, 512], F32))

        def E(eng, fn, reads, writes, **kw):
            return P.emit(eng, lambda e: getattr(e, fn)(**kw), reads, writes)

        Wq = sb("Wq", [128, KC, DK], BF16); rWq = Res()
        Wk = sb("Wk", [128, KC, DK], BF16); rWk = Res()
        Wv = sb("Wv", [128, KC, DV], BF16); rWv = Res()
        Wr = sb("Wr", [128, KC, DV], BF16); rWr = Res()
        Wa1 = sb("Wa1", [128, 2, KC, 16], BF16); rWa1 = Res()
        Wa2 = sb("Wa2", [16, 2, DK], BF16); rWa2 = Res()
        P.dma("pool", Wq[:], wq.rearrange("(kc p) f -> p kc f", p=128), writes=[rWq])
        P.dma("pool", Wk[:], wk.rearrange("(kc p) f -> p kc f", p=128), writes=[rWk])
        P.dma("pool", Wv[:], wv.rearrange("(kc p) f -> p kc f", p=128), writes=[rWv])
        P.dma("pool", Wr[:], wr.rearrange("(kc p) f -> p kc f", p=128), writes=[rWr])
        for d in range(2):
            P.dma("pool", Wa1[:, d], wa1[d].rearrange("(kc p) f -> p kc f", p=128), writes=[rWa1])
        P.dma("pool", Wa2[:], wa2.rearrange("d r f -> r d f"), writes=[rWa2])
        rT = Res()
        BA = sb("BA", [128, 2, DK], F32)
        P.dma("sp", BA[:], bab.rearrange("d p f -> p d f"), writes=[rT])
        BR = sb("BR", [128, DV], F32); P.dma("sp", BR[:], brb, writes=[rT])
        NG = sb("NG", [64, DV], F32); P.dma("sp", NG[:], ngb, writes=[rT])
        CR = sb("CR", [128, 128], F32); P.dma("sp", CR[:], tcr, writes=[rT])
        SR = sb("SR", [128, 128], F32); P.dma("sp", SR[:], tsr, writes=[rT])
        CC = sb("CC", [128, 64], F32); P.dma("sp", CC[:], tcc, writes=[rT])
        SC = sb("SC", [128, 64], F32); P.dma("sp", SC[:], tsc, writes=[rT])
        TRI = sb("TRI", [128, 4, 128], F32); P.dma("sp", TRI[:], ttri.rearrange("a p f -> p a f"), writes=[rT])
        MSK = sb("MSK", [64, 2, 64], F32); P.dma("sp", MSK[:], tmsk.rearrange("a p f -> p a f"), writes=[rT])
        IDb = sb("IDb", [128, 128], BF16); P.dma("pool", IDb[:], tid, writes=[rT])

        H = [sb("H%d" % i, [128, KC, GT], BF16) for i in range(2)]; rH = [Res(), Res()]
        QS = sb("QS", [128, GT], F32); rQS = Res()
        QW = sb("QW", [128, GT], F32); rQW = Res()
        QRs = [sb("QR%d" % i, [128, 2, GT], F32) for i in range(2)]; rQRs = [[Res(), Res()] for _ in range(2)]
        KRs = [sb("KR%d" % i, [128, 2, GT], F32) for i in range(2)]; rKRs = [[Res(), Res()] for _ in range(2)]
        ZL = sb("ZL", [16, GT], BF16); rZL = Res()
        ZB = sb("ZB", [128, 2, DK], F32); rZB = Res()
        EQs = [sb("EQ%d" % i, [128, 2, GT], F32) for i in range(2)]; rEQs = [Res(), Res()]
        EKs = [sb("EK%d" % i, [128, 2, GT], F32) for i in range(2)]; rEKs = [Res(), Res()]
        ELs = [sb("EL%d" % i, [128, 2, GT], F32) for i in range(2)]; rELs = [Res(), Res()]
        QEs = [sb("QE%d" % i, [128, 2, GT], BF16) for i in range(2)]; rQEs = [Res(), Res()]
        KEs = [sb("KE%d" % i, [128, 2, GT], BF16) for i in range(2)]; rKEs = [Res(), Res()]
        KLTs = [sb("KLT%d" % i, [128, 2, GT], BF16) for i in range(2)]; rKLTs = [Res(), Res()]
        S = sb("S", [128, 2, DV], F32); rS = [Res(), Res()]
        Sb = sb("Sb", [128, 2, DV], BF16); rSb = [Res(), Res()]
        VT = [sb("VT%d" % i, [64, DV], BF16) for i in range(4)]; rVT = [Res() for _ in range(4)]
        KL = [sb("KL%d" % i, [64, DK], BF16) for i in range(2)]; rKL = [Res(), Res()]
        SCT = [sb("SCT%d" % i, [64, 64], BF16) for i in range(2)]; rSCT = [Res(), Res()]
        OS = [sb("OS%d" % i, [64, DV], F32) for i in range(2)]; rOS = [Res(), Res()]
        OF = [sb("OF%d" % i, [64, DV], F32) for i in range(2)]; rOF = [Res(), Res()]
        SQ = sb("SQ", [64, DV], F32); rSQ = Res()
        RT = [sb("RT%d" % i, [64, DV], F32) for i in range(4)]; rRT = [Res() for _ in range(4)]
        SS = sb("SS", [64, 1], F32); rSS = Res()
        OUT = [sb("OUT%d" % i, [64, DV], BF16) for i in range(2)]; rOUT = [Res(), Res()]
        pP = [psb("pP0"), psb("pP1")]; rpP = [Res(excl=True), Res(excl=True)]
        pZ = psb("pZ"); rpZ = Res(excl=True)
        pC = psb("pC"); rpC = Res(excl=True)
        pV = psb("pV"); rpV = Res(excl=True)
        pK = psb("pK"); rpK = Res(excl=True)
        pO = psb("pO"); rpO = Res(excl=True)
        pS = psb("pS"); rpS = Res(excl=True)

        for dc in range(2):
            E("dve", "memset", [], [rS[dc]], ap=S[:, dc, :], constant=0.0)
            E("dve", "memset", [], [rSb[dc]], ap=Sb[:, dc, :], constant=0.0)

        outs = []
        cnt = {"h": 0, "p": 0, "c": 0}

        def rope_evac(ps, rps, dst, rdst, row0, is_q, lat):
            sc = (1.0 / 16.0) if is_q else 1.0
            if not lat:
                E("act", "activation", [rps], [rdst], out=dst, in_=ps, func=AF.Identity, scale=sc)
                return
            E("act", "activation", [rps], [rQS], out=QS[:], in_=ps, func=AF.Identity, scale=sc)
            E(PENG, "tensor_copy", [rQS], [rQW], out=QW[0:64, :], in_=QS[64:128, :])
            E(PENG, "tensor_copy", [rQS], [rQW], out=QW[64:128, :], in_=QS[0:64, :])
            return

        def make_tile(slot, d, src, t0, lat, grow0, ch_order, tok0, final):
            Ht, rHt = H[slot], rH[slot]
            QR, rQR, KR, rKR = QRs[slot], rQRs[slot], KRs[slot], rKRs[slot]
            EQ, rEQ, EK, rEK, EL, rEL = EQs[slot], rEQs[slot], EKs[slot], rEKs[slot], ELs[slot], rELs[slot]
            QE, rQE, KE, rKE, KLT, rKLT = QEs[slot], rQEs[slot], KEs[slot], rKEs[slot], KLTs[slot], rKLTs[slot]

            def pA():
                v = src.rearrange("(c p) t -> p c t", p=128)
                for hh in range(2):
                    cs = slice(hh * 8, (hh + 1) * 8)
                    P.dma("sp" if hh == 0 else "act", Ht[:, cs, :], v[:, cs, t0:t0 + GT], writes=[rHt])
                for (Wm, rW, dst, rdst, is_q) in ((Wq, rWq, QR, rQR, True), (Wk, rWk, KR, rKR, False)):
                    for dc in range(2):
                        pi = cnt["p"] % 2
                        cnt["p"] += 1
                        pp, rpp = pP[pi], rpP[pi]
                        for k in range(KC):
                            P.mm(pp[:, :GT], Wm[:, k, dc * 128:(dc + 1) * 128], Ht[:, k, :], k == 0, k == KC - 1,
                                 reads=[rW, rHt], writes=[rpp])
                        sc = (1.0 / 16.0) if is_q else 1.0
                        if not lat:
                            E("act", "activation", [rpp], [rdst[dc]], out=dst[:, dc, :], in_=pp[:, :GT], func=AF.Identity, scale=sc)
                        else:
                            E("act", "activation", [rpp], [rQS], out=QS[:], in_=pp[:, :GT], func=AF.Identity, scale=sc)
                            E(PENG, "tensor_copy", [rQS], [rQW], out=QW[0:64, :], in_=QS[64:128, :])
                            E(PENG, "tensor_copy", [rQS], [rQW], out=QW[64:128, :], in_=QS[0:64, :])
                            q3 = lambda a: a.rearrange("p (r c) -> p r c", c=64)
                            if dc == 0:
                                ct = CR[:, grow0:grow0 + NCH].unsqueeze(2).to_broadcast([128, NCH, 64])
                                stb = SR[:, grow0:grow0 + NCH].unsqueeze(2).to_broadcast([128, NCH, 64])
                            else:
                                ct = CC[:].unsqueeze(1).to_broadcast([128, NCH, 64])
                                stb = SC[:].unsqueeze(1).to_broadcast([128, NCH, 64])
                            E("dve", "tensor_tensor", [rQS, rT], [rdst[dc]], out=q3(dst[:, dc, :]), in0=q3(QS[:]), in1=ct, op=ALU.mult)
                            E(PENG, "tensor_tensor", [rQW, rT], [rQW], out=q3(QW[:]), in0=q3(QW[:]), in1=stb, op=ALU.mult)
                            E("dve", "tensor_tensor", [rQW, rdst[dc]], [rdst[dc]], out=dst[:, dc, :], in0=dst[:, dc, :], in1=QW[:], op=ALU.add)

            def pB():
                pi = cnt["p"] % 2
                cnt["p"] += 1
                pp, rpp = pP[pi], rpP[pi]
                for k in range(KC):
                    P.mm(pp[0:16, :GT], Wa1[:, d, k, :], Ht[:, k, :], k == 0, k == KC - 1, reads=[rWa1, rHt], writes=[rpp])
                E("act", "activation", [rpp], [rZL], out=ZL[:], in_=pp[0:16, :GT], func=AF.Copy)

            def pC_():
                for pr in range(2):
                    P.mm(pZ[:, pr * DK:(pr + 1) * DK], ZL[:, pr * 128:(pr + 1) * 128], Wa2[:, d, :], True, True,
                         reads=[rZL, rWa2], writes=[rpZ])
                z3 = pZ[:].rearrange("p (a f) -> p a f", a=2)
                E("dve", "tensor_tensor", [rpZ, rT], [rZB], out=ZB[:], in0=z3,
                  in1=BA[:, d, :].unsqueeze(1).to_broadcast([128, 2, DK]), op=ALU.add)
                E("act", "activation", [rZB], [rZB], out=ZB[:], in_=ZB[:], func=AF.Exp, scale=-1.0)
                E("act", "activation", [rZB], [rZB], out=ZB[:], in_=ZB[:], func=AF.Ln, bias=1.0)

            def pD(which):
                tri = TRI[:, 2 * d + which, :]
                for dc in range(2):
                    for pr in range(2):
                        P.mm(pC[:, dc * GT + pr * 128:dc * GT + (pr + 1) * 128], ZB[:, pr, dc * 128:(dc + 1) * 128], tri,
                             True, True, reads=[rZB, rT], writes=[rpC])
                pc3 = pC[:].rearrange("p (a f) -> p a f", a=2)
                if which == 0:
                    E("act", "activation", [rpC], [rEQ], out=EQ[:], in_=pc3, func=AF.Exp)
                    E("act", "activation", [rpC], [rEK], out=EK[:], in_=pc3, func=AF.Exp, scale=-1.0)
                    E("dve", "tensor_tensor", rQR + [rEQ], [rQE], out=QE[:], in0=QR[:], in1=EQ[:], op=ALU.mult)
                    E("dve", "tensor_tensor", rKR + [rEK], [rKE], out=KE[:], in0=KR[:], in1=EK[:], op=ALU.mult)
                else:
                    E("act", "activation", [rpC], [rEL], out=EL[:], in_=pc3, func=AF.Exp)
                    E("dve", "tensor_tensor", rKR + [rEL], [rKLT], out=KLT[:], in0=KR[:], in1=EL[:], op=ALU.mult)

            pieces = [pA, pB, pC_, lambda: pD(0), lambda: pD(1)]

            def stage_v(g):
                gs = slice(g * 128, (g + 1) * 128)
                for k in range(KC):
                    P.mm(pV[:, :], Ht[:, k, gs], Wv[:, k, :], k == 0, k == KC - 1, reads=[rHt, rWv], writes=[rpV])
                E("act", "activation", [rpV], [rVT[2 * g]], out=VT[2 * g][:], in_=pV[0:64, :], func=AF.Copy)
                E("dve", "tensor_copy", [rpV], [rVT[2 * g + 1]], out=VT[2 * g + 1][:], in_=pV[64:128, :])
                if final:
                    for k in range(KC):
                        P.mm(pZ[:, :], Ht[:, k, gs], Wr[:, k, :], k == 0, k == KC - 1, reads=[rHt, rWr], writes=[rpZ])
                    for hh_ in range(2):
                        c_ = 2 * g + hh_
                        ps_ = slice(hh_ * 64, (hh_ + 1) * 64)
                        E("dve", "tensor_tensor", [rpZ, rT], [rRT[c_]], out=RT[c_][:], in0=pZ[ps_, :], in1=BR[ps_, :], op=ALU.add)
                        E("act", "activation", [rRT[c_]], [rRT[c_]], out=RT[c_][:], in_=RT[c_][:], func=AF.Silu)

            def stage_a(ch, ci):
                cs = slice(ch * 64, (ch + 1) * 64)
                for dc in range(2):
                    P.mm(pK[0:64, dc * 128:(dc + 1) * 128], KLT[:, dc, cs], IDb[:], True, True, reads=[rKLT, rT], writes=[rpK])
                for dc in range(2):
                    P.mm(pK[0:64, 256:320], KE[:, dc, cs], QE[:, dc, cs], dc == 0, dc == 1, reads=[rKE, rQE], writes=[rpK])
                E("act", "activation", [rpK], [rKL[ci]], out=KL[ci][:], in_=pK[0:64, 0:256], func=AF.Copy)
                E("dve", "tensor_tensor", [rpK, rT], [rSCT[ci]], out=SCT[ci][:], in0=pK[0:64, 256:320], in1=MSK[:, d, :], op=ALU.mult)

            def stage_b(ch, ci):
                cs = slice(ch * 64, (ch + 1) * 64)
                grow = tok0 + ch * 64
                P.mm(pO[0:64, :], SCT[ci][:], VT[ch][:], True, False, reads=[rSCT[ci], rVT[ch]], writes=[rpO])
                for dc in range(2):
                    P.mm(pO[0:64, :], QE[:, dc, cs], Sb[:, dc, :], False, dc == 1, reads=[rQE, rSb[dc]], writes=[rpO])
                lcol = ch * 64 + (63 if d == 0 else 0)
                for dc in range(2):
                    P.mm(pS[:, :], KL[ci][:, dc * 128:(dc + 1) * 128], VT[ch][:], True, True, reads=[rKL[ci], rVT[ch]], writes=[rpS])
                    E("dve", "scalar_tensor_tensor", [rpS, rS[dc], rEQ], [rS[dc]], out=S[:, dc, :], in0=S[:, dc, :],
                      scalar=EQ[:, dc, lcol:lcol + 1], in1=pS[:, :], op0=ALU.mult, op1=ALU.add)
                    E("act", "activation", [rS[dc]], [rSb[dc]], out=Sb[:, dc, :], in_=S[:, dc, :], func=AF.Copy)
                if not final:
                    E("act", "activation", [rpO], [rOS[ci]], out=OS[ci][:], in_=pO[0:64, :], func=AF.Copy)
                    P.dma("sp", of_s[grow:grow + 64, :], OS[ci][:], reads=[rOS[ci]], writes=[rOFS[grow // 64]])
                else:
                    P.dma("sp", OF[ci][:], of_s[grow:grow + 64, :], reads=[rOFS[grow // 64]], writes=[rOF[ci]])
                    E("dve", "tensor_tensor", [rpO, rOF[ci]], [rOS[ci]], out=OS[ci][:], in0=pO[0:64, :], in1=OF[ci][:], op=ALU.add)
                    E("dve", "tensor_tensor", [rOS[ci]], [rSQ], out=SQ[:], in0=OS[ci][:], in1=OS[ci][:], op=ALU.mult)
                    E("dve", "reduce_sum", [rSQ], [rSS], out=SS[:], in_=SQ[:], axis=AX.X)
                    E("dve", "tensor_scalar", [rSS], [rSS], out=SS[:], in0=SS[:], scalar1=1.0 / DV, scalar2=EPS, op0=ALU.mult, op1=ALU.add)
                    E("act", "activation", [rSS], [rSS], out=SS[:], in_=SS[:], func=AF.Sqrt)
                    E("dve", "reciprocal", [rSS], [rSS], out=SS[:], in_=SS[:])
                    E("dve", "scalar_tensor_tensor", [rOS[ci], rSS, rT], [rOS[ci]], out=OS[ci][:], in0=OS[ci][:], scalar=SS[:, 0:1],
                      in1=NG[:], op0=ALU.mult, op1=ALU.mult)
                    E("dve", "tensor_tensor", [rOS[ci], rRT[ch]], [rOUT[ci]], out=OUT[ci][:], in0=OS[ci][:], in1=RT[ch][:], op=ALU.mult)
                    outs.append(P.dma("sp", oy[grow:grow + 64, :], OUT[ci][:], reads=[rOUT[ci]]))

            done_v = set()

            def ensure_v(ch):
                if ch // 2 not in done_v:
                    done_v.add(ch // 2)
                    stage_v(ch // 2)

            def step0():
                ensure_v(ch_order[0])
                stage_a(ch_order[0], 0)

            def mk_step(i_, ch):
                def f():
                    if i_ + 1 < len(ch_order):
                        ensure_v(ch_order[i_ + 1])
                        stage_a(ch_order[i_ + 1], (i_ + 1) % 2)
                    stage_b(ch, i_ % 2)
                return f
            steps = [step0] + [mk_step(i_, ch) for i_, ch in enumerate(ch_order)]
            return pieces, steps

        def run_pass(d, final):
            order = list(range(NCH)) if d == 0 else list(range(NCH))[::-1]
            tl_ = [(hcT, 0, False, 0, 0)] + [(hT, ti * GT, True, ti * NCH, CTX + ti * GT)
                                             for ti in (range(nlt) if d == 0 else range(nlt)[::-1])]
            mk = lambda i: make_tile(i % 2, d, tl_[i][0], tl_[i][1], tl_[i][2], tl_[i][3], order, tl_[i][4], final)
            cur = mk(0)
            for p_ in cur[0]:
                p_()
            for i in range(len(tl_)):
                nxt = mk(i + 1) if i + 1 < len(tl_) else ([], [])
                pend = list(nxt[0])
                for st_ in cur[1]:
                    st_()
                    if pend:
                        pend.pop(0)()
                for p_ in pend:
                    p_()
                cur = nxt

        rOFS = [Res() for _ in range(NTOK // 64)]
        run_pass(0, False)
        for dc in range(2):
            E("dve", "memset", [], [rS[dc]], ap=S[:, dc, :], constant=0.0)
            E("dve", "memset", [], [rSb[dc]], ap=Sb[:, dc, :], constant=0.0)
        run_pass(1, True)
        P.final_wait("sp", outs)
        with nc.Block() as block:
            P.replay(block)
    return nc


def lb_inputs(inp, h1, hc1, core, tabs):
    import ml_dtypes
    b, hd = core // 4, core % 4
    ks = slice(hd * DK, (hd + 1) * DK)
    vs = slice(hd * DV, (hd + 1) * DV)
    d = dict(
        hT=np.ascontiguousarray(h1[b].T).astype(ml_dtypes.bfloat16),
        hcT=np.ascontiguousarray(hc1[b].T).astype(ml_dtypes.bfloat16),
        wq=np.ascontiguousarray(inp["gla_w_q"][0][:, ks]), wk=np.ascontiguousarray(inp["gla_w_k"][0][:, ks]),
        wv=np.ascontiguousarray(inp["gla_w_v"][0][:, vs]), wr=np.ascontiguousarray(inp["gla_w_r"][0][:, vs]),
        wa1=np.ascontiguousarray(inp["gla_w_a1"][0]), wa2=np.ascontiguousarray(inp["gla_w_a2"][0][:, :, ks]),
        bab=np.ascontiguousarray(np.broadcast_to(inp["gla_b_a"][0][:, None, ks], (2, 128, DK))),
        brb=np.ascontiguousarray(np.broadcast_to(inp["gla_b_r"][0][None, vs], (128, DV))),
        ngb=np.ascontiguousarray(np.broadcast_to(inp["gla_norm_g"][0][None, :], (64, DV))),
        tcr=tabs["cr"], tsr=tabs["sr"], tcc=tabs["cc"], tsc=tabs["sc"], ttri=tabs["tri"], tmsk=tabs["msk"], tid=tabs["ident"])
    return d


NH = 16
DH = 128
HPC = 4
ROWS = 128
GW_ = 64
NT = 512
NEG = -30000.0
NTOKA = SEQ + CTX
NVT = NTOKA // 128


def na_bias_tables(rpb):
    col = np.arange(64)
    cs = np.clip(col - 8, 0, 48)
    out = np.full((NH, 8, 64, 8, 64), NEG, np.float32)
    for v in range(8):
        delta = v - 7
        for j in range(8):
            ri = delta + j + 7
            for q in range(64):
                kc = cs[q] + np.arange(16)
                out[:, v, q, j, kc] = rpb[:, ri, kc - q + 15]
    return out.reshape(NH, 8, 64, 512)


def build_LD():
    nc = new_nc()
    dt = lambda n, s, d, k="ExternalInput": nc.dram_tensor(n, s, d, kind=k).ap()
    hT = dt("hT", [D, NTOKA], BF16)
    wq = dt("wq", [HPC, D, DH], F32)
    wk = dt("wk", [HPC, D, DH], F32)
    wv = dt("wv", [HPC, D, DH], F32)
    btab = dt("btab", [HPC, 8, 64, 512], F32)
    tid = dt("tid", [64, 64], F32)
    oy = dt("oy", [SEQ, HPC * DH], BF16, "ExternalOutput")
    with contextlib.ExitStack() as st:
        P = Prog(nc, st)
        sb = lambda n, s, d: st.enter_context(nc.sbuf_tensor("s_" + n, s, d))
        psb = lambda n: st.enter_context(nc.psum_tensor(n, [128, 512], F32))

        def E(eng, fn, reads, writes, **kw):
            return P.emit(eng, lambda e: getattr(e, fn)(**kw), reads, writes)

        W = [sb("W%d" % i, [128, 3, KC, DH], BF16) for i in range(2)]; rW = [Res(), Res()]
        BT = sb("BT", [64, 8, 512], F32); rBT = Res()
        ID = sb("ID", [64, 64], BF16); rID = Res()
        P.dma("pool", ID[:], tid, writes=[rID])
        H = [sb("H%d" % i, [128, KC, NT], BF16) for i in range(2)]; rH = [Res(), Res()]
        QT = sb("QT", [128, SEQ], BF16); rQT = Res()
        KT = sb("KT", [128, NTOKA], BF16); rKT = Res()
        VE = sb("VE", [128, NVT, DH + 1], BF16); rVE = Res()
        VO = sb("VO", [128, NVT, DH + 1], BF16); rVO = Res()
        SL = [sb("SL%d" % i, [64, 768], F32) for i in range(2)]; rSL = [Res(), Res()]
        MX = [sb("MX%d" % i, [64, 2], F32) for i in range(2)]; rMX = [Res(), Res()]
        PB = [sb("PB%d" % i, [64, 768], BF16) for i in range(2)]; rPB = [Res(), Res()]
        PT = [sb("PT%d" % i, [128, 384], BF16) for i in range(2)]; rPT = [Res(), Res()]
        RS = [sb("RS%d" % i, [64, 1], F32) for i in range(2)]; rRS = [Res(), Res()]
        OB = [sb("OB%d" % i, [64, DH], BF16) for i in range(2)]; rOB = [Res(), Res()]
        pP = [psb("pP0"), psb("pP1")]; rpP = [Res(excl=True), Res(excl=True)]
        pA = [psb("pA0"), psb("pA1")]; rpA = [Res(excl=True), Res(excl=True)]
        pB = psb("pB"); rpB = Res(excl=True)
        pT = [psb("pT0"), psb("pT1")]; rpT = [Res(excl=True), Res(excl=True)]
        pO = psb("pO"); rpO = Res(excl=True)

        E("dve", "memset", [], [rVE], ap=VE[:, :, DH:DH + 1], constant=1.0)
        outs = []
        hv = hT.rearrange("(c p) t -> p c t", p=128)
        cnt = {"h": 0, "p": 0, "u": 0}
        scale = DH ** -0.5
        for hh in range(HPC):
            Wt, rWt = W[hh % 2], rW[hh % 2]
            for i, wsrc in enumerate((wq, wk, wv)):
                P.dma("pool", Wt[:, i], wsrc[hh].rearrange("(kc p) f -> p kc f", p=128), writes=[rWt])
            P.dma("sp", BT[:], btab[hh].rearrange("v q k -> q v k"), writes=[rBT])
            for ti in range(NTOKA // NT + (1 if NTOKA % NT else 0)):
                t0 = ti * NT
                n = min(NT, NTOKA - t0)
                hi = cnt["h"] % 2
                cnt["h"] += 1
                Ht, rHt = H[hi], rH[hi]
                for half in range(2):
                    cs = slice(half * 8, (half + 1) * 8)
                    P.dma("sp" if half == 0 else "act", Ht[:, cs, :n], hv[:, cs, t0:t0 + n], writes=[rHt])
                lat = t0 < SEQ
                for which in ((0, 1) if lat else (1,)):
                    pi = cnt["p"] % 2
                    cnt["p"] += 1
                    pp, rpp = pP[pi], rpP[pi]
                    for k in range(KC):
                        P.mm(pp[:, :n], Wt[:, which, k, :], Ht[:, k, :n], k == 0, k == KC - 1, reads=[rWt, rHt], writes=[rpp])
                    if which == 0:
                        E("act", "activation", [rpp], [rQT], out=QT[:, t0:t0 + n], in_=pp[:, :n], func=AF.Identity, scale=scale)
                    else:
                        E("dve", "tensor_copy", [rpp], [rKT], out=KT[:, t0:t0 + n], in_=pp[:, :n])
                for s in range(n // 128):
                    pi = cnt["p"] % 2
                    cnt["p"] += 1
                    pp, rpp = pP[pi], rpP[pi]
                    for k in range(KC):
                        P.mm(pp[:, :DH], Ht[:, k, s * 128:(s + 1) * 128], Wt[:, 2, k, :], k == 0, k == KC - 1,
                             reads=[rWt, rHt], writes=[rpp])
                    E("act" if s % 2 else "dve", "activation" if s % 2 else "tensor_copy", [rpp], [rVE],
                      **(dict(out=VE[:, t0 // 128 + s, 0:DH], in_=pp[:, :DH], func=AF.Copy) if s % 2 else
                         dict(out=VE[:, t0 // 128 + s, 0:DH], in_=pp[:, :DH])))
            nlat = SEQ // 128
            P.dma("sp", VO[0:64, 0:nlat, :], VE[64:128, 0:nlat, :], reads=[rVE], writes=[rVO])
            P.dma("act", VO[64:128, 0:nlat - 1, :], VE[0:64, 1:nlat, :], reads=[rVE], writes=[rVO])
            def s1(r):
                u = r % 2
                rs = min(max(r - 4, 0), ROWS - 8)
                var = rs - r + 7
                qs = QT[:, r * 64:(r + 1) * 64]
                P.mm(pA[u][0:64, :], qs, KT[:, rs * 64:rs * 64 + 512], True, True, reads=[rQT, rKT], writes=[rpA[u]])
                P.mm(pB[0:64, 0:CTX], qs, KT[:, SEQ:SEQ + CTX], True, True, reads=[rQT, rKT], writes=[rpB])
                E("dve", "tensor_tensor", [rpA[u], rBT], [rSL[u]], out=SL[u][:, 0:512], in0=pA[u][0:64, :], in1=BT[:, var, :], op=ALU.add)
                E("act", "activation", [rpB], [rSL[u]], out=SL[u][:, 512:768], in_=pB[0:64, 0:CTX], func=AF.Copy)
                E("dve", "reduce_max", [rSL[u]], [rMX[u]], out=MX[u][:, 0:1], in_=SL[u][:], axis=AX.X)
                E("dve", "tensor_single_scalar", [rMX[u]], [rMX[u]], out=MX[u][:, 1:2], in_=MX[u][:, 0:1], scalar=-1.0, op=ALU.mult)
                E("act", "activation", [rSL[u], rMX[u]], [rPB[u]], out=PB[u][:], in_=SL[u][:], func=AF.Exp, bias=MX[u][:, 1:2])

            def s2(r):
                u = r % 2
                for kc in range(6):
                    P.mm(pT[u][:, kc * 64:(kc + 1) * 64], PB[u][:, kc * 128:(kc + 1) * 128], ID[:], True, True,
                         reads=[rPB[u], rID], writes=[rpT[u]])
                E("act" if r % 4 < 2 else "dve", "activation" if r % 4 < 2 else "tensor_copy", [rpT[u]], [rPT[u]],
                  **(dict(out=PT[u][:], in_=pT[u][:, 0:384], func=AF.Copy) if r % 4 < 2 else dict(out=PT[u][:], in_=pT[u][:, 0:384])))

            def s3(r):
                u = r % 2
                rs = min(max(r - 4, 0), ROWS - 8)
                for kc in range(6):
                    if kc < 4:
                        if rs % 2 == 0:
                            vt, rv = VE[:, rs // 2 + kc, :], rVE
                        else:
                            vt, rv = VO[:, (rs - 1) // 2 + kc, :], rVO
                    else:
                        vt, rv = VE[:, SEQ // 128 + (kc - 4), :], rVE
                    P.mm(pO[0:64, 0:DH + 1], PT[u][:, kc * 64:(kc + 1) * 64], vt, kc == 0, kc == 5, reads=[rPT[u], rv], writes=[rpO])
                E("dve", "reciprocal", [rpO], [rRS[u]], out=RS[u][:], in_=pO[0:64, DH:DH + 1])
                E("act", "activation", [rpO, rRS[u]], [rOB[u]], out=OB[u][:], in_=pO[0:64, 0:DH], func=AF.Identity, scale=RS[u][:, 0:1])
                outs.append(P.dma("sp", oy[r * 64:(r + 1) * 64, hh * DH:(hh + 1) * DH], OB[u][:], reads=[rOB[u]]))

            for it in range(ROWS + 2):
                if it < ROWS:
                    s1(it)
                if 0 <= it - 1 < ROWS:
                    s2(it - 1)
                if 0 <= it - 2 < ROWS:
                    s3(it - 2)
        P.final_wait("sp", outs)
        with nc.Block() as block:
            P.replay(block)
    return nc


def ld_inputs(inp, h2, hc2, core, btab_all):
    import ml_dtypes
    b, hg = core // 4, core % 4
    hcat = np.concatenate([h2[b], hc2[b]], 0)
    wqkv = inp["na_w_qkv"][0]
    sl = lambda base: np.ascontiguousarray(
        np.stack([wqkv[:, base + (hg * HPC + i) * DH: base + (hg * HPC + i + 1) * DH] for i in range(HPC)]))
    return dict(hT=np.ascontiguousarray(hcat.T).astype(ml_dtypes.bfloat16),
                wq=sl(0), wk=sl(D), wv=sl(2 * D),
                btab=np.ascontiguousarray(btab_all[hg * HPC:(hg + 1) * HPC]),
                tid=np.eye(64, dtype=np.float32))


_CACHE = {}


def _run(nc, maps):
    return run_bass_kernel_spmd(nc, maps, core_ids=list(range(NCORE))).results


def kernel(**inputs):
    import ml_dtypes
    inp = {k: np.asarray(v) for k, v in inputs.items()}
    cores = range(NCORE)
    m, WB = run_LP(inp)
    fw_ = lambda l, i: (WB["g%d%d" % (l, i)], WB["u%d%d" % (l, i)], WB["d%d%d" % (l, i)])
    nc, L = build_LA()
    eye = np.eye(128, dtype=np.float32)
    maps = []
    for core in cores:
        xb, mk = lat_blocks(inp["x"], core, HALO, NBA, TCA)
        cb, cmk = ctx_block(inp["ctx"], core, HALO)
        msk = np.zeros((NBA, 128, TPA + CTP), np.float32)
        msk[:, :, :TPA] = mk[:, None, :]
        msk[0, :, TPA:] = cmk[None, :]
        g00, u00, d00 = fw_(0, 0)
        g01, u01, d01 = fw_(0, 1)
        g10, u10, d10 = fw_(1, 0)
        maps.append(dict(xin=xb, msk=msk, cin=cb, pv=pv_fill(L, inp, m, core, [0, 1], [0]), idn=eye,
                         g00=g00, u00=u00, d00=d00, g01=g01, u01=u01, d01=d01, g10=g10, u10=u10, d10=d10,
                         cwi=WB["cwi0"], cwo=WB["cwo0"]))
    res = _run(nc, maps)
    x1 = gather_lat([r["xo"] for r in res], nb=NBA, tc=TCA)
    h1 = gather_lat([r["ho"] for r in res], nb=NBA, tc=TCA)
    xc1 = gather_ctx([r["xco"] for r in res])
    hc1 = gather_ctx([r["hco"] for r in res])
    del maps, res
    tabs = gla_tables()
    res = _run(build_LB(), [lb_inputs(inp, h1, hc1, core, tabs) for core in cores])
    y1 = np.zeros((2, SEQ, D), ml_dtypes.bfloat16)
    yc1 = np.zeros((2, CTX, D), ml_dtypes.bfloat16)
    for core in cores:
        b, hd = core // 4, core % 4
        o = np.asarray(res[core]["oy"])
        yc1[b, :, hd * DV:(hd + 1) * DV] = o[:CTX]
        y1[b, :, hd * DV:(hd + 1) * DV] = o[CTX:]
    del res
    nc, L = build_LC()
    maps = []
    for core in cores:
        g11, u11, d11 = fw_(1, 1)
        g20, u20, d20 = fw_(2, 0)
        maps.append(dict(xin=lat_blocks(x1, core, 0)[0], yin=lat_blocks(y1, core, 0)[0],
                         cin=ctx_block(xc1, core, 0)[0], ycin=ctx_block(yc1, core, 0)[0],
                         pv=pv_fill(L, inp, m, core, [1, 2], []),
                         g11=g11, u11=u11, d11=d11, g20=g20, u20=u20, d20=d20, wo=WB["gwo"]))
    res = _run(nc, maps)
    x2 = gather_lat([r["xo"] for r in res])
    h2 = gather_lat([r["ho"] for r in res])
    hc2 = gather_ctx([r["hco"] for r in res])
    del maps, res
    btab = na_bias_tables(inp["na_rpb"][0])
    res = _run(build_LD(), [ld_inputs(inp, h2, hc2, core, btab) for core in cores])
    y2 = np.zeros((2, SEQ, D), ml_dtypes.bfloat16)
    for core in cores:
        b, hg = core // 4, core % 4
        y2[b, :, hg * HPC * DH:(hg + 1) * HPC * DH] = np.asarray(res[core]["oy"])
    del res
    nc, L = build_LE()
    maps = []
    for core in cores:
        xb, mk = lat_blocks(x2, core, HALO)
        g21, u21, d21 = fw_(2, 1)
        g30, u30, d30 = fw_(3, 0)
        g31, u31, d31 = fw_(3, 1)
        maps.append(dict(xin=xb, yin=lat_blocks(y2, core, HALO)[0], idn=eye,
                         msk=np.ascontiguousarray(np.broadcast_to(mk[:, None, :], (NB, 128, TP))),
                         pv=pv_fill(L, inp, m, core, [2, 3], [3]),
                         g21=g21, u21=u21, d21=d21, g30=g30, u30=u30, d30=d30, g31=g31, u31=u31, d31=d31,
                         wo=WB["nwo"], cwi=WB["cwi1"], cwo=WB["cwo1"]))
    res = _run(nc, maps)
    out = gather_lat([r["xo"] for r in res])
    return out.astype(np.float32)
```
